# Optimizing a Trainium2 kernel written in Bass

```python
import math
import jax, jax.numpy as jnp
from jax import lax
import numpy as np

D_MODEL = 1024
BATCH = 2
SEQ = 8192
DEPTH = 1

HEAD_DIM = 64
ROT_DIM = HEAD_DIM // 4
ROPE_THETA = 500000.0
BLOCK = 128
EPS = 1e-6
DIL_PAIRS = ((128, 1), (512, 4), (2048, 16))
A_SLOTS = 8
A_HEADS = A_SLOTS * len(DIL_PAIRS)
B_HEADS = 8
B_KV = 2
CMP_LEN = 32
CMP_STRIDE = 16
CMP_HIDDEN = 4 * HEAD_DIM
SEL_LEN = 64
SEL_TOP = 16
WIN_LEN = 512
FORCE = 1e4
D_FF = 4 * D_MODEL
A_QKV = 3 * A_HEADS * HEAD_DIM
B_Q = B_HEADS * HEAD_DIM
B_KV_COLS = 6 * B_KV * HEAD_DIM
B_GATE = 3 * B_HEADS
MERGE_GATE = 2 * D_MODEL
IN_COLS = A_QKV + B_Q + B_KV_COLS + B_GATE + MERGE_GATE

kernel_name = "hybrid_dilated_nsa_gated_block"


def rms_norm(x, g):
    xf = x.astype(jnp.float32)
    y = xf * lax.rsqrt(jnp.mean(xf * xf, axis=-1, keepdims=True) + EPS)
    return (y * g.astype(jnp.float32)).astype(x.dtype)


def rope_tables(seq):
    pos = jnp.arange(seq, dtype=jnp.float32)
    inv = ROPE_THETA ** (-jnp.arange(0, ROT_DIM, 2, dtype=jnp.float32) / ROT_DIM)
    ang = pos[:, None] * inv[None, :]
    return jnp.cos(ang), jnp.sin(ang)


def partial_rotary(x, cos, sin):
    half = ROT_DIM // 2
    c = cos[None, :, None, :].astype(x.dtype)
    s = sin[None, :, None, :].astype(x.dtype)
    x1 = x[..., :half]
    x2 = x[..., half:ROT_DIM]
    return jnp.concatenate([x1 * c - x2 * s, x2 * c + x1 * s, x[..., ROT_DIM:]], axis=-1)


def banded_attention(q, k, v, max_offset):
    n, L, H, hd = q.shape
    hk = k.shape[2]
    rep = H // hk
    n_prev = -(-max_offset // BLOCK)
    nb = -(-L // BLOCK)
    lp = nb * BLOCK
    pad_end = ((0, 0), (0, lp - L), (0, 0), (0, 0))
    q = jnp.pad(q, pad_end)
    front = ((0, 0), (n_prev * BLOCK, lp - L), (0, 0), (0, 0))
    kb = jnp.pad(k, front).reshape(n, nb + n_prev, BLOCK, hk, hd)
    vb = jnp.pad(v, front).reshape(n, nb + n_prev, BLOCK, hk, hd)
    kwin = jnp.concatenate([kb[:, o:o + nb] for o in range(n_prev + 1)], axis=2)
    vwin = jnp.concatenate([vb[:, o:o + nb] for o in range(n_prev + 1)], axis=2)
    qb = q.reshape(n, nb, BLOCK, hk, rep, hd)
    qpos = jnp.arange(nb)[:, None] * BLOCK + jnp.arange(BLOCK)[None, :]
    kpos = (jnp.arange(nb)[:, None] - n_prev) * BLOCK + jnp.arange((n_prev + 1) * BLOCK)[None, :]
    diff = qpos[:, :, None] - kpos[:, None, :]
    mask = (diff >= 0) & (diff <= max_offset) & (kpos[:, None, :] >= 0)
    s = jnp.einsum('bnqkrd,bnwkd->bnkrqw', qb, kwin,
                   preferred_element_type=jnp.float32) * (hd ** -0.5)
    s = jnp.where(mask[None, :, None, None], s, -1e30)
    m = jnp.max(s, axis=-1, keepdims=True)
    p = jnp.exp(s - m)
    den = jnp.sum(p, axis=-1, keepdims=True)
    o = jnp.einsum('bnkrqw,bnwkd->bnqkrd', (p / den).astype(v.dtype), vwin)
    lse = (m + jnp.log(den))[..., 0]
    o = o.reshape(n, lp, H, hd)[:, :L]
    lse = lse.transpose(0, 1, 4, 2, 3).reshape(n, lp, H)[:, :L]
    return o, lse


def dilated_attention(q, k, v):
    b, s_len, _, hd = q.shape
    outs, lses = [], []
    for g, (w, d) in enumerate(DIL_PAIRS):
        sl = slice(g * A_SLOTS, (g + 1) * A_SLOTS)

        def split(t):
            return t[:, :, sl].reshape(b, s_len // d, d, A_SLOTS, hd).transpose(0, 2, 1, 3, 4) \
                .reshape(b * d, s_len // d, A_SLOTS, hd)

        o, lse = banded_attention(split(q), split(k), split(v), w // d)
        outs.append(o.reshape(b, d, s_len // d, A_SLOTS, hd).transpose(0, 2, 1, 3, 4)
                    .reshape(b, s_len, A_SLOTS, hd))
        lses.append(lse.reshape(b, d, s_len // d, A_SLOTS).transpose(0, 2, 1, 3)
                    .reshape(b, s_len, A_SLOTS))
    wts = jax.nn.softmax(jnp.stack(lses, axis=0), axis=0)
    y = jnp.sum(wts[..., None] * jnp.stack(outs, axis=0).astype(jnp.float32), axis=0)
    return y.astype(q.dtype).reshape(b, s_len, A_SLOTS * hd)


def compress(t, pos_emb, w1, w2):
    b, s_len, g, hd = t.shape
    ch = t.reshape(b, s_len // CMP_STRIDE, CMP_STRIDE, g, hd)
    blocks = jnp.concatenate([ch[:, :-1], ch[:, 1:]], axis=2)
    blocks = blocks + pos_emb[None, None, :, None, :]
    nc = blocks.shape[1]
    flat = blocks.transpose(0, 1, 3, 2, 4).reshape(b, nc, g, CMP_LEN * hd)
    return jax.nn.gelu(flat @ w1) @ w2


def nsa_compressed_selected(q, kc, vc, ks, vs):
    b, s_len, hq, hd = q.shape
    g = kc.shape[2]
    rep = hq // g
    nc = kc.shape[1]
    ns = s_len // SEL_LEN
    n_sel = min(SEL_TOP, ns)
    nq = s_len // BLOCK
    scale = hd ** -0.5
    c_start = jnp.arange(nc) * CMP_STRIDE
    s_start = jnp.arange(ns) * SEL_LEN
    overlap = ((c_start[:, None] < s_start[None, :] + SEL_LEN) &
               (c_start[:, None] + CMP_LEN > s_start[None, :])).astype(jnp.float32)
    c_end = c_start + CMP_LEN - 1
    ks_b = ks.reshape(b, ns, SEL_LEN, g, hd).transpose(0, 3, 1, 2, 4)
    vs_b = vs.reshape(b, ns, SEL_LEN, g, hd).transpose(0, 3, 1, 2, 4)
    b_idx = jnp.arange(b)[:, None, None, None]
    g_idx = jnp.arange(g)[None, None, :, None]
    blk = jnp.arange(ns)
    tok = jnp.arange(SEL_LEN)
    qc = q.reshape(b, nq, BLOCK, g, rep, hd).transpose(1, 0, 2, 3, 4, 5)

    def chunk(args):
        q_c, c = args
        t = c * BLOCK + jnp.arange(BLOCK)
        s = jnp.einsum('bqgrd,bngd->bqgrn', q_c, kc, preferred_element_type=jnp.float32) * scale
        cmask = (c_end[None, :] <= t[:, None])[None, :, None, None, :]
        s = jnp.where(cmask, s, -1e30)
        p = jnp.where(cmask, jnp.exp(s - jnp.max(s, axis=-1, keepdims=True)), 0.0)
        p = p / jnp.maximum(jnp.sum(p, axis=-1, keepdims=True), 1e-30)
        o_cmp = jnp.einsum('bqgrn,bngd->bqgrd', p.astype(vc.dtype), vc)
        imp = jnp.einsum('bqgn,nm->bqgm', jnp.sum(p, axis=3), overlap)
        cur = (t // SEL_LEN)[:, None]
        forced = (blk[None, :] == 0) | (blk[None, :] == cur) | (blk[None, :] == cur - 1)
        imp = jnp.where(forced[None, :, None, :], FORCE, imp)
        imp = jnp.where((blk[None, :] > cur)[None, :, None, :], -FORCE, imp)
        _, idx = lax.top_k(imp, n_sel)
        ksel = ks_b[b_idx, g_idx, idx].reshape(b, BLOCK, g, n_sel * SEL_LEN, hd)
        vsel = vs_b[b_idx, g_idx, idx].reshape(b, BLOCK, g, n_sel * SEL_LEN, hd)
        kpos = (idx[..., None] * SEL_LEN + tok).reshape(b, BLOCK, g, n_sel * SEL_LEN)
        smask = (kpos <= t[None, :, None, None])[:, :, :, None, :]
        s2 = jnp.einsum('bqgrd,bqgkd->bqgrk', q_c, ksel, preferred_element_type=jnp.float32) * scale
        s2 = jnp.where(smask, s2, -1e30)
        p2 = jax.nn.softmax(s2, axis=-1)
        o_slc = jnp.einsum('bqgrk,bqgkd->bqgrd', p2.astype(vsel.dtype), vsel)
        return o_cmp, o_slc

    o_cmp, o_slc = lax.map(chunk, (qc, jnp.arange(nq)))
    o_cmp = o_cmp.transpose(1, 0, 2, 3, 4, 5).reshape(b, s_len, hq, hd)
    o_slc = o_slc.transpose(1, 0, 2, 3, 4, 5).reshape(b, s_len, hq, hd)
    return o_cmp, o_slc


def hybrid_layer(x, g_mix, w_in, cmp_pos_k, cmp_w1_k, cmp_w2_k, cmp_pos_v, cmp_w1_v, cmp_w2_v,
                 w_branch_a, w_branch_b, w_out, g_mlp, w_up, w_down, cos, sin):
    b, s_len, d = x.shape
    u = rms_norm(x, g_mix)
    proj = u @ w_in
    o1 = A_QKV
    o2 = o1 + B_Q
    o3 = o2 + B_KV_COLS
    o4 = o3 + B_GATE
    qkv_a, q_b, kv_b, gate_b, gate_m = jnp.split(proj, [o1, o2, o3, o4], axis=-1)
    qkv_a = qkv_a.reshape(b, s_len, 3, A_HEADS, HEAD_DIM)
    qa = partial_rotary(qkv_a[:, :, 0], cos, sin)
    ka = partial_rotary(qkv_a[:, :, 1], cos, sin)
    va = qkv_a[:, :, 2]
    q_b = partial_rotary(q_b.reshape(b, s_len, B_HEADS, HEAD_DIM), cos, sin)
    kv_b = kv_b.reshape(b, s_len, 6, B_KV, HEAD_DIM)
    k_cmp = partial_rotary(kv_b[:, :, 0], cos, sin)
    v_cmp = kv_b[:, :, 1]
    k_slc = partial_rotary(kv_b[:, :, 2], cos, sin)
    v_slc = kv_b[:, :, 3]
    k_win = partial_rotary(kv_b[:, :, 4], cos, sin)
    v_win = kv_b[:, :, 5]
    y_a = dilated_attention(qa, ka, va)
    kc = compress(k_cmp, cmp_pos_k, cmp_w1_k, cmp_w2_k)
    vc = compress(v_cmp, cmp_pos_v, cmp_w1_v, cmp_w2_v)
    o_cmp, o_slc = nsa_compressed_selected(q_b, kc, vc, k_slc, v_slc)
    o_win, _ = banded_attention(q_b, k_win, v_win, WIN_LEN - 1)
    gb = jax.nn.sigmoid(gate_b.reshape(b, s_len, B_HEADS, 3))
    y_b = (gb[..., 0:1] * o_cmp + gb[..., 1:2] * o_slc + gb[..., 2:3] * o_win).reshape(b, s_len, B_Q)
    gm = jax.nn.sigmoid(gate_m.reshape(b, s_len, 2, d))
    merged = gm[:, :, 0] * (y_a @ w_branch_a) + gm[:, :, 1] * (y_b @ w_branch_b)
    x = x + merged @ w_out
    h = rms_norm(x, g_mlp) @ w_up
    x = x + jnp.square(jax.nn.relu(h)) @ w_down
    return x


def setup_inputs(seed: int = 0) -> dict:
    key = jax.random.key(seed)
    ks = jax.random.split(key, 16)
    f = jnp.float32
    L = DEPTH

    def nrm(k, shape, fan_in):
        return jax.random.normal(k, shape, f) * (fan_in ** -0.5)

    hd = HEAD_DIM
    return {
        "x": jax.random.normal(ks[0], (BATCH, SEQ, D_MODEL), f),
        "norm_mix_g": 1.0 + 0.02 * jax.random.normal(ks[1], (L, D_MODEL), f),
        "w_in": nrm(ks[2], (L, D_MODEL, IN_COLS), D_MODEL),
        "cmp_pos_k": 0.02 * jax.random.normal(ks[3], (L, CMP_LEN, hd), f),
        "cmp_w1_k": nrm(ks[4], (L, CMP_LEN * hd, CMP_HIDDEN), CMP_LEN * hd),
        "cmp_w2_k": nrm(ks[5], (L, CMP_HIDDEN, hd), CMP_HIDDEN),
        "cmp_pos_v": 0.02 * jax.random.normal(ks[6], (L, CMP_LEN, hd), f),
        "cmp_w1_v": nrm(ks[7], (L, CMP_LEN * hd, CMP_HIDDEN), CMP_LEN * hd),
        "cmp_w2_v": nrm(ks[8], (L, CMP_HIDDEN, hd), CMP_HIDDEN),
        "w_branch_a": nrm(ks[9], (L, A_SLOTS * hd, D_MODEL), A_SLOTS * hd),
        "w_branch_b": nrm(ks[10], (L, B_Q, D_MODEL), B_Q),
        "w_out": nrm(ks[11], (L, D_MODEL, D_MODEL), D_MODEL),
        "norm_mlp_g": 1.0 + 0.02 * jax.random.normal(ks[12], (L, D_MODEL), f),
        "w_up": nrm(ks[13], (L, D_MODEL, D_FF), D_MODEL),
        "w_down": nrm(ks[14], (L, D_FF, D_MODEL), D_FF),
        "norm_final_g": 1.0 + 0.02 * jax.random.normal(ks[15], (D_MODEL,), f),
    }


def reference(x, norm_mix_g, w_in, cmp_pos_k, cmp_w1_k, cmp_w2_k, cmp_pos_v, cmp_w1_v, cmp_w2_v,
              w_branch_a, w_branch_b, w_out, norm_mlp_g, w_up, w_down, norm_final_g):
    cos, sin = rope_tables(x.shape[1])
    for l in range(DEPTH):
        x = hybrid_layer(x, norm_mix_g[l], w_in[l], cmp_pos_k[l], cmp_w1_k[l], cmp_w2_k[l],
                         cmp_pos_v[l], cmp_w1_v[l], cmp_w2_v[l], w_branch_a[l], w_branch_b[l],
                         w_out[l], norm_mlp_g[l], w_up[l], w_down[l], cos, sin)
    return rms_norm(x, norm_final_g)
```

```python
import os
import numpy as np
import ml_dtypes
from contextlib import ExitStack
import concourse.bass as bass
import concourse.mybir as mybir
from concourse.bass_utils import run_bass_kernel_spmd

F32 = mybir.dt.float32
BF16 = mybir.dt.bfloat16
ALU = mybir.AluOpType
AF = mybir.ActivationFunctionType
NPBF = ml_dtypes.bfloat16

D = 1024
S_LEN = 8192
OWN = 2048
NT = 16
EPS = 1e-6
SCALE = 0.125
IN_COLS = 7960
C_QA, C_KA, C_VA = 0, 1536, 3072
C_QB = 4608
C_KVB = 5120
C_GB = 5888
C_GM = 5912
DILS = (1, 4, 16)

ENGS = ("pe", "act", "dve", "pool")
NDMA = 24


class Res:
    __slots__ = ("lw", "rd", "excl")

    def __init__(self):
        self.lw = None
        self.rd = {}
        self.excl = False


class Tn:
    __slots__ = ("t", "r")

    def __init__(self, t):
        self.t = t
        self.r = Res()


class Sched:
    def __init__(self, nc):
        self.nc = nc
        self.q = {e: [] for e in ENGS + ("sp",)}
        self.cnt = {e: 0 for e in ENGS}
        self.dcnt = [0] * NDMA
        self.seen = {e: {} for e in ENGS + ("sp",)}
        self.dnext = 0

    def _deps(self, reads, writes, mykey=None):
        deps = {}
        for r in reads:
            r = r.r if isinstance(r, Tn) else r
            if r.lw is not None and r.lw[1] > deps.get(r.lw[0], 0):
                deps[r.lw[0]] = r.lw[1]
            if r.excl:
                for k, v in r.rd.items():
                    if k != mykey and v > deps.get(k, 0):
                        deps[k] = v
        for w in writes:
            w = w.r if isinstance(w, Tn) else w
            if w.lw is not None and w.lw[0] != mykey and w.lw[1] > deps.get(w.lw[0], 0):
                deps[w.lw[0]] = w.lw[1]
            for k, v in w.rd.items():
                if v > deps.get(k, 0):
                    deps[k] = v
        return deps

    def _waits(self, eng, deps):
        waits = []
        seen = self.seen[eng]
        for k, v in deps.items():
            if v > seen.get(k, 0):
                waits.append((k, v))
                seen[k] = v
        return waits

    def _mark(self, key, my, reads, writes):
        for r in reads:
            r = r.r if isinstance(r, Tn) else r
            if my > r.rd.get(key, 0):
                r.rd[key] = my
        for w in writes:
            w = w.r if isinstance(w, Tn) else w
            w.lw = (key, my)
            w.rd = {}

    def op(self, eng, fn, reads=(), writes=()):
        deps = self._deps(reads, writes, ("e", eng))
        if eng == "pe":
            deps.pop(("e", "pe"), None)
        self.cnt[eng] += 1
        my = self.cnt[eng]
        key = ("e", eng)
        self.q[eng].append((self._waits(eng, deps), fn, key, my))
        self._mark(key, my, reads, writes)

    def dma(self, fn, reads=(), writes=()):
        deps = self._deps(reads, writes)
        k = self.dnext
        self.dnext = (self.dnext + 1) % NDMA
        key = ("d", k)
        if self.dcnt[k] > 0:
            deps[key] = max(deps.get(key, 0), self.dcnt[k])
        self.dcnt[k] += 16
        my = self.dcnt[k]
        self.q["sp"].append((self._waits("sp", deps), fn, key, my))
        self._mark(key, my, reads, writes)

    def barrier(self):
        allc = {}
        for e in ENGS:
            if self.cnt[e]:
                allc[("e", e)] = self.cnt[e]
        for k in range(NDMA):
            if self.dcnt[k]:
                allc[("d", k)] = self.dcnt[k]
        for e in ENGS + ("sp",):
            w = self._waits(e, dict(allc))
            if w:
                self.q[e].append((w, None, None, 0))

    def emit(self):
        nc = self.nc
        with ExitStack() as es:
            esem = {e: es.enter_context(nc.semaphore("s_" + e)) for e in ENGS}
            dsem = [es.enter_context(nc.semaphore("s_d%d" % i)) for i in range(NDMA)]

            def semof(key):
                return esem[key[1]] if key[0] == "e" else dsem[key[1]]
            fin = {}
            for e in ENGS:
                if self.cnt[e]:
                    fin[("e", e)] = self.cnt[e]
            for k in range(NDMA):
                if self.dcnt[k]:
                    fin[("d", k)] = self.dcnt[k]
            allsems = list(esem.values()) + dsem
            with nc.Block() as b0:
                @b0.sync
                def _(e):
                    for sm in allsems:
                        e.sem_clear(sm)
            block = es.enter_context(nc.Block())

            sig = {e: set() for e in ENGS}
            for name in self.q:
                for waits, fn, key, my in self.q[name]:
                    for (k, v) in waits:
                        if k[0] == "e":
                            sig[k[1]].add(v)
            for e in ENGS:
                if self.cnt[e]:
                    sig[e].add(self.cnt[e])
            rank = {}
            for e in ENGS:
                for i, v in enumerate(sorted(sig[e])):
                    rank[(e, v)] = i + 1

            def wval(k, v):
                return rank[(k[1], v)] if k[0] == "e" else v

            def run(name, engobj, final=False):
                for waits, fn, key, my in self.q[name]:
                    for (k, v) in waits:
                        engobj.wait_ge(semof(k), wval(k, v))
                    if fn is not None:
                        ins = fn(engobj)
                        if key[0] == "d":
                            ins.then_inc(semof(key), 16)
                        elif my in sig[key[1]]:
                            ins.then_inc(semof(key), 1)
                if final:
                    for k, v in fin.items():
                        engobj.wait_ge(semof(k), wval(k, v))

            @block.sync
            def _(e):
                run("sp", e, final=True)

            @block.tensor
            def _(e):
                run("pe", e)

            @block.scalar
            def _(e):
                run("act", e)

            @block.vector
            def _(e):
                run("dve", e)

            @block.gpsimd
            def _(e):
                run("pool", e)


def I(name, *a, **k):
    return lambda e: getattr(e, name)(*a, **k)


class Ring:
    def __init__(self, items):
        self.items = items
        self.i = 0

    def next(self):
        it = self.items[self.i % len(self.items)]
        self.i += 1
        return it


NROPE = 148
BFC = dict(ident=0, tri_diag=128, tri_prev=256, win_far=384, m4=512, e32=2560, ones=4608)
NBFC = 4736
F32C = dict(swap=0, id32=128)
NF32C = 160
PCF = dict(rope=0, thrc=NROPE * 16, pv=NROPE * 16 + 16, crel=NROPE * 16 + 80, hv=NROPE * 16 + 84)
NPCF = NROPE * 16 + 85
PCB = dict(eown=0, hv64=2048)
NPCB = 2112


def _static_tables():
    bf = np.zeros((128, NBFC), np.float32)
    k = np.arange(128)[:, None]
    q = np.arange(128)[None, :]
    bf[:, 0:128] = np.eye(128)
    bf[:, 128:256] = (q >= k)
    bf[:, 256:384] = (q <= k)
    bf[:, 384:512] = (q < k)
    for m in range(4):
        blk = np.zeros((128, 512), np.float32)
        for tq in range(4):
            if tq == m:
                blk[:, tq * 128:(tq + 1) * 128] = (q >= k)
            elif tq > m:
                blk[:, tq * 128:(tq + 1) * 128] = 1.0
        bf[:, 512 + m * 512: 512 + (m + 1) * 512] = blk
    b = np.arange(128)[:, None]
    for kt in range(16):
        i = np.arange(128)[None, :]
        bf[:, 2560 + kt * 128: 2560 + (kt + 1) * 128] = ((b % 32) == 2 * kt + (i >= 64))
    bf[:, 4608:4736] = 1.0
    f = np.zeros((128, NF32C), np.float32)
    f[:, 0:128] = (np.abs(k - q) == 64)
    f[0:32, 128:160] = np.eye(32)
    t16 = np.ascontiguousarray(np.broadcast_to(16.0 * np.arange(2048, dtype=np.float32)[None, :], (128, 2048)))
    return bf.astype(NPBF), f, t16


def _rope_rows(pos):
    inv = (500000.0 ** (-np.arange(0, 16, 2, dtype=np.float32) / np.float32(16))).astype(np.float32)
    ang = (pos.astype(np.float32)[:, None] * inv[None, :]).astype(np.float32)
    return np.concatenate([np.cos(ang), np.sin(ang)], axis=1).astype(np.float32)


def _percore_tables(qtr):
    T0 = OWN * qtr
    i = np.arange(128)
    f = np.zeros((128, NPCF), np.float32)
    rope = np.zeros((128, NROPE, 16), np.float32)
    for t in range(32):
        rope[:, t] = _rope_rows(T0 - OWN + 128 * t + i)
    for r in range(4):
        for j in range(-1, 4):
            rope[:, 32 + r * 5 + j + 1] = _rope_rows(T0 - OWN + 2048 + 512 * j + r + 4 * i)
    for r in range(16):
        for j in range(-1, 1):
            rope[:, 52 + r * 2 + j + 1] = _rope_rows(T0 - OWN + 2048 + 2048 * j + r + 16 * i)
    for kt in range(64):
        rope[:, 84 + kt] = _rope_rows(128 * kt + i)
    f[:, 0:NROPE * 16] = rope.reshape(128, -1)
    for ti in range(16):
        f[:, PCF["thrc"] + ti] = T0 + 128 * ti + i - 31
    for kt in range(64):
        f[:, PCF["pv"] + kt] = 1.0 if 128 * kt < T0 else 0.0
    for bt in range(4):
        f[:, PCF["crel"] + bt] = 16.0 * (16 * (128 * bt + i) + 31 - T0)
    f[:, PCF["hv"]] = 0.0 if qtr == 0 else 1.0
    lo = np.full((128, 16, 128), -3e4, np.float32)
    hi = np.full((128, 16, 128), 3e4, np.float32)
    m = np.arange(128)[None, :]
    for ti in range(16):
        cur = ((T0 + 128 * ti + i) // 64)[:, None]
        forced = (m == 0) | (m == cur) | (m == cur - 1)
        fut = m > cur
        lo[:, ti][forced] = 1e4
        hi[:, ti][forced] = 1e4
        lo[:, ti][fut] = -3e4
        hi[:, ti][fut] = -3e4
    lohi = np.concatenate([lo.reshape(128, -1), hi.reshape(128, -1)], axis=1)
    bfp = np.zeros((128, NPCB), np.float32)
    b = np.arange(128)[:, None]
    for j in range(16):
        ii = np.arange(128)[None, :]
        bfp[:, j * 128:(j + 1) * 128] = (b == 2 * (T0 // 128 + j) + (ii >= 64))
    bfp[:, 2048:2112] = 0.0 if qtr == 0 else 1.0
    return f, lohi.astype(np.float32), bfp.astype(NPBF)


class StopBuild(Exception):
    pass


class Builder:
    def __init__(self, debug=False, stop_after=None):
        self.debug = debug
        self.stop_after = stop_after
        self.nc = nc = bass.Bass("TRN2", target_bir_lowering=False)
        self.S = Sched(nc)
        self._decl = {}
        self._shapes = {
            "x_own": ([OWN, D], F32), "x_halo": ([OWN, D], F32), "x_full": ([S_LEN, D], F32), "w_in": ([D, IN_COLS], F32),
            "g_mix": ([128, 8], F32), "g_mlp": ([128, 8], F32), "g_fin": ([128, D], F32),
            "cmp_w1_k": ([2048, 256], F32), "cmp_w1_v": ([2048, 256], F32), "cmp_w2_k": ([256, 64], F32), "cmp_w2_v": ([256, 64], F32),
            "cmp_pos_k": ([32, 64], F32), "cmp_pos_v": ([32, 64], F32), "w_a": ([512, D], F32), "w_b": ([512, D], F32),
            "w_out": ([D, D], F32), "w_up": ([D, 4096], F32), "w_down": ([4096, D], F32),
            "c_bf": ([128, NBFC], BF16), "c_f32": ([128, NF32C], F32), "c_t16": ([128, 2048], F32),
            "pc_f": ([128, NPCF], F32), "pc_lohi": ([128, 4096], F32), "pc_bf": ([128, NPCB], BF16),
        }
        self.out = nc.dram_tensor("out", [OWN, D], F32, kind="ExternalOutput").ap()
        self.dbg = {}

    def __getattr__(self, name):
        sh = self.__dict__.get("_shapes", {})
        if name in sh:
            if name not in self._decl:
                self._decl[name] = self.nc.dram_tensor(name, list(sh[name][0]), sh[name][1], kind="ExternalInput").ap()
            return self._decl[name]
        raise AttributeError(name)

    def sb(self, es, name, shape, dt):
        return Tn(es.enter_context(self.nc.sbuf_tensor(name, list(shape), dt)))

    def ps(self, es, name, shape, dt):
        t = Tn(es.enter_context(self.nc.psum_tensor(name, list(shape), dt)))
        t.r.excl = True
        return t

    def ring(self, es, name, shape, dt, n):
        return Ring([self.sb(es, "%s%d" % (name, i), shape, dt) for i in range(n)])

    def dump(self, name, tn, ap, shape, dt):
        if not self.debug:
            return
        o = self.nc.dram_tensor("dbg_" + name, list(shape), dt, kind="ExternalOutput").ap()
        self.dbg[name] = True
        self.S.dma(I("dma_start", out=o, in_=ap), reads=[tn])

    def load_wslab(self, src_ap, ncols, gain, kchunks=8):
        S = self.S
        st = self.wst.next()
        sl = self.wsl.next()
        S.dma(I("dma_start", out=st.t[:, 0:kchunks, 0:ncols], in_=src_ap.rearrange("(c p) n -> p c n", p=128)), writes=[st])
        if gain is not None:
            gb = gain.t[:, 0:kchunks].unsqueeze(2).to_broadcast([128, kchunks, ncols])
            S.op("pool", I("tensor_tensor", out=sl.t[:, 0:kchunks, 0:ncols], in0=st.t[:, 0:kchunks, 0:ncols], in1=gb, op=ALU.mult),
                 reads=[st, gain], writes=[sl])
        else:
            S.op("pool", I("tensor_copy", out=sl.t[:, 0:kchunks, 0:ncols], in_=st.t[:, 0:kchunks, 0:ncols]), reads=[st], writes=[sl])
        return sl

    def cast_into(self, dst_tn, dst_ap_fn, src_ap, kchunks, ncols, gain, piece=512):
        S = self.S
        for c0 in range(0, ncols, piece):
            n = min(piece, ncols - c0)
            st = self.wst.next()
            S.dma(I("dma_start", out=st.t[:, 0:kchunks, 0:n], in_=src_ap[:, c0:c0 + n].rearrange("(c p) n -> p c n", p=128)), writes=[st])
            dst = dst_ap_fn(c0, n)
            engs = getattr(self, "cast_engs", ("pool",))
            self._ci = getattr(self, "_ci", 0) + 1
            ce = engs[self._ci % len(engs)]
            if gain is not None:
                gb = gain.t[:, 0:kchunks].unsqueeze(2).to_broadcast([128, kchunks, n])
                S.op(ce, I("tensor_tensor", out=dst, in0=st.t[:, 0:kchunks, 0:n], in1=gb, op=ALU.mult),
                     reads=[st, gain], writes=[dst_tn])
            else:
                S.op(ce, I("tensor_copy", out=dst, in_=st.t[:, 0:kchunks, 0:n]), reads=[st], writes=[dst_tn])

    def norm_tile(self, x_ap, ut_tn, ut_ap):
        S = self.S
        xt = self.xring.next()
        S.dma(I("dma_start", out=xt.t[:], in_=x_ap), writes=[xt])
        self.norm_sb(xt, xt.t[:], ut_tn, ut_ap)

    def norm_sb(self, xt, x_sb_ap, ut_tn, ut_ap, keep_rstd=None):
        S = self.S
        jk = self.junk.next()
        st = self.stat.next()
        S.op("act", I("activation", out=jk.t[:], in_=x_sb_ap, func=AF.Square, accum_out=st.t[:, 0:1]), reads=[xt], writes=[jk, st])
        S.op("act", I("activation", out=st.t[:, 1:2], in_=st.t[:, 0:1], func=AF.Sqrt, scale=1.0 / D, bias=self.epsc.t[:, 0:1]), reads=[st, self.epsc], writes=[st])
        S.op("dve", I("reciprocal", out=st.t[:, 2:3], in_=st.t[:, 1:2]), reads=[st], writes=[st])
        xn = self.xnring.next()
        S.op("dve", I("tensor_scalar", out=xn.t[:], in0=x_sb_ap, scalar1=st.t[:, 2:3], scalar2=None, op0=ALU.mult), reads=[xt, st], writes=[xn])
        pt = self.psT
        for c in range(8):
            S.op("pe", I("transpose", out=pt.t[:, c * 128:(c + 1) * 128], in_=xn.t[:, c * 128:(c + 1) * 128], identity=self.ident), reads=[xn, self.cbf], writes=[pt])
        S.op("act", I("copy", out=ut_ap, in_=pt.t[:, 0:1024].rearrange("p (c t) -> p c t", c=8)), reads=[pt], writes=[ut_tn])
        return st

    def proj_tm(self, lhs_fn, lhs_tn, slab, c0, ncols, ps):
        for c in range(8):
            self.S.op("pe", I("matmul", ps.t[:, 0:ncols], lhsT=lhs_fn(c), rhs=slab.t[:, c, c0:c0 + ncols], start=(c == 0), stop=(c == 7)),
                      reads=[lhs_tn, slab], writes=[ps])

    def rope_evac(self, ps, pc0, nh, ropeidx, dst_tn, dst_ap, perm=False):
        S = self.S
        ro = PCF["rope"] + ropeidx * 16
        ta = self.rtmp.next()
        if not perm:
            psv = ps.t[:, pc0:pc0 + 64 * nh].rearrange("p (h d) -> p h d", h=nh)
            dv = dst_ap.rearrange("p (h d) -> p h d", h=nh)
            tav = ta.t[:, 0:nh * 32].rearrange("p (h d) -> p h d", h=nh)
            cos1 = self.pcf.t[:, ro:ro + 8].unsqueeze(1).to_broadcast([128, nh, 8])
            sin1 = self.pcf.t[:, ro + 8:ro + 16].unsqueeze(1).to_broadcast([128, nh, 8])
            sl = lambda v, a, b: v[:, :, a:b]
        else:
            psv = ps.t[:, pc0:pc0 + 512].rearrange("p (two hp d) -> p two hp d", two=2, hp=4)
            dv = dst_ap.rearrange("p (hp two d) -> p two hp d", two=2, hp=4)
            tav = ta.t[:, 0:256].rearrange("p (two hp d) -> p two hp d", two=2, hp=4)
            cos1 = self.pcf.t[:, ro:ro + 8].unsqueeze(1).unsqueeze(1).to_broadcast([128, 2, 4, 8])
            sin1 = self.pcf.t[:, ro + 8:ro + 16].unsqueeze(1).unsqueeze(1).to_broadcast([128, 2, 4, 8])
            sl = lambda v, a, b: v[:, :, :, a:b]
        S.op("dve", I("tensor_copy", out=sl(dv, 16, 64), in_=sl(psv, 16, 64)), reads=[ps], writes=[dst_tn])
        S.op("dve", I("tensor_tensor", out=sl(tav, 0, 8), in0=sl(psv, 0, 8), in1=cos1, op=ALU.mult), reads=[ps, self.pcf], writes=[ta])
        S.op("dve", I("tensor_tensor", out=sl(tav, 8, 16), in0=sl(psv, 8, 16), in1=cos1, op=ALU.mult), reads=[ps, self.pcf], writes=[ta])
        S.op("dve", I("tensor_tensor", out=sl(tav, 16, 24), in0=sl(psv, 8, 16), in1=sin1, op=ALU.mult), reads=[ps, self.pcf], writes=[ta])
        S.op("dve", I("tensor_tensor", out=sl(tav, 24, 32), in0=sl(psv, 0, 8), in1=sin1, op=ALU.mult), reads=[ps, self.pcf], writes=[ta])
        S.op("dve", I("tensor_tensor", out=sl(dv, 0, 8), in0=sl(tav, 0, 8), in1=sl(tav, 16, 24), op=ALU.subtract), reads=[ta], writes=[dst_tn])
        S.op("dve", I("tensor_tensor", out=sl(dv, 8, 16), in0=sl(tav, 8, 16), in1=sl(tav, 24, 32), op=ALU.add), reads=[ta], writes=[dst_tn])

    def build(self):
        nc, S = self.nc, self.S
        with ExitStack() as es0:
            self.cbf = self.sb(es0, "cbf", [128, NBFC], BF16)
            self.cf32 = self.sb(es0, "cf32", [128, NF32C], F32)
            self.pcf = self.sb(es0, "pcf", [128, NPCF], F32)
            self.pcb = self.sb(es0, "pcb", [128, NPCB], BF16)
            self.gmix = self.sb(es0, "gmix", [128, 8], F32)
            self.gmlp = self.sb(es0, "gmlp", [128, 8], F32)
            self.epsc = self.sb(es0, "epsc", [128, 1], F32)
            S.dma(I("dma_start", out=self.cbf.t[:], in_=self.c_bf), writes=[self.cbf])
            S.dma(I("dma_start", out=self.cf32.t[:], in_=self.c_f32), writes=[self.cf32])
            S.dma(I("dma_start", out=self.pcf.t[:], in_=self.pc_f), writes=[self.pcf])
            S.dma(I("dma_start", out=self.pcb.t[:], in_=self.pc_bf), writes=[self.pcb])
            S.dma(I("dma_start", out=self.gmix.t[:], in_=self.g_mix), writes=[self.gmix])
            S.dma(I("dma_start", out=self.gmlp.t[:], in_=self.g_mlp), writes=[self.gmlp])
            S.op("dve", I("memset", self.epsc.t[:], EPS), writes=[self.epsc])
            self.ident = self.cbf.t[:, 0:128]
            self.xring = self.ring(es0, "xr", [128, D], F32, 2)
            self.junk = self.ring(es0, "jk", [128, D], BF16, 1)
            self.stat = self.ring(es0, "st", [128, 4], F32, 4)
            self.xnring = self.ring(es0, "xn", [128, D], BF16, 2)
            self.rtmp = self.ring(es0, "rtmp", [128, 256], F32, 2)
            self.ptr = self.ring(es0, "ptr", [128, 512], BF16, 3)
            self.psA = Ring([self.ps(es0, "psA%d" % i, [128, 512], F32) for i in range(2)])
            self.psT = self.ps(es0, "psT", [128, 1024], BF16)
            self.psS = Ring([self.ps(es0, "psS%d" % i, [128, 512], F32) for i in range(2)])
            self.psO = Ring([self.ps(es0, "psO%d" % i, [128, 512], F32) for i in range(2)])
            self.psX = self.ps(es0, "psX", [128, 512], F32)
            self.yaT = self.sb(es0, "yaT", [128, 4, OWN], BF16)
            self.stopped = False
            self.phase_A(es0)
            if self.stopped:
                S.barrier()
                if self.stop_after in ("A3", "A"):
                    self.dump("yaT", self.yaT, self.yaT.t[:], [128, 4, OWN], BF16)
                self.fake_out()
                S.emit()
                return nc
            self.ybT = self.sb(es0, "ybT", [128, 4, OWN], BF16)
            S.barrier()
            if self.stop_after == "A":
                self.dump("yaT", self.yaT, self.yaT.t[:], [128, 4, OWN], BF16)
                self.fake_out()
                S.emit()
                return nc
            self.phase_B(es0)
            S.barrier()
            if self.stopped:
                self.fake_out()
                S.emit()
                return nc
            if self.stop_after == "B":
                self.dump("yaT", self.yaT, self.yaT.t[:], [128, 4, OWN], BF16)
                self.dump("ybT", self.ybT, self.ybT.t[:], [128, 4, OWN], BF16)
                self.fake_out()
                S.emit()
                return nc
            self.phase_C(es0)
            if self.debug:
                self.dump("yaT", self.yaT, self.yaT.t[:], [128, 4, OWN], BF16)
                self.dump("ybT", self.ybT, self.ybT.t[:], [128, 4, OWN], BF16)
            S.emit()
        return nc

    def fake_out(self):
        S = self.S
        xt = self.xring.next()
        for t in range(NT):
            S.dma(I("dma_start", out=xt.t[:], in_=self.x_own[t * 128:(t + 1) * 128, :]), writes=[xt])
            S.dma(I("dma_start", out=self.out[t * 128:(t + 1) * 128, :], in_=xt.t[:]), reads=[xt])

    def attn_unit(self, score_mms, n, mask_fn, pv_list):
        S = self.S
        if getattr(self, "_collect", None) is not None:
            self._collect.append((score_mms, n, mask_fn, pv_list, None))
            return
        pss = self.psS.next()
        for i, (l, r, rd) in enumerate(score_mms):
            S.op("pe", I("matmul", pss.t[:, 0:n], lhsT=l, rhs=r, start=(i == 0), stop=(i == len(score_mms) - 1)),
                 reads=rd, writes=[pss])
        pt = self.ptr.next()
        S.op("act", I("activation", out=pt.t[:, 0:n], in_=pss.t[:, 0:n], func=AF.Exp, scale=SCALE), reads=[pss], writes=[pt])
        if mask_fn is not None:
            mask_fn(pt)
        for (pso, out_ap, vaug, c0, ncol, st, sp, rd) in pv_list:
            S.op("pe", I("matmul", out_ap, lhsT=vaug, rhs=pt.t[:, c0:c0 + ncol], start=st, stop=sp),
                 reads=[pt] + rd, writes=[pso])

    def attn_seq(self, units):
        S = self.S
        prev = None
        for u in list(units) + [None]:
            cur = None
            if u is not None:
                score_mms, n = u[0], u[1]
                pss = self.psS.next()
                for i, (l, r, rd) in enumerate(score_mms):
                    S.op("pe", I("matmul", pss.t[:, 0:n], lhsT=l, rhs=r, start=(i == 0), stop=(i == len(score_mms) - 1)), reads=rd, writes=[pss])
                cur = (u, pss)
            if prev is not None:
                (pu, ppss) = prev
                n = pu[1]
                pt = self.ptr.next()
                S.op("act", I("activation", out=pt.t[:, 0:n], in_=ppss.t[:, 0:n], func=AF.Exp, scale=SCALE), reads=[ppss], writes=[pt])
                if pu[2] is not None:
                    pu[2](pt)
                for (pso, out_ap, vaug, c0, ncol, st, sp, rd) in pu[3]:
                    S.op("pe", I("matmul", out_ap, lhsT=vaug, rhs=pt.t[:, c0:c0 + ncol], start=st, stop=sp), reads=[pt] + rd, writes=[pso])
                if len(pu) > 4 and pu[4] is not None:
                    pu[4]()
            prev = cur

    def mask_mul(self, pt, c0, n, mask_ap):
        self.S.op("dve", I("tensor_tensor", out=pt.t[:, c0:c0 + n], in0=pt.t[:, c0:c0 + n], in1=mask_ap, op=ALU.mult), reads=[pt, self.cbf], writes=[pt])

    def phase_A(self, es0):
        S = self.S
        with ExitStack() as esA:
            self.phase_A_body(esA)
        S.barrier()

    def phase_A_body(self, esA):
        S = self.S
        self.uTh = self.sb(esA, "uTh", [128, 8, OWN], BF16)
        self.uTo = self.sb(esA, "uTo", [128, 8, OWN], BF16)
        self.wst = self.ring(esA, "wstA", [128, 8, 384], F32, 2)
        self.wsl = self.ring(esA, "wslA", [128, 8, 384], BF16, 2)
        for t in range(NT):
            self.norm_tile(self.x_halo[t * 128:(t + 1) * 128, :], self.uTh, self.uTh.t[:, :, t * 128:(t + 1) * 128])
        for t in range(NT):
            self.norm_tile(self.x_own[t * 128:(t + 1) * 128, :], self.uTo, self.uTo.t[:, :, t * 128:(t + 1) * 128])
        if self.stop_after == "A0":
            self.stopped = True
            return
        with ExitStack() as es:
            self.phase_A_inner(es)

    def phase_A_inner(self, es):
        S = self.S
        if True:
            qT = self.sb(es, "a_qT", [128, OWN], BF16)
            kT = self.sb(es, "a_kT", [128, 32 * 128], BF16)
            vaug = self.sb(es, "a_v", [128, 32, 2, 128], BF16)
            qk = self.ring(es, "a_qk", [128, 256], BF16, 2)
            if os.environ.get("PADLOW"):
                pad = self.sb(es, "a_pad", [128, int(os.environ["PADLOW"]) * 256], F32)
            acc = [self.sb(es, "a_acc%d" % i, [128, OWN], F32) for i in range(2)]
            rd_ = self.ring(es, "a_rd", [128, 512], F32, 2)
            ones64 = self.cbf.t[:, BFC["ones"]:BFC["ones"] + 64]
            hv64 = self.pcb.t[:, PCB["hv64"]:PCB["hv64"] + 64]
            hvcol = self.pcf.t[:, PCF["hv"]:PCF["hv"] + 1]
            for p in range(4):
                for g, d in enumerate(DILS):
                    nt = NT // d
                    st = self.wst.next()
                    sl = self.wsl.next()
                    for i, cb in enumerate((C_QA, C_KA, C_VA)):
                        c0 = cb + g * 512 + p * 128
                        for kc in range(8):
                            S.dma(I("dma_start", out=st.t[:, kc, i * 128:(i + 1) * 128], in_=self.w_in[kc * 128:(kc + 1) * 128, c0:c0 + 128]), writes=[st])
                    gb = self.gmix.t[:, 0:8].unsqueeze(2).to_broadcast([128, 8, 384])
                    if int(os.environ.get("A1CUT", "99")) >= 0:
                        S.op("pool", I("tensor_tensor", out=sl.t[:, :, 0:384], in0=st.t[:, :, 0:384], in1=gb, op=ALU.mult), reads=[st, self.gmix], writes=[sl])
                    if int(os.environ.get("A1CUT", "99")) <= 0:
                        self.stopped = True
                        return
                    for r in range(d):
                        for j in range(-1, nt):
                            slot = r * (nt + 1) + j + 1
                            start = 2048 + 128 * d * j + r
                            if start < 2048:
                                ut, s0 = self.uTh, start
                            else:
                                ut, s0 = self.uTo, start - 2048
                            lhs = lambda c, ut=ut, s0=s0, d=d: ut.t[:, c, s0:s0 + 127 * d + 1:d]
                            ridx = (15 + slot) if g == 0 else ((32 + slot) if g == 1 else (52 + slot))
                            ps = self.psA.next()
                            halo = (j == -1)
                            if halo:
                                self.proj_tm(lhs, ut, sl, 128, 256, ps)
                                kc0, vc0 = 0, 128
                            else:
                                self.proj_tm(lhs, ut, sl, 0, 384, ps)
                                kc0, vc0 = 128, 256

                            CUT = int(os.environ.get("A1CUT", "99"))
                            if CUT <= 1:
                                continue
                            t = qk.next()
                            if not halo:
                                self.rope_evac(ps, 0, 4, ridx, t, t.t[:, 0:256])
                            else:
                                self.rope_evac(ps, kc0, 2, ridx, t, t.t[:, 128:256])
                            if CUT <= 2:
                                continue
                            vsrc = ps.t[:, vc0:vc0 + 128].rearrange("p (h d) -> p h d", h=2)
                            if halo:
                                S.op("dve", I("tensor_scalar", out=vaug.t[:, slot, :, 0:64], in0=vsrc, scalar1=hvcol, scalar2=None, op0=ALU.mult), reads=[ps, self.pcf], writes=[vaug])
                                for hh in range(2):
                                    S.op("pool", I("tensor_copy", out=vaug.t[:, slot, hh, 64:128], in_=hv64), reads=[self.pcb], writes=[vaug])
                            else:
                                S.op("act", I("copy", out=vaug.t[:, slot, :, 0:64], in_=vsrc), reads=[ps], writes=[vaug])
                                for hh in range(2):
                                    S.op("pool", I("tensor_copy", out=vaug.t[:, slot, hh, 64:128], in_=ones64), reads=[self.cbf], writes=[vaug])
                            if CUT <= 3:
                                continue
                            pt = self.psT
                            if not halo:
                                S.op("pe", I("transpose", out=pt.t[:, 0:128], in_=t.t[:, 0:128], identity=self.ident), reads=[t, self.cbf], writes=[pt])
                            S.op("pe", I("transpose", out=pt.t[:, 128:256], in_=t.t[:, 128:256], identity=self.ident), reads=[t, self.cbf], writes=[pt])
                            if not halo:
                                qi = r * nt + j
                                S.op("act", I("copy", out=qT.t[:, qi * 128:(qi + 1) * 128], in_=pt.t[:, 0:128]), reads=[pt], writes=[qT])
                            S.op("dve", I("tensor_copy", out=kT.t[:, slot * 128:(slot + 1) * 128], in_=pt.t[:, 128:256]), reads=[pt], writes=[kT])
                    if self.stop_after == "A1" and int(os.environ.get("A1CUT", "99")) < 99:
                        self.stopped = True
                        return
                    if self.stop_after == "A1":
                        self.dump("qT", qT, qT.t[:], [128, OWN], BF16)
                        self.dump("kT", kT, kT.t[:, 0:17 * 128], [128, 17 * 128], BF16)
                        self.dump("vaug", vaug, vaug.t[:, 0:17], [128, 17, 2, 128], BF16)
                        self.stopped = True
                        return
                    for hh in range(2):
                        pb = 64 * hh
                        banks = {}
                        units = []
                        for r in range(d):
                            for j in range(-1, nt):
                                slot = r * (nt + 1) + j + 1
                                qlo = max(j, 0)
                                qhi = min(j + 1, nt - 1)
                                nq = qhi - qlo + 1
                                qc0 = (r * nt + qlo) * 128
                                n = nq * 128
                                if j == -1:
                                    mk = [(0, 128, self.cbf.t[:, BFC["tri_prev"]:BFC["tri_prev"] + 128])]
                                elif nq == 1:
                                    mk = [(0, 128, self.cbf.t[:, BFC["tri_diag"]:BFC["tri_diag"] + 128])]
                                else:
                                    mk = [(0, 256, self.cbf.t[:, BFC["tri_diag"]:BFC["tri_diag"] + 256])]

                                def mask_fn(pt, mk=mk):
                                    for (c0, nn, ap) in mk:
                                        self.mask_mul(pt, c0, nn, ap)
                                pv = []
                                for qt in range(qlo, qhi + 1):
                                    qi = r * nt + qt
                                    if (qt == j + 1) and (qi % 4 == 0):
                                        banks[qi // 4] = self.psO.next()
                                    pso = banks[qi // 4]
                                    col = (qi % 4) * 128
                                    pv.append((pso, pso.t[:, col:col + 128], vaug.t[:, slot, hh, :], (qt - qlo) * 128, 128, qt == j + 1, qt == j, [vaug]))
                                after = None
                                if j >= 0 and (r * nt + j) % 4 == 3:
                                    bk = (r * nt + j) // 4
                                    pso = banks[bk]
                                    av = acc[hh].t[:]
                                    if d == 1:
                                        dst = av[:, bk * 512:(bk + 1) * 512]
                                        src = pso.t[:, 0:512]
                                    elif d == 4:
                                        dst = av.rearrange("p (i r) -> p r i", r=4)[:, r, :]
                                        src = pso.t[:, 0:512]
                                    else:
                                        dst = av.rearrange("p (i r) -> p r i", r=16)[:, 4 * bk:4 * bk + 4, :]
                                        src = pso.t[:, 0:512].rearrange("p (r i) -> p r i", r=4)

                                    def after(dst=dst, src=src, pso=pso, hh=hh, g=g):
                                        if g == 0:
                                            S.op("act", I("copy", out=dst, in_=src), reads=[pso], writes=[acc[hh]])
                                        else:
                                            S.op("dve", I("tensor_tensor", out=dst, in0=dst, in1=src, op=ALU.add), reads=[pso, acc[hh]], writes=[acc[hh]])
                                units.append(([(kT.t[pb:pb + 64, slot * 128:(slot + 1) * 128], qT.t[pb:pb + 64, qc0:qc0 + n], [kT, qT])], n, mask_fn, pv, after))
                        self.attn_seq(units)
                if self.stop_after == "A2":
                    self.stopped = True
                    return
                for hh in range(2):
                    for c in range(4):
                        self.finalize(acc[hh], acc[hh].t[:, c * 512:(c + 1) * 512], hh, None, rd_, self.yaT, self.yaT.t[:, p, c * 512:(c + 1) * 512], first=True, last=True, ybacc=None)
                if self.stop_after == "A3":
                    self.stopped = True
                    return
        S.barrier()

    def finalize(self, src_tn, src_ap, hh, gate_row, rdring, dst_tn, dst_ap, first, last, ybacc):
        S = self.S
        psx = self.psX
        S.op("pe", I("matmul", psx.t[:, 0:512], lhsT=self.cf32.t[:, 0:128], rhs=src_ap, start=True, stop=True), reads=[src_tn, self.cf32], writes=[psx])
        rd = rdring.next()
        lo, hi = 64 * hh, 64 * hh + 64
        if hh == 0:
            den = psx.t[0:64, 0:512]
            num = src_ap[0:64, :]
        else:
            den = src_ap[64:128, :]
            num = psx.t[64:128, 0:512]
        S.op("dve", I("tensor_scalar", out=rd.t[lo:hi, :], in0=den, scalar1=1e-30, scalar2=None, op0=ALU.max), reads=[psx, src_tn], writes=[rd])
        S.op("dve", I("reciprocal", out=rd.t[lo:hi, :], in_=rd.t[lo:hi, :]), reads=[rd], writes=[rd])
        if gate_row is None:
            S.op("dve", I("tensor_tensor", out=dst_ap[lo:hi, :], in0=num, in1=rd.t[lo:hi, :], op=ALU.mult), reads=[psx, src_tn, rd], writes=[dst_tn])
            return
        S.op("dve", I("tensor_tensor", out=rd.t[lo:hi, :], in0=num, in1=rd.t[lo:hi, :], op=ALU.mult), reads=[psx, src_tn, rd], writes=[rd])
        gsel, gbT, gcols = gate_row
        S.op("pe", I("matmul", psx.t[:, 0:512], lhsT=gsel.t[:], rhs=gcols, start=True, stop=True), reads=[gbT, gsel, rd], writes=[psx])
        if first:
            S.op("dve", I("tensor_tensor", out=ybacc.t[lo:hi, :], in0=rd.t[lo:hi, :], in1=psx.t[lo:hi, 0:512], op=ALU.mult), reads=[psx, rd], writes=[ybacc])
        else:
            S.op("dve", I("tensor_tensor", out=rd.t[lo:hi, :], in0=rd.t[lo:hi, :], in1=psx.t[lo:hi, 0:512], op=ALU.mult), reads=[psx, rd], writes=[rd])
            if last:
                S.op("dve", I("tensor_tensor", out=dst_ap[lo:hi, :], in0=rd.t[lo:hi, :], in1=ybacc.t[lo:hi, :], op=ALU.add), reads=[rd, ybacc], writes=[dst_tn])
            else:
                S.op("dve", I("tensor_tensor", out=ybacc.t[lo:hi, :], in0=rd.t[lo:hi, :], in1=ybacc.t[lo:hi, :], op=ALU.add), reads=[rd, ybacc], writes=[ybacc])

    def phase_B(self, es0):
        S = self.S
        cbf, pcf, pcb = self.cbf, self.pcf, self.pcb
        ones64 = cbf.t[:, BFC["ones"]:BFC["ones"] + 64]
        ones2 = cbf.t[:, BFC["ones"]:BFC["ones"] + 128].rearrange("p (g d) -> p g d", g=2)
        with ExitStack() as esB:
            kslcT = self.sb(esB, "b_kslcT", [128, 48 * 128], BF16)
            vslc = self.sb(esB, "b_vslc", [128, 48, 2, 128], BF16)
            kcT = self.sb(esB, "b_kcT", [128, 512], BF16)
            vc = self.sb(esB, "b_vc", [128, 4, 2, 128], BF16)
            S.op("pool", I("memset", kcT.t[:], 0.0), writes=[kcT])
            S.op("pool", I("memset", vc.t[:], 0.0), writes=[vc])
            with ExitStack() as es:
                self.wst = self.ring(es, "wstB", [128, 8, 256], F32, 2)
                kcmpT = self.sb(es, "b_kcmpT", [128, S_LEN], BF16)
                vcmpT = self.sb(es, "b_vcmpT", [128, S_LEN], BF16)
                slab = self.sb(es, "b_slab", [128, 8, 512], BF16)
                uTt = self.ring(es, "b_uTt", [128, 8, 128], BF16, 2)
                tm = self.ring(es, "b_tm", [128, 512], BF16, 2)
                for dcol, scol in ((0, 0), (128, 256), (256, 128), (384, 384)):
                    self.cast_into(slab, lambda c0, n, dcol=dcol: slab.t[:, :, dcol + c0:dcol + c0 + n], self.w_in[:, C_KVB + scol:C_KVB + scol + 128], 8, 128, self.gmix, piece=128)
                for kt in range(64):
                    u = uTt.next()
                    self.norm_tile(self.x_full[kt * 128:(kt + 1) * 128, :], u, u.t[:])
                    ps = self.psA.next()
                    self.proj_tm(lambda c, u=u: u.t[:, c, :], u, slab, 0, 512, ps)
                    t = tm.next()
                    self.rope_evac(ps, 0, 4, 84 + kt, t, t.t[:, 0:256])
                    S.op("act", I("copy", out=t.t[:, 256:384], in_=ps.t[:, 256:384]), reads=[ps], writes=[t])
                    if kt < 48:
                        pvc = pcf.t[:, PCF["pv"] + kt:PCF["pv"] + kt + 1]
                        S.op("dve", I("tensor_scalar", out=vslc.t[:, kt, :, 0:64], in0=ps.t[:, 384:512].rearrange("p (g d) -> p g d", g=2), scalar1=pvc, scalar2=None, op0=ALU.mult), reads=[ps, pcf], writes=[vslc])
                        S.op("pool", I("tensor_scalar", out=vslc.t[:, kt, :, 64:128], in0=ones2, scalar1=pvc, scalar2=None, op0=ALU.mult), reads=[cbf, pcf], writes=[vslc])
                    pt = self.psT
                    for k in range(3):
                        S.op("pe", I("transpose", out=pt.t[:, k * 128:(k + 1) * 128], in_=t.t[:, k * 128:(k + 1) * 128], identity=self.ident), reads=[t, cbf], writes=[pt])
                    S.op("act", I("copy", out=kcmpT.t[:, kt * 128:(kt + 1) * 128], in_=pt.t[:, 0:128]), reads=[pt], writes=[kcmpT])
                    S.op("act", I("copy", out=vcmpT.t[:, kt * 128:(kt + 1) * 128], in_=pt.t[:, 256:384]), reads=[pt], writes=[vcmpT])
                    if kt < 48:
                        S.op("act", I("copy", out=kslcT.t[:, kt * 128:(kt + 1) * 128], in_=pt.t[:, 128:256]), reads=[pt], writes=[kslcT])
                if self.stop_after == "B2":
                    self.dump("kslcT", kslcT, kslcT.t[:], [128, 48 * 128], BF16)
                    self.dump("kcmpT", kcmpT, kcmpT.t[:], [128, S_LEN], BF16)
                    self.dump("vslc", vslc, vslc.t[:], [128, 48, 2, 128], BF16)
                    self.stopped = True
                    return
                w1sb = self.sb(es, "b_w1", [128, 32, 256], BF16)
                w2sb = self.sb(es, "b_w2", [128, 2, 128], BF16)
                posb = self.sb(es, "b_posb", [32, 128], BF16)
                posf = self.sb(es, "b_posf", [32, 64], F32)
                posT = self.sb(es, "b_posT", [128, 32], BF16)
                b1sb = self.sb(es, "b_b1", [128, 2], F32)
                gel = [self.sb(es, "b_gel%d" % i, [128, 512], BF16) for i in range(2)]
                hA = self.sb(es, "b_hA", [128, 512], F32)
                hB = self.sb(es, "b_hB", [128, 512], F32)
                for kv in range(2):
                    src = kcmpT if kv == 0 else vcmpT
                    w1d = self.cmp_w1_k if kv == 0 else self.cmp_w1_v
                    w2d = self.cmp_w2_k if kv == 0 else self.cmp_w2_v
                    posd = self.cmp_pos_k if kv == 0 else self.cmp_pos_v
                    w1v = w1d.rearrange("(j d) h -> d j h", d=64)
                    for j0 in range(0, 32, 8):
                        st = self.wst.next()
                        for half in range(2):
                            S.dma(I("dma_start", out=st.t[64 * half:64 * half + 64, :, :], in_=w1v[:, j0:j0 + 8, :]), writes=[st])
                        S.op("pool", I("tensor_copy", out=w1sb.t[:, j0:j0 + 8, :], in_=st.t[:, :, :]), reads=[st], writes=[w1sb])
                    st = self.wst.next()
                    S.dma(I("dma_start", out=st.t[:, 0:2, 0:64], in_=w2d.rearrange("(c p) n -> p c n", p=128)), writes=[st])
                    S.op("pool", I("tensor_copy", out=w2sb.t[:, :, 0:64], in_=st.t[:, 0:2, 0:64]), reads=[st], writes=[w2sb])
                    S.op("pool", I("tensor_copy", out=w2sb.t[:, :, 64:128], in_=st.t[:, 0:2, 0:64]), reads=[st], writes=[w2sb])
                    S.dma(I("dma_start", out=posf.t[:], in_=posd), writes=[posf])
                    S.op("dve", I("tensor_copy", out=posb.t[:, 0:64], in_=posf.t[:]), reads=[posf], writes=[posb])
                    S.op("dve", I("tensor_copy", out=posb.t[:, 64:128], in_=posf.t[:]), reads=[posf], writes=[posb])
                    pt = self.psT
                    S.op("pe", I("transpose", out=pt.t[:, 0:32], in_=posb.t[:], identity=cbf.t[0:32, 0:32]), reads=[posb, cbf], writes=[pt])
                    S.op("act", I("copy", out=posT.t[:], in_=pt.t[:, 0:32]), reads=[pt], writes=[posT])
                    psx = self.psX
                    for mh in range(2):
                        for j in range(32):
                            S.op("pe", I("matmul", psx.t[:, mh:mh + 1], lhsT=w1sb.t[0:64, j, mh * 128:(mh + 1) * 128], rhs=posT.t[0:64, j:j + 1], start=(j == 0), stop=(j == 31)), reads=[w1sb, posT], writes=[psx])
                    S.op("dve", I("tensor_copy", out=b1sb.t[:], in_=psx.t[:, 0:2]), reads=[psx], writes=[b1sb])
                    for g in range(2):
                        pb = 64 * g
                        for mh in range(2):
                            ps = self.psA.next()
                            for j in range(32):
                                S.op("pe", I("matmul", ps.t[:, 0:511], lhsT=w1sb.t[pb:pb + 64, j, mh * 128:(mh + 1) * 128], rhs=src.t[pb:pb + 64, j:j + 16 * 510 + 1:16], start=(j == 0), stop=(j == 31)), reads=[w1sb, src], writes=[ps])
                            S.op("act", I("activation", out=hA.t[:, 0:511], in_=ps.t[:, 0:511], func=AF.Identity, bias=b1sb.t[:, mh:mh + 1]), reads=[ps, b1sb], writes=[hA])
                            S.op("dve", I("tensor_tensor", out=hB.t[:, 0:511], in0=hA.t[:, 0:511], in1=hA.t[:, 0:511], op=ALU.mult), reads=[hA], writes=[hB])
                            S.op("dve", I("tensor_scalar", out=hB.t[:, 0:511], in0=hB.t[:, 0:511], scalar1=0.044715, scalar2=1.0, op0=ALU.mult, op1=ALU.add), reads=[hB], writes=[hB])
                            S.op("dve", I("tensor_tensor", out=hB.t[:, 0:511], in0=hB.t[:, 0:511], in1=hA.t[:, 0:511], op=ALU.mult), reads=[hA, hB], writes=[hB])
                            S.op("act", I("activation", out=hB.t[:, 0:511], in_=hB.t[:, 0:511], func=AF.Sigmoid, scale=2.0 * 0.7978845608028654), reads=[hB], writes=[hB])
                            S.op("dve", I("tensor_tensor", out=gel[mh].t[:, 0:511], in0=hA.t[:, 0:511], in1=hB.t[:, 0:511], op=ALU.mult), reads=[hA, hB], writes=[gel[mh]])
                        if kv == 0:
                            ps = self.psA.next()
                            for mh in range(2):
                                S.op("pe", I("matmul", ps.t[:, 0:511], lhsT=w2sb.t[:, mh, :], rhs=gel[mh].t[:, 0:511], start=(mh == 0), stop=(mh == 1)), reads=[w2sb, gel[mh]], writes=[ps])
                            S.op("act", I("copy", out=kcT.t[pb:pb + 64, 0:511], in_=ps.t[pb:pb + 64, 0:511]), reads=[ps], writes=[kcT])
                        else:
                            for bt in range(4):
                                n = 128 if bt < 3 else 127
                                ps = self.psA.next()
                                for mh in range(2):
                                    S.op("pe", I("matmul", ps.t[0:n, 0:64], lhsT=gel[mh].t[:, bt * 128:bt * 128 + n], rhs=w2sb.t[:, mh, 0:64], start=(mh == 0), stop=(mh == 1)), reads=[w2sb, gel[mh]], writes=[ps])
                                S.op("act", I("copy", out=vc.t[0:n, bt, g, 0:64], in_=ps.t[0:n, 0:64]), reads=[ps], writes=[vc])
                                S.op("pool", I("tensor_copy", out=vc.t[0:n, bt, g, 64:128], in_=ones64[0:n, :]), reads=[cbf], writes=[vc])
            S.barrier()
            if self.stop_after == "B3":
                self.dump("kcT", kcT, kcT.t[:], [128, 512], BF16)
                self.dump("vc", vc, vc.t[:], [128, 4, 2, 128], BF16)
                self.stopped = True
                return
            qbT = self.sb(esB, "b_qbT", [128, 4, OWN], BF16)
            gbT = self.sb(esB, "b_gbT", [32, OWN], F32)
            kwinT = self.sb(esB, "b_kwinT", [128, 20 * 128], BF16)
            vwin = self.sb(esB, "b_vwin", [128, 20, 2, 128], BF16)
            kso = self.sb(esB, "b_kso", [128, OWN], BF16)
            vso = self.sb(esB, "b_vso", [128, 16, 2, 128], BF16)
            S.op("pool", I("memset", gbT.t[:], 0.0), writes=[gbT])
            hvcol = pcf.t[:, PCF["hv"]:PCF["hv"] + 1]
            with ExitStack() as es:
                self.wst = self.ring(es, "wstB1", [128, 8, 256], F32, 2)
                slq = self.sb(es, "b_slq", [128, 8, 512], BF16)
                slkv = self.sb(es, "b_slkv", [128, 8, 512], BF16)
                slg = self.sb(es, "b_slg", [128, 8, 32], BF16)
                uTt = self.ring(es, "b_uTt1", [128, 8, 128], BF16, 2)
                tq = self.ring(es, "b_tq", [128, 512], BF16, 2)
                tk = self.ring(es, "b_tk", [128, 256], BF16, 2)
                self.cast_into(slq, lambda c0, n: slq.t[:, :, c0:c0 + n], self.w_in[:, C_QB:C_QB + 512], 8, 512, self.gmix, piece=256)
                self.cast_into(slkv, lambda c0, n: slkv.t[:, :, c0:c0 + n], self.w_in[:, C_KVB + 256:C_KVB + 768], 8, 512, self.gmix, piece=256)
                self.cast_into(slg, lambda c0, n: slg.t[:, :, c0:c0 + n], self.w_in[:, C_GB:C_GB + 24], 8, 24, self.gmix, piece=256)
                for e_ in range(12, 32):
                    slot = e_ - 12
                    own = e_ >= 16
                    i = e_ - 16
                    u = uTt.next()
                    xs = self.x_own[i * 128:(i + 1) * 128, :] if own else self.x_halo[e_ * 128:(e_ + 1) * 128, :]
                    self.norm_tile(xs, u, u.t[:])
                    lhs = lambda c, u=u: u.t[:, c, :]
                    if own:
                        ps = self.psA.next()
                        self.proj_tm(lhs, u, slq, 0, 512, ps)
                        t = tq.next()
                        self.rope_evac(ps, 0, 8, e_, t, t.t[:, 0:512], perm=True)
                        pt = self.psT
                        for hp in range(4):
                            S.op("pe", I("transpose", out=pt.t[:, hp * 128:(hp + 1) * 128], in_=t.t[:, hp * 128:(hp + 1) * 128], identity=self.ident), reads=[t, cbf], writes=[pt])
                        S.op("act", I("copy", out=qbT.t[:, :, i * 128:(i + 1) * 128], in_=pt.t[:, 0:512].rearrange("p (h t) -> p h t", h=4)), reads=[pt], writes=[qbT])
                        psx = self.psX
                        for c in range(8):
                            S.op("pe", I("matmul", psx.t[0:24, 0:128], lhsT=slg.t[:, c, 0:24], rhs=u.t[:, c, :], start=(c == 0), stop=(c == 7)), reads=[slg, u], writes=[psx])
                        S.op("act", I("activation", out=gbT.t[0:24, i * 128:(i + 1) * 128], in_=psx.t[0:24, 0:128], func=AF.Sigmoid), reads=[psx], writes=[gbT])
                    ps = self.psA.next()
                    t = tk.next()
                    if own:
                        self.proj_tm(lhs, u, slkv, 0, 512, ps)
                        self.rope_evac(ps, 0, 2, e_, t, t.t[:, 0:128])
                        self.rope_evac(ps, 256, 2, e_, t, t.t[:, 128:256])
                        S.op("dve", I("tensor_copy", out=vso.t[:, i, :, 0:64], in_=ps.t[:, 128:256].rearrange("p (g d) -> p g d", g=2)), reads=[ps], writes=[vso])
                        S.op("pool", I("tensor_copy", out=vso.t[:, i, :, 64:128], in_=ones2), reads=[cbf], writes=[vso])
                        S.op("dve", I("tensor_copy", out=vwin.t[:, slot, :, 0:64], in_=ps.t[:, 384:512].rearrange("p (g d) -> p g d", g=2)), reads=[ps], writes=[vwin])
                        S.op("pool", I("tensor_copy", out=vwin.t[:, slot, :, 64:128], in_=ones2), reads=[cbf], writes=[vwin])
                    else:
                        self.proj_tm(lhs, u, slkv, 256, 256, ps)
                        self.rope_evac(ps, 0, 2, e_, t, t.t[:, 128:256])
                        S.op("dve", I("tensor_scalar", out=vwin.t[:, slot, :, 0:64], in0=ps.t[:, 128:256].rearrange("p (g d) -> p g d", g=2), scalar1=hvcol, scalar2=None, op0=ALU.mult), reads=[ps, pcf], writes=[vwin])
                        S.op("pool", I("tensor_scalar", out=vwin.t[:, slot, :, 64:128], in0=ones2, scalar1=hvcol, scalar2=None, op0=ALU.mult), reads=[cbf, pcf], writes=[vwin])
                    pt = self.psT
                    if own:
                        S.op("pe", I("transpose", out=pt.t[:, 0:128], in_=t.t[:, 0:128], identity=self.ident), reads=[t, cbf], writes=[pt])
                    S.op("pe", I("transpose", out=pt.t[:, 128:256], in_=t.t[:, 128:256], identity=self.ident), reads=[t, cbf], writes=[pt])
                    if own:
                        S.op("act", I("copy", out=kso.t[:, i * 128:(i + 1) * 128], in_=pt.t[:, 0:128]), reads=[pt], writes=[kso])
                    S.op("act", I("copy", out=kwinT.t[:, slot * 128:(slot + 1) * 128], in_=pt.t[:, 128:256]), reads=[pt], writes=[kwinT])
            S.barrier()
            biasT = self.sb(esB, "b_biasT", [128, 2, OWN], BF16)
            t16 = self.sb(esB, "b_t16", [128, 2048], F32)
            S.dma(I("dma_start", out=t16.t[:], in_=self.c_t16), writes=[t16])
            with ExitStack() as es:
                et = self.ring(es, "b_et", [128, 512], F32, 4)
                pp = self.ring(es, "b_pp", [128, 520], F32, 4)
                lohi = self.ring(es, "b_lohi", [128, 256], F32, 2)
                imp = self.ring(es, "b_imp", [128, 128], F32, 2)
                imp2 = self.ring(es, "b_imp2", [128, 128], F32, 2)
                sm = self.ring(es, "b_sm", [128, 24], F32, 8)
                btm = self.ring(es, "b_btm", [128, 128], BF16, 2)
                for pq in pp.items:
                    S.op("pool", I("memset", pq.t[:], 0.0), writes=[pq])
                for i in range(NT):
                    lh = lohi.next()
                    S.dma(I("dma_start", out=lh.t[:, 0:128], in_=self.pc_lohi[:, i * 128:(i + 1) * 128]), writes=[lh])
                    S.dma(I("dma_start", out=lh.t[:, 128:256], in_=self.pc_lohi[:, 2048 + i * 128:2048 + (i + 1) * 128]), writes=[lh])
                    thr_i = pcf.t[:, PCF["thrc"] + i:PCF["thrc"] + i + 1]
                    for g in range(2):
                        pb = 64 * g
                        P = pp.next()
                        P2 = pp.next()
                        for r in range(4):
                            ps = self.psS.next()
                            S.op("pe", I("matmul", ps.t[:, 0:511], lhsT=qbT.t[pb:pb + 64, r, i * 128:(i + 1) * 128], rhs=kcT.t[pb:pb + 64, 0:511], start=True, stop=True), reads=[qbT, kcT], writes=[ps])
                            e_ = et.next()
                            s_ = sm.next()
                            eng = "dve"
                            Pr = P if r % 2 == 0 else P2
                            S.op("act", I("activation", out=e_.t[:, 0:511], in_=ps.t[:, 0:511], func=AF.Exp, scale=SCALE), reads=[ps], writes=[e_])
                            S.op(eng, I("scalar_tensor_tensor", out=e_.t[:, 0:511], in0=t16.t[:, 0:511], scalar=thr_i, in1=e_.t[:, 0:511], op0=ALU.is_le, op1=ALU.mult, accum_out=s_.t[:, 0:1]), reads=[t16, pcf, e_], writes=[e_, s_])
                            S.op(eng, I("tensor_scalar", out=s_.t[:, 1:2], in0=s_.t[:, 0:1], scalar1=1e-30, scalar2=None, op0=ALU.max), reads=[s_], writes=[s_])
                            S.op("dve", I("reciprocal", out=s_.t[:, 2:3], in_=s_.t[:, 1:2]), reads=[s_], writes=[s_])
                            if r < 2:
                                S.op(eng, I("tensor_scalar", out=Pr.t[:, 1:512], in0=e_.t[:, 0:511], scalar1=s_.t[:, 2:3], scalar2=None, op0=ALU.mult), reads=[e_, s_], writes=[Pr])
                            else:
                                S.op(eng, I("scalar_tensor_tensor", out=Pr.t[:, 1:512], in0=e_.t[:, 0:511], scalar=s_.t[:, 2:3], in1=Pr.t[:, 1:512], op0=ALU.mult, op1=ALU.add), reads=[e_, s_, Pr], writes=[Pr])
                        S.op("dve", I("tensor_tensor", out=P.t[:, 1:512], in0=P.t[:, 1:512], in1=P2.t[:, 1:512], op=ALU.add), reads=[P, P2], writes=[P])
                        im = imp.next()
                        S.op("dve", I("tensor_tensor", out=im.t[:], in0=P.t[:, 0:512:4], in1=P.t[:, 1:513:4], op=ALU.add), reads=[P], writes=[im])
                        for k in range(2, 5):
                            S.op("dve", I("tensor_tensor", out=im.t[:], in0=im.t[:], in1=P.t[:, k:k + 512:4], op=ALU.add), reads=[P, im], writes=[im])
                        S.op("dve", I("tensor_tensor", out=im.t[:], in0=im.t[:], in1=lh.t[:, 0:128], op=ALU.max), reads=[lh, im], writes=[im])
                        S.op("dve", I("tensor_tensor", out=im.t[:], in0=im.t[:], in1=lh.t[:, 128:256], op=ALU.min), reads=[lh, im], writes=[im])
                        s_ = sm.next()
                        i2 = imp2.next()
                        S.op("dve", I("max", out=s_.t[:, 0:8], in_=im.t[:]), reads=[im], writes=[s_])
                        S.op("dve", I("match_replace", out=i2.t[:], in_to_replace=s_.t[:, 0:8], in_values=im.t[:], imm_value=-1e9), reads=[im, s_], writes=[i2])
                        S.op("dve", I("max", out=s_.t[:, 8:16], in_=i2.t[:]), reads=[i2], writes=[s_])
                        S.op("dve", I("tensor_scalar", out=s_.t[:, 16:17], in0=s_.t[:, 15:16], scalar1=-1.5e4, scalar2=None, op0=ALU.max), reads=[s_], writes=[s_])
                        bt_ = btm.next()
                        S.op("dve", I("tensor_scalar", out=bt_.t[:], in0=im.t[:], scalar1=s_.t[:, 16:17], scalar2=-30000.0, op0=ALU.is_lt, op1=ALU.mult), reads=[im, s_], writes=[bt_])
                        pt = self.psT
                        S.op("pe", I("transpose", out=pt.t[:, 0:128], in_=bt_.t[:], identity=self.ident), reads=[bt_, cbf], writes=[pt])
                        S.op("act", I("copy", out=biasT.t[:, g, i * 128:(i + 1) * 128], in_=pt.t[:, 0:128]), reads=[pt], writes=[biasT])
            S.barrier()
            if self.stop_after == "B6":
                self.dump("biasT", biasT, biasT.t[:], [128, 2, OWN], BF16)
                self.dump("qbT", qbT, qbT.t[:], [128, 4, OWN], BF16)
                self.stopped = True
                return
            with ExitStack() as es:
                osb = self.ring(es, "b_osb", [128, 512], F32, 2)
                rdr = self.ring(es, "b_rd", [128, 512], F32, 2)
                ybacc = self.sb(es, "b_ybacc", [128, 512], F32)
                gsel = self.ring(es, "b_gsel", [32, 128], F32, 3)
                id32 = self.cf32.t[0:32, F32C["id32"]:F32C["id32"] + 32]
                eown = pcb.t[:, PCB["eown"]:PCB["eown"] + 2048]
                e32 = cbf.t[:, BFC["e32"]:BFC["e32"] + 2048]
                m4 = cbf.t[:, BFC["m4"]:BFC["m4"] + 2048]
                tri_diag = cbf.t[:, BFC["tri_diag"]:BFC["tri_diag"] + 128]
                win_far = cbf.t[:, BFC["win_far"]:BFC["win_far"] + 128]
                BRS = os.environ.get("BRS", "012")
                for hp in range(4):
                    for gi in range(2):
                        g = gi
                        h = hp + 4 * gi
                        pb = 64 * gi
                        sels = []
                        for br in range(3):
                            gs = gsel.next()
                            jrow = h * 3 + br
                            S.op("pool", I("tensor_copy", out=gs.t[:], in_=id32[:, jrow:jrow + 1].to_broadcast([32, 128])), reads=[self.cf32], writes=[gs])
                            sels.append(gs)
                        for c in range(4):
                            Q = qbT.t[pb:pb + 64, hp, c * 512:(c + 1) * 512]
                            gcols = gbT.t[0:32, c * 512:(c + 1) * 512]
                            dst = self.ybT.t[:, hp, c * 512:(c + 1) * 512]
                            pso = self.psO.next()
                            self._collect = []
                            for bt in range(4):
                                crel = pcf.t[:, PCF["crel"] + bt:PCF["crel"] + bt + 1]

                                def mask_c(pt, crel=crel, c=c):
                                    S.op("dve", I("scalar_tensor_tensor", out=pt.t[:, 0:512], in0=t16.t[:, c * 512:(c + 1) * 512], scalar=crel, in1=pt.t[:, 0:512], op0=ALU.is_ge, op1=ALU.mult), reads=[t16, pcf, pt], writes=[pt])
                                self.attn_unit([(kcT.t[pb:pb + 64, bt * 128:(bt + 1) * 128], Q, [kcT, qbT])], 512, mask_c,
                                               [(pso, pso.t[:, 0:512], vc.t[:, bt, g, :], 0, 512, bt == 0, bt == 3, [vc])])
                            self.attn_seq(self._collect)
                            self._collect = None
                            o = osb.next()
                            S.op("act", I("copy", out=o.t[:], in_=pso.t[:]), reads=[pso], writes=[o])
                            self.finalize(o, o.t[:], gi, (sels[0], gbT, gcols), rdr, self.ybT, dst, first=True, last=False, ybacc=ybacc)
                            pso = self.psO.next()
                            self._collect = []
                            for kt in range(48):
                                pb32 = 32 * (kt // 16)
                                kc_ = (kt % 16) * 128
                                self.attn_unit([(kslcT.t[pb:pb + 64, kt * 128:(kt + 1) * 128], Q, [kslcT, qbT]),
                                                (e32[pb32:pb32 + 32, kc_:kc_ + 128], biasT.t[pb32:pb32 + 32, g, c * 512:(c + 1) * 512], [cbf, biasT])], 512, None,
                                               [(pso, pso.t[:, 0:512], vslc.t[:, kt, g, :], 0, 512, kt == 0, False, [vslc])])
                            for j in range(4 * c + 4):
                                mf = None
                                if j >= 4 * c:
                                    mk = m4[:, (j - 4 * c) * 512:(j - 4 * c + 1) * 512]

                                    def mf(pt, mk=mk):
                                        self.mask_mul(pt, 0, 512, mk)
                                self.attn_unit([(kso.t[pb:pb + 64, j * 128:(j + 1) * 128], Q, [kso, qbT]),
                                                (eown[:, j * 128:(j + 1) * 128], biasT.t[:, g, c * 512:(c + 1) * 512], [pcb, biasT])], 512, mf,
                                               [(pso, pso.t[:, 0:512], vso.t[:, j, g, :], 0, 512, False, j == 4 * c + 3, [vso])])
                            self.attn_seq(self._collect)
                            self._collect = None
                            o = osb.next()
                            S.op("act", I("copy", out=o.t[:], in_=pso.t[:]), reads=[pso], writes=[o])
                            self.finalize(o, o.t[:], gi, (sels[1], gbT, gcols), rdr, self.ybT, dst, first=False, last=False, ybacc=ybacc)
                            pso = self.psO.next()
                            self._collect = []
                            for tq_ in range(4):
                                i = 4 * c + tq_
                                sq = 4 + i
                                for s_ in range(sq - 4, sq + 1):
                                    mf = None
                                    if s_ == sq - 4:
                                        def mf(pt):
                                            self.mask_mul(pt, 0, 128, win_far)
                                    elif s_ == sq:
                                        def mf(pt):
                                            self.mask_mul(pt, 0, 128, tri_diag)
                                    self.attn_unit([(kwinT.t[pb:pb + 64, s_ * 128:(s_ + 1) * 128], qbT.t[pb:pb + 64, hp, i * 128:(i + 1) * 128], [kwinT, qbT])], 128, mf,
                                                   [(pso, pso.t[:, tq_ * 128:(tq_ + 1) * 128], vwin.t[:, s_, g, :], 0, 128, s_ == sq - 4, s_ == sq, [vwin])])
                            self.attn_seq(self._collect)
                            self._collect = None
                            o = osb.next()
                            S.op("act", I("copy", out=o.t[:], in_=pso.t[:]), reads=[pso], writes=[o])
                            self.finalize(o, o.t[:], gi, (sels[2], gbT, gcols), rdr, self.ybT, dst, first=False, last=True, ybacc=ybacc)
        S.barrier()

    def phase_C(self, es0):
        S = self.S
        with ExitStack() as es:
            self.wst = self.ring(es, "wstC", [128, 8, 256], F32, 2)
            self.cast_engs = ("pool", "dve")
            slabs = self.ring(es, "c_slab", [128, 8, 512], BF16, 3)
            self.wbslab = self.sb(es, "c_wb", [128, 4, 512], BF16)
            gfin = self.sb(es, "c_gfin", [128, D], F32)
            xc = self.sb(es, "c_xc", [128, 4, D], F32)
            uTc = self.sb(es, "c_uTc", [128, 8, 512], BF16)
            mTc = self.sb(es, "c_mTc", [128, 8, 512], BF16)
            u2Tc = self.sb(es, "c_u2Tc", [128, 8, 512], BF16)
            hT = self.sb(es, "c_hT", [128, 32, 512], BF16)
            sg = self.ring(es, "c_sg", [128, 512], BF16, 2)
            tf = self.ring(es, "c_tf", [128, 512], F32, 3)
            S.dma(I("dma_start", out=gfin.t[:], in_=self.g_fin), writes=[gfin])

            def slab_from(src_ap, kchunks, gain):
                sl = slabs.next()
                self.cast_into(sl, lambda c0, n, sl=sl: sl.t[:, 0:kchunks, c0:c0 + n], src_ap, kchunks, 512, gain, piece=256)
                return sl

            for c in range(4):
                cs = slice(c * 512, (c + 1) * 512)
                for tt in range(4):
                    r0 = c * 512 + tt * 128
                    S.dma(I("dma_start", out=xc.t[:, tt, :], in_=self.x_own[r0:r0 + 128, :]), writes=[xc])
                    self.norm_sb(xc, xc.t[:, tt, :], uTc, uTc.t[:, :, tt * 128:(tt + 1) * 128])
                for ctg in range(2):
                    gA = slab_from(self.w_in[:, C_GM + ctg * 512:C_GM + ctg * 512 + 512], 8, self.gmix)
                    gB = slab_from(self.w_in[:, C_GM + 1024 + ctg * 512:C_GM + 1024 + ctg * 512 + 512], 8, self.gmix)
                    wa = slab_from(self.w_a[:, ctg * 512:(ctg + 1) * 512], 4, None)
                    wb = self.wbslab
                    for c0 in range(0, 512, 256):
                        st = self.wst.next()
                        for two in range(2):
                            S.dma(I("dma_start", out=st.t[64 * two:64 * two + 64, 0:4, 0:256],
                                    in_=self.w_b[two * 256:(two + 1) * 256, ctg * 512 + c0:ctg * 512 + c0 + 256].rearrange("(hp d) n -> d hp n", d=64)), writes=[st])
                        S.op("pool", I("tensor_copy", out=wb.t[:, 0:4, c0:c0 + 256], in_=st.t[:, 0:4, 0:256]), reads=[st], writes=[wb])
                    for j in range(4):
                        ct = ctg * 4 + j
                        js = slice(j * 128, (j + 1) * 128)
                        sgs = []
                        for gw in (gA, gB):
                            ps = self.psA.next()
                            for k in range(8):
                                S.op("pe", I("matmul", ps.t[:, 0:512], lhsT=gw.t[:, k, js], rhs=uTc.t[:, k, :], start=(k == 0), stop=(k == 7)), reads=[gw, uTc], writes=[ps])
                            sgt = sg.next()
                            S.op("act", I("activation", out=sgt.t[:], in_=ps.t[:, 0:512], func=AF.Sigmoid), reads=[ps], writes=[sgt])
                            sgs.append(sgt)
                        psa = self.psS.next()
                        for k in range(4):
                            S.op("pe", I("matmul", psa.t[:, 0:512], lhsT=wa.t[:, k, js], rhs=self.yaT.t[:, k, cs], start=(k == 0), stop=(k == 3)), reads=[wa, self.yaT], writes=[psa])
                        psb = self.psO.next()
                        for k in range(4):
                            S.op("pe", I("matmul", psb.t[:, 0:512], lhsT=wb.t[:, k, js], rhs=self.ybT.t[:, k, cs], start=(k == 0), stop=(k == 3)), reads=[wb, self.ybT], writes=[psb])
                        t0 = tf.next()
                        t1 = tf.next()
                        S.op("dve", I("tensor_tensor", out=t0.t[:], in0=psa.t[:, 0:512], in1=sgs[0].t[:], op=ALU.mult), reads=[psa, sgs[0]], writes=[t0])
                        S.op("dve", I("tensor_tensor", out=t1.t[:], in0=psb.t[:, 0:512], in1=sgs[1].t[:], op=ALU.mult), reads=[psb, sgs[1]], writes=[t1])
                        S.op("dve", I("tensor_tensor", out=mTc.t[:, ct, :], in0=t0.t[:], in1=t1.t[:], op=ALU.add), reads=[t0, t1], writes=[mTc])
                for nh in range(2):
                    wo = slab_from(self.w_out[:, nh * 512:(nh + 1) * 512], 8, None)
                    for tt in range(4):
                        ps = self.psA.next()
                        for k in range(8):
                            S.op("pe", I("matmul", ps.t[:, 0:512], lhsT=mTc.t[:, k, tt * 128:(tt + 1) * 128], rhs=wo.t[:, k, :], start=(k == 0), stop=(k == 7)), reads=[wo, mTc], writes=[ps])
                        S.op("dve", I("tensor_tensor", out=xc.t[:, tt, nh * 512:(nh + 1) * 512], in0=ps.t[:, 0:512], in1=xc.t[:, tt, nh * 512:(nh + 1) * 512], op=ALU.add), reads=[ps, xc], writes=[xc])
                for tt in range(4):
                    self.norm_sb(xc, xc.t[:, tt, :], u2Tc, u2Tc.t[:, :, tt * 128:(tt + 1) * 128])
                for s_ in range(8):
                    wu = slab_from(self.w_up[:, s_ * 512:(s_ + 1) * 512], 8, self.gmlp)
                    for j in range(4):
                        ft = 4 * s_ + j
                        ps = self.psA.next()
                        for k in range(8):
                            S.op("pe", I("matmul", ps.t[:, 0:512], lhsT=wu.t[:, k, j * 128:(j + 1) * 128], rhs=u2Tc.t[:, k, :], start=(k == 0), stop=(k == 7)), reads=[wu, u2Tc], writes=[ps])
                        r = tf.next()
                        S.op("act", I("activation", out=r.t[:], in_=ps.t[:, 0:512], func=AF.Relu), reads=[ps], writes=[r])
                        S.op("dve", I("tensor_tensor", out=hT.t[:, ft, :], in0=r.t[:], in1=r.t[:], op=ALU.mult), reads=[r], writes=[hT])
                accs = [self.psA.items[0], self.psA.items[1], self.psS.items[0], self.psS.items[1]]
                for nh in range(2):
                    for kg in range(4):
                        wd = slab_from(self.w_down[kg * 1024:(kg + 1) * 1024, nh * 512:(nh + 1) * 512], 8, None)
                        for tt in range(4):
                            for k in range(8):
                                S.op("pe", I("matmul", accs[tt].t[:, 0:512], lhsT=hT.t[:, kg * 8 + k, tt * 128:(tt + 1) * 128], rhs=wd.t[:, k, :], start=(kg == 0 and k == 0), stop=(kg == 3 and k == 7)), reads=[wd, hT], writes=[accs[tt]])
                    for tt in range(4):
                        S.op("dve", I("tensor_tensor", out=xc.t[:, tt, nh * 512:(nh + 1) * 512], in0=accs[tt].t[:, 0:512], in1=xc.t[:, tt, nh * 512:(nh + 1) * 512], op=ALU.add), reads=[accs[tt], xc], writes=[xc])
                for tt in range(4):
                    jk = self.junk.next()
                    st = self.stat.next()
                    S.op("act", I("activation", out=jk.t[:], in_=xc.t[:, tt, :], func=AF.Square, accum_out=st.t[:, 0:1]), reads=[xc], writes=[jk, st])
                    S.op("act", I("activation", out=st.t[:, 1:2], in_=st.t[:, 0:1], func=AF.Sqrt, scale=1.0 / D, bias=self.epsc.t[:, 0:1]), reads=[st, self.epsc], writes=[st])
                    S.op("dve", I("reciprocal", out=st.t[:, 2:3], in_=st.t[:, 1:2]), reads=[st], writes=[st])
                    S.op("dve", I("scalar_tensor_tensor", out=xc.t[:, tt, :], in0=xc.t[:, tt, :], scalar=st.t[:, 2:3], in1=gfin.t[:], op0=ALU.mult, op1=ALU.mult), reads=[xc, st, gfin], writes=[xc])
                    r0 = c * 512 + tt * 128
                    S.dma(I("dma_start", out=self.out[r0:r0 + 128, :], in_=xc.t[:, tt, :]), reads=[xc])

def make_in_maps(inputs):
    x = np.ascontiguousarray(np.asarray(inputs["x"], np.float32))
    cbf, cf32, t16 = _static_tables()
    sq = lambda n: np.ascontiguousarray(np.asarray(inputs[n], np.float32)[0])
    gl = lambda v: np.ascontiguousarray(np.asarray(v, np.float32).reshape(8, 128).T)
    common = {
        "w_in": sq("w_in"), "g_mix": gl(inputs["norm_mix_g"][0]), "g_mlp": gl(inputs["norm_mlp_g"][0]),
        "g_fin": np.ascontiguousarray(np.broadcast_to(np.asarray(inputs["norm_final_g"], np.float32)[None, :], (128, D))),
        "cmp_w1_k": sq("cmp_w1_k"), "cmp_w1_v": sq("cmp_w1_v"), "cmp_w2_k": sq("cmp_w2_k"), "cmp_w2_v": sq("cmp_w2_v"),
        "cmp_pos_k": sq("cmp_pos_k"), "cmp_pos_v": sq("cmp_pos_v"),
        "w_a": sq("w_branch_a"), "w_b": sq("w_branch_b"), "w_out": sq("w_out"), "w_up": sq("w_up"), "w_down": sq("w_down"),
        "c_bf": cbf, "c_f32": cf32, "c_t16": t16,
    }
    tabs = [_percore_tables(q) for q in range(4)]
    maps = []
    for c in range(8):
        b, q = c // 4, c % 4
        T0 = OWN * q
        halo = x[b, T0 - OWN:T0] if q > 0 else np.zeros((OWN, D), np.float32)
        m = dict(common)
        m.update({"x_own": np.ascontiguousarray(x[b, T0:T0 + OWN]), "x_halo": np.ascontiguousarray(halo), "x_full": x[b],
                  "pc_f": tabs[q][0], "pc_lohi": tabs[q][1], "pc_bf": tabs[q][2]})
        maps.append(m)
    return maps


_CACHE = {}


def kernel(**inputs):
    if "nc" not in _CACHE:
        b = Builder()
        _CACHE["nc"] = b.build()
        _CACHE["decl"] = set(b._decl.keys())
    nc = _CACHE["nc"]
    maps = make_in_maps(inputs)
    decl = _CACHE["decl"]
    maps = [{k: v for k, v in m.items() if k in decl} for m in maps]
    res = run_bass_kernel_spmd(nc, maps, core_ids=list(range(8)))
    out = np.zeros((2, S_LEN, D), np.float32)
    for c in range(8):
        b, q = c // 4, c % 4
        out[b, OWN * q:OWN * (q + 1)] = res.results[c]["out"]
    return out
```

```python
import os
import numpy as np
import ml_dtypes
from contextlib import ExitStack
import concourse.bass as bass
import concourse.mybir as mybir
from concourse.bass_utils import run_bass_kernel_spmd

F32 = mybir.dt.float32
BF16 = mybir.dt.bfloat16
ALU = mybir.AluOpType
AF = mybir.ActivationFunctionType
NPBF = ml_dtypes.bfloat16

D = 1024
S_LEN = 8192
OWN = 2048
NT = 16
EPS = 1e-6
SCALE = 0.125
IN_COLS = 7960
C_QA, C_KA, C_VA = 0, 1536, 3072
C_QB = 4608
C_KVB = 5120
C_GB = 5888
C_GM = 5912
DILS = (1, 4, 16)

ENGS = ("pe", "act", "dve", "pool")
NDMA = 24


class Res:
    __slots__ = ("lw", "rd", "excl")

    def __init__(self):
        self.lw = None
        self.rd = {}
        self.excl = False


class Tn:
    __slots__ = ("t", "r")

    def __init__(self, t):
        self.t = t
        self.r = Res()


class Sched:
    def __init__(self, nc):
        self.nc = nc
        self.q = {e: [] for e in ENGS + ("sp",)}
        self.cnt = {e: 0 for e in ENGS}
        self.dcnt = [0] * NDMA
        self.seen = {e: {} for e in ENGS + ("sp",)}
        self.dnext = 0

    def _deps(self, reads, writes, mykey=None):
        deps = {}
        for r in reads:
            r = r.r if isinstance(r, Tn) else r
            if r.lw is not None and r.lw[1] > deps.get(r.lw[0], 0):
                deps[r.lw[0]] = r.lw[1]
            if r.excl:
                for k, v in r.rd.items():
                    if k != mykey and v > deps.get(k, 0):
                        deps[k] = v
        for w in writes:
            w = w.r if isinstance(w, Tn) else w
            if w.lw is not None and w.lw[0] != mykey and w.lw[1] > deps.get(w.lw[0], 0):
                deps[w.lw[0]] = w.lw[1]
            for k, v in w.rd.items():
                if v > deps.get(k, 0):
                    deps[k] = v
        return deps

    def _waits(self, eng, deps):
        waits = []
        seen = self.seen[eng]
        for k, v in deps.items():
            if v > seen.get(k, 0):
                waits.append((k, v))
                seen[k] = v
        return waits

    def _mark(self, key, my, reads, writes):
        for r in reads:
            r = r.r if isinstance(r, Tn) else r
            if my > r.rd.get(key, 0):
                r.rd[key] = my
        for w in writes:
            w = w.r if isinstance(w, Tn) else w
            w.lw = (key, my)
            w.rd = {}

    def op(self, eng, fn, reads=(), writes=()):
        deps = self._deps(reads, writes, ("e", eng))
        if eng == "pe":
            deps.pop(("e", "pe"), None)
        self.cnt[eng] += 1
        my = self.cnt[eng]
        key = ("e", eng)
        self.q[eng].append((self._waits(eng, deps), fn, key, my))
        self._mark(key, my, reads, writes)

    def dma(self, fn, reads=(), writes=()):
        deps = self._deps(reads, writes)
        k = self.dnext
        self.dnext = (self.dnext + 1) % NDMA
        key = ("d", k)
        if self.dcnt[k] > 0:
            deps[key] = max(deps.get(key, 0), self.dcnt[k])
        self.dcnt[k] += 16
        my = self.dcnt[k]
        self.q["sp"].append((self._waits("sp", deps), fn, key, my))
        self._mark(key, my, reads, writes)

    def barrier(self):
        allc = {}
        for e in ENGS:
            if self.cnt[e]:
                allc[("e", e)] = self.cnt[e]
        for k in range(NDMA):
            if self.dcnt[k]:
                allc[("d", k)] = self.dcnt[k]
        for e in ENGS + ("sp",):
            w = self._waits(e, dict(allc))
            if w:
                self.q[e].append((w, None, None, 0))

    def emit(self):
        nc = self.nc
        with ExitStack() as es:
            esem = {e: es.enter_context(nc.semaphore("s_" + e)) for e in ENGS}
            dsem = [es.enter_context(nc.semaphore("s_d%d" % i)) for i in range(NDMA)]

            def semof(key):
                return esem[key[1]] if key[0] == "e" else dsem[key[1]]
            fin = {}
            for e in ENGS:
                if self.cnt[e]:
                    fin[("e", e)] = self.cnt[e]
            for k in range(NDMA):
                if self.dcnt[k]:
                    fin[("d", k)] = self.dcnt[k]
            allsems = list(esem.values()) + dsem
            with nc.Block() as b0:
                @b0.sync
                def _(e):
                    for sm in allsems:
                        e.sem_clear(sm)
            block = es.enter_context(nc.Block())

            sig = {e: set() for e in ENGS}
            for name in self.q:
                for waits, fn, key, my in self.q[name]:
                    for (k, v) in waits:
                        if k[0] == "e":
                            sig[k[1]].add(v)
            for e in ENGS:
                if self.cnt[e]:
                    sig[e].add(self.cnt[e])
            rank = {}
            for e in ENGS:
                for i, v in enumerate(sorted(sig[e])):
                    rank[(e, v)] = i + 1

            def wval(k, v):
                return rank[(k[1], v)] if k[0] == "e" else v

            def run(name, engobj, final=False):
                for waits, fn, key, my in self.q[name]:
                    for (k, v) in waits:
                        engobj.wait_ge(semof(k), wval(k, v))
                    if fn is not None:
                        ins = fn(engobj)
                        if key[0] == "d":
                            ins.then_inc(semof(key), 16)
                        elif my in sig[key[1]]:
                            ins.then_inc(semof(key), 1)
                if final:
                    for k, v in fin.items():
                        engobj.wait_ge(semof(k), wval(k, v))

            @block.sync
            def _(e):
                run("sp", e, final=True)

            @block.tensor
            def _(e):
                run("pe", e)

            @block.scalar
            def _(e):
                run("act", e)

            @block.vector
            def _(e):
                run("dve", e)

            @block.gpsimd
            def _(e):
                run("pool", e)


def I(name, *a, **k):
    return lambda e: getattr(e, name)(*a, **k)


class Ring:
    def __init__(self, items):
        self.items = items
        self.i = 0

    def next(self):
        it = self.items[self.i % len(self.items)]
        self.i += 1
        return it


NROPE = 148
BFC = dict(ident=0, tri_diag=128, tri_prev=256, win_far=384, m4=512, e32=2560, ones=4608)
NBFC = 4736
F32C = dict(swap=0, id32=128)
NF32C = 160
PCF = dict(rope=0, thrc=NROPE * 16, pv=NROPE * 16 + 16, crel=NROPE * 16 + 80, hv=NROPE * 16 + 84)
NPCF = NROPE * 16 + 85
PCB = dict(eown=0, hv64=2048)
NPCB = 2112


def _static_tables():
    bf = np.zeros((128, NBFC), np.float32)
    k = np.arange(128)[:, None]
    q = np.arange(128)[None, :]
    bf[:, 0:128] = np.eye(128)
    bf[:, 128:256] = (q >= k)
    bf[:, 256:384] = (q <= k)
    bf[:, 384:512] = (q < k)
    for m in range(4):
        blk = np.zeros((128, 512), np.float32)
        for tq in range(4):
            if tq == m:
                blk[:, tq * 128:(tq + 1) * 128] = (q >= k)
            elif tq > m:
                blk[:, tq * 128:(tq + 1) * 128] = 1.0
        bf[:, 512 + m * 512: 512 + (m + 1) * 512] = blk
    b = np.arange(128)[:, None]
    for kt in range(16):
        i = np.arange(128)[None, :]
        bf[:, 2560 + kt * 128: 2560 + (kt + 1) * 128] = ((b % 32) == 2 * kt + (i >= 64))
    bf[:, 4608:4736] = 1.0
    f = np.zeros((128, NF32C), np.float32)
    f[:, 0:128] = (np.abs(k - q) == 64)
    f[0:32, 128:160] = np.eye(32)
    t16 = np.ascontiguousarray(np.broadcast_to(16.0 * np.arange(2048, dtype=np.float32)[None, :], (128, 2048)))
    return bf.astype(NPBF), f, t16


def _rope_rows(pos):
    inv = (500000.0 ** (-np.arange(0, 16, 2, dtype=np.float32) / np.float32(16))).astype(np.float32)
    ang = (pos.astype(np.float32)[:, None] * inv[None, :]).astype(np.float32)
    return np.concatenate([np.cos(ang), np.sin(ang)], axis=1).astype(np.float32)


def _percore_tables(qtr):
    T0 = OWN * qtr
    i = np.arange(128)
    f = np.zeros((128, NPCF), np.float32)
    rope = np.zeros((128, NROPE, 16), np.float32)
    for t in range(32):
        rope[:, t] = _rope_rows(T0 - OWN + 128 * t + i)
    for r in range(4):
        for j in range(-1, 4):
            rope[:, 32 + r * 5 + j + 1] = _rope_rows(T0 - OWN + 2048 + 512 * j + r + 4 * i)
    for r in range(16):
        for j in range(-1, 1):
            rope[:, 52 + r * 2 + j + 1] = _rope_rows(T0 - OWN + 2048 + 2048 * j + r + 16 * i)
    for kt in range(64):
        rope[:, 84 + kt] = _rope_rows(128 * kt + i)
    f[:, 0:NROPE * 16] = rope.reshape(128, -1)
    for ti in range(16):
        f[:, PCF["thrc"] + ti] = T0 + 128 * ti + i - 31
    for kt in range(64):
        f[:, PCF["pv"] + kt] = 1.0 if 128 * kt < T0 else 0.0
    for bt in range(4):
        f[:, PCF["crel"] + bt] = 16.0 * (16 * (128 * bt + i) + 31 - T0)
    f[:, PCF["hv"]] = 0.0 if qtr == 0 else 1.0
    lo = np.full((128, 16, 128), -3e4, np.float32)
    hi = np.full((128, 16, 128), 3e4, np.float32)
    m = np.arange(128)[None, :]
    for ti in range(16):
        cur = ((T0 + 128 * ti + i) // 64)[:, None]
        forced = (m == 0) | (m == cur) | (m == cur - 1)
        fut = m > cur
        lo[:, ti][forced] = 1e4
        hi[:, ti][forced] = 1e4
        lo[:, ti][fut] = -3e4
        hi[:, ti][fut] = -3e4
    lohi = np.concatenate([lo.reshape(128, -1), hi.reshape(128, -1)], axis=1)
    bfp = np.zeros((128, NPCB), np.float32)
    b = np.arange(128)[:, None]
    for j in range(16):
        ii = np.arange(128)[None, :]
        bfp[:, j * 128:(j + 1) * 128] = (b == 2 * (T0 // 128 + j) + (ii >= 64))
    bfp[:, 2048:2112] = 0.0 if qtr == 0 else 1.0
    return f, lohi.astype(np.float32), bfp.astype(NPBF)


class StopBuild(Exception):
    pass


class Builder:
    def __init__(self, debug=False, stop_after=None):
        self.debug = debug
        self.stop_after = stop_after
        self.nc = nc = bass.Bass("TRN2", target_bir_lowering=False)
        self.S = Sched(nc)
        self._decl = {}
        self._shapes = {
            "x_own": ([OWN, D], F32), "x_halo": ([OWN, D], F32), "x_full": ([S_LEN, D], F32), "w_in": ([D, IN_COLS], F32),
            "g_mix": ([128, 8], F32), "g_mlp": ([128, 8], F32), "g_fin": ([128, D], F32),
            "cmp_w1_k": ([2048, 256], F32), "cmp_w1_v": ([2048, 256], F32), "cmp_w2_k": ([256, 64], F32), "cmp_w2_v": ([256, 64], F32),
            "cmp_pos_k": ([32, 64], F32), "cmp_pos_v": ([32, 64], F32), "w_a": ([512, D], F32), "w_b": ([512, D], F32),
            "w_out": ([D, D], F32), "w_up": ([D, 4096], F32), "w_down": ([4096, D], F32),
            "c_bf": ([128, NBFC], BF16), "c_f32": ([128, NF32C], F32), "c_t16": ([128, 2048], F32),
            "pc_f": ([128, NPCF], F32), "pc_lohi": ([128, 4096], F32), "pc_bf": ([128, NPCB], BF16),
        }
        self.out = nc.dram_tensor("out", [OWN, D], F32, kind="ExternalOutput").ap()
        self.dbg = {}

    def __getattr__(self, name):
        sh = self.__dict__.get("_shapes", {})
        if name in sh:
            if name not in self._decl:
                self._decl[name] = self.nc.dram_tensor(name, list(sh[name][0]), sh[name][1], kind="ExternalInput").ap()
            return self._decl[name]
        raise AttributeError(name)

    def sb(self, es, name, shape, dt):
        return Tn(es.enter_context(self.nc.sbuf_tensor(name, list(shape), dt)))

    def ps(self, es, name, shape, dt):
        t = Tn(es.enter_context(self.nc.psum_tensor(name, list(shape), dt)))
        t.r.excl = True
        return t

    def ring(self, es, name, shape, dt, n):
        return Ring([self.sb(es, "%s%d" % (name, i), shape, dt) for i in range(n)])

    def dump(self, name, tn, ap, shape, dt):
        if not self.debug:
            return
        o = self.nc.dram_tensor("dbg_" + name, list(shape), dt, kind="ExternalOutput").ap()
        self.dbg[name] = True
        self.S.dma(I("dma_start", out=o, in_=ap), reads=[tn])

    def load_wslab(self, src_ap, ncols, gain, kchunks=8):
        S = self.S
        st = self.wst.next()
        sl = self.wsl.next()
        S.dma(I("dma_start", out=st.t[:, 0:kchunks, 0:ncols], in_=src_ap.rearrange("(c p) n -> p c n", p=128)), writes=[st])
        if gain is not None:
            gb = gain.t[:, 0:kchunks].unsqueeze(2).to_broadcast([128, kchunks, ncols])
            S.op("pool", I("tensor_tensor", out=sl.t[:, 0:kchunks, 0:ncols], in0=st.t[:, 0:kchunks, 0:ncols], in1=gb, op=ALU.mult),
                 reads=[st, gain], writes=[sl])
        else:
            S.op("pool", I("tensor_copy", out=sl.t[:, 0:kchunks, 0:ncols], in_=st.t[:, 0:kchunks, 0:ncols]), reads=[st], writes=[sl])
        return sl

    def cast_into(self, dst_tn, dst_ap_fn, src_ap, kchunks, ncols, gain, piece=512):
        S = self.S
        for c0 in range(0, ncols, piece):
            n = min(piece, ncols - c0)
            st = self.wst.next()
            S.dma(I("dma_start", out=st.t[:, 0:kchunks, 0:n], in_=src_ap[:, c0:c0 + n].rearrange("(c p) n -> p c n", p=128)), writes=[st])
            dst = dst_ap_fn(c0, n)
            engs = getattr(self, "cast_engs", ("pool",))
            self._ci = getattr(self, "_ci", 0) + 1
            ce = engs[self._ci % len(engs)]
            if gain is not None:
                gb = gain.t[:, 0:kchunks].unsqueeze(2).to_broadcast([128, kchunks, n])
                S.op(ce, I("tensor_tensor", out=dst, in0=st.t[:, 0:kchunks, 0:n], in1=gb, op=ALU.mult),
                     reads=[st, gain], writes=[dst_tn])
            else:
                S.op(ce, I("tensor_copy", out=dst, in_=st.t[:, 0:kchunks, 0:n]), reads=[st], writes=[dst_tn])

    def norm_tile(self, x_ap, ut_tn, ut_ap):
        S = self.S
        xt = self.xring.next()
        S.dma(I("dma_start", out=xt.t[:], in_=x_ap), writes=[xt])
        self.norm_sb(xt, xt.t[:], ut_tn, ut_ap)

    def norm_sb(self, xt, x_sb_ap, ut_tn, ut_ap, keep_rstd=None):
        S = self.S
        jk = self.junk.next()
        st = self.stat.next()
        S.op("act", I("activation", out=jk.t[:], in_=x_sb_ap, func=AF.Square, accum_out=st.t[:, 0:1]), reads=[xt], writes=[jk, st])
        S.op("act", I("activation", out=st.t[:, 1:2], in_=st.t[:, 0:1], func=AF.Sqrt, scale=1.0 / D, bias=self.epsc.t[:, 0:1]), reads=[st, self.epsc], writes=[st])
        S.op("dve", I("reciprocal", out=st.t[:, 2:3], in_=st.t[:, 1:2]), reads=[st], writes=[st])
        xn = self.xnring.next()
        S.op("dve", I("tensor_scalar", out=xn.t[:], in0=x_sb_ap, scalar1=st.t[:, 2:3], scalar2=None, op0=ALU.mult), reads=[xt, st], writes=[xn])
        pt = self.psT
        for c in range(8):
            S.op("pe", I("transpose", out=pt.t[:, c * 128:(c + 1) * 128], in_=xn.t[:, c * 128:(c + 1) * 128], identity=self.ident), reads=[xn, self.cbf], writes=[pt])
        S.op("act", I("copy", out=ut_ap, in_=pt.t[:, 0:1024].rearrange("p (c t) -> p c t", c=8)), reads=[pt], writes=[ut_tn])
        return st

    def proj_tm(self, lhs_fn, lhs_tn, slab, c0, ncols, ps):
        for c in range(8):
            self.S.op("pe", I("matmul", ps.t[:, 0:ncols], lhsT=lhs_fn(c), rhs=slab.t[:, c, c0:c0 + ncols], start=(c == 0), stop=(c == 7)),
                      reads=[lhs_tn, slab], writes=[ps])

    def rope_evac(self, ps, pc0, nh, ropeidx, dst_tn, dst_ap, perm=False):
        S = self.S
        ro = PCF["rope"] + ropeidx * 16
        ta = self.rtmp.next()
        if not perm:
            psv = ps.t[:, pc0:pc0 + 64 * nh].rearrange("p (h d) -> p h d", h=nh)
            dv = dst_ap.rearrange("p (h d) -> p h d", h=nh)
            tav = ta.t[:, 0:nh * 32].rearrange("p (h d) -> p h d", h=nh)
            cos1 = self.pcf.t[:, ro:ro + 8].unsqueeze(1).to_broadcast([128, nh, 8])
            sin1 = self.pcf.t[:, ro + 8:ro + 16].unsqueeze(1).to_broadcast([128, nh, 8])
            sl = lambda v, a, b: v[:, :, a:b]
        else:
            psv = ps.t[:, pc0:pc0 + 512].rearrange("p (two hp d) -> p two hp d", two=2, hp=4)
            dv = dst_ap.rearrange("p (hp two d) -> p two hp d", two=2, hp=4)
            tav = ta.t[:, 0:256].rearrange("p (two hp d) -> p two hp d", two=2, hp=4)
            cos1 = self.pcf.t[:, ro:ro + 8].unsqueeze(1).unsqueeze(1).to_broadcast([128, 2, 4, 8])
            sin1 = self.pcf.t[:, ro + 8:ro + 16].unsqueeze(1).unsqueeze(1).to_broadcast([128, 2, 4, 8])
            sl = lambda v, a, b: v[:, :, :, a:b]
        S.op("dve", I("tensor_copy", out=sl(dv, 16, 64), in_=sl(psv, 16, 64)), reads=[ps], writes=[dst_tn])
        S.op("dve", I("tensor_tensor", out=sl(tav, 0, 8), in0=sl(psv, 0, 8), in1=cos1, op=ALU.mult), reads=[ps, self.pcf], writes=[ta])
        S.op("dve", I("tensor_tensor", out=sl(tav, 8, 16), in0=sl(psv, 8, 16), in1=cos1, op=ALU.mult), reads=[ps, self.pcf], writes=[ta])
        S.op("dve", I("tensor_tensor", out=sl(tav, 16, 24), in0=sl(psv, 8, 16), in1=sin1, op=ALU.mult), reads=[ps, self.pcf], writes=[ta])
        S.op("dve", I("tensor_tensor", out=sl(tav, 24, 32), in0=sl(psv, 0, 8), in1=sin1, op=ALU.mult), reads=[ps, self.pcf], writes=[ta])
        S.op("dve", I("tensor_tensor", out=sl(dv, 0, 8), in0=sl(tav, 0, 8), in1=sl(tav, 16, 24), op=ALU.subtract), reads=[ta], writes=[dst_tn])
        S.op("dve", I("tensor_tensor", out=sl(dv, 8, 16), in0=sl(tav, 8, 16), in1=sl(tav, 24, 32), op=ALU.add), reads=[ta], writes=[dst_tn])

    def build(self):
        nc, S = self.nc, self.S
        with ExitStack() as es0:
            self.cbf = self.sb(es0, "cbf", [128, NBFC], BF16)
            self.cf32 = self.sb(es0, "cf32", [128, NF32C], F32)
            self.pcf = self.sb(es0, "pcf", [128, NPCF], F32)
            self.pcb = self.sb(es0, "pcb", [128, NPCB], BF16)
            self.gmix = self.sb(es0, "gmix", [128, 8], F32)
            self.gmlp = self.sb(es0, "gmlp", [128, 8], F32)
            self.epsc = self.sb(es0, "epsc", [128, 1], F32)
            S.dma(I("dma_start", out=self.cbf.t[:], in_=self.c_bf), writes=[self.cbf])
            S.dma(I("dma_start", out=self.cf32.t[:], in_=self.c_f32), writes=[self.cf32])
            S.dma(I("dma_start", out=self.pcf.t[:], in_=self.pc_f), writes=[self.pcf])
            S.dma(I("dma_start", out=self.pcb.t[:], in_=self.pc_bf), writes=[self.pcb])
            S.dma(I("dma_start", out=self.gmix.t[:], in_=self.g_mix), writes=[self.gmix])
            S.dma(I("dma_start", out=self.gmlp.t[:], in_=self.g_mlp), writes=[self.gmlp])
            S.op("dve", I("memset", self.epsc.t[:], EPS), writes=[self.epsc])
            self.ident = self.cbf.t[:, 0:128]
            self.xring = self.ring(es0, "xr", [128, D], F32, 2)
            self.junk = self.ring(es0, "jk", [128, D], BF16, 1)
            self.stat = self.ring(es0, "st", [128, 4], F32, 4)
            self.xnring = self.ring(es0, "xn", [128, D], BF16, 2)
            self.rtmp = self.ring(es0, "rtmp", [128, 256], F32, 2)
            self.ptr = self.ring(es0, "ptr", [128, 512], BF16, 3)
            self.psA = Ring([self.ps(es0, "psA%d" % i, [128, 512], F32) for i in range(2)])
            self.psT = self.ps(es0, "psT", [128, 1024], BF16)
            self.psS = Ring([self.ps(es0, "psS%d" % i, [128, 512], F32) for i in range(2)])
            self.psO = Ring([self.ps(es0, "psO%d" % i, [128, 512], F32) for i in range(2)])
            self.psX = self.ps(es0, "psX", [128, 512], F32)
            self.yaT = self.sb(es0, "yaT", [128, 4, OWN], BF16)
            self.stopped = False
            self.phase_A(es0)
            if self.stopped:
                S.barrier()
                if self.stop_after in ("A3", "A"):
                    self.dump("yaT", self.yaT, self.yaT.t[:], [128, 4, OWN], BF16)
                self.fake_out()
                S.emit()
                return nc
            self.ybT = self.sb(es0, "ybT", [128, 4, OWN], BF16)
            S.barrier()
            if self.stop_after == "A":
                self.dump("yaT", self.yaT, self.yaT.t[:], [128, 4, OWN], BF16)
                self.fake_out()
                S.emit()
                return nc
            self.phase_B(es0)
            S.barrier()
            if self.stopped:
                self.fake_out()
                S.emit()
                return nc
            if self.stop_after == "B":
                self.dump("yaT", self.yaT, self.yaT.t[:], [128, 4, OWN], BF16)
                self.dump("ybT", self.ybT, self.ybT.t[:], [128, 4, OWN], BF16)
                self.fake_out()
                S.emit()
                return nc
            self.phase_C(es0)
            if self.debug:
                self.dump("yaT", self.yaT, self.yaT.t[:], [128, 4, OWN], BF16)
                self.dump("ybT", self.ybT, self.ybT.t[:], [128, 4, OWN], BF16)
            S.emit()
        return nc

    def fake_out(self):
        S = self.S
        xt = self.xring.next()
        for t in range(NT):
            S.dma(I("dma_start", out=xt.t[:], in_=self.x_own[t * 128:(t + 1) * 128, :]), writes=[xt])
            S.dma(I("dma_start", out=self.out[t * 128:(t + 1) * 128, :], in_=xt.t[:]), reads=[xt])

    def attn_unit(self, score_mms, n, mask_fn, pv_list):
        S = self.S
        if getattr(self, "_collect", None) is not None:
            self._collect.append((score_mms, n, mask_fn, pv_list, None))
            return
        pss = self.psS.next()
        for i, (l, r, rd) in enumerate(score_mms):
            S.op("pe", I("matmul", pss.t[:, 0:n], lhsT=l, rhs=r, start=(i == 0), stop=(i == len(score_mms) - 1)),
                 reads=rd, writes=[pss])
        pt = self.ptr.next()
        S.op("act", I("activation", out=pt.t[:, 0:n], in_=pss.t[:, 0:n], func=AF.Exp, scale=SCALE), reads=[pss], writes=[pt])
        if mask_fn is not None:
            mask_fn(pt)
        for (pso, out_ap, vaug, c0, ncol, st, sp, rd) in pv_list:
            S.op("pe", I("matmul", out_ap, lhsT=vaug, rhs=pt.t[:, c0:c0 + ncol], start=st, stop=sp),
                 reads=[pt] + rd, writes=[pso])

    def attn_seq(self, units):
        S = self.S
        prev = None
        for u in list(units) + [None]:
            cur = None
            if u is not None:
                score_mms, n = u[0], u[1]
                pss = self.psS.next()
                for i, (l, r, rd) in enumerate(score_mms):
                    S.op("pe", I("matmul", pss.t[:, 0:n], lhsT=l, rhs=r, start=(i == 0), stop=(i == len(score_mms) - 1)), reads=rd, writes=[pss])
                cur = (u, pss)
            if prev is not None:
                (pu, ppss) = prev
                n = pu[1]
                pt = self.ptr.next()
                S.op("act", I("activation", out=pt.t[:, 0:n], in_=ppss.t[:, 0:n], func=AF.Exp, scale=SCALE), reads=[ppss], writes=[pt])
                if pu[2] is not None:
                    pu[2](pt)
                for (pso, out_ap, vaug, c0, ncol, st, sp, rd) in pu[3]:
                    S.op("pe", I("matmul", out_ap, lhsT=vaug, rhs=pt.t[:, c0:c0 + ncol], start=st, stop=sp), reads=[pt] + rd, writes=[pso])
                if len(pu) > 4 and pu[4] is not None:
                    pu[4]()
            prev = cur

    def mask_mul(self, pt, c0, n, mask_ap):
        self.S.op("dve", I("tensor_tensor", out=pt.t[:, c0:c0 + n], in0=pt.t[:, c0:c0 + n], in1=mask_ap, op=ALU.mult), reads=[pt, self.cbf], writes=[pt])

    def phase_A(self, es0):
        S = self.S
        with ExitStack() as esA:
            self.phase_A_body(esA)
        S.barrier()

    def phase_A_body(self, esA):
        S = self.S
        self.uTh = self.sb(esA, "uTh", [128, 8, OWN], BF16)
        self.uTo = self.sb(esA, "uTo", [128, 8, OWN], BF16)
        self.wst = self.ring(esA, "wstA", [128, 8, 384], F32, 2)
        self.wsl = self.ring(esA, "wslA", [128, 8, 384], BF16, 2)
        for t in range(NT):
            self.norm_tile(self.x_halo[t * 128:(t + 1) * 128, :], self.uTh, self.uTh.t[:, :, t * 128:(t + 1) * 128])
        for t in range(NT):
            self.norm_tile(self.x_own[t * 128:(t + 1) * 128, :], self.uTo, self.uTo.t[:, :, t * 128:(t + 1) * 128])
        if self.stop_after == "A0":
            self.stopped = True
            return
        with ExitStack() as es:
            self.phase_A_inner(es)

    def phase_A_inner(self, es):
        S = self.S
        if True:
            qT = self.sb(es, "a_qT", [128, OWN], BF16)
            kT = self.sb(es, "a_kT", [128, 32 * 128], BF16)
            vaug = self.sb(es, "a_v", [128, 32, 2, 128], BF16)
            qk = self.ring(es, "a_qk", [128, 256], BF16, 2)
            if os.environ.get("PADLOW"):
                pad = self.sb(es, "a_pad", [128, int(os.environ["PADLOW"]) * 256], F32)
            acc = [self.sb(es, "a_acc%d" % i, [128, OWN], F32) for i in range(2)]
            rd_ = self.ring(es, "a_rd", [128, 512], F32, 2)
            ones64 = self.cbf.t[:, BFC["ones"]:BFC["ones"] + 64]
            hv64 = self.pcb.t[:, PCB["hv64"]:PCB["hv64"] + 64]
            hvcol = self.pcf.t[:, PCF["hv"]:PCF["hv"] + 1]
            for p in range(4):
                for g, d in enumerate(DILS):
                    nt = NT // d
                    st = self.wst.next()
                    sl = self.wsl.next()
                    for i, cb in enumerate((C_QA, C_KA, C_VA)):
                        c0 = cb + g * 512 + p * 128
                        for kc in range(8):
                            S.dma(I("dma_start", out=st.t[:, kc, i * 128:(i + 1) * 128], in_=self.w_in[kc * 128:(kc + 1) * 128, c0:c0 + 128]), writes=[st])
                    gb = self.gmix.t[:, 0:8].unsqueeze(2).to_broadcast([128, 8, 384])
                    if int(os.environ.get("A1CUT", "99")) >= 0:
                        S.op("pool", I("tensor_tensor", out=sl.t[:, :, 0:384], in0=st.t[:, :, 0:384], in1=gb, op=ALU.mult), reads=[st, self.gmix], writes=[sl])
                    if int(os.environ.get("A1CUT", "99")) <= 0:
                        self.stopped = True
                        return
                    for r in range(d):
                        for j in range(-1, nt):
                            slot = r * (nt + 1) + j + 1
                            start = 2048 + 128 * d * j + r
                            if start < 2048:
                                ut, s0 = self.uTh, start
                            else:
                                ut, s0 = self.uTo, start - 2048
                            lhs = lambda c, ut=ut, s0=s0, d=d: ut.t[:, c, s0:s0 + 127 * d + 1:d]
                            ridx = (15 + slot) if g == 0 else ((32 + slot) if g == 1 else (52 + slot))
                            ps = self.psA.next()
                            halo = (j == -1)
                            if halo:
                                self.proj_tm(lhs, ut, sl, 128, 256, ps)
                                kc0, vc0 = 0, 128
                            else:
                                self.proj_tm(lhs, ut, sl, 0, 384, ps)
                                kc0, vc0 = 128, 256

                            CUT = int(os.environ.get("A1CUT", "99"))
                            if CUT <= 1:
                                continue
                            t = qk.next()
                            if not halo:
                                self.rope_evac(ps, 0, 4, ridx, t, t.t[:, 0:256])
                            else:
                                self.rope_evac(ps, kc0, 2, ridx, t, t.t[:, 128:256])
                            if CUT <= 2:
                                continue
                            vsrc = ps.t[:, vc0:vc0 + 128].rearrange("p (h d) -> p h d", h=2)
                            if halo:
                                S.op("dve", I("tensor_scalar", out=vaug.t[:, slot, :, 0:64], in0=vsrc, scalar1=hvcol, scalar2=None, op0=ALU.mult), reads=[ps, self.pcf], writes=[vaug])
                                for hh in range(2):
                                    S.op("pool", I("tensor_copy", out=vaug.t[:, slot, hh, 64:128], in_=hv64), reads=[self.pcb], writes=[vaug])
                            else:
                                S.op("act", I("copy", out=vaug.t[:, slot, :, 0:64], in_=vsrc), reads=[ps], writes=[vaug])
                                for hh in range(2):
                                    S.op("pool", I("tensor_copy", out=vaug.t[:, slot, hh, 64:128], in_=ones64), reads=[self.cbf], writes=[vaug])
                            if CUT <= 3:
                                continue
                            pt = self.psT
                            if not halo:
                                S.op("pe", I("transpose", out=pt.t[:, 0:128], in_=t.t[:, 0:128], identity=self.ident), reads=[t, self.cbf], writes=[pt])
                            S.op("pe", I("transpose", out=pt.t[:, 128:256], in_=t.t[:, 128:256], identity=self.ident), reads=[t, self.cbf], writes=[pt])
                            if not halo:
                                qi = r * nt + j
                                S.op("act", I("copy", out=qT.t[:, qi * 128:(qi + 1) * 128], in_=pt.t[:, 0:128]), reads=[pt], writes=[qT])
                            S.op("dve", I("tensor_copy", out=kT.t[:, slot * 128:(slot + 1) * 128], in_=pt.t[:, 128:256]), reads=[pt], writes=[kT])
                    if self.stop_after == "A1" and int(os.environ.get("A1CUT", "99")) < 99:
                        self.stopped = True
                        return
                    if self.stop_after == "A1":
                        self.dump("qT", qT, qT.t[:], [128, OWN], BF16)
                        self.dump("kT", kT, kT.t[:, 0:17 * 128], [128, 17 * 128], BF16)
                        self.dump("vaug", vaug, vaug.t[:, 0:17], [128, 17, 2, 128], BF16)
                        self.stopped = True
                        return
                    for hh in range(2):
                        pb = 64 * hh
                        banks = {}
                        units = []
                        for r in range(d):
                            for j in range(-1, nt):
                                slot = r * (nt + 1) + j + 1
                                qlo = max(j, 0)
                                qhi = min(j + 1, nt - 1)
                                nq = qhi - qlo + 1
                                qc0 = (r * nt + qlo) * 128
                                n = nq * 128
                                if j == -1:
                                    mk = [(0, 128, self.cbf.t[:, BFC["tri_prev"]:BFC["tri_prev"] + 128])]
                                elif nq == 1:
                                    mk = [(0, 128, self.cbf.t[:, BFC["tri_diag"]:BFC["tri_diag"] + 128])]
                                else:
                                    mk = [(0, 256, self.cbf.t[:, BFC["tri_diag"]:BFC["tri_diag"] + 256])]

                                def mask_fn(pt, mk=mk):
                                    for (c0, nn, ap) in mk:
                                        self.mask_mul(pt, c0, nn, ap)
                                pv = []
                                for qt in range(qlo, qhi + 1):
                                    qi = r * nt + qt
                                    if (qt == j + 1) and (qi % 4 == 0):
                                        banks[qi // 4] = self.psO.next()
                                    pso = banks[qi // 4]
                                    col = (qi % 4) * 128
                                    pv.append((pso, pso.t[:, col:col + 128], vaug.t[:, slot, hh, :], (qt - qlo) * 128, 128, qt == j + 1, qt == j, [vaug]))
                                after = None
                                if j >= 0 and (r * nt + j) % 4 == 3:
                                    bk = (r * nt + j) // 4
                                    pso = banks[bk]
                                    av = acc[hh].t[:]
                                    if d == 1:
                                        dst = av[:, bk * 512:(bk + 1) * 512]
                                        src = pso.t[:, 0:512]
                                    elif d == 4:
                                        dst = av.rearrange("p (i r) -> p r i", r=4)[:, r, :]
                                        src = pso.t[:, 0:512]
                                    else:
                                        dst = av.rearrange("p (i r) -> p r i", r=16)[:, 4 * bk:4 * bk + 4, :]
                                        src = pso.t[:, 0:512].rearrange("p (r i) -> p r i", r=4)

                                    def after(dst=dst, src=src, pso=pso, hh=hh, g=g):
                                        if g == 0:
                                            S.op("act", I("copy", out=dst, in_=src), reads=[pso], writes=[acc[hh]])
                                        else:
                                            S.op("dve", I("tensor_tensor", out=dst, in0=dst, in1=src, op=ALU.add), reads=[pso, acc[hh]], writes=[acc[hh]])
                                units.append(([(kT.t[pb:pb + 64, slot * 128:(slot + 1) * 128], qT.t[pb:pb + 64, qc0:qc0 + n], [kT, qT])], n, mask_fn, pv, after))
                        self.attn_seq(units)
                if self.stop_after == "A2":
                    self.stopped = True
                    return
                for hh in range(2):
                    for c in range(4):
                        self.finalize(acc[hh], acc[hh].t[:, c * 512:(c + 1) * 512], hh, None, rd_, self.yaT, self.yaT.t[:, p, c * 512:(c + 1) * 512], first=True, last=True, ybacc=None)
                if self.stop_after == "A3":
                    self.stopped = True
                    return
        S.barrier()

    def finalize(self, src_tn, src_ap, hh, gate_row, rdring, dst_tn, dst_ap, first, last, ybacc):
        S = self.S
        psx = self.psX
        S.op("pe", I("matmul", psx.t[:, 0:512], lhsT=self.cf32.t[:, 0:128], rhs=src_ap, start=True, stop=True), reads=[src_tn, self.cf32], writes=[psx])
        rd = rdring.next()
        lo, hi = 64 * hh, 64 * hh + 64
        if hh == 0:
            den = psx.t[0:64, 0:512]
            num = src_ap[0:64, :]
        else:
            den = src_ap[64:128, :]
            num = psx.t[64:128, 0:512]
        S.op("dve", I("tensor_scalar", out=rd.t[lo:hi, :], in0=den, scalar1=1e-30, scalar2=None, op0=ALU.max), reads=[psx, src_tn], writes=[rd])
        S.op("dve", I("reciprocal", out=rd.t[lo:hi, :], in_=rd.t[lo:hi, :]), reads=[rd], writes=[rd])
        if gate_row is None:
            S.op("dve", I("tensor_tensor", out=dst_ap[lo:hi, :], in0=num, in1=rd.t[lo:hi, :], op=ALU.mult), reads=[psx, src_tn, rd], writes=[dst_tn])
            return
        S.op("dve", I("tensor_tensor", out=rd.t[lo:hi, :], in0=num, in1=rd.t[lo:hi, :], op=ALU.mult), reads=[psx, src_tn, rd], writes=[rd])
        gsel, gbT, gcols = gate_row
        S.op("pe", I("matmul", psx.t[:, 0:512], lhsT=gsel.t[:], rhs=gcols, start=True, stop=True), reads=[gbT, gsel, rd], writes=[psx])
        if first:
            S.op("dve", I("tensor_tensor", out=ybacc.t[lo:hi, :], in0=rd.t[lo:hi, :], in1=psx.t[lo:hi, 0:512], op=ALU.mult), reads=[psx, rd], writes=[ybacc])
        else:
            S.op("dve", I("tensor_tensor", out=rd.t[lo:hi, :], in0=rd.t[lo:hi, :], in1=psx.t[lo:hi, 0:512], op=ALU.mult), reads=[psx, rd], writes=[rd])
            if last:
                S.op("dve", I("tensor_tensor", out=dst_ap[lo:hi, :], in0=rd.t[lo:hi, :], in1=ybacc.t[lo:hi, :], op=ALU.add), reads=[rd, ybacc], writes=[dst_tn])
            else:
                S.op("dve", I("tensor_tensor", out=ybacc.t[lo:hi, :], in0=rd.t[lo:hi, :], in1=ybacc.t[lo:hi, :], op=ALU.add), reads=[rd, ybacc], writes=[ybacc])

    def phase_B(self, es0):
        S = self.S
        cbf, pcf, pcb = self.cbf, self.pcf, self.pcb
        ones64 = cbf.t[:, BFC["ones"]:BFC["ones"] + 64]
        ones2 = cbf.t[:, BFC["ones"]:BFC["ones"] + 128].rearrange("p (g d) -> p g d", g=2)
        with ExitStack() as esB:
            kslcT = self.sb(esB, "b_kslcT", [128, 48 * 128], BF16)
            vslc = self.sb(esB, "b_vslc", [128, 48, 2, 128], BF16)
            kcT = self.sb(esB, "b_kcT", [128, 512], BF16)
            vc = self.sb(esB, "b_vc", [128, 4, 2, 128], BF16)
            S.op("pool", I("memset", kcT.t[:], 0.0), writes=[kcT])
            S.op("pool", I("memset", vc.t[:], 0.0), writes=[vc])
            with ExitStack() as es:
                self.wst = self.ring(es, "wstB", [128, 8, 256], F32, 2)
                kcmpT = self.sb(es, "b_kcmpT", [128, S_LEN], BF16)
                vcmpT = self.sb(es, "b_vcmpT", [128, S_LEN], BF16)
                slab = self.sb(es, "b_slab", [128, 8, 512], BF16)
                uTt = self.ring(es, "b_uTt", [128, 8, 128], BF16, 2)
                tm = self.ring(es, "b_tm", [128, 512], BF16, 2)
                for dcol, scol in ((0, 0), (128, 256), (256, 128), (384, 384)):
                    self.cast_into(slab, lambda c0, n, dcol=dcol: slab.t[:, :, dcol + c0:dcol + c0 + n], self.w_in[:, C_KVB + scol:C_KVB + scol + 128], 8, 128, self.gmix, piece=128)
                for kt in range(64):
                    u = uTt.next()
                    self.norm_tile(self.x_full[kt * 128:(kt + 1) * 128, :], u, u.t[:])
                    ps = self.psA.next()
                    self.proj_tm(lambda c, u=u: u.t[:, c, :], u, slab, 0, 512, ps)
                    t = tm.next()
                    self.rope_evac(ps, 0, 4, 84 + kt, t, t.t[:, 0:256])
                    S.op("act", I("copy", out=t.t[:, 256:384], in_=ps.t[:, 256:384]), reads=[ps], writes=[t])
                    if kt < 48:
                        pvc = pcf.t[:, PCF["pv"] + kt:PCF["pv"] + kt + 1]
                        S.op("dve", I("tensor_scalar", out=vslc.t[:, kt, :, 0:64], in0=ps.t[:, 384:512].rearrange("p (g d) -> p g d", g=2), scalar1=pvc, scalar2=None, op0=ALU.mult), reads=[ps, pcf], writes=[vslc])
                        S.op("pool", I("tensor_scalar", out=vslc.t[:, kt, :, 64:128], in0=ones2, scalar1=pvc, scalar2=None, op0=ALU.mult), reads=[cbf, pcf], writes=[vslc])
                    pt = self.psT
                    for k in range(3):
                        S.op("pe", I("transpose", out=pt.t[:, k * 128:(k + 1) * 128], in_=t.t[:, k * 128:(k + 1) * 128], identity=self.ident), reads=[t, cbf], writes=[pt])
                    S.op("act", I("copy", out=kcmpT.t[:, kt * 128:(kt + 1) * 128], in_=pt.t[:, 0:128]), reads=[pt], writes=[kcmpT])
                    S.op("act", I("copy", out=vcmpT.t[:, kt * 128:(kt + 1) * 128], in_=pt.t[:, 256:384]), reads=[pt], writes=[vcmpT])
                    if kt < 48:
                        S.op("act", I("copy", out=kslcT.t[:, kt * 128:(kt + 1) * 128], in_=pt.t[:, 128:256]), reads=[pt], writes=[kslcT])
                if self.stop_after == "B2":
                    self.dump("kslcT", kslcT, kslcT.t[:], [128, 48 * 128], BF16)
                    self.dump("kcmpT", kcmpT, kcmpT.t[:], [128, S_LEN], BF16)
                    self.dump("vslc", vslc, vslc.t[:], [128, 48, 2, 128], BF16)
                    self.stopped = True
                    return
                w1sb = self.sb(es, "b_w1", [128, 32, 256], BF16)
                w2sb = self.sb(es, "b_w2", [128, 2, 128], BF16)
                posb = self.sb(es, "b_posb", [32, 128], BF16)
                posf = self.sb(es, "b_posf", [32, 64], F32)
                posT = self.sb(es, "b_posT", [128, 32], BF16)
                b1sb = self.sb(es, "b_b1", [128, 2], F32)
                gel = [self.sb(es, "b_gel%d" % i, [128, 512], BF16) for i in range(2)]
                hA = self.sb(es, "b_hA", [128, 512], F32)
                hB = self.sb(es, "b_hB", [128, 512], F32)
                for kv in range(2):
                    src = kcmpT if kv == 0 else vcmpT
                    w1d = self.cmp_w1_k if kv == 0 else self.cmp_w1_v
                    w2d = self.cmp_w2_k if kv == 0 else self.cmp_w2_v
                    posd = self.cmp_pos_k if kv == 0 else self.cmp_pos_v
                    w1v = w1d.rearrange("(j d) h -> d j h", d=64)
                    for j0 in range(0, 32, 8):
                        st = self.wst.next()
                        for half in range(2):
                            S.dma(I("dma_start", out=st.t[64 * half:64 * half + 64, :, :], in_=w1v[:, j0:j0 + 8, :]), writes=[st])
                        S.op("pool", I("tensor_copy", out=w1sb.t[:, j0:j0 + 8, :], in_=st.t[:, :, :]), reads=[st], writes=[w1sb])
                    st = self.wst.next()
                    S.dma(I("dma_start", out=st.t[:, 0:2, 0:64], in_=w2d.rearrange("(c p) n -> p c n", p=128)), writes=[st])
                    S.op("pool", I("tensor_copy", out=w2sb.t[:, :, 0:64], in_=st.t[:, 0:2, 0:64]), reads=[st], writes=[w2sb])
                    S.op("pool", I("tensor_copy", out=w2sb.t[:, :, 64:128], in_=st.t[:, 0:2, 0:64]), reads=[st], writes=[w2sb])
                    S.dma(I("dma_start", out=posf.t[:], in_=posd), writes=[posf])
                    S.op("dve", I("tensor_copy", out=posb.t[:, 0:64], in_=posf.t[:]), reads=[posf], writes=[posb])
                    S.op("dve", I("tensor_copy", out=posb.t[:, 64:128], in_=posf.t[:]), reads=[posf], writes=[posb])
                    pt = self.psT
                    S.op("pe", I("transpose", out=pt.t[:, 0:32], in_=posb.t[:], identity=cbf.t[0:32, 0:32]), reads=[posb, cbf], writes=[pt])
                    S.op("act", I("copy", out=posT.t[:], in_=pt.t[:, 0:32]), reads=[pt], writes=[posT])
                    psx = self.psX
                    for mh in range(2):
                        for j in range(32):
                            S.op("pe", I("matmul", psx.t[:, mh:mh + 1], lhsT=w1sb.t[0:64, j, mh * 128:(mh + 1) * 128], rhs=posT.t[0:64, j:j + 1], start=(j == 0), stop=(j == 31)), reads=[w1sb, posT], writes=[psx])
                    S.op("dve", I("tensor_copy", out=b1sb.t[:], in_=psx.t[:, 0:2]), reads=[psx], writes=[b1sb])
                    for g in range(2):
                        pb = 64 * g
                        for mh in range(2):
                            ps = self.psA.next()
                            for j in range(32):
                                S.op("pe", I("matmul", ps.t[:, 0:511], lhsT=w1sb.t[pb:pb + 64, j, mh * 128:(mh + 1) * 128], rhs=src.t[pb:pb + 64, j:j + 16 * 510 + 1:16], start=(j == 0), stop=(j == 31)), reads=[w1sb, src], writes=[ps])
                            S.op("act", I("activation", out=hA.t[:, 0:511], in_=ps.t[:, 0:511], func=AF.Identity, bias=b1sb.t[:, mh:mh + 1]), reads=[ps, b1sb], writes=[hA])
                            S.op("dve", I("tensor_tensor", out=hB.t[:, 0:511], in0=hA.t[:, 0:511], in1=hA.t[:, 0:511], op=ALU.mult), reads=[hA], writes=[hB])
                            S.op("dve", I("tensor_scalar", out=hB.t[:, 0:511], in0=hB.t[:, 0:511], scalar1=0.044715, scalar2=1.0, op0=ALU.mult, op1=ALU.add), reads=[hB], writes=[hB])
                            S.op("dve", I("tensor_tensor", out=hB.t[:, 0:511], in0=hB.t[:, 0:511], in1=hA.t[:, 0:511], op=ALU.mult), reads=[hA, hB], writes=[hB])
                            S.op("act", I("activation", out=hB.t[:, 0:511], in_=hB.t[:, 0:511], func=AF.Sigmoid, scale=2.0 * 0.7978845608028654), reads=[hB], writes=[hB])
                            S.op("dve", I("tensor_tensor", out=gel[mh].t[:, 0:511], in0=hA.t[:, 0:511], in1=hB.t[:, 0:511], op=ALU.mult), reads=[hA, hB], writes=[gel[mh]])
                        if kv == 0:
                            ps = self.psA.next()
                            for mh in range(2):
                                S.op("pe", I("matmul", ps.t[:, 0:511], lhsT=w2sb.t[:, mh, :], rhs=gel[mh].t[:, 0:511], start=(mh == 0), stop=(mh == 1)), reads=[w2sb, gel[mh]], writes=[ps])
                            S.op("act", I("copy", out=kcT.t[pb:pb + 64, 0:511], in_=ps.t[pb:pb + 64, 0:511]), reads=[ps], writes=[kcT])
                        else:
                            for bt in range(4):
                                n = 128 if bt < 3 else 127
                                ps = self.psA.next()
                                for mh in range(2):
                                    S.op("pe", I("matmul", ps.t[0:n, 0:64], lhsT=gel[mh].t[:, bt * 128:bt * 128 + n], rhs=w2sb.t[:, mh, 0:64], start=(mh == 0), stop=(mh == 1)), reads=[w2sb, gel[mh]], writes=[ps])
                                S.op("act", I("copy", out=vc.t[0:n, bt, g, 0:64], in_=ps.t[0:n, 0:64]), reads=[ps], writes=[vc])
                                S.op("pool", I("tensor_copy", out=vc.t[0:n, bt, g, 64:128], in_=ones64[0:n, :]), reads=[cbf], writes=[vc])
            S.barrier()
            if self.stop_after == "B3":
                self.dump("kcT", kcT, kcT.t[:], [128, 512], BF16)
                self.dump("vc", vc, vc.t[:], [128, 4, 2, 128], BF16)
                self.stopped = True
                return
            qbT = self.sb(esB, "b_qbT", [128, 4, OWN], BF16)
            gbT = self.sb(esB, "b_gbT", [32, OWN], F32)
            kwinT = self.sb(esB, "b_kwinT", [128, 20 * 128], BF16)
            vwin = self.sb(esB, "b_vwin", [128, 20, 2, 128], BF16)
            kso = self.sb(esB, "b_kso", [128, OWN], BF16)
            vso = self.sb(esB, "b_vso", [128, 16, 2, 128], BF16)
            S.op("pool", I("memset", gbT.t[:], 0.0), writes=[gbT])
            hvcol = pcf.t[:, PCF["hv"]:PCF["hv"] + 1]
            with ExitStack() as es:
                self.wst = self.ring(es, "wstB1", [128, 8, 256], F32, 2)
                slq = self.sb(es, "b_slq", [128, 8, 512], BF16)
                slkv = self.sb(es, "b_slkv", [128, 8, 512], BF16)
                slg = self.sb(es, "b_slg", [128, 8, 32], BF16)
                uTt = self.ring(es, "b_uTt1", [128, 8, 128], BF16, 2)
                tq = self.ring(es, "b_tq", [128, 512], BF16, 2)
                tk = self.ring(es, "b_tk", [128, 256], BF16, 2)
                self.cast_into(slq, lambda c0, n: slq.t[:, :, c0:c0 + n], self.w_in[:, C_QB:C_QB + 512], 8, 512, self.gmix, piece=256)
                self.cast_into(slkv, lambda c0, n: slkv.t[:, :, c0:c0 + n], self.w_in[:, C_KVB + 256:C_KVB + 768], 8, 512, self.gmix, piece=256)
                self.cast_into(slg, lambda c0, n: slg.t[:, :, c0:c0 + n], self.w_in[:, C_GB:C_GB + 24], 8, 24, self.gmix, piece=256)
                for e_ in range(12, 32):
                    slot = e_ - 12
                    own = e_ >= 16
                    i = e_ - 16
                    u = uTt.next()
                    xs = self.x_own[i * 128:(i + 1) * 128, :] if own else self.x_halo[e_ * 128:(e_ + 1) * 128, :]
                    self.norm_tile(xs, u, u.t[:])
                    lhs = lambda c, u=u: u.t[:, c, :]
                    if own:
                        ps = self.psA.next()
                        self.proj_tm(lhs, u, slq, 0, 512, ps)
                        t = tq.next()
                        self.rope_evac(ps, 0, 8, e_, t, t.t[:, 0:512], perm=True)
                        pt = self.psT
                        for hp in range(4):
                            S.op("pe", I("transpose", out=pt.t[:, hp * 128:(hp + 1) * 128], in_=t.t[:, hp * 128:(hp + 1) * 128], identity=self.ident), reads=[t, cbf], writes=[pt])
                        S.op("act", I("copy", out=qbT.t[:, :, i * 128:(i + 1) * 128], in_=pt.t[:, 0:512].rearrange("p (h t) -> p h t", h=4)), reads=[pt], writes=[qbT])
                        psx = self.psX
                        for c in range(8):
                            S.op("pe", I("matmul", psx.t[0:24, 0:128], lhsT=slg.t[:, c, 0:24], rhs=u.t[:, c, :], start=(c == 0), stop=(c == 7)), reads=[slg, u], writes=[psx])
                        S.op("act", I("activation", out=gbT.t[0:24, i * 128:(i + 1) * 128], in_=psx.t[0:24, 0:128], func=AF.Sigmoid), reads=[psx], writes=[gbT])
                    ps = self.psA.next()
                    t = tk.next()
                    if own:
                        self.proj_tm(lhs, u, slkv, 0, 512, ps)
                        self.rope_evac(ps, 0, 2, e_, t, t.t[:, 0:128])
                        self.rope_evac(ps, 256, 2, e_, t, t.t[:, 128:256])
                        S.op("dve", I("tensor_copy", out=vso.t[:, i, :, 0:64], in_=ps.t[:, 128:256].rearrange("p (g d) -> p g d", g=2)), reads=[ps], writes=[vso])
                        S.op("pool", I("tensor_copy", out=vso.t[:, i, :, 64:128], in_=ones2), reads=[cbf], writes=[vso])
                        S.op("dve", I("tensor_copy", out=vwin.t[:, slot, :, 0:64], in_=ps.t[:, 384:512].rearrange("p (g d) -> p g d", g=2)), reads=[ps], writes=[vwin])
                        S.op("pool", I("tensor_copy", out=vwin.t[:, slot, :, 64:128], in_=ones2), reads=[cbf], writes=[vwin])
                    else:
                        self.proj_tm(lhs, u, slkv, 256, 256, ps)
                        self.rope_evac(ps, 0, 2, e_, t, t.t[:, 128:256])
                        S.op("dve", I("tensor_scalar", out=vwin.t[:, slot, :, 0:64], in0=ps.t[:, 128:256].rearrange("p (g d) -> p g d", g=2), scalar1=hvcol, scalar2=None, op0=ALU.mult), reads=[ps, pcf], writes=[vwin])
                        S.op("pool", I("tensor_scalar", out=vwin.t[:, slot, :, 64:128], in0=ones2, scalar1=hvcol, scalar2=None, op0=ALU.mult), reads=[cbf, pcf], writes=[vwin])
                    pt = self.psT
                    if own:
                        S.op("pe", I("transpose", out=pt.t[:, 0:128], in_=t.t[:, 0:128], identity=self.ident), reads=[t, cbf], writes=[pt])
                    S.op("pe", I("transpose", out=pt.t[:, 128:256], in_=t.t[:, 128:256], identity=self.ident), reads=[t, cbf], writes=[pt])
                    if own:
                        S.op("act", I("copy", out=kso.t[:, i * 128:(i + 1) * 128], in_=pt.t[:, 0:128]), reads=[pt], writes=[kso])
                    S.op("act", I("copy", out=kwinT.t[:, slot * 128:(slot + 1) * 128], in_=pt.t[:, 128:256]), reads=[pt], writes=[kwinT])
            S.barrier()
            biasT = self.sb(esB, "b_biasT", [128, 2, OWN], BF16)
            t16 = self.sb(esB, "b_t16", [128, 2048], F32)
            S.dma(I("dma_start", out=t16.t[:], in_=self.c_t16), writes=[t16])
            with ExitStack() as es:
                et = self.ring(es, "b_et", [128, 512], F32, 4)
                pp = self.ring(es, "b_pp", [128, 520], F32, 4)
                lohi = self.ring(es, "b_lohi", [128, 256], F32, 2)
                imp = self.ring(es, "b_imp", [128, 128], F32, 2)
                imp2 = self.ring(es, "b_imp2", [128, 128], F32, 2)
                sm = self.ring(es, "b_sm", [128, 24], F32, 8)
                btm = self.ring(es, "b_btm", [128, 128], BF16, 2)
                for pq in pp.items:
                    S.op("pool", I("memset", pq.t[:], 0.0), writes=[pq])
                for i in range(NT):
                    lh = lohi.next()
                    S.dma(I("dma_start", out=lh.t[:, 0:128], in_=self.pc_lohi[:, i * 128:(i + 1) * 128]), writes=[lh])
                    S.dma(I("dma_start", out=lh.t[:, 128:256], in_=self.pc_lohi[:, 2048 + i * 128:2048 + (i + 1) * 128]), writes=[lh])
                    thr_i = pcf.t[:, PCF["thrc"] + i:PCF["thrc"] + i + 1]
                    def gen(g, i=i, lh=lh, thr_i=thr_i):
                        pb = 64 * g
                        P = pp.next()
                        P2 = pp.next()
                        for r in range(4):
                            ps = self.psS.next()
                            S.op("pe", I("matmul", ps.t[:, 0:511], lhsT=qbT.t[pb:pb + 64, r, i * 128:(i + 1) * 128], rhs=kcT.t[pb:pb + 64, 0:511], start=True, stop=True), reads=[qbT, kcT], writes=[ps])
                            e_ = et.next()
                            s_ = sm.next()
                            eng = "dve"
                            Pr = P if r % 2 == 0 else P2
                            S.op("act", I("activation", out=e_.t[:, 0:511], in_=ps.t[:, 0:511], func=AF.Exp, scale=SCALE), reads=[ps], writes=[e_])
                            S.op(eng, I("scalar_tensor_tensor", out=e_.t[:, 0:511], in0=t16.t[:, 0:511], scalar=thr_i, in1=e_.t[:, 0:511], op0=ALU.is_le, op1=ALU.mult, accum_out=s_.t[:, 0:1]), reads=[t16, pcf, e_], writes=[e_, s_])
                            yield
                            S.op(eng, I("tensor_scalar", out=s_.t[:, 1:2], in0=s_.t[:, 0:1], scalar1=1e-30, scalar2=None, op0=ALU.max), reads=[s_], writes=[s_])
                            yield
                            S.op("dve", I("reciprocal", out=s_.t[:, 2:3], in_=s_.t[:, 1:2]), reads=[s_], writes=[s_])
                            yield
                            if r < 2:
                                S.op(eng, I("tensor_scalar", out=Pr.t[:, 1:512], in0=e_.t[:, 0:511], scalar1=s_.t[:, 2:3], scalar2=None, op0=ALU.mult), reads=[e_, s_], writes=[Pr])
                                yield
                            else:
                                S.op(eng, I("scalar_tensor_tensor", out=Pr.t[:, 1:512], in0=e_.t[:, 0:511], scalar=s_.t[:, 2:3], in1=Pr.t[:, 1:512], op0=ALU.mult, op1=ALU.add), reads=[e_, s_, Pr], writes=[Pr])
                                yield
                        S.op("dve", I("tensor_tensor", out=P.t[:, 1:512], in0=P.t[:, 1:512], in1=P2.t[:, 1:512], op=ALU.add), reads=[P, P2], writes=[P])
                        yield
                        im = imp.next()
                        S.op("dve", I("tensor_tensor", out=im.t[:], in0=P.t[:, 0:512:4], in1=P.t[:, 1:513:4], op=ALU.add), reads=[P], writes=[im])
                        yield
                        for k in range(2, 5):
                            S.op("dve", I("tensor_tensor", out=im.t[:], in0=im.t[:], in1=P.t[:, k:k + 512:4], op=ALU.add), reads=[P, im], writes=[im])
                            yield
                        S.op("dve", I("tensor_tensor", out=im.t[:], in0=im.t[:], in1=lh.t[:, 0:128], op=ALU.max), reads=[lh, im], writes=[im])
                        yield
                        S.op("dve", I("tensor_tensor", out=im.t[:], in0=im.t[:], in1=lh.t[:, 128:256], op=ALU.min), reads=[lh, im], writes=[im])
                        yield
                        s_ = sm.next()
                        i2 = imp2.next()
                        S.op("dve", I("max", out=s_.t[:, 0:8], in_=im.t[:]), reads=[im], writes=[s_])
                        yield
                        S.op("dve", I("match_replace", out=i2.t[:], in_to_replace=s_.t[:, 0:8], in_values=im.t[:], imm_value=-1e9), reads=[im, s_], writes=[i2])
                        yield
                        S.op("dve", I("max", out=s_.t[:, 8:16], in_=i2.t[:]), reads=[i2], writes=[s_])
                        yield
                        S.op("dve", I("tensor_scalar", out=s_.t[:, 16:17], in0=s_.t[:, 15:16], scalar1=-1.5e4, scalar2=None, op0=ALU.max), reads=[s_], writes=[s_])
                        yield
                        bt_ = btm.next()
                        S.op("dve", I("tensor_scalar", out=bt_.t[:], in0=im.t[:], scalar1=s_.t[:, 16:17], scalar2=-30000.0, op0=ALU.is_lt, op1=ALU.mult), reads=[im, s_], writes=[bt_])
                        yield
                        pt = self.psT
                        S.op("pe", I("transpose", out=pt.t[:, 0:128], in_=bt_.t[:], identity=self.ident), reads=[bt_, cbf], writes=[pt])
                        S.op("act", I("copy", out=biasT.t[:, g, i * 128:(i + 1) * 128], in_=pt.t[:, 0:128]), reads=[pt], writes=[biasT])

                    gens = [gen(0), gen(1)]
                    while gens:
                        for gg in list(gens):
                            try:
                                next(gg)
                            except StopIteration:
                                gens.remove(gg)
            S.barrier()
            if self.stop_after == "B6":
                self.dump("biasT", biasT, biasT.t[:], [128, 2, OWN], BF16)
                self.dump("qbT", qbT, qbT.t[:], [128, 4, OWN], BF16)
                self.stopped = True
                return
            with ExitStack() as es:
                osb = self.ring(es, "b_osb", [128, 512], F32, 2)
                rdr = self.ring(es, "b_rd", [128, 512], F32, 2)
                ybacc = self.sb(es, "b_ybacc", [128, 512], F32)
                gsel = self.ring(es, "b_gsel", [32, 128], F32, 3)
                id32 = self.cf32.t[0:32, F32C["id32"]:F32C["id32"] + 32]
                eown = pcb.t[:, PCB["eown"]:PCB["eown"] + 2048]
                e32 = cbf.t[:, BFC["e32"]:BFC["e32"] + 2048]
                m4 = cbf.t[:, BFC["m4"]:BFC["m4"] + 2048]
                tri_diag = cbf.t[:, BFC["tri_diag"]:BFC["tri_diag"] + 128]
                win_far = cbf.t[:, BFC["win_far"]:BFC["win_far"] + 128]
                BRS = os.environ.get("BRS", "012")
                for hp in range(4):
                    for gi in range(2):
                        g = gi
                        h = hp + 4 * gi
                        pb = 64 * gi
                        sels = []
                        for br in range(3):
                            gs = gsel.next()
                            jrow = h * 3 + br
                            S.op("pool", I("tensor_copy", out=gs.t[:], in_=id32[:, jrow:jrow + 1].to_broadcast([32, 128])), reads=[self.cf32], writes=[gs])
                            sels.append(gs)
                        for c in range(4):
                            Q = qbT.t[pb:pb + 64, hp, c * 512:(c + 1) * 512]
                            gcols = gbT.t[0:32, c * 512:(c + 1) * 512]
                            dst = self.ybT.t[:, hp, c * 512:(c + 1) * 512]
                            pso = self.psO.next()
                            self._collect = []
                            for bt in range(4):
                                crel = pcf.t[:, PCF["crel"] + bt:PCF["crel"] + bt + 1]

                                def mask_c(pt, crel=crel, c=c):
                                    S.op("dve", I("scalar_tensor_tensor", out=pt.t[:, 0:512], in0=t16.t[:, c * 512:(c + 1) * 512], scalar=crel, in1=pt.t[:, 0:512], op0=ALU.is_ge, op1=ALU.mult), reads=[t16, pcf, pt], writes=[pt])
                                self.attn_unit([(kcT.t[pb:pb + 64, bt * 128:(bt + 1) * 128], Q, [kcT, qbT])], 512, mask_c,
                                               [(pso, pso.t[:, 0:512], vc.t[:, bt, g, :], 0, 512, bt == 0, bt == 3, [vc])])
                            self.attn_seq(self._collect)
                            self._collect = None
                            o = osb.next()
                            S.op("act", I("copy", out=o.t[:], in_=pso.t[:]), reads=[pso], writes=[o])
                            self.finalize(o, o.t[:], gi, (sels[0], gbT, gcols), rdr, self.ybT, dst, first=True, last=False, ybacc=ybacc)
                            pso = self.psO.next()
                            self._collect = []
                            for kt in range(48):
                                pb32 = 32 * (kt // 16)
                                kc_ = (kt % 16) * 128
                                self.attn_unit([(kslcT.t[pb:pb + 64, kt * 128:(kt + 1) * 128], Q, [kslcT, qbT]),
                                                (e32[pb32:pb32 + 32, kc_:kc_ + 128], biasT.t[pb32:pb32 + 32, g, c * 512:(c + 1) * 512], [cbf, biasT])], 512, None,
                                               [(pso, pso.t[:, 0:512], vslc.t[:, kt, g, :], 0, 512, kt == 0, False, [vslc])])
                            for j in range(4 * c + 4):
                                mf = None
                                if j >= 4 * c:
                                    mk = m4[:, (j - 4 * c) * 512:(j - 4 * c + 1) * 512]

                                    def mf(pt, mk=mk):
                                        self.mask_mul(pt, 0, 512, mk)
                                self.attn_unit([(kso.t[pb:pb + 64, j * 128:(j + 1) * 128], Q, [kso, qbT]),
                                                (eown[:, j * 128:(j + 1) * 128], biasT.t[:, g, c * 512:(c + 1) * 512], [pcb, biasT])], 512, mf,
                                               [(pso, pso.t[:, 0:512], vso.t[:, j, g, :], 0, 512, False, j == 4 * c + 3, [vso])])
                            self.attn_seq(self._collect)
                            self._collect = None
                            o = osb.next()
                            S.op("act", I("copy", out=o.t[:], in_=pso.t[:]), reads=[pso], writes=[o])
                            self.finalize(o, o.t[:], gi, (sels[1], gbT, gcols), rdr, self.ybT, dst, first=False, last=False, ybacc=ybacc)
                            pso = self.psO.next()
                            self._collect = []
                            for tq_ in range(4):
                                i = 4 * c + tq_
                                sq = 4 + i
                                for s_ in range(sq - 4, sq + 1):
                                    mf = None
                                    if s_ == sq - 4:
                                        def mf(pt):
                                            self.mask_mul(pt, 0, 128, win_far)
                                    elif s_ == sq:
                                        def mf(pt):
                                            self.mask_mul(pt, 0, 128, tri_diag)
                                    self.attn_unit([(kwinT.t[pb:pb + 64, s_ * 128:(s_ + 1) * 128], qbT.t[pb:pb + 64, hp, i * 128:(i + 1) * 128], [kwinT, qbT])], 128, mf,
                                                   [(pso, pso.t[:, tq_ * 128:(tq_ + 1) * 128], vwin.t[:, s_, g, :], 0, 128, s_ == sq - 4, s_ == sq, [vwin])])
                            self.attn_seq(self._collect)
                            self._collect = None
                            o = osb.next()
                            S.op("act", I("copy", out=o.t[:], in_=pso.t[:]), reads=[pso], writes=[o])
                            self.finalize(o, o.t[:], gi, (sels[2], gbT, gcols), rdr, self.ybT, dst, first=False, last=True, ybacc=ybacc)
        S.barrier()

    def phase_C(self, es0):
        S = self.S
        with ExitStack() as es:
            self.wst = self.ring(es, "wstC", [128, 8, 256], F32, 2)
            self.cast_engs = ("pool",)
            slabs = self.ring(es, "c_slab", [128, 8, 512], BF16, 3)
            self.wbslab = self.sb(es, "c_wb", [128, 4, 512], BF16)
            gfin = self.sb(es, "c_gfin", [128, D], F32)
            xc = self.sb(es, "c_xc", [128, 4, D], F32)
            uTc = self.sb(es, "c_uTc", [128, 8, 512], BF16)
            mTc = self.sb(es, "c_mTc", [128, 8, 512], BF16)
            u2Tc = self.sb(es, "c_u2Tc", [128, 8, 512], BF16)
            hT = self.sb(es, "c_hT", [128, 32, 512], BF16)
            sg = self.ring(es, "c_sg", [128, 512], BF16, 2)
            tf = self.ring(es, "c_tf", [128, 512], F32, 3)
            S.dma(I("dma_start", out=gfin.t[:], in_=self.g_fin), writes=[gfin])

            def slab_from(src_ap, kchunks, gain):
                sl = slabs.next()
                self.cast_into(sl, lambda c0, n, sl=sl: sl.t[:, 0:kchunks, c0:c0 + n], src_ap, kchunks, 512, gain, piece=256)
                return sl

            for c in range(4):
                cs = slice(c * 512, (c + 1) * 512)
                for tt in range(4):
                    r0 = c * 512 + tt * 128
                    S.dma(I("dma_start", out=xc.t[:, tt, :], in_=self.x_own[r0:r0 + 128, :]), writes=[xc])
                    self.norm_sb(xc, xc.t[:, tt, :], uTc, uTc.t[:, :, tt * 128:(tt + 1) * 128])
                for ctg in range(2):
                    gA = slab_from(self.w_in[:, C_GM + ctg * 512:C_GM + ctg * 512 + 512], 8, self.gmix)
                    gB = slab_from(self.w_in[:, C_GM + 1024 + ctg * 512:C_GM + 1024 + ctg * 512 + 512], 8, self.gmix)
                    wa = slab_from(self.w_a[:, ctg * 512:(ctg + 1) * 512], 4, None)
                    wb = self.wbslab
                    for c0 in range(0, 512, 256):
                        st = self.wst.next()
                        for two in range(2):
                            S.dma(I("dma_start", out=st.t[64 * two:64 * two + 64, 0:4, 0:256],
                                    in_=self.w_b[two * 256:(two + 1) * 256, ctg * 512 + c0:ctg * 512 + c0 + 256].rearrange("(hp d) n -> d hp n", d=64)), writes=[st])
                        S.op("pool", I("tensor_copy", out=wb.t[:, 0:4, c0:c0 + 256], in_=st.t[:, 0:4, 0:256]), reads=[st], writes=[wb])
                    for j in range(4):
                        ct = ctg * 4 + j
                        js = slice(j * 128, (j + 1) * 128)
                        sgs = []
                        for gw in (gA, gB):
                            ps = self.psA.next()
                            for k in range(8):
                                S.op("pe", I("matmul", ps.t[:, 0:512], lhsT=gw.t[:, k, js], rhs=uTc.t[:, k, :], start=(k == 0), stop=(k == 7)), reads=[gw, uTc], writes=[ps])
                            sgt = sg.next()
                            S.op("act", I("activation", out=sgt.t[:], in_=ps.t[:, 0:512], func=AF.Sigmoid), reads=[ps], writes=[sgt])
                            sgs.append(sgt)
                        psa = self.psS.next()
                        for k in range(4):
                            S.op("pe", I("matmul", psa.t[:, 0:512], lhsT=wa.t[:, k, js], rhs=self.yaT.t[:, k, cs], start=(k == 0), stop=(k == 3)), reads=[wa, self.yaT], writes=[psa])
                        psb = self.psO.next()
                        for k in range(4):
                            S.op("pe", I("matmul", psb.t[:, 0:512], lhsT=wb.t[:, k, js], rhs=self.ybT.t[:, k, cs], start=(k == 0), stop=(k == 3)), reads=[wb, self.ybT], writes=[psb])
                        t0 = tf.next()
                        t1 = tf.next()
                        S.op("dve", I("tensor_tensor", out=t0.t[:], in0=psa.t[:, 0:512], in1=sgs[0].t[:], op=ALU.mult), reads=[psa, sgs[0]], writes=[t0])
                        S.op("dve", I("tensor_tensor", out=t1.t[:], in0=psb.t[:, 0:512], in1=sgs[1].t[:], op=ALU.mult), reads=[psb, sgs[1]], writes=[t1])
                        S.op("dve", I("tensor_tensor", out=mTc.t[:, ct, :], in0=t0.t[:], in1=t1.t[:], op=ALU.add), reads=[t0, t1], writes=[mTc])
                for nh in range(2):
                    wo = slab_from(self.w_out[:, nh * 512:(nh + 1) * 512], 8, None)
                    for tt in range(4):
                        ps = self.psA.next()
                        for k in range(8):
                            S.op("pe", I("matmul", ps.t[:, 0:512], lhsT=mTc.t[:, k, tt * 128:(tt + 1) * 128], rhs=wo.t[:, k, :], start=(k == 0), stop=(k == 7)), reads=[wo, mTc], writes=[ps])
                        S.op("dve", I("tensor_tensor", out=xc.t[:, tt, nh * 512:(nh + 1) * 512], in0=ps.t[:, 0:512], in1=xc.t[:, tt, nh * 512:(nh + 1) * 512], op=ALU.add), reads=[ps, xc], writes=[xc])
                for tt in range(4):
                    self.norm_sb(xc, xc.t[:, tt, :], u2Tc, u2Tc.t[:, :, tt * 128:(tt + 1) * 128])
                for s_ in range(8):
                    wu = slab_from(self.w_up[:, s_ * 512:(s_ + 1) * 512], 8, self.gmlp)
                    for j in range(4):
                        ft = 4 * s_ + j
                        ps = self.psA.next()
                        for k in range(8):
                            S.op("pe", I("matmul", ps.t[:, 0:512], lhsT=wu.t[:, k, j * 128:(j + 1) * 128], rhs=u2Tc.t[:, k, :], start=(k == 0), stop=(k == 7)), reads=[wu, u2Tc], writes=[ps])
                        r = tf.next()
                        S.op("act", I("activation", out=r.t[:], in_=ps.t[:, 0:512], func=AF.Relu), reads=[ps], writes=[r])
                        S.op("dve", I("tensor_tensor", out=hT.t[:, ft, :], in0=r.t[:], in1=r.t[:], op=ALU.mult), reads=[r], writes=[hT])
                accs = [self.psA.items[0], self.psA.items[1], self.psS.items[0], self.psS.items[1]]
                for nh in range(2):
                    for kg in range(4):
                        wd = slab_from(self.w_down[kg * 1024:(kg + 1) * 1024, nh * 512:(nh + 1) * 512], 8, None)
                        for tt in range(4):
                            for k in range(8):
                                S.op("pe", I("matmul", accs[tt].t[:, 0:512], lhsT=hT.t[:, kg * 8 + k, tt * 128:(tt + 1) * 128], rhs=wd.t[:, k, :], start=(kg == 0 and k == 0), stop=(kg == 3 and k == 7)), reads=[wd, hT], writes=[accs[tt]])
                    for tt in range(4):
                        S.op("dve", I("tensor_tensor", out=xc.t[:, tt, nh * 512:(nh + 1) * 512], in0=accs[tt].t[:, 0:512], in1=xc.t[:, tt, nh * 512:(nh + 1) * 512], op=ALU.add), reads=[accs[tt], xc], writes=[xc])
                for tt in range(4):
                    jk = self.junk.next()
                    st = self.stat.next()
                    S.op("act", I("activation", out=jk.t[:], in_=xc.t[:, tt, :], func=AF.Square, accum_out=st.t[:, 0:1]), reads=[xc], writes=[jk, st])
                    S.op("act", I("activation", out=st.t[:, 1:2], in_=st.t[:, 0:1], func=AF.Sqrt, scale=1.0 / D, bias=self.epsc.t[:, 0:1]), reads=[st, self.epsc], writes=[st])
                    S.op("dve", I("reciprocal", out=st.t[:, 2:3], in_=st.t[:, 1:2]), reads=[st], writes=[st])
                    S.op("dve", I("scalar_tensor_tensor", out=xc.t[:, tt, :], in0=xc.t[:, tt, :], scalar=st.t[:, 2:3], in1=gfin.t[:], op0=ALU.mult, op1=ALU.mult), reads=[xc, st, gfin], writes=[xc])
                    r0 = c * 512 + tt * 128
                    S.dma(I("dma_start", out=self.out[r0:r0 + 128, :], in_=xc.t[:, tt, :]), reads=[xc])

def make_in_maps(inputs):
    x = np.ascontiguousarray(np.asarray(inputs["x"], np.float32))
    cbf, cf32, t16 = _static_tables()
    sq = lambda n: np.ascontiguousarray(np.asarray(inputs[n], np.float32)[0])
    gl = lambda v: np.ascontiguousarray(np.asarray(v, np.float32).reshape(8, 128).T)
    common = {
        "w_in": sq("w_in"), "g_mix": gl(inputs["norm_mix_g"][0]), "g_mlp": gl(inputs["norm_mlp_g"][0]),
        "g_fin": np.ascontiguousarray(np.broadcast_to(np.asarray(inputs["norm_final_g"], np.float32)[None, :], (128, D))),
        "cmp_w1_k": sq("cmp_w1_k"), "cmp_w1_v": sq("cmp_w1_v"), "cmp_w2_k": sq("cmp_w2_k"), "cmp_w2_v": sq("cmp_w2_v"),
        "cmp_pos_k": sq("cmp_pos_k"), "cmp_pos_v": sq("cmp_pos_v"),
        "w_a": sq("w_branch_a"), "w_b": sq("w_branch_b"), "w_out": sq("w_out"), "w_up": sq("w_up"), "w_down": sq("w_down"),
        "c_bf": cbf, "c_f32": cf32, "c_t16": t16,
    }
    tabs = [_percore_tables(q) for q in range(4)]
    maps = []
    for c in range(8):
        b, q = c // 4, c % 4
        T0 = OWN * q
        halo = x[b, T0 - OWN:T0] if q > 0 else np.zeros((OWN, D), np.float32)
        m = dict(common)
        m.update({"x_own": np.ascontiguousarray(x[b, T0:T0 + OWN]), "x_halo": np.ascontiguousarray(halo), "x_full": x[b],
                  "pc_f": tabs[q][0], "pc_lohi": tabs[q][1], "pc_bf": tabs[q][2]})
        maps.append(m)
    return maps


_CACHE = {}


def kernel(**inputs):
    if "nc" not in _CACHE:
        b = Builder()
        _CACHE["nc"] = b.build()
        _CACHE["decl"] = set(b._decl.keys())
    nc = _CACHE["nc"]
    maps = make_in_maps(inputs)
    decl = _CACHE["decl"]
    maps = [{k: v for k, v in m.items() if k in decl} for m in maps]
    res = run_bass_kernel_spmd(nc, maps, core_ids=list(range(8)))
    out = np.zeros((2, S_LEN, D), np.float32)
    for c in range(8):
        b, q = c // 4, c % 4
        out[b, OWN * q:OWN * (q + 1)] = res.results[c]["out"]
    return out
```

```python
import os
import numpy as np
import ml_dtypes
from contextlib import ExitStack
import concourse.bass as bass
import concourse.mybir as mybir
from concourse.bass_utils import run_bass_kernel_spmd

F32 = mybir.dt.float32
BF16 = mybir.dt.bfloat16
ALU = mybir.AluOpType
AF = mybir.ActivationFunctionType
NPBF = ml_dtypes.bfloat16

D = 1024
S_LEN = 8192
OWN = 2048
NT = 16
EPS = 1e-6
SCALE = 0.125
IN_COLS = 7960
C_QA, C_KA, C_VA = 0, 1536, 3072
C_QB = 4608
C_KVB = 5120
C_GB = 5888
C_GM = 5912
DILS = (1, 4, 16)

ENGS = ("pe", "act", "dve", "pool")
NDMA = 24


class Res:
    __slots__ = ("lw", "rd", "excl")

    def __init__(self):
        self.lw = None
        self.rd = {}
        self.excl = False


class Tn:
    __slots__ = ("t", "r")

    def __init__(self, t):
        self.t = t
        self.r = Res()


class Sched:
    def __init__(self, nc):
        self.nc = nc
        self.q = {e: [] for e in ENGS + ("sp",)}
        self.cnt = {e: 0 for e in ENGS}
        self.dcnt = [0] * NDMA
        self.seen = {e: {} for e in ENGS + ("sp",)}
        self.dnext = 0

    def _deps(self, reads, writes, mykey=None):
        deps = {}
        for r in reads:
            r = r.r if isinstance(r, Tn) else r
            if r.lw is not None and r.lw[1] > deps.get(r.lw[0], 0):
                deps[r.lw[0]] = r.lw[1]
            if r.excl:
                for k, v in r.rd.items():
                    if k != mykey and v > deps.get(k, 0):
                        deps[k] = v
        for w in writes:
            w = w.r if isinstance(w, Tn) else w
            if w.lw is not None and w.lw[0] != mykey and w.lw[1] > deps.get(w.lw[0], 0):
                deps[w.lw[0]] = w.lw[1]
            for k, v in w.rd.items():
                if v > deps.get(k, 0):
                    deps[k] = v
        return deps

    def _waits(self, eng, deps):
        waits = []
        seen = self.seen[eng]
        for k, v in deps.items():
            if v > seen.get(k, 0):
                waits.append((k, v))
                seen[k] = v
        return waits

    def _mark(self, key, my, reads, writes):
        for r in reads:
            r = r.r if isinstance(r, Tn) else r
            if my > r.rd.get(key, 0):
                r.rd[key] = my
        for w in writes:
            w = w.r if isinstance(w, Tn) else w
            w.lw = (key, my)
            w.rd = {}

    def op(self, eng, fn, reads=(), writes=()):
        deps = self._deps(reads, writes, ("e", eng))
        if eng == "pe":
            deps.pop(("e", "pe"), None)
        self.cnt[eng] += 1
        my = self.cnt[eng]
        key = ("e", eng)
        self.q[eng].append((self._waits(eng, deps), fn, key, my))
        self._mark(key, my, reads, writes)

    def dma(self, fn, reads=(), writes=()):
        deps = self._deps(reads, writes)
        k = self.dnext
        self.dnext = (self.dnext + 1) % NDMA
        key = ("d", k)
        if self.dcnt[k] > 0:
            deps[key] = max(deps.get(key, 0), self.dcnt[k])
        self.dcnt[k] += 16
        my = self.dcnt[k]
        self.q["sp"].append((self._waits("sp", deps), fn, key, my))
        self._mark(key, my, reads, writes)

    def barrier(self):
        allc = {}
        for e in ENGS:
            if self.cnt[e]:
                allc[("e", e)] = self.cnt[e]
        for k in range(NDMA):
            if self.dcnt[k]:
                allc[("d", k)] = self.dcnt[k]
        for e in ENGS + ("sp",):
            w = self._waits(e, dict(allc))
            if w:
                self.q[e].append((w, None, None, 0))

    def emit(self):
        nc = self.nc
        with ExitStack() as es:
            esem = {e: es.enter_context(nc.semaphore("s_" + e)) for e in ENGS}
            dsem = [es.enter_context(nc.semaphore("s_d%d" % i)) for i in range(NDMA)]

            def semof(key):
                return esem[key[1]] if key[0] == "e" else dsem[key[1]]
            fin = {}
            for e in ENGS:
                if self.cnt[e]:
                    fin[("e", e)] = self.cnt[e]
            for k in range(NDMA):
                if self.dcnt[k]:
                    fin[("d", k)] = self.dcnt[k]
            allsems = list(esem.values()) + dsem
            with nc.Block() as b0:
                @b0.sync
                def _(e):
                    for sm in allsems:
                        e.sem_clear(sm)
            block = es.enter_context(nc.Block())

            sig = {e: set() for e in ENGS}
            for name in self.q:
                for waits, fn, key, my in self.q[name]:
                    for (k, v) in waits:
                        if k[0] == "e":
                            sig[k[1]].add(v)
            for e in ENGS:
                if self.cnt[e]:
                    sig[e].add(self.cnt[e])
            rank = {}
            for e in ENGS:
                for i, v in enumerate(sorted(sig[e])):
                    rank[(e, v)] = i + 1

            def wval(k, v):
                return rank[(k[1], v)] if k[0] == "e" else v

            def run(name, engobj, final=False):
                for waits, fn, key, my in self.q[name]:
                    for (k, v) in waits:
                        engobj.wait_ge(semof(k), wval(k, v))
                    if fn is not None:
                        ins = fn(engobj)
                        if key[0] == "d":
                            ins.then_inc(semof(key), 16)
                        elif my in sig[key[1]]:
                            ins.then_inc(semof(key), 1)
                if final:
                    for k, v in fin.items():
                        engobj.wait_ge(semof(k), wval(k, v))

            @block.sync
            def _(e):
                run("sp", e, final=True)

            @block.tensor
            def _(e):
                run("pe", e)

            @block.scalar
            def _(e):
                run("act", e)

            @block.vector
            def _(e):
                run("dve", e)

            @block.gpsimd
            def _(e):
                run("pool", e)


def I(name, *a, **k):
    return lambda e: getattr(e, name)(*a, **k)


class Ring:
    def __init__(self, items):
        self.items = items
        self.i = 0

    def next(self):
        it = self.items[self.i % len(self.items)]
        self.i += 1
        return it


NROPE = 148
BFC = dict(ident=0, tri_diag=128, tri_prev=256, win_far=384, m4=512, e32=2560, ones=4608)
NBFC = 4736
F32C = dict(swap=0, id32=128)
NF32C = 160
PCF = dict(rope=0, thrc=NROPE * 16, pv=NROPE * 16 + 16, crel=NROPE * 16 + 80, hv=NROPE * 16 + 84)
NPCF = NROPE * 16 + 85
PCB = dict(eown=0, hv64=2048)
NPCB = 2112


def _static_tables():
    bf = np.zeros((128, NBFC), np.float32)
    k = np.arange(128)[:, None]
    q = np.arange(128)[None, :]
    bf[:, 0:128] = np.eye(128)
    bf[:, 128:256] = (q >= k)
    bf[:, 256:384] = (q <= k)
    bf[:, 384:512] = (q < k)
    for m in range(4):
        blk = np.zeros((128, 512), np.float32)
        for tq in range(4):
            if tq == m:
                blk[:, tq * 128:(tq + 1) * 128] = (q >= k)
            elif tq > m:
                blk[:, tq * 128:(tq + 1) * 128] = 1.0
        bf[:, 512 + m * 512: 512 + (m + 1) * 512] = blk
    b = np.arange(128)[:, None]
    for kt in range(16):
        i = np.arange(128)[None, :]
        bf[:, 2560 + kt * 128: 2560 + (kt + 1) * 128] = ((b % 32) == 2 * kt + (i >= 64))
    bf[:, 4608:4736] = 1.0
    f = np.zeros((128, NF32C), np.float32)
    f[:, 0:128] = (np.abs(k - q) == 64)
    f[0:32, 128:160] = np.eye(32)
    t16 = np.ascontiguousarray(np.broadcast_to(16.0 * np.arange(2048, dtype=np.float32)[None, :], (128, 2048)))
    return bf.astype(NPBF), f, t16


def _rope_rows(pos):
    inv = (500000.0 ** (-np.arange(0, 16, 2, dtype=np.float32) / np.float32(16))).astype(np.float32)
    ang = (pos.astype(np.float32)[:, None] * inv[None, :]).astype(np.float32)
    return np.concatenate([np.cos(ang), np.sin(ang)], axis=1).astype(np.float32)


def _percore_tables(qtr):
    T0 = OWN * qtr
    i = np.arange(128)
    f = np.zeros((128, NPCF), np.float32)
    rope = np.zeros((128, NROPE, 16), np.float32)
    for t in range(32):
        rope[:, t] = _rope_rows(T0 - OWN + 128 * t + i)
    for r in range(4):
        for j in range(-1, 4):
            rope[:, 32 + r * 5 + j + 1] = _rope_rows(T0 - OWN + 2048 + 512 * j + r + 4 * i)
    for r in range(16):
        for j in range(-1, 1):
            rope[:, 52 + r * 2 + j + 1] = _rope_rows(T0 - OWN + 2048 + 2048 * j + r + 16 * i)
    for kt in range(64):
        rope[:, 84 + kt] = _rope_rows(128 * kt + i)
    f[:, 0:NROPE * 16] = rope.reshape(128, -1)
    for ti in range(16):
        f[:, PCF["thrc"] + ti] = T0 + 128 * ti + i - 31
    for kt in range(64):
        f[:, PCF["pv"] + kt] = 1.0 if 128 * kt < T0 else 0.0
    for bt in range(4):
        f[:, PCF["crel"] + bt] = 16.0 * (16 * (128 * bt + i) + 31 - T0)
    f[:, PCF["hv"]] = 0.0 if qtr == 0 else 1.0
    lo = np.full((128, 16, 128), -3e4, np.float32)
    hi = np.full((128, 16, 128), 3e4, np.float32)
    m = np.arange(128)[None, :]
    for ti in range(16):
        cur = ((T0 + 128 * ti + i) // 64)[:, None]
        forced = (m == 0) | (m == cur) | (m == cur - 1)
        fut = m > cur
        lo[:, ti][forced] = 1e4
        hi[:, ti][forced] = 1e4
        lo[:, ti][fut] = -3e4
        hi[:, ti][fut] = -3e4
    lohi = np.concatenate([lo.reshape(128, -1), hi.reshape(128, -1)], axis=1)
    bfp = np.zeros((128, NPCB), np.float32)
    b = np.arange(128)[:, None]
    for j in range(16):
        ii = np.arange(128)[None, :]
        bfp[:, j * 128:(j + 1) * 128] = (b == 2 * (T0 // 128 + j) + (ii >= 64))
    bfp[:, 2048:2112] = 0.0 if qtr == 0 else 1.0
    return f, lohi.astype(np.float32), bfp.astype(NPBF)


class StopBuild(Exception):
    pass


class Builder:
    def __init__(self, debug=False, stop_after=None):
        self.debug = debug
        self.stop_after = stop_after
        self.nc = nc = bass.Bass("TRN2", target_bir_lowering=False)
        self.S = Sched(nc)
        self._decl = {}
        self._shapes = {
            "x_own": ([OWN, D], F32), "x_halo": ([OWN, D], F32), "x_full": ([S_LEN, D], F32), "w_in": ([D, IN_COLS], F32),
            "g_mix": ([128, 8], F32), "g_mlp": ([128, 8], F32), "g_fin": ([128, D], F32),
            "cmp_w1_k": ([2048, 256], F32), "cmp_w1_v": ([2048, 256], F32), "cmp_w2_k": ([256, 64], F32), "cmp_w2_v": ([256, 64], F32),
            "cmp_pos_k": ([32, 64], F32), "cmp_pos_v": ([32, 64], F32), "w_a": ([512, D], F32), "w_b": ([512, D], F32),
            "w_out": ([D, D], F32), "w_up": ([D, 4096], F32), "w_down": ([4096, D], F32),
            "c_bf": ([128, NBFC], BF16), "c_f32": ([128, NF32C], F32), "c_t16": ([128, 2048], F32),
            "pc_f": ([128, NPCF], F32), "pc_lohi": ([128, 4096], F32), "pc_bf": ([128, NPCB], BF16),
        }
        self.out = nc.dram_tensor("out", [OWN, D], F32, kind="ExternalOutput").ap()
        self.dbg = {}

    def __getattr__(self, name):
        sh = self.__dict__.get("_shapes", {})
        if name in sh:
            if name not in self._decl:
                self._decl[name] = self.nc.dram_tensor(name, list(sh[name][0]), sh[name][1], kind="ExternalInput").ap()
            return self._decl[name]
        raise AttributeError(name)

    def sb(self, es, name, shape, dt):
        return Tn(es.enter_context(self.nc.sbuf_tensor(name, list(shape), dt)))

    def ps(self, es, name, shape, dt):
        t = Tn(es.enter_context(self.nc.psum_tensor(name, list(shape), dt)))
        t.r.excl = True
        return t

    def ring(self, es, name, shape, dt, n):
        return Ring([self.sb(es, "%s%d" % (name, i), shape, dt) for i in range(n)])

    def dump(self, name, tn, ap, shape, dt):
        if not self.debug:
            return
        o = self.nc.dram_tensor("dbg_" + name, list(shape), dt, kind="ExternalOutput").ap()
        self.dbg[name] = True
        self.S.dma(I("dma_start", out=o, in_=ap), reads=[tn])

    def load_wslab(self, src_ap, ncols, gain, kchunks=8):
        S = self.S
        st = self.wst.next()
        sl = self.wsl.next()
        S.dma(I("dma_start", out=st.t[:, 0:kchunks, 0:ncols], in_=src_ap.rearrange("(c p) n -> p c n", p=128)), writes=[st])
        if gain is not None:
            gb = gain.t[:, 0:kchunks].unsqueeze(2).to_broadcast([128, kchunks, ncols])
            S.op("pool", I("tensor_tensor", out=sl.t[:, 0:kchunks, 0:ncols], in0=st.t[:, 0:kchunks, 0:ncols], in1=gb, op=ALU.mult),
                 reads=[st, gain], writes=[sl])
        else:
            S.op("pool", I("tensor_copy", out=sl.t[:, 0:kchunks, 0:ncols], in_=st.t[:, 0:kchunks, 0:ncols]), reads=[st], writes=[sl])
        return sl

    def cast_into(self, dst_tn, dst_ap_fn, src_ap, kchunks, ncols, gain, piece=512):
        S = self.S
        for c0 in range(0, ncols, piece):
            n = min(piece, ncols - c0)
            st = self.wst.next()
            S.dma(I("dma_start", out=st.t[:, 0:kchunks, 0:n], in_=src_ap[:, c0:c0 + n].rearrange("(c p) n -> p c n", p=128)), writes=[st])
            dst = dst_ap_fn(c0, n)
            engs = getattr(self, "cast_engs", ("pool",))
            self._ci = getattr(self, "_ci", 0) + 1
            ce = engs[self._ci % len(engs)]
            if gain is not None:
                gb = gain.t[:, 0:kchunks].unsqueeze(2).to_broadcast([128, kchunks, n])
                S.op(ce, I("tensor_tensor", out=dst, in0=st.t[:, 0:kchunks, 0:n], in1=gb, op=ALU.mult),
                     reads=[st, gain], writes=[dst_tn])
            else:
                S.op(ce, I("tensor_copy", out=dst, in_=st.t[:, 0:kchunks, 0:n]), reads=[st], writes=[dst_tn])

    def norm_tile(self, x_ap, ut_tn, ut_ap):
        S = self.S
        xt = self.xring.next()
        S.dma(I("dma_start", out=xt.t[:], in_=x_ap), writes=[xt])
        self.norm_sb(xt, xt.t[:], ut_tn, ut_ap)

    def norm_sb(self, xt, x_sb_ap, ut_tn, ut_ap, keep_rstd=None):
        S = self.S
        jk = self.junk.next()
        st = self.stat.next()
        S.op("act", I("activation", out=jk.t[:], in_=x_sb_ap, func=AF.Square, accum_out=st.t[:, 0:1]), reads=[xt], writes=[jk, st])
        S.op("act", I("activation", out=st.t[:, 1:2], in_=st.t[:, 0:1], func=AF.Sqrt, scale=1.0 / D, bias=self.epsc.t[:, 0:1]), reads=[st, self.epsc], writes=[st])
        S.op("dve", I("reciprocal", out=st.t[:, 2:3], in_=st.t[:, 1:2]), reads=[st], writes=[st])
        xn = self.xnring.next()
        S.op("dve", I("tensor_scalar", out=xn.t[:], in0=x_sb_ap, scalar1=st.t[:, 2:3], scalar2=None, op0=ALU.mult), reads=[xt, st], writes=[xn])
        pt = self.psT
        for c in range(8):
            S.op("pe", I("transpose", out=pt.t[:, c * 128:(c + 1) * 128], in_=xn.t[:, c * 128:(c + 1) * 128], identity=self.ident), reads=[xn, self.cbf], writes=[pt])
        S.op("act", I("copy", out=ut_ap, in_=pt.t[:, 0:1024].rearrange("p (c t) -> p c t", c=8)), reads=[pt], writes=[ut_tn])
        return st

    def proj_tm(self, lhs_fn, lhs_tn, slab, c0, ncols, ps):
        for c in range(8):
            self.S.op("pe", I("matmul", ps.t[:, 0:ncols], lhsT=lhs_fn(c), rhs=slab.t[:, c, c0:c0 + ncols], start=(c == 0), stop=(c == 7)),
                      reads=[lhs_tn, slab], writes=[ps])

    def rope_evac(self, ps, pc0, nh, ropeidx, dst_tn, dst_ap, perm=False):
        S = self.S
        ro = PCF["rope"] + ropeidx * 16
        ta = self.rtmp.next()
        if not perm:
            psv = ps.t[:, pc0:pc0 + 64 * nh].rearrange("p (h d) -> p h d", h=nh)
            dv = dst_ap.rearrange("p (h d) -> p h d", h=nh)
            tav = ta.t[:, 0:nh * 32].rearrange("p (h d) -> p h d", h=nh)
            cos1 = self.pcf.t[:, ro:ro + 8].unsqueeze(1).to_broadcast([128, nh, 8])
            sin1 = self.pcf.t[:, ro + 8:ro + 16].unsqueeze(1).to_broadcast([128, nh, 8])
            sl = lambda v, a, b: v[:, :, a:b]
        else:
            psv = ps.t[:, pc0:pc0 + 512].rearrange("p (two hp d) -> p two hp d", two=2, hp=4)
            dv = dst_ap.rearrange("p (hp two d) -> p two hp d", two=2, hp=4)
            tav = ta.t[:, 0:256].rearrange("p (two hp d) -> p two hp d", two=2, hp=4)
            cos1 = self.pcf.t[:, ro:ro + 8].unsqueeze(1).unsqueeze(1).to_broadcast([128, 2, 4, 8])
            sin1 = self.pcf.t[:, ro + 8:ro + 16].unsqueeze(1).unsqueeze(1).to_broadcast([128, 2, 4, 8])
            sl = lambda v, a, b: v[:, :, :, a:b]
        S.op("dve", I("tensor_copy", out=sl(dv, 16, 64), in_=sl(psv, 16, 64)), reads=[ps], writes=[dst_tn])
        S.op("dve", I("tensor_tensor", out=sl(tav, 0, 8), in0=sl(psv, 0, 8), in1=cos1, op=ALU.mult), reads=[ps, self.pcf], writes=[ta])
        S.op("dve", I("tensor_tensor", out=sl(tav, 8, 16), in0=sl(psv, 8, 16), in1=cos1, op=ALU.mult), reads=[ps, self.pcf], writes=[ta])
        S.op("dve", I("tensor_tensor", out=sl(tav, 16, 24), in0=sl(psv, 8, 16), in1=sin1, op=ALU.mult), reads=[ps, self.pcf], writes=[ta])
        S.op("dve", I("tensor_tensor", out=sl(tav, 24, 32), in0=sl(psv, 0, 8), in1=sin1, op=ALU.mult), reads=[ps, self.pcf], writes=[ta])
        S.op("dve", I("tensor_tensor", out=sl(dv, 0, 8), in0=sl(tav, 0, 8), in1=sl(tav, 16, 24), op=ALU.subtract), reads=[ta], writes=[dst_tn])
        S.op("dve", I("tensor_tensor", out=sl(dv, 8, 16), in0=sl(tav, 8, 16), in1=sl(tav, 24, 32), op=ALU.add), reads=[ta], writes=[dst_tn])

    def build(self):
        nc, S = self.nc, self.S
        with ExitStack() as es0:
            self.cbf = self.sb(es0, "cbf", [128, NBFC], BF16)
            self.cf32 = self.sb(es0, "cf32", [128, NF32C], F32)
            self.pcf = self.sb(es0, "pcf", [128, NPCF], F32)
            self.pcb = self.sb(es0, "pcb", [128, NPCB], BF16)
            self.gmix = self.sb(es0, "gmix", [128, 8], F32)
            self.gmlp = self.sb(es0, "gmlp", [128, 8], F32)
            self.epsc = self.sb(es0, "epsc", [128, 1], F32)
            S.dma(I("dma_start", out=self.cbf.t[:], in_=self.c_bf), writes=[self.cbf])
            S.dma(I("dma_start", out=self.cf32.t[:], in_=self.c_f32), writes=[self.cf32])
            S.dma(I("dma_start", out=self.pcf.t[:], in_=self.pc_f), writes=[self.pcf])
            S.dma(I("dma_start", out=self.pcb.t[:], in_=self.pc_bf), writes=[self.pcb])
            S.dma(I("dma_start", out=self.gmix.t[:], in_=self.g_mix), writes=[self.gmix])
            S.dma(I("dma_start", out=self.gmlp.t[:], in_=self.g_mlp), writes=[self.gmlp])
            S.op("dve", I("memset", self.epsc.t[:], EPS), writes=[self.epsc])
            self.ident = self.cbf.t[:, 0:128]
            self.xring = self.ring(es0, "xr", [128, D], F32, 2)
            self.junk = self.ring(es0, "jk", [128, D], BF16, 1)
            self.stat = self.ring(es0, "st", [128, 4], F32, 4)
            self.xnring = self.ring(es0, "xn", [128, D], BF16, 2)
            self.rtmp = self.ring(es0, "rtmp", [128, 256], F32, 2)
            self.ptr = self.ring(es0, "ptr", [128, 512], BF16, 3)
            self.psA = Ring([self.ps(es0, "psA%d" % i, [128, 512], F32) for i in range(2)])
            self.psT = self.ps(es0, "psT", [128, 1024], BF16)
            self.psS = Ring([self.ps(es0, "psS%d" % i, [128, 512], F32) for i in range(2)])
            self.psO = Ring([self.ps(es0, "psO%d" % i, [128, 512], F32) for i in range(2)])
            self.psX = self.ps(es0, "psX", [128, 512], F32)
            self.yaT = self.sb(es0, "yaT", [128, 4, OWN], BF16)
            self.stopped = False
            self.phase_A(es0)
            if self.stopped:
                S.barrier()
                if self.stop_after in ("A3", "A"):
                    self.dump("yaT", self.yaT, self.yaT.t[:], [128, 4, OWN], BF16)
                self.fake_out()
                S.emit()
                return nc
            self.ybT = self.sb(es0, "ybT", [128, 4, OWN], BF16)
            S.barrier()
            if self.stop_after == "A":
                self.dump("yaT", self.yaT, self.yaT.t[:], [128, 4, OWN], BF16)
                self.fake_out()
                S.emit()
                return nc
            self.phase_B(es0)
            S.barrier()
            if self.stopped:
                self.fake_out()
                S.emit()
                return nc
            if self.stop_after == "B":
                self.dump("yaT", self.yaT, self.yaT.t[:], [128, 4, OWN], BF16)
                self.dump("ybT", self.ybT, self.ybT.t[:], [128, 4, OWN], BF16)
                self.fake_out()
                S.emit()
                return nc
            self.phase_C(es0)
            if self.debug:
                self.dump("yaT", self.yaT, self.yaT.t[:], [128, 4, OWN], BF16)
                self.dump("ybT", self.ybT, self.ybT.t[:], [128, 4, OWN], BF16)
            S.emit()
        return nc

    def fake_out(self):
        S = self.S
        xt = self.xring.next()
        for t in range(NT):
            S.dma(I("dma_start", out=xt.t[:], in_=self.x_own[t * 128:(t + 1) * 128, :]), writes=[xt])
            S.dma(I("dma_start", out=self.out[t * 128:(t + 1) * 128, :], in_=xt.t[:]), reads=[xt])

    def attn_unit(self, score_mms, n, mask_fn, pv_list):
        S = self.S
        if getattr(self, "_collect", None) is not None:
            self._collect.append((score_mms, n, mask_fn, pv_list, None))
            return
        pss = self.psS.next()
        for i, (l, r, rd) in enumerate(score_mms):
            S.op("pe", I("matmul", pss.t[:, 0:n], lhsT=l, rhs=r, start=(i == 0), stop=(i == len(score_mms) - 1)),
                 reads=rd, writes=[pss])
        pt = self.ptr.next()
        S.op("act", I("activation", out=pt.t[:, 0:n], in_=pss.t[:, 0:n], func=AF.Exp, scale=SCALE), reads=[pss], writes=[pt])
        if mask_fn is not None:
            mask_fn(pt)
        for (pso, out_ap, vaug, c0, ncol, st, sp, rd) in pv_list:
            S.op("pe", I("matmul", out_ap, lhsT=vaug, rhs=pt.t[:, c0:c0 + ncol], start=st, stop=sp),
                 reads=[pt] + rd, writes=[pso])

    def attn_seq(self, units):
        S = self.S
        prev = None
        for u in list(units) + [None]:
            cur = None
            if u is not None:
                score_mms, n = u[0], u[1]
                pss = self.psS.next()
                for i, (l, r, rd) in enumerate(score_mms):
                    S.op("pe", I("matmul", pss.t[:, 0:n], lhsT=l, rhs=r, start=(i == 0), stop=(i == len(score_mms) - 1)), reads=rd, writes=[pss])
                cur = (u, pss)
            if prev is not None:
                (pu, ppss) = prev
                n = pu[1]
                pt = self.ptr.next()
                S.op("act", I("activation", out=pt.t[:, 0:n], in_=ppss.t[:, 0:n], func=AF.Exp, scale=SCALE), reads=[ppss], writes=[pt])
                if pu[2] is not None:
                    pu[2](pt)
                for (pso, out_ap, vaug, c0, ncol, st, sp, rd) in pu[3]:
                    S.op("pe", I("matmul", out_ap, lhsT=vaug, rhs=pt.t[:, c0:c0 + ncol], start=st, stop=sp), reads=[pt] + rd, writes=[pso])
                if len(pu) > 4 and pu[4] is not None:
                    pu[4]()
            prev = cur

    def attn_steps(self, steps, ring):
        S = self.S
        prev = None
        for stp in list(steps) + [None]:
            cur = None
            if stp is not None:
                cur = []
                for u in stp:
                    score_mms, n = u[0], u[1]
                    pss = ring.next()
                    cur.append((u, pss))
                nmm = max(len(u[0]) for u in stp)
                for i in range(nmm):
                    for (u, pss) in cur:
                        if i < len(u[0]):
                            l, r, rd = u[0][i]
                            S.op("pe", I("matmul", pss.t[:, 0:u[1]], lhsT=l, rhs=r, start=(i == 0), stop=(i == len(u[0]) - 1)), reads=rd, writes=[pss])
            if prev is not None:
                for (pu, ppss) in prev:
                    n = pu[1]
                    pt = self.ptr.next()
                    S.op("act", I("activation", out=pt.t[:, 0:n], in_=ppss.t[:, 0:n], func=AF.Exp, scale=SCALE), reads=[ppss], writes=[pt])
                    if pu[2] is not None:
                        pu[2](pt)
                    for (pso, out_ap, vaug, c0, ncol, st, sp, rd) in pu[3]:
                        S.op("pe", I("matmul", out_ap, lhsT=vaug, rhs=pt.t[:, c0:c0 + ncol], start=st, stop=sp), reads=[pt] + rd, writes=[pso])
            prev = cur

    def mask_mul(self, pt, c0, n, mask_ap):
        self.S.op("dve", I("tensor_tensor", out=pt.t[:, c0:c0 + n], in0=pt.t[:, c0:c0 + n], in1=mask_ap, op=ALU.mult), reads=[pt, self.cbf], writes=[pt])

    def phase_A(self, es0):
        S = self.S
        with ExitStack() as esA:
            self.phase_A_body(esA)
        S.barrier()

    def phase_A_body(self, esA):
        S = self.S
        self.uTh = self.sb(esA, "uTh", [128, 8, OWN], BF16)
        self.uTo = self.sb(esA, "uTo", [128, 8, OWN], BF16)
        self.wst = self.ring(esA, "wstA", [128, 8, 384], F32, 2)
        self.wsl = self.ring(esA, "wslA", [128, 8, 384], BF16, 2)
        for t in range(NT):
            self.norm_tile(self.x_halo[t * 128:(t + 1) * 128, :], self.uTh, self.uTh.t[:, :, t * 128:(t + 1) * 128])
        for t in range(NT):
            self.norm_tile(self.x_own[t * 128:(t + 1) * 128, :], self.uTo, self.uTo.t[:, :, t * 128:(t + 1) * 128])
        if self.stop_after == "A0":
            self.stopped = True
            return
        with ExitStack() as es:
            self.phase_A_inner(es)

    def phase_A_inner(self, es):
        S = self.S
        if True:
            qT = self.sb(es, "a_qT", [128, OWN], BF16)
            kT = self.sb(es, "a_kT", [128, 32 * 128], BF16)
            vaug = self.sb(es, "a_v", [128, 32, 2, 128], BF16)
            qk = self.ring(es, "a_qk", [128, 256], BF16, 2)
            if os.environ.get("PADLOW"):
                pad = self.sb(es, "a_pad", [128, int(os.environ["PADLOW"]) * 256], F32)
            acc = [self.sb(es, "a_acc%d" % i, [128, OWN], F32) for i in range(2)]
            rd_ = self.ring(es, "a_rd", [128, 512], F32, 2)
            ones64 = self.cbf.t[:, BFC["ones"]:BFC["ones"] + 64]
            hv64 = self.pcb.t[:, PCB["hv64"]:PCB["hv64"] + 64]
            hvcol = self.pcf.t[:, PCF["hv"]:PCF["hv"] + 1]
            for p in range(4):
                for g, d in enumerate(DILS):
                    nt = NT // d
                    st = self.wst.next()
                    sl = self.wsl.next()
                    for i, cb in enumerate((C_QA, C_KA, C_VA)):
                        c0 = cb + g * 512 + p * 128
                        for kc in range(8):
                            S.dma(I("dma_start", out=st.t[:, kc, i * 128:(i + 1) * 128], in_=self.w_in[kc * 128:(kc + 1) * 128, c0:c0 + 128]), writes=[st])
                    gb = self.gmix.t[:, 0:8].unsqueeze(2).to_broadcast([128, 8, 384])
                    if int(os.environ.get("A1CUT", "99")) >= 0:
                        S.op("pool", I("tensor_tensor", out=sl.t[:, :, 0:384], in0=st.t[:, :, 0:384], in1=gb, op=ALU.mult), reads=[st, self.gmix], writes=[sl])
                    if int(os.environ.get("A1CUT", "99")) <= 0:
                        self.stopped = True
                        return
                    for r in range(d):
                        for j in range(-1, nt):
                            slot = r * (nt + 1) + j + 1
                            start = 2048 + 128 * d * j + r
                            if start < 2048:
                                ut, s0 = self.uTh, start
                            else:
                                ut, s0 = self.uTo, start - 2048
                            lhs = lambda c, ut=ut, s0=s0, d=d: ut.t[:, c, s0:s0 + 127 * d + 1:d]
                            ridx = (15 + slot) if g == 0 else ((32 + slot) if g == 1 else (52 + slot))
                            ps = self.psA.next()
                            halo = (j == -1)
                            if halo:
                                self.proj_tm(lhs, ut, sl, 128, 256, ps)
                                kc0, vc0 = 0, 128
                            else:
                                self.proj_tm(lhs, ut, sl, 0, 384, ps)
                                kc0, vc0 = 128, 256

                            CUT = int(os.environ.get("A1CUT", "99"))
                            if CUT <= 1:
                                continue
                            t = qk.next()
                            if not halo:
                                self.rope_evac(ps, 0, 4, ridx, t, t.t[:, 0:256])
                            else:
                                self.rope_evac(ps, kc0, 2, ridx, t, t.t[:, 128:256])
                            if CUT <= 2:
                                continue
                            vsrc = ps.t[:, vc0:vc0 + 128].rearrange("p (h d) -> p h d", h=2)
                            if halo:
                                S.op("dve", I("tensor_scalar", out=vaug.t[:, slot, :, 0:64], in0=vsrc, scalar1=hvcol, scalar2=None, op0=ALU.mult), reads=[ps, self.pcf], writes=[vaug])
                                for hh in range(2):
                                    S.op("pool", I("tensor_copy", out=vaug.t[:, slot, hh, 64:128], in_=hv64), reads=[self.pcb], writes=[vaug])
                            else:
                                S.op("act", I("copy", out=vaug.t[:, slot, :, 0:64], in_=vsrc), reads=[ps], writes=[vaug])
                                for hh in range(2):
                                    S.op("pool", I("tensor_copy", out=vaug.t[:, slot, hh, 64:128], in_=ones64), reads=[self.cbf], writes=[vaug])
                            if CUT <= 3:
                                continue
                            pt = self.psT
                            if not halo:
                                S.op("pe", I("transpose", out=pt.t[:, 0:128], in_=t.t[:, 0:128], identity=self.ident), reads=[t, self.cbf], writes=[pt])
                            S.op("pe", I("transpose", out=pt.t[:, 128:256], in_=t.t[:, 128:256], identity=self.ident), reads=[t, self.cbf], writes=[pt])
                            if not halo:
                                qi = r * nt + j
                                S.op("act", I("copy", out=qT.t[:, qi * 128:(qi + 1) * 128], in_=pt.t[:, 0:128]), reads=[pt], writes=[qT])
                            S.op("dve", I("tensor_copy", out=kT.t[:, slot * 128:(slot + 1) * 128], in_=pt.t[:, 128:256]), reads=[pt], writes=[kT])
                    if self.stop_after == "A1" and int(os.environ.get("A1CUT", "99")) < 99:
                        self.stopped = True
                        return
                    if self.stop_after == "A1":
                        self.dump("qT", qT, qT.t[:], [128, OWN], BF16)
                        self.dump("kT", kT, kT.t[:, 0:17 * 128], [128, 17 * 128], BF16)
                        self.dump("vaug", vaug, vaug.t[:, 0:17], [128, 17, 2, 128], BF16)
                        self.stopped = True
                        return
                    for hh in range(2):
                        pb = 64 * hh
                        banks = {}
                        units = []
                        for r in range(d):
                            for j in range(-1, nt):
                                slot = r * (nt + 1) + j + 1
                                qlo = max(j, 0)
                                qhi = min(j + 1, nt - 1)
                                nq = qhi - qlo + 1
                                qc0 = (r * nt + qlo) * 128
                                n = nq * 128
                                if j == -1:
                                    mk = [(0, 128, self.cbf.t[:, BFC["tri_prev"]:BFC["tri_prev"] + 128])]
                                elif nq == 1:
                                    mk = [(0, 128, self.cbf.t[:, BFC["tri_diag"]:BFC["tri_diag"] + 128])]
                                else:
                                    mk = [(0, 256, self.cbf.t[:, BFC["tri_diag"]:BFC["tri_diag"] + 256])]

                                def mask_fn(pt, mk=mk):
                                    for (c0, nn, ap) in mk:
                                        self.mask_mul(pt, c0, nn, ap)
                                pv = []
                                for qt in range(qlo, qhi + 1):
                                    qi = r * nt + qt
                                    if (qt == j + 1) and (qi % 4 == 0):
                                        banks[qi // 4] = self.psO.next()
                                    pso = banks[qi // 4]
                                    col = (qi % 4) * 128
                                    pv.append((pso, pso.t[:, col:col + 128], vaug.t[:, slot, hh, :], (qt - qlo) * 128, 128, qt == j + 1, qt == j, [vaug]))
                                after = None
                                if j >= 0 and (r * nt + j) % 4 == 3:
                                    bk = (r * nt + j) // 4
                                    pso = banks[bk]
                                    av = acc[hh].t[:]
                                    if d == 1:
                                        dst = av[:, bk * 512:(bk + 1) * 512]
                                        src = pso.t[:, 0:512]
                                    elif d == 4:
                                        dst = av.rearrange("p (i r) -> p r i", r=4)[:, r, :]
                                        src = pso.t[:, 0:512]
                                    else:
                                        dst = av.rearrange("p (i r) -> p r i", r=16)[:, 4 * bk:4 * bk + 4, :]
                                        src = pso.t[:, 0:512].rearrange("p (r i) -> p r i", r=4)

                                    def after(dst=dst, src=src, pso=pso, hh=hh, g=g):
                                        if g == 0:
                                            S.op("act", I("copy", out=dst, in_=src), reads=[pso], writes=[acc[hh]])
                                        else:
                                            S.op("dve", I("tensor_tensor", out=dst, in0=dst, in1=src, op=ALU.add), reads=[pso, acc[hh]], writes=[acc[hh]])
                                units.append(([(kT.t[pb:pb + 64, slot * 128:(slot + 1) * 128], qT.t[pb:pb + 64, qc0:qc0 + n], [kT, qT])], n, mask_fn, pv, after))
                        self.attn_seq(units)
                if self.stop_after == "A2":
                    self.stopped = True
                    return
                for hh in range(2):
                    for c in range(4):
                        self.finalize(acc[hh], acc[hh].t[:, c * 512:(c + 1) * 512], hh, None, rd_, self.yaT, self.yaT.t[:, p, c * 512:(c + 1) * 512], first=True, last=True, ybacc=None)
                if self.stop_after == "A3":
                    self.stopped = True
                    return
        S.barrier()

    def finalize(self, src_tn, src_ap, hh, gate_row, rdring, dst_tn, dst_ap, first, last, ybacc):
        S = self.S
        psx = self.psX
        S.op("pe", I("matmul", psx.t[:, 0:512], lhsT=self.cf32.t[:, 0:128], rhs=src_ap, start=True, stop=True), reads=[src_tn, self.cf32], writes=[psx])
        rd = rdring.next()
        lo, hi = 64 * hh, 64 * hh + 64
        if hh == 0:
            den = psx.t[0:64, 0:512]
            num = src_ap[0:64, :]
        else:
            den = src_ap[64:128, :]
            num = psx.t[64:128, 0:512]
        S.op("dve", I("tensor_scalar", out=rd.t[lo:hi, :], in0=den, scalar1=1e-30, scalar2=None, op0=ALU.max), reads=[psx, src_tn], writes=[rd])
        S.op("dve", I("reciprocal", out=rd.t[lo:hi, :], in_=rd.t[lo:hi, :]), reads=[rd], writes=[rd])
        if gate_row is None:
            S.op("dve", I("tensor_tensor", out=dst_ap[lo:hi, :], in0=num, in1=rd.t[lo:hi, :], op=ALU.mult), reads=[psx, src_tn, rd], writes=[dst_tn])
            return
        S.op("dve", I("tensor_tensor", out=rd.t[lo:hi, :], in0=num, in1=rd.t[lo:hi, :], op=ALU.mult), reads=[psx, src_tn, rd], writes=[rd])
        gsel, gbT, gcols = gate_row
        S.op("pe", I("matmul", psx.t[:, 0:512], lhsT=gsel.t[:], rhs=gcols, start=True, stop=True), reads=[gbT, gsel, rd], writes=[psx])
        if first:
            S.op("dve", I("tensor_tensor", out=ybacc.t[lo:hi, :], in0=rd.t[lo:hi, :], in1=psx.t[lo:hi, 0:512], op=ALU.mult), reads=[psx, rd], writes=[ybacc])
        else:
            S.op("dve", I("tensor_tensor", out=rd.t[lo:hi, :], in0=rd.t[lo:hi, :], in1=psx.t[lo:hi, 0:512], op=ALU.mult), reads=[psx, rd], writes=[rd])
            if last:
                S.op("dve", I("tensor_tensor", out=dst_ap[lo:hi, :], in0=rd.t[lo:hi, :], in1=ybacc.t[lo:hi, :], op=ALU.add), reads=[rd, ybacc], writes=[dst_tn])
            else:
                S.op("dve", I("tensor_tensor", out=ybacc.t[lo:hi, :], in0=rd.t[lo:hi, :], in1=ybacc.t[lo:hi, :], op=ALU.add), reads=[rd, ybacc], writes=[ybacc])

    def phase_B(self, es0):
        S = self.S
        cbf, pcf, pcb = self.cbf, self.pcf, self.pcb
        ones64 = cbf.t[:, BFC["ones"]:BFC["ones"] + 64]
        ones2 = cbf.t[:, BFC["ones"]:BFC["ones"] + 128].rearrange("p (g d) -> p g d", g=2)
        with ExitStack() as esB:
            kslcT = self.sb(esB, "b_kslcT", [128, 48 * 128], BF16)
            vslc = self.sb(esB, "b_vslc", [128, 48, 2, 128], BF16)
            kcT = self.sb(esB, "b_kcT", [128, 512], BF16)
            vc = self.sb(esB, "b_vc", [128, 4, 2, 128], BF16)
            S.op("pool", I("memset", kcT.t[:], 0.0), writes=[kcT])
            S.op("pool", I("memset", vc.t[:], 0.0), writes=[vc])
            with ExitStack() as es:
                self.wst = self.ring(es, "wstB", [128, 8, 256], F32, 2)
                kcmpT = self.sb(es, "b_kcmpT", [128, S_LEN], BF16)
                vcmpT = self.sb(es, "b_vcmpT", [128, S_LEN], BF16)
                slab = self.sb(es, "b_slab", [128, 8, 512], BF16)
                uTt = self.ring(es, "b_uTt", [128, 8, 128], BF16, 2)
                tm = self.ring(es, "b_tm", [128, 512], BF16, 2)
                for dcol, scol in ((0, 0), (128, 256), (256, 128), (384, 384)):
                    self.cast_into(slab, lambda c0, n, dcol=dcol: slab.t[:, :, dcol + c0:dcol + c0 + n], self.w_in[:, C_KVB + scol:C_KVB + scol + 128], 8, 128, self.gmix, piece=128)
                for kt in range(64):
                    u = uTt.next()
                    self.norm_tile(self.x_full[kt * 128:(kt + 1) * 128, :], u, u.t[:])
                    ps = self.psA.next()
                    self.proj_tm(lambda c, u=u: u.t[:, c, :], u, slab, 0, 512, ps)
                    t = tm.next()
                    self.rope_evac(ps, 0, 4, 84 + kt, t, t.t[:, 0:256])
                    S.op("act", I("copy", out=t.t[:, 256:384], in_=ps.t[:, 256:384]), reads=[ps], writes=[t])
                    if kt < 48:
                        pvc = pcf.t[:, PCF["pv"] + kt:PCF["pv"] + kt + 1]
                        S.op("dve", I("tensor_scalar", out=vslc.t[:, kt, :, 0:64], in0=ps.t[:, 384:512].rearrange("p (g d) -> p g d", g=2), scalar1=pvc, scalar2=None, op0=ALU.mult), reads=[ps, pcf], writes=[vslc])
                        S.op("pool", I("tensor_scalar", out=vslc.t[:, kt, :, 64:128], in0=ones2, scalar1=pvc, scalar2=None, op0=ALU.mult), reads=[cbf, pcf], writes=[vslc])
                    pt = self.psT
                    for k in range(3):
                        S.op("pe", I("transpose", out=pt.t[:, k * 128:(k + 1) * 128], in_=t.t[:, k * 128:(k + 1) * 128], identity=self.ident), reads=[t, cbf], writes=[pt])
                    S.op("act", I("copy", out=kcmpT.t[:, kt * 128:(kt + 1) * 128], in_=pt.t[:, 0:128]), reads=[pt], writes=[kcmpT])
                    S.op("act", I("copy", out=vcmpT.t[:, kt * 128:(kt + 1) * 128], in_=pt.t[:, 256:384]), reads=[pt], writes=[vcmpT])
                    if kt < 48:
                        S.op("act", I("copy", out=kslcT.t[:, kt * 128:(kt + 1) * 128], in_=pt.t[:, 128:256]), reads=[pt], writes=[kslcT])
                if self.stop_after == "B2":
                    self.dump("kslcT", kslcT, kslcT.t[:], [128, 48 * 128], BF16)
                    self.dump("kcmpT", kcmpT, kcmpT.t[:], [128, S_LEN], BF16)
                    self.dump("vslc", vslc, vslc.t[:], [128, 48, 2, 128], BF16)
                    self.stopped = True
                    return
                w1sb = self.sb(es, "b_w1", [128, 32, 256], BF16)
                w2sb = self.sb(es, "b_w2", [128, 2, 128], BF16)
                posb = self.sb(es, "b_posb", [32, 128], BF16)
                posf = self.sb(es, "b_posf", [32, 64], F32)
                posT = self.sb(es, "b_posT", [128, 32], BF16)
                b1sb = self.sb(es, "b_b1", [128, 2], F32)
                gel = [self.sb(es, "b_gel%d" % i, [128, 512], BF16) for i in range(2)]
                hA = self.sb(es, "b_hA", [128, 512], F32)
                hB = self.sb(es, "b_hB", [128, 512], F32)
                for kv in range(2):
                    src = kcmpT if kv == 0 else vcmpT
                    w1d = self.cmp_w1_k if kv == 0 else self.cmp_w1_v
                    w2d = self.cmp_w2_k if kv == 0 else self.cmp_w2_v
                    posd = self.cmp_pos_k if kv == 0 else self.cmp_pos_v
                    w1v = w1d.rearrange("(j d) h -> d j h", d=64)
                    for j0 in range(0, 32, 8):
                        st = self.wst.next()
                        for half in range(2):
                            S.dma(I("dma_start", out=st.t[64 * half:64 * half + 64, :, :], in_=w1v[:, j0:j0 + 8, :]), writes=[st])
                        S.op("pool", I("tensor_copy", out=w1sb.t[:, j0:j0 + 8, :], in_=st.t[:, :, :]), reads=[st], writes=[w1sb])
                    st = self.wst.next()
                    S.dma(I("dma_start", out=st.t[:, 0:2, 0:64], in_=w2d.rearrange("(c p) n -> p c n", p=128)), writes=[st])
                    S.op("pool", I("tensor_copy", out=w2sb.t[:, :, 0:64], in_=st.t[:, 0:2, 0:64]), reads=[st], writes=[w2sb])
                    S.op("pool", I("tensor_copy", out=w2sb.t[:, :, 64:128], in_=st.t[:, 0:2, 0:64]), reads=[st], writes=[w2sb])
                    S.dma(I("dma_start", out=posf.t[:], in_=posd), writes=[posf])
                    S.op("dve", I("tensor_copy", out=posb.t[:, 0:64], in_=posf.t[:]), reads=[posf], writes=[posb])
                    S.op("dve", I("tensor_copy", out=posb.t[:, 64:128], in_=posf.t[:]), reads=[posf], writes=[posb])
                    pt = self.psT
                    S.op("pe", I("transpose", out=pt.t[:, 0:32], in_=posb.t[:], identity=cbf.t[0:32, 0:32]), reads=[posb, cbf], writes=[pt])
                    S.op("act", I("copy", out=posT.t[:], in_=pt.t[:, 0:32]), reads=[pt], writes=[posT])
                    psx = self.psX
                    for mh in range(2):
                        for j in range(32):
                            S.op("pe", I("matmul", psx.t[:, mh:mh + 1], lhsT=w1sb.t[0:64, j, mh * 128:(mh + 1) * 128], rhs=posT.t[0:64, j:j + 1], start=(j == 0), stop=(j == 31)), reads=[w1sb, posT], writes=[psx])
                    S.op("dve", I("tensor_copy", out=b1sb.t[:], in_=psx.t[:, 0:2]), reads=[psx], writes=[b1sb])
                    for g in range(2):
                        pb = 64 * g
                        for mh in range(2):
                            ps = self.psA.next()
                            for j in range(32):
                                S.op("pe", I("matmul", ps.t[:, 0:511], lhsT=w1sb.t[pb:pb + 64, j, mh * 128:(mh + 1) * 128], rhs=src.t[pb:pb + 64, j:j + 16 * 510 + 1:16], start=(j == 0), stop=(j == 31)), reads=[w1sb, src], writes=[ps])
                            S.op("act", I("activation", out=hA.t[:, 0:511], in_=ps.t[:, 0:511], func=AF.Identity, bias=b1sb.t[:, mh:mh + 1]), reads=[ps, b1sb], writes=[hA])
                            S.op("dve", I("tensor_tensor", out=hB.t[:, 0:511], in0=hA.t[:, 0:511], in1=hA.t[:, 0:511], op=ALU.mult), reads=[hA], writes=[hB])
                            S.op("dve", I("tensor_scalar", out=hB.t[:, 0:511], in0=hB.t[:, 0:511], scalar1=0.044715, scalar2=1.0, op0=ALU.mult, op1=ALU.add), reads=[hB], writes=[hB])
                            S.op("dve", I("tensor_tensor", out=hB.t[:, 0:511], in0=hB.t[:, 0:511], in1=hA.t[:, 0:511], op=ALU.mult), reads=[hA, hB], writes=[hB])
                            S.op("act", I("activation", out=hB.t[:, 0:511], in_=hB.t[:, 0:511], func=AF.Sigmoid, scale=2.0 * 0.7978845608028654), reads=[hB], writes=[hB])
                            S.op("dve", I("tensor_tensor", out=gel[mh].t[:, 0:511], in0=hA.t[:, 0:511], in1=hB.t[:, 0:511], op=ALU.mult), reads=[hA, hB], writes=[gel[mh]])
                        if kv == 0:
                            ps = self.psA.next()
                            for mh in range(2):
                                S.op("pe", I("matmul", ps.t[:, 0:511], lhsT=w2sb.t[:, mh, :], rhs=gel[mh].t[:, 0:511], start=(mh == 0), stop=(mh == 1)), reads=[w2sb, gel[mh]], writes=[ps])
                            S.op("act", I("copy", out=kcT.t[pb:pb + 64, 0:511], in_=ps.t[pb:pb + 64, 0:511]), reads=[ps], writes=[kcT])
                        else:
                            for bt in range(4):
                                n = 128 if bt < 3 else 127
                                ps = self.psA.next()
                                for mh in range(2):
                                    S.op("pe", I("matmul", ps.t[0:n, 0:64], lhsT=gel[mh].t[:, bt * 128:bt * 128 + n], rhs=w2sb.t[:, mh, 0:64], start=(mh == 0), stop=(mh == 1)), reads=[w2sb, gel[mh]], writes=[ps])
                                S.op("act", I("copy", out=vc.t[0:n, bt, g, 0:64], in_=ps.t[0:n, 0:64]), reads=[ps], writes=[vc])
                                S.op("pool", I("tensor_copy", out=vc.t[0:n, bt, g, 64:128], in_=ones64[0:n, :]), reads=[cbf], writes=[vc])
            S.barrier()
            if self.stop_after == "B3":
                self.dump("kcT", kcT, kcT.t[:], [128, 512], BF16)
                self.dump("vc", vc, vc.t[:], [128, 4, 2, 128], BF16)
                self.stopped = True
                return
            qbT = self.sb(esB, "b_qbT", [128, 4, OWN], BF16)
            gbT = self.sb(esB, "b_gbT", [32, OWN], F32)
            kwinT = self.sb(esB, "b_kwinT", [128, 20 * 128], BF16)
            vwin = self.sb(esB, "b_vwin", [128, 20, 2, 128], BF16)
            kso = self.sb(esB, "b_kso", [128, OWN], BF16)
            vso = self.sb(esB, "b_vso", [128, 16, 2, 128], BF16)
            S.op("pool", I("memset", gbT.t[:], 0.0), writes=[gbT])
            hvcol = pcf.t[:, PCF["hv"]:PCF["hv"] + 1]
            with ExitStack() as es:
                self.wst = self.ring(es, "wstB1", [128, 8, 256], F32, 2)
                slq = self.sb(es, "b_slq", [128, 8, 512], BF16)
                slkv = self.sb(es, "b_slkv", [128, 8, 512], BF16)
                slg = self.sb(es, "b_slg", [128, 8, 32], BF16)
                uTt = self.ring(es, "b_uTt1", [128, 8, 128], BF16, 2)
                tq = self.ring(es, "b_tq", [128, 512], BF16, 2)
                tk = self.ring(es, "b_tk", [128, 256], BF16, 2)
                self.cast_into(slq, lambda c0, n: slq.t[:, :, c0:c0 + n], self.w_in[:, C_QB:C_QB + 512], 8, 512, self.gmix, piece=256)
                self.cast_into(slkv, lambda c0, n: slkv.t[:, :, c0:c0 + n], self.w_in[:, C_KVB + 256:C_KVB + 768], 8, 512, self.gmix, piece=256)
                self.cast_into(slg, lambda c0, n: slg.t[:, :, c0:c0 + n], self.w_in[:, C_GB:C_GB + 24], 8, 24, self.gmix, piece=256)
                for e_ in range(12, 32):
                    slot = e_ - 12
                    own = e_ >= 16
                    i = e_ - 16
                    u = uTt.next()
                    xs = self.x_own[i * 128:(i + 1) * 128, :] if own else self.x_halo[e_ * 128:(e_ + 1) * 128, :]
                    self.norm_tile(xs, u, u.t[:])
                    lhs = lambda c, u=u: u.t[:, c, :]
                    if own:
                        ps = self.psA.next()
                        self.proj_tm(lhs, u, slq, 0, 512, ps)
                        t = tq.next()
                        self.rope_evac(ps, 0, 8, e_, t, t.t[:, 0:512], perm=True)
                        pt = self.psT
                        for hp in range(4):
                            S.op("pe", I("transpose", out=pt.t[:, hp * 128:(hp + 1) * 128], in_=t.t[:, hp * 128:(hp + 1) * 128], identity=self.ident), reads=[t, cbf], writes=[pt])
                        S.op("act", I("copy", out=qbT.t[:, :, i * 128:(i + 1) * 128], in_=pt.t[:, 0:512].rearrange("p (h t) -> p h t", h=4)), reads=[pt], writes=[qbT])
                        psx = self.psX
                        for c in range(8):
                            S.op("pe", I("matmul", psx.t[0:24, 0:128], lhsT=slg.t[:, c, 0:24], rhs=u.t[:, c, :], start=(c == 0), stop=(c == 7)), reads=[slg, u], writes=[psx])
                        S.op("act", I("activation", out=gbT.t[0:24, i * 128:(i + 1) * 128], in_=psx.t[0:24, 0:128], func=AF.Sigmoid), reads=[psx], writes=[gbT])
                    ps = self.psA.next()
                    t = tk.next()
                    if own:
                        self.proj_tm(lhs, u, slkv, 0, 512, ps)
                        self.rope_evac(ps, 0, 2, e_, t, t.t[:, 0:128])
                        self.rope_evac(ps, 256, 2, e_, t, t.t[:, 128:256])
                        S.op("dve", I("tensor_copy", out=vso.t[:, i, :, 0:64], in_=ps.t[:, 128:256].rearrange("p (g d) -> p g d", g=2)), reads=[ps], writes=[vso])
                        S.op("pool", I("tensor_copy", out=vso.t[:, i, :, 64:128], in_=ones2), reads=[cbf], writes=[vso])
                        S.op("dve", I("tensor_copy", out=vwin.t[:, slot, :, 0:64], in_=ps.t[:, 384:512].rearrange("p (g d) -> p g d", g=2)), reads=[ps], writes=[vwin])
                        S.op("pool", I("tensor_copy", out=vwin.t[:, slot, :, 64:128], in_=ones2), reads=[cbf], writes=[vwin])
                    else:
                        self.proj_tm(lhs, u, slkv, 256, 256, ps)
                        self.rope_evac(ps, 0, 2, e_, t, t.t[:, 128:256])
                        S.op("dve", I("tensor_scalar", out=vwin.t[:, slot, :, 0:64], in0=ps.t[:, 128:256].rearrange("p (g d) -> p g d", g=2), scalar1=hvcol, scalar2=None, op0=ALU.mult), reads=[ps, pcf], writes=[vwin])
                        S.op("pool", I("tensor_scalar", out=vwin.t[:, slot, :, 64:128], in0=ones2, scalar1=hvcol, scalar2=None, op0=ALU.mult), reads=[cbf, pcf], writes=[vwin])
                    pt = self.psT
                    if own:
                        S.op("pe", I("transpose", out=pt.t[:, 0:128], in_=t.t[:, 0:128], identity=self.ident), reads=[t, cbf], writes=[pt])
                    S.op("pe", I("transpose", out=pt.t[:, 128:256], in_=t.t[:, 128:256], identity=self.ident), reads=[t, cbf], writes=[pt])
                    if own:
                        S.op("act", I("copy", out=kso.t[:, i * 128:(i + 1) * 128], in_=pt.t[:, 0:128]), reads=[pt], writes=[kso])
                    S.op("act", I("copy", out=kwinT.t[:, slot * 128:(slot + 1) * 128], in_=pt.t[:, 128:256]), reads=[pt], writes=[kwinT])
            S.barrier()
            biasT = self.sb(esB, "b_biasT", [128, 2, OWN], BF16)
            t16 = self.sb(esB, "b_t16", [128, 2048], F32)
            S.dma(I("dma_start", out=t16.t[:], in_=self.c_t16), writes=[t16])
            with ExitStack() as es:
                et = self.ring(es, "b_et", [128, 512], F32, 4)
                pp = self.ring(es, "b_pp", [128, 520], F32, 4)
                lohi = self.ring(es, "b_lohi", [128, 256], F32, 2)
                imp = self.ring(es, "b_imp", [128, 128], F32, 2)
                imp2 = self.ring(es, "b_imp2", [128, 128], F32, 2)
                sm = self.ring(es, "b_sm", [128, 24], F32, 8)
                btm = self.ring(es, "b_btm", [128, 128], BF16, 2)
                for pq in pp.items:
                    S.op("pool", I("memset", pq.t[:], 0.0), writes=[pq])
                for i in range(NT):
                    lh = lohi.next()
                    S.dma(I("dma_start", out=lh.t[:, 0:128], in_=self.pc_lohi[:, i * 128:(i + 1) * 128]), writes=[lh])
                    S.dma(I("dma_start", out=lh.t[:, 128:256], in_=self.pc_lohi[:, 2048 + i * 128:2048 + (i + 1) * 128]), writes=[lh])
                    thr_i = pcf.t[:, PCF["thrc"] + i:PCF["thrc"] + i + 1]
                    def gen(g, i=i, lh=lh, thr_i=thr_i):
                        pb = 64 * g
                        P = pp.next()
                        P2 = pp.next()
                        for r in range(4):
                            ps = self.psS.next()
                            S.op("pe", I("matmul", ps.t[:, 0:511], lhsT=qbT.t[pb:pb + 64, r, i * 128:(i + 1) * 128], rhs=kcT.t[pb:pb + 64, 0:511], start=True, stop=True), reads=[qbT, kcT], writes=[ps])
                            e_ = et.next()
                            s_ = sm.next()
                            eng = "dve"
                            Pr = P if r % 2 == 0 else P2
                            S.op("act", I("activation", out=e_.t[:, 0:511], in_=ps.t[:, 0:511], func=AF.Exp, scale=SCALE), reads=[ps], writes=[e_])
                            S.op(eng, I("scalar_tensor_tensor", out=e_.t[:, 0:511], in0=t16.t[:, 0:511], scalar=thr_i, in1=e_.t[:, 0:511], op0=ALU.is_le, op1=ALU.mult, accum_out=s_.t[:, 0:1]), reads=[t16, pcf, e_], writes=[e_, s_])
                            yield
                            S.op(eng, I("tensor_scalar", out=s_.t[:, 1:2], in0=s_.t[:, 0:1], scalar1=1e-30, scalar2=None, op0=ALU.max), reads=[s_], writes=[s_])
                            yield
                            S.op("dve", I("reciprocal", out=s_.t[:, 2:3], in_=s_.t[:, 1:2]), reads=[s_], writes=[s_])
                            yield
                            if r < 2:
                                S.op(eng, I("tensor_scalar", out=Pr.t[:, 1:512], in0=e_.t[:, 0:511], scalar1=s_.t[:, 2:3], scalar2=None, op0=ALU.mult), reads=[e_, s_], writes=[Pr])
                                yield
                            else:
                                S.op(eng, I("scalar_tensor_tensor", out=Pr.t[:, 1:512], in0=e_.t[:, 0:511], scalar=s_.t[:, 2:3], in1=Pr.t[:, 1:512], op0=ALU.mult, op1=ALU.add), reads=[e_, s_, Pr], writes=[Pr])
                                yield
                        S.op("dve", I("tensor_tensor", out=P.t[:, 1:512], in0=P.t[:, 1:512], in1=P2.t[:, 1:512], op=ALU.add), reads=[P, P2], writes=[P])
                        yield
                        im = imp.next()
                        S.op("dve", I("tensor_tensor", out=im.t[:], in0=P.t[:, 0:512:4], in1=P.t[:, 1:513:4], op=ALU.add), reads=[P], writes=[im])
                        yield
                        for k in range(2, 5):
                            S.op("dve", I("tensor_tensor", out=im.t[:], in0=im.t[:], in1=P.t[:, k:k + 512:4], op=ALU.add), reads=[P, im], writes=[im])
                            yield
                        S.op("dve", I("tensor_tensor", out=im.t[:], in0=im.t[:], in1=lh.t[:, 0:128], op=ALU.max), reads=[lh, im], writes=[im])
                        yield
                        S.op("dve", I("tensor_tensor", out=im.t[:], in0=im.t[:], in1=lh.t[:, 128:256], op=ALU.min), reads=[lh, im], writes=[im])
                        yield
                        s_ = sm.next()
                        i2 = imp2.next()
                        S.op("dve", I("max", out=s_.t[:, 0:8], in_=im.t[:]), reads=[im], writes=[s_])
                        yield
                        S.op("dve", I("match_replace", out=i2.t[:], in_to_replace=s_.t[:, 0:8], in_values=im.t[:], imm_value=-1e9), reads=[im, s_], writes=[i2])
                        yield
                        S.op("dve", I("max", out=s_.t[:, 8:16], in_=i2.t[:]), reads=[i2], writes=[s_])
                        yield
                        S.op("dve", I("tensor_scalar", out=s_.t[:, 16:17], in0=s_.t[:, 15:16], scalar1=-1.5e4, scalar2=None, op0=ALU.max), reads=[s_], writes=[s_])
                        yield
                        bt_ = btm.next()
                        S.op("dve", I("tensor_scalar", out=bt_.t[:], in0=im.t[:], scalar1=s_.t[:, 16:17], scalar2=-30000.0, op0=ALU.is_lt, op1=ALU.mult), reads=[im, s_], writes=[bt_])
                        yield
                        pt = self.psT
                        S.op("pe", I("transpose", out=pt.t[:, 0:128], in_=bt_.t[:], identity=self.ident), reads=[bt_, cbf], writes=[pt])
                        S.op("act", I("copy", out=biasT.t[:, g, i * 128:(i + 1) * 128], in_=pt.t[:, 0:128]), reads=[pt], writes=[biasT])

                    gens = [gen(0), gen(1)]
                    while gens:
                        for gg in list(gens):
                            try:
                                next(gg)
                            except StopIteration:
                                gens.remove(gg)
            S.barrier()
            if self.stop_after == "B6":
                self.dump("biasT", biasT, biasT.t[:], [128, 2, OWN], BF16)
                self.dump("qbT", qbT, qbT.t[:], [128, 4, OWN], BF16)
                self.stopped = True
                return
            with ExitStack() as es:
                osb = self.ring(es, "b_osb", [128, 512], F32, 2)
                rdr = self.ring(es, "b_rd", [128, 512], F32, 2)
                ybacc = self.sb(es, "b_ybacc", [128, 512], F32)
                ybacc2 = self.sb(es, "b_ybacc2", [128, 512], F32)
                self.ptr = self.ring(es, "b_ptr", [128, 512], BF16, 5)
                gsel = self.ring(es, "b_gsel", [32, 128], F32, 6)
                id32 = self.cf32.t[0:32, F32C["id32"]:F32C["id32"] + 32]
                eown = pcb.t[:, PCB["eown"]:PCB["eown"] + 2048]
                e32 = cbf.t[:, BFC["e32"]:BFC["e32"] + 2048]
                m4 = cbf.t[:, BFC["m4"]:BFC["m4"] + 2048]
                tri_diag = cbf.t[:, BFC["tri_diag"]:BFC["tri_diag"] + 128]
                win_far = cbf.t[:, BFC["win_far"]:BFC["win_far"] + 128]
                BRS = os.environ.get("BRS", "012")
                ps4 = Ring([self.psS.items[0], self.psS.items[1], self.psA.items[0], self.psA.items[1]])
                ybaccs = [ybacc, ybacc2]
                for hp in range(4):
                    sels = {}
                    for gi in range(2):
                        h = hp + 4 * gi
                        for br in range(3):
                            gs = gsel.next()
                            jrow = h * 3 + br
                            S.op("pool", I("tensor_copy", out=gs.t[:], in_=id32[:, jrow:jrow + 1].to_broadcast([32, 128])), reads=[self.cf32], writes=[gs])
                            sels[(gi, br)] = gs
                    for c in range(4):
                        gcols = gbT.t[0:32, c * 512:(c + 1) * 512]
                        dst = self.ybT.t[:, hp, c * 512:(c + 1) * 512]
                        Qs = [qbT.t[64 * gi:64 * gi + 64, hp, c * 512:(c + 1) * 512] for gi in range(2)]

                        def fin(psos, br, first, last):
                            for gi in range(2):
                                o = osb.next()
                                S.op("act", I("copy", out=o.t[:], in_=psos[gi].t[:]), reads=[psos[gi]], writes=[o])
                                self.finalize(o, o.t[:], gi, (sels[(gi, br)], gbT, gcols), rdr, self.ybT, dst, first=first, last=last, ybacc=ybaccs[gi])
                        psos = [self.psO.next(), self.psO.next()]
                        steps = []
                        for bt in range(4):
                            crel = pcf.t[:, PCF["crel"] + bt:PCF["crel"] + bt + 1]

                            def mask_c(pt, crel=crel, c=c):
                                S.op("dve", I("scalar_tensor_tensor", out=pt.t[:, 0:512], in0=t16.t[:, c * 512:(c + 1) * 512], scalar=crel, in1=pt.t[:, 0:512], op0=ALU.is_ge, op1=ALU.mult), reads=[t16, pcf, pt], writes=[pt])
                            stp = []
                            for gi in range(2):
                                pb = 64 * gi
                                stp.append(([(kcT.t[pb:pb + 64, bt * 128:(bt + 1) * 128], Qs[gi], [kcT, qbT])], 512, mask_c,
                                            [(psos[gi], psos[gi].t[:, 0:512], vc.t[:, bt, gi, :], 0, 512, bt == 0, bt == 3, [vc])]))
                            steps.append(stp)
                        self.attn_steps(steps, ps4)
                        fin(psos, 0, True, False)
                        psos = [self.psO.next(), self.psO.next()]
                        steps = []
                        for kt in range(48):
                            pb32 = 32 * (kt // 16)
                            kc_ = (kt % 16) * 128
                            stp = []
                            for gi in range(2):
                                pb = 64 * gi
                                stp.append(([(kslcT.t[pb:pb + 64, kt * 128:(kt + 1) * 128], Qs[gi], [kslcT, qbT]),
                                             (e32[pb32:pb32 + 32, kc_:kc_ + 128], biasT.t[pb32:pb32 + 32, gi, c * 512:(c + 1) * 512], [cbf, biasT])], 512, None,
                                            [(psos[gi], psos[gi].t[:, 0:512], vslc.t[:, kt, gi, :], 0, 512, kt == 0, False, [vslc])]))
                            steps.append(stp)
                        for j in range(4 * c + 4):
                            mf = None
                            if j >= 4 * c:
                                mk = m4[:, (j - 4 * c) * 512:(j - 4 * c + 1) * 512]

                                def mf(pt, mk=mk):
                                    self.mask_mul(pt, 0, 512, mk)
                            stp = []
                            for gi in range(2):
                                pb = 64 * gi
                                stp.append(([(kso.t[pb:pb + 64, j * 128:(j + 1) * 128], Qs[gi], [kso, qbT]),
                                             (eown[:, j * 128:(j + 1) * 128], biasT.t[:, gi, c * 512:(c + 1) * 512], [pcb, biasT])], 512, mf,
                                            [(psos[gi], psos[gi].t[:, 0:512], vso.t[:, j, gi, :], 0, 512, False, j == 4 * c + 3, [vso])]))
                            steps.append(stp)
                        self.attn_steps(steps, ps4)
                        fin(psos, 1, False, False)
                        psos = [self.psO.next(), self.psO.next()]
                        steps = []
                        for tq_ in range(4):
                            i = 4 * c + tq_
                            sq = 4 + i
                            for s_ in range(sq - 4, sq + 1):
                                mf = None
                                if s_ == sq - 4:
                                    def mf(pt):
                                        self.mask_mul(pt, 0, 128, win_far)
                                elif s_ == sq:
                                    def mf(pt):
                                        self.mask_mul(pt, 0, 128, tri_diag)
                                stp = []
                                for gi in range(2):
                                    pb = 64 * gi
                                    stp.append(([(kwinT.t[pb:pb + 64, s_ * 128:(s_ + 1) * 128], qbT.t[pb:pb + 64, hp, i * 128:(i + 1) * 128], [kwinT, qbT])], 128, mf,
                                                [(psos[gi], psos[gi].t[:, tq_ * 128:(tq_ + 1) * 128], vwin.t[:, s_, gi, :], 0, 128, s_ == sq - 4, s_ == sq, [vwin])]))
                                steps.append(stp)
                        self.attn_steps(steps, ps4)
                        fin(psos, 2, False, True)
        S.barrier()

    def phase_C(self, es0):
        S = self.S
        with ExitStack() as es:
            self.wst = self.ring(es, "wstC", [128, 8, 256], F32, 2)
            self.cast_engs = ("pool",)
            slabs = self.ring(es, "c_slab", [128, 8, 512], BF16, 3)
            self.wbslab = self.sb(es, "c_wb", [128, 4, 512], BF16)
            gfin = self.sb(es, "c_gfin", [128, D], F32)
            xc = self.sb(es, "c_xc", [128, 4, D], F32)
            uTc = self.sb(es, "c_uTc", [128, 8, 512], BF16)
            mTc = self.sb(es, "c_mTc", [128, 8, 512], BF16)
            u2Tc = self.sb(es, "c_u2Tc", [128, 8, 512], BF16)
            hT = self.sb(es, "c_hT", [128, 32, 512], BF16)
            sg = self.ring(es, "c_sg", [128, 512], BF16, 2)
            tf = self.ring(es, "c_tf", [128, 512], F32, 3)
            S.dma(I("dma_start", out=gfin.t[:], in_=self.g_fin), writes=[gfin])

            def slab_from(src_ap, kchunks, gain):
                sl = slabs.next()
                self.cast_into(sl, lambda c0, n, sl=sl: sl.t[:, 0:kchunks, c0:c0 + n], src_ap, kchunks, 512, gain, piece=256)
                return sl

            for c in range(4):
                cs = slice(c * 512, (c + 1) * 512)
                for tt in range(4):
                    r0 = c * 512 + tt * 128
                    S.dma(I("dma_start", out=xc.t[:, tt, :], in_=self.x_own[r0:r0 + 128, :]), writes=[xc])
                    self.norm_sb(xc, xc.t[:, tt, :], uTc, uTc.t[:, :, tt * 128:(tt + 1) * 128])
                for ctg in range(2):
                    gA = slab_from(self.w_in[:, C_GM + ctg * 512:C_GM + ctg * 512 + 512], 8, self.gmix)
                    gB = slab_from(self.w_in[:, C_GM + 1024 + ctg * 512:C_GM + 1024 + ctg * 512 + 512], 8, self.gmix)
                    wa = slab_from(self.w_a[:, ctg * 512:(ctg + 1) * 512], 4, None)
                    wb = self.wbslab
                    for c0 in range(0, 512, 256):
                        st = self.wst.next()
                        for two in range(2):
                            S.dma(I("dma_start", out=st.t[64 * two:64 * two + 64, 0:4, 0:256],
                                    in_=self.w_b[two * 256:(two + 1) * 256, ctg * 512 + c0:ctg * 512 + c0 + 256].rearrange("(hp d) n -> d hp n", d=64)), writes=[st])
                        S.op("pool", I("tensor_copy", out=wb.t[:, 0:4, c0:c0 + 256], in_=st.t[:, 0:4, 0:256]), reads=[st], writes=[wb])
                    for j in range(4):
                        ct = ctg * 4 + j
                        js = slice(j * 128, (j + 1) * 128)
                        sgs = []
                        for gw in (gA, gB):
                            ps = self.psA.next()
                            for k in range(8):
                                S.op("pe", I("matmul", ps.t[:, 0:512], lhsT=gw.t[:, k, js], rhs=uTc.t[:, k, :], start=(k == 0), stop=(k == 7)), reads=[gw, uTc], writes=[ps])
                            sgt = sg.next()
                            S.op("act", I("activation", out=sgt.t[:], in_=ps.t[:, 0:512], func=AF.Sigmoid), reads=[ps], writes=[sgt])
                            sgs.append(sgt)
                        psa = self.psS.next()
                        for k in range(4):
                            S.op("pe", I("matmul", psa.t[:, 0:512], lhsT=wa.t[:, k, js], rhs=self.yaT.t[:, k, cs], start=(k == 0), stop=(k == 3)), reads=[wa, self.yaT], writes=[psa])
                        psb = self.psO.next()
                        for k in range(4):
                            S.op("pe", I("matmul", psb.t[:, 0:512], lhsT=wb.t[:, k, js], rhs=self.ybT.t[:, k, cs], start=(k == 0), stop=(k == 3)), reads=[wb, self.ybT], writes=[psb])
                        t0 = tf.next()
                        t1 = tf.next()
                        S.op("dve", I("tensor_tensor", out=t0.t[:], in0=psa.t[:, 0:512], in1=sgs[0].t[:], op=ALU.mult), reads=[psa, sgs[0]], writes=[t0])
                        S.op("dve", I("tensor_tensor", out=t1.t[:], in0=psb.t[:, 0:512], in1=sgs[1].t[:], op=ALU.mult), reads=[psb, sgs[1]], writes=[t1])
                        S.op("dve", I("tensor_tensor", out=mTc.t[:, ct, :], in0=t0.t[:], in1=t1.t[:], op=ALU.add), reads=[t0, t1], writes=[mTc])
                for nh in range(2):
                    wo = slab_from(self.w_out[:, nh * 512:(nh + 1) * 512], 8, None)
                    for tt in range(4):
                        ps = self.psA.next()
                        for k in range(8):
                            S.op("pe", I("matmul", ps.t[:, 0:512], lhsT=mTc.t[:, k, tt * 128:(tt + 1) * 128], rhs=wo.t[:, k, :], start=(k == 0), stop=(k == 7)), reads=[wo, mTc], writes=[ps])
                        S.op("dve", I("tensor_tensor", out=xc.t[:, tt, nh * 512:(nh + 1) * 512], in0=ps.t[:, 0:512], in1=xc.t[:, tt, nh * 512:(nh + 1) * 512], op=ALU.add), reads=[ps, xc], writes=[xc])
                for tt in range(4):
                    self.norm_sb(xc, xc.t[:, tt, :], u2Tc, u2Tc.t[:, :, tt * 128:(tt + 1) * 128])
                for s_ in range(8):
                    wu = slab_from(self.w_up[:, s_ * 512:(s_ + 1) * 512], 8, self.gmlp)
                    for j in range(4):
                        ft = 4 * s_ + j
                        ps = self.psA.next()
                        for k in range(8):
                            S.op("pe", I("matmul", ps.t[:, 0:512], lhsT=wu.t[:, k, j * 128:(j + 1) * 128], rhs=u2Tc.t[:, k, :], start=(k == 0), stop=(k == 7)), reads=[wu, u2Tc], writes=[ps])
                        r = tf.next()
                        S.op("act", I("activation", out=r.t[:], in_=ps.t[:, 0:512], func=AF.Relu), reads=[ps], writes=[r])
                        S.op("dve", I("tensor_tensor", out=hT.t[:, ft, :], in0=r.t[:], in1=r.t[:], op=ALU.mult), reads=[r], writes=[hT])
                accs = [self.psA.items[0], self.psA.items[1], self.psS.items[0], self.psS.items[1]]
                for nh in range(2):
                    for kg in range(4):
                        wd = slab_from(self.w_down[kg * 1024:(kg + 1) * 1024, nh * 512:(nh + 1) * 512], 8, None)
                        for tt in range(4):
                            for k in range(8):
                                S.op("pe", I("matmul", accs[tt].t[:, 0:512], lhsT=hT.t[:, kg * 8 + k, tt * 128:(tt + 1) * 128], rhs=wd.t[:, k, :], start=(kg == 0 and k == 0), stop=(kg == 3 and k == 7)), reads=[wd, hT], writes=[accs[tt]])
                    for tt in range(4):
                        S.op("dve", I("tensor_tensor", out=xc.t[:, tt, nh * 512:(nh + 1) * 512], in0=accs[tt].t[:, 0:512], in1=xc.t[:, tt, nh * 512:(nh + 1) * 512], op=ALU.add), reads=[accs[tt], xc], writes=[xc])
                for tt in range(4):
                    jk = self.junk.next()
                    st = self.stat.next()
                    S.op("act", I("activation", out=jk.t[:], in_=xc.t[:, tt, :], func=AF.Square, accum_out=st.t[:, 0:1]), reads=[xc], writes=[jk, st])
                    S.op("act", I("activation", out=st.t[:, 1:2], in_=st.t[:, 0:1], func=AF.Sqrt, scale=1.0 / D, bias=self.epsc.t[:, 0:1]), reads=[st, self.epsc], writes=[st])
                    S.op("dve", I("reciprocal", out=st.t[:, 2:3], in_=st.t[:, 1:2]), reads=[st], writes=[st])
                    S.op("dve", I("scalar_tensor_tensor", out=xc.t[:, tt, :], in0=xc.t[:, tt, :], scalar=st.t[:, 2:3], in1=gfin.t[:], op0=ALU.mult, op1=ALU.mult), reads=[xc, st, gfin], writes=[xc])
                    r0 = c * 512 + tt * 128
                    S.dma(I("dma_start", out=self.out[r0:r0 + 128, :], in_=xc.t[:, tt, :]), reads=[xc])

def make_in_maps(inputs):
    x = np.ascontiguousarray(np.asarray(inputs["x"], np.float32))
    cbf, cf32, t16 = _static_tables()
    sq = lambda n: np.ascontiguousarray(np.asarray(inputs[n], np.float32)[0])
    gl = lambda v: np.ascontiguousarray(np.asarray(v, np.float32).reshape(8, 128).T)
    common = {
        "w_in": sq("w_in"), "g_mix": gl(inputs["norm_mix_g"][0]), "g_mlp": gl(inputs["norm_mlp_g"][0]),
        "g_fin": np.ascontiguousarray(np.broadcast_to(np.asarray(inputs["norm_final_g"], np.float32)[None, :], (128, D))),
        "cmp_w1_k": sq("cmp_w1_k"), "cmp_w1_v": sq("cmp_w1_v"), "cmp_w2_k": sq("cmp_w2_k"), "cmp_w2_v": sq("cmp_w2_v"),
        "cmp_pos_k": sq("cmp_pos_k"), "cmp_pos_v": sq("cmp_pos_v"),
        "w_a": sq("w_branch_a"), "w_b": sq("w_branch_b"), "w_out": sq("w_out"), "w_up": sq("w_up"), "w_down": sq("w_down"),
        "c_bf": cbf, "c_f32": cf32, "c_t16": t16,
    }
    tabs = [_percore_tables(q) for q in range(4)]
    maps = []
    for c in range(8):
        b, q = c // 4, c % 4
        T0 = OWN * q
        halo = x[b, T0 - OWN:T0] if q > 0 else np.zeros((OWN, D), np.float32)
        m = dict(common)
        m.update({"x_own": np.ascontiguousarray(x[b, T0:T0 + OWN]), "x_halo": np.ascontiguousarray(halo), "x_full": x[b],
                  "pc_f": tabs[q][0], "pc_lohi": tabs[q][1], "pc_bf": tabs[q][2]})
        maps.append(m)
    return maps


_CACHE = {}


def kernel(**inputs):
    if "nc" not in _CACHE:
        b = Builder()
        _CACHE["nc"] = b.build()
        _CACHE["decl"] = set(b._decl.keys())
    nc = _CACHE["nc"]
    maps = make_in_maps(inputs)
    decl = _CACHE["decl"]
    maps = [{k: v for k, v in m.items() if k in decl} for m in maps]
    res = run_bass_kernel_spmd(nc, maps, core_ids=list(range(8)))
    out = np.zeros((2, S_LEN, D), np.float32)
    for c in range(8):
        b, q = c // 4, c % 4
        out[b, OWN * q:OWN * (q + 1)] = res.results[c]["out"]
    return out
```

```python
import os
import numpy as np
import ml_dtypes
from contextlib import ExitStack
import concourse.bass as bass
import concourse.mybir as mybir
from concourse.bass_utils import run_bass_kernel_spmd

F32 = mybir.dt.float32
BF16 = mybir.dt.bfloat16
ALU = mybir.AluOpType
AF = mybir.ActivationFunctionType
NPBF = ml_dtypes.bfloat16

D = 1024
S_LEN = 8192
OWN = 2048
NT = 16
EPS = 1e-6
SCALE = 0.125
IN_COLS = 7960
C_QA, C_KA, C_VA = 0, 1536, 3072
C_QB = 4608
C_KVB = 5120
C_GB = 5888
C_GM = 5912
DILS = (1, 4, 16)

ENGS = ("pe", "act", "dve", "pool")
NDMA = 24


class Res:
    __slots__ = ("lw", "rd", "excl")

    def __init__(self):
        self.lw = None
        self.rd = {}
        self.excl = False


class Tn:
    __slots__ = ("t", "r")

    def __init__(self, t):
        self.t = t
        self.r = Res()


class Sched:
    def __init__(self, nc):
        self.nc = nc
        self.q = {e: [] for e in ENGS + ("sp",)}
        self.cnt = {e: 0 for e in ENGS}
        self.dcnt = [0] * NDMA
        self.seen = {e: {} for e in ENGS + ("sp",)}
        self.dnext = 0
        self.pgen = {}
        self.plast = {}

    def dma_sw(self, fns, writes, slot):
        deps = self._deps([], writes)
        self.pgen[slot] = self.pgen.get(slot, 0)
        key = ("p", slot)
        base = self.pgen[slot]
        waits = self._waits("pool", deps)
        for i, fn in enumerate(fns):
            self.q["pool"].append((waits if i == 0 else [], fn, key, base + 16 * (i + 1)))
        self.pgen[slot] = base + 16 * len(fns)
        self.plast[slot] = (key, self.pgen[slot])
        self._mark(key, self.pgen[slot], [], writes)

    def _deps(self, reads, writes, mykey=None):
        deps = {}
        for r in reads:
            r = r.r if isinstance(r, Tn) else r
            if r.lw is not None and r.lw[1] > deps.get(r.lw[0], 0):
                deps[r.lw[0]] = r.lw[1]
            if r.excl:
                for k, v in r.rd.items():
                    if k != mykey and v > deps.get(k, 0):
                        deps[k] = v
        for w in writes:
            w = w.r if isinstance(w, Tn) else w
            if w.lw is not None and w.lw[0] != mykey and w.lw[1] > deps.get(w.lw[0], 0):
                deps[w.lw[0]] = w.lw[1]
            for k, v in w.rd.items():
                if v > deps.get(k, 0):
                    deps[k] = v
        return deps

    def _waits(self, eng, deps):
        waits = []
        seen = self.seen[eng]
        for k, v in deps.items():
            if v > seen.get(k, 0):
                waits.append((k, v))
                seen[k] = v
        return waits

    def _mark(self, key, my, reads, writes):
        for r in reads:
            r = r.r if isinstance(r, Tn) else r
            if my > r.rd.get(key, 0):
                r.rd[key] = my
        for w in writes:
            w = w.r if isinstance(w, Tn) else w
            w.lw = (key, my)
            w.rd = {}

    def op(self, eng, fn, reads=(), writes=()):
        deps = self._deps(reads, writes, ("e", eng))
        if eng == "pe":
            deps.pop(("e", "pe"), None)
        self.cnt[eng] += 1
        my = self.cnt[eng]
        key = ("e", eng)
        self.q[eng].append((self._waits(eng, deps), fn, key, my))
        self._mark(key, my, reads, writes)

    def dma(self, fn, reads=(), writes=(), queue="sp"):
        deps = self._deps(reads, writes)
        k = self.dnext
        self.dnext = (self.dnext + 1) % NDMA
        key = ("d", k)
        if self.dcnt[k] > 0:
            deps[key] = max(deps.get(key, 0), self.dcnt[k])
        self.dcnt[k] += 16
        my = self.dcnt[k]
        self.q[queue].append((self._waits(queue, deps), fn, key, my))
        self._mark(key, my, reads, writes)

    def barrier(self):
        allc = {}
        for e in ENGS:
            if self.cnt[e]:
                allc[("e", e)] = self.cnt[e]
        for k in range(NDMA):
            if self.dcnt[k]:
                allc[("d", k)] = self.dcnt[k]
        for slot, (key, v) in self.plast.items():
            allc[key] = v
        for e in ENGS + ("sp",):
            w = self._waits(e, dict(allc))
            if w:
                self.q[e].append((w, None, None, 0))

    def emit(self):
        nc = self.nc
        with ExitStack() as es:
            esem = {e: es.enter_context(nc.semaphore("s_" + e)) for e in ENGS}
            dsem = [es.enter_context(nc.semaphore("s_d%d" % i)) for i in range(NDMA)]

            psem = {slot: es.enter_context(nc.semaphore("s_p%d" % i)) for i, slot in enumerate(sorted(self.pgen))}

            def semof(key):
                if key[0] == "p":
                    return psem[key[1]]
                return esem[key[1]] if key[0] == "e" else dsem[key[1]]
            fin = {}
            for e in ENGS:
                if self.cnt[e]:
                    fin[("e", e)] = self.cnt[e]
            for k in range(NDMA):
                if self.dcnt[k]:
                    fin[("d", k)] = self.dcnt[k]
            for slot, (key, v) in self.plast.items():
                fin[key] = v
            allsems = list(esem.values()) + dsem + list(psem.values())
            with nc.Block() as b0:
                @b0.sync
                def _(e):
                    for sm in allsems:
                        e.sem_clear(sm)
            block = es.enter_context(nc.Block())

            sig = {e: set() for e in ENGS}
            for name in self.q:
                for waits, fn, key, my in self.q[name]:
                    for (k, v) in waits:
                        if k[0] == "e":
                            sig[k[1]].add(v)
            for e in ENGS:
                if self.cnt[e]:
                    sig[e].add(self.cnt[e])
            rank = {}
            for e in ENGS:
                for i, v in enumerate(sorted(sig[e])):
                    rank[(e, v)] = i + 1

            def wval(k, v):
                return rank[(k[1], v)] if k[0] == "e" else v

            def run(name, engobj, final=False):
                for waits, fn, key, my in self.q[name]:
                    for (k, v) in waits:
                        engobj.wait_ge(semof(k), wval(k, v))
                    if isinstance(fn, tuple):
                        engobj.sem_clear(psem[fn[1]])
                    elif fn is not None:
                        ins = fn(engobj)
                        if key[0] in ("d", "p"):
                            ins.then_inc(semof(key), 16)
                        elif my in sig[key[1]]:
                            ins.then_inc(semof(key), 1)
                if final:
                    for k, v in fin.items():
                        engobj.wait_ge(semof(k), wval(k, v))

            @block.sync
            def _(e):
                run("sp", e, final=True)

            @block.tensor
            def _(e):
                run("pe", e)

            @block.scalar
            def _(e):
                run("act", e)

            @block.vector
            def _(e):
                run("dve", e)

            @block.gpsimd
            def _(e):
                run("pool", e)


def I(name, *a, **k):
    return lambda e: getattr(e, name)(*a, **k)


class Ring:
    def __init__(self, items):
        self.items = items
        self.i = 0

    def next(self):
        it = self.items[self.i % len(self.items)]
        self.i += 1
        return it


NROPE = 148
BFC = dict(ident=0, tri_diag=128, tri_prev=256, win_far=384, m4=512, e32=2560, ones=4608)
NBFC = 4736
F32C = dict(swap=0, id32=128)
NF32C = 160
PCF = dict(rope=0, thrc=NROPE * 16, pv=NROPE * 16 + 16, crel=NROPE * 16 + 80, hv=NROPE * 16 + 84)
NPCF = NROPE * 16 + 85
PCB = dict(eown=0, hv64=2048)
NPCB = 2112


def _static_tables():
    bf = np.zeros((128, NBFC), np.float32)
    k = np.arange(128)[:, None]
    q = np.arange(128)[None, :]
    bf[:, 0:128] = np.eye(128)
    bf[:, 128:256] = (q >= k)
    bf[:, 256:384] = (q <= k)
    bf[:, 384:512] = (q < k)
    for m in range(4):
        blk = np.zeros((128, 512), np.float32)
        for tq in range(4):
            if tq == m:
                blk[:, tq * 128:(tq + 1) * 128] = (q >= k)
            elif tq > m:
                blk[:, tq * 128:(tq + 1) * 128] = 1.0
        bf[:, 512 + m * 512: 512 + (m + 1) * 512] = blk
    b = np.arange(128)[:, None]
    for kt in range(16):
        i = np.arange(128)[None, :]
        bf[:, 2560 + kt * 128: 2560 + (kt + 1) * 128] = ((b % 32) == 2 * kt + (i >= 64))
    bf[:, 4608:4736] = 1.0
    f = np.zeros((128, NF32C), np.float32)
    f[:, 0:128] = (np.abs(k - q) == 64)
    f[0:32, 128:160] = np.eye(32)
    t16 = np.ascontiguousarray(np.broadcast_to(16.0 * np.arange(2048, dtype=np.float32)[None, :], (128, 2048)))
    return bf.astype(NPBF), f, t16


def _rope_rows(pos):
    inv = (500000.0 ** (-np.arange(0, 16, 2, dtype=np.float32) / np.float32(16))).astype(np.float32)
    ang = (pos.astype(np.float32)[:, None] * inv[None, :]).astype(np.float32)
    return np.concatenate([np.cos(ang), np.sin(ang)], axis=1).astype(np.float32)


def _percore_tables(qtr):
    T0 = OWN * qtr
    i = np.arange(128)
    f = np.zeros((128, NPCF), np.float32)
    rope = np.zeros((128, NROPE, 16), np.float32)
    for t in range(32):
        rope[:, t] = _rope_rows(T0 - OWN + 128 * t + i)
    for r in range(4):
        for j in range(-1, 4):
            rope[:, 32 + r * 5 + j + 1] = _rope_rows(T0 - OWN + 2048 + 512 * j + r + 4 * i)
    for r in range(16):
        for j in range(-1, 1):
            rope[:, 52 + r * 2 + j + 1] = _rope_rows(T0 - OWN + 2048 + 2048 * j + r + 16 * i)
    for kt in range(64):
        rope[:, 84 + kt] = _rope_rows(128 * kt + i)
    f[:, 0:NROPE * 16] = rope.reshape(128, -1)
    for ti in range(16):
        f[:, PCF["thrc"] + ti] = T0 + 128 * ti + i - 31
    for kt in range(64):
        f[:, PCF["pv"] + kt] = 1.0 if 128 * kt < T0 else 0.0
    for bt in range(4):
        f[:, PCF["crel"] + bt] = 16.0 * (16 * (128 * bt + i) + 31 - T0)
    f[:, PCF["hv"]] = 0.0 if qtr == 0 else 1.0
    lo = np.full((128, 16, 128), -3e4, np.float32)
    hi = np.full((128, 16, 128), 3e4, np.float32)
    m = np.arange(128)[None, :]
    for ti in range(16):
        cur = ((T0 + 128 * ti + i) // 64)[:, None]
        forced = (m == 0) | (m == cur) | (m == cur - 1)
        fut = m > cur
        lo[:, ti][forced] = 1e4
        hi[:, ti][forced] = 1e4
        lo[:, ti][fut] = -3e4
        hi[:, ti][fut] = -3e4
    lohi = np.concatenate([lo.reshape(128, -1), hi.reshape(128, -1)], axis=1)
    bfp = np.zeros((128, NPCB), np.float32)
    b = np.arange(128)[:, None]
    for j in range(16):
        ii = np.arange(128)[None, :]
        bfp[:, j * 128:(j + 1) * 128] = (b == 2 * (T0 // 128 + j) + (ii >= 64))
    bfp[:, 2048:2112] = 0.0 if qtr == 0 else 1.0
    return f, lohi.astype(np.float32), bfp.astype(NPBF)


class StopBuild(Exception):
    pass


class Builder:
    def __init__(self, debug=False, stop_after=None):
        self.debug = debug
        self.stop_after = stop_after
        self.nc = nc = bass.Bass("TRN2", target_bir_lowering=False)
        self.S = Sched(nc)
        self._decl = {}
        self._shapes = {
            "x_own": ([OWN, D], F32), "x_halo": ([OWN, D], F32), "x_full": ([S_LEN, D], F32), "w_in": ([D, IN_COLS], F32),
            "g_mix": ([128, 8], F32), "g_mlp": ([128, 8], F32), "g_fin": ([128, D], F32),
            "cmp_w1_k": ([2048, 256], F32), "cmp_w1_v": ([2048, 256], F32), "cmp_w2_k": ([256, 64], F32), "cmp_w2_v": ([256, 64], F32),
            "cmp_pos_k": ([32, 64], F32), "cmp_pos_v": ([32, 64], F32), "w_a": ([512, D], F32), "w_b": ([512, D], F32),
            "w_out": ([D, D], F32), "w_up": ([D, 4096], F32), "w_down": ([4096, D], F32),
            "c_bf": ([128, NBFC], BF16), "c_f32": ([128, NF32C], F32), "c_t16": ([128, 2048], F32),
            "pc_f": ([128, NPCF], F32), "pc_lohi": ([128, 4096], F32), "pc_bf": ([128, NPCB], BF16),
        }
        self.out = nc.dram_tensor("out", [OWN, D], F32, kind="ExternalOutput").ap()
        self.dbg = {}

    def __getattr__(self, name):
        sh = self.__dict__.get("_shapes", {})
        if name in sh:
            if name not in self._decl:
                self._decl[name] = self.nc.dram_tensor(name, list(sh[name][0]), sh[name][1], kind="ExternalInput").ap()
            return self._decl[name]
        raise AttributeError(name)

    def sb(self, es, name, shape, dt):
        return Tn(es.enter_context(self.nc.sbuf_tensor(name, list(shape), dt)))

    def ps(self, es, name, shape, dt):
        t = Tn(es.enter_context(self.nc.psum_tensor(name, list(shape), dt)))
        t.r.excl = True
        return t

    def ring(self, es, name, shape, dt, n):
        return Ring([self.sb(es, "%s%d" % (name, i), shape, dt) for i in range(n)])

    def dump(self, name, tn, ap, shape, dt):
        if not self.debug:
            return
        o = self.nc.dram_tensor("dbg_" + name, list(shape), dt, kind="ExternalOutput").ap()
        self.dbg[name] = True
        self.S.dma(I("dma_start", out=o, in_=ap), reads=[tn])

    def load_wslab(self, src_ap, ncols, gain, kchunks=8):
        S = self.S
        st = self.wst.next()
        sl = self.wsl.next()
        S.dma(I("dma_start", out=st.t[:, 0:kchunks, 0:ncols], in_=src_ap.rearrange("(c p) n -> p c n", p=128)), writes=[st])
        if gain is not None:
            gb = gain.t[:, 0:kchunks].unsqueeze(2).to_broadcast([128, kchunks, ncols])
            S.op("pool", I("tensor_tensor", out=sl.t[:, 0:kchunks, 0:ncols], in0=st.t[:, 0:kchunks, 0:ncols], in1=gb, op=ALU.mult),
                 reads=[st, gain], writes=[sl])
        else:
            S.op("pool", I("tensor_copy", out=sl.t[:, 0:kchunks, 0:ncols], in_=st.t[:, 0:kchunks, 0:ncols]), reads=[st], writes=[sl])
        return sl

    def cast_into(self, dst_tn, dst_ap_fn, src_ap, kchunks, ncols, gain, piece=512):
        S = self.S
        for c0 in range(0, ncols, piece):
            n = min(piece, ncols - c0)
            st = self.wst.next()
            S.dma(I("dma_start", out=st.t[:, 0:kchunks, 0:n], in_=src_ap[:, c0:c0 + n].rearrange("(c p) n -> p c n", p=128)), writes=[st])
            dst = dst_ap_fn(c0, n)
            engs = getattr(self, "cast_engs", ("pool",))
            self._ci = getattr(self, "_ci", 0) + 1
            ce = engs[self._ci % len(engs)]
            if gain is not None:
                gb = gain.t[:, 0:kchunks].unsqueeze(2).to_broadcast([128, kchunks, n])
                S.op(ce, I("tensor_tensor", out=dst, in0=st.t[:, 0:kchunks, 0:n], in1=gb, op=ALU.mult),
                     reads=[st, gain], writes=[dst_tn])
            else:
                S.op(ce, I("tensor_copy", out=dst, in_=st.t[:, 0:kchunks, 0:n]), reads=[st], writes=[dst_tn])

    def norm_tile(self, x_ap, ut_tn, ut_ap):
        S = self.S
        xt = self.xring.next()
        S.dma(I("dma_start", out=xt.t[:], in_=x_ap), writes=[xt])
        self.norm_sb(xt, xt.t[:], ut_tn, ut_ap)

    def norm_sb(self, xt, x_sb_ap, ut_tn, ut_ap, keep_rstd=None):
        S = self.S
        jk = self.junk.next()
        st = self.stat.next()
        S.op("act", I("activation", out=jk.t[:], in_=x_sb_ap, func=AF.Square, accum_out=st.t[:, 0:1]), reads=[xt], writes=[jk, st])
        S.op("act", I("activation", out=st.t[:, 1:2], in_=st.t[:, 0:1], func=AF.Sqrt, scale=1.0 / D, bias=self.epsc.t[:, 0:1]), reads=[st, self.epsc], writes=[st])
        S.op("dve", I("reciprocal", out=st.t[:, 2:3], in_=st.t[:, 1:2]), reads=[st], writes=[st])
        xn = self.xnring.next()
        S.op("dve", I("tensor_scalar", out=xn.t[:], in0=x_sb_ap, scalar1=st.t[:, 2:3], scalar2=None, op0=ALU.mult), reads=[xt, st], writes=[xn])
        pt = self.psT
        for c in range(8):
            S.op("pe", I("transpose", out=pt.t[:, c * 128:(c + 1) * 128], in_=xn.t[:, c * 128:(c + 1) * 128], identity=self.ident), reads=[xn, self.cbf], writes=[pt])
        if keep_rstd is None:
            S.op("act", I("copy", out=ut_ap, in_=pt.t[:, 0:1024].rearrange("p (c t) -> p c t", c=8)), reads=[pt], writes=[ut_tn])
        else:
            gb = keep_rstd.t[:, 0:8].unsqueeze(2).to_broadcast([128, 8, 128])
            S.op("dve", I("tensor_tensor", out=ut_ap, in0=pt.t[:, 0:1024].rearrange("p (c t) -> p c t", c=8), in1=gb, op=ALU.mult), reads=[pt, keep_rstd], writes=[ut_tn])
        return st

    def proj_tm(self, lhs_fn, lhs_tn, slab, c0, ncols, ps):
        for c in range(8):
            self.S.op("pe", I("matmul", ps.t[:, 0:ncols], lhsT=lhs_fn(c), rhs=slab.t[:, c, c0:c0 + ncols], start=(c == 0), stop=(c == 7)),
                      reads=[lhs_tn, slab], writes=[ps])

    def rope_evac(self, ps, pc0, nh, ropeidx, dst_tn, dst_ap, perm=False):
        S = self.S
        ro = PCF["rope"] + ropeidx * 16
        ta = self.rtmp.next()
        if not perm:
            psv = ps.t[:, pc0:pc0 + 64 * nh].rearrange("p (h d) -> p h d", h=nh)
            dv = dst_ap.rearrange("p (h d) -> p h d", h=nh)
            tav = ta.t[:, 0:nh * 32].rearrange("p (h d) -> p h d", h=nh)
            cos1 = self.pcf.t[:, ro:ro + 8].unsqueeze(1).to_broadcast([128, nh, 8])
            sin1 = self.pcf.t[:, ro + 8:ro + 16].unsqueeze(1).to_broadcast([128, nh, 8])
            sl = lambda v, a, b: v[:, :, a:b]
        else:
            psv = ps.t[:, pc0:pc0 + 512].rearrange("p (two hp d) -> p two hp d", two=2, hp=4)
            dv = dst_ap.rearrange("p (hp two d) -> p two hp d", two=2, hp=4)
            tav = ta.t[:, 0:256].rearrange("p (two hp d) -> p two hp d", two=2, hp=4)
            cos1 = self.pcf.t[:, ro:ro + 8].unsqueeze(1).unsqueeze(1).to_broadcast([128, 2, 4, 8])
            sin1 = self.pcf.t[:, ro + 8:ro + 16].unsqueeze(1).unsqueeze(1).to_broadcast([128, 2, 4, 8])
            sl = lambda v, a, b: v[:, :, :, a:b]
        S.op("dve", I("tensor_copy", out=sl(dv, 16, 64), in_=sl(psv, 16, 64)), reads=[ps], writes=[dst_tn])
        S.op("dve", I("tensor_tensor", out=sl(tav, 0, 8), in0=sl(psv, 0, 8), in1=cos1, op=ALU.mult), reads=[ps, self.pcf], writes=[ta])
        S.op("dve", I("tensor_tensor", out=sl(tav, 8, 16), in0=sl(psv, 8, 16), in1=cos1, op=ALU.mult), reads=[ps, self.pcf], writes=[ta])
        S.op("dve", I("tensor_tensor", out=sl(tav, 16, 24), in0=sl(psv, 8, 16), in1=sin1, op=ALU.mult), reads=[ps, self.pcf], writes=[ta])
        S.op("dve", I("tensor_tensor", out=sl(tav, 24, 32), in0=sl(psv, 0, 8), in1=sin1, op=ALU.mult), reads=[ps, self.pcf], writes=[ta])
        S.op("dve", I("tensor_tensor", out=sl(dv, 0, 8), in0=sl(tav, 0, 8), in1=sl(tav, 16, 24), op=ALU.subtract), reads=[ta], writes=[dst_tn])
        S.op("dve", I("tensor_tensor", out=sl(dv, 8, 16), in0=sl(tav, 8, 16), in1=sl(tav, 24, 32), op=ALU.add), reads=[ta], writes=[dst_tn])

    def build(self):
        nc, S = self.nc, self.S
        with ExitStack() as es0:
            self.cbf = self.sb(es0, "cbf", [128, NBFC], BF16)
            self.cf32 = self.sb(es0, "cf32", [128, NF32C], F32)
            self.pcf = self.sb(es0, "pcf", [128, NPCF], F32)
            self.pcb = self.sb(es0, "pcb", [128, NPCB], BF16)
            self.gmix = self.sb(es0, "gmix", [128, 8], F32)
            self.gmlp = self.sb(es0, "gmlp", [128, 8], F32)
            self.epsc = self.sb(es0, "epsc", [128, 1], F32)
            S.dma(I("dma_start", out=self.cbf.t[:], in_=self.c_bf), writes=[self.cbf])
            S.dma(I("dma_start", out=self.cf32.t[:], in_=self.c_f32), writes=[self.cf32])
            S.dma(I("dma_start", out=self.pcf.t[:], in_=self.pc_f), writes=[self.pcf])
            S.dma(I("dma_start", out=self.pcb.t[:], in_=self.pc_bf), writes=[self.pcb])
            S.dma(I("dma_start", out=self.gmix.t[:], in_=self.g_mix), writes=[self.gmix])
            S.dma(I("dma_start", out=self.gmlp.t[:], in_=self.g_mlp), writes=[self.gmlp])
            S.op("dve", I("memset", self.epsc.t[:], EPS), writes=[self.epsc])
            self.ident = self.cbf.t[:, 0:128]
            self.xring = self.ring(es0, "xr", [128, D], F32, 2)
            self.junk = self.ring(es0, "jk", [128, D], BF16, 1)
            self.stat = self.ring(es0, "st", [128, 4], F32, 4)
            self.xnring = self.ring(es0, "xn", [128, D], BF16, 2)
            self.rtmp = self.ring(es0, "rtmp", [128, 256], F32, 2)
            self.ptr = self.ring(es0, "ptr", [128, 512], BF16, 3)
            self.psA = Ring([self.ps(es0, "psA%d" % i, [128, 512], F32) for i in range(2)])
            self.psT = self.ps(es0, "psT", [128, 1024], BF16)
            self.psS = Ring([self.ps(es0, "psS%d" % i, [128, 512], F32) for i in range(2)])
            self.psO = Ring([self.ps(es0, "psO%d" % i, [128, 512], F32) for i in range(2)])
            self.psX = self.ps(es0, "psX", [128, 512], F32)
            self.yaT = self.sb(es0, "yaT", [128, 4, OWN], BF16)
            self.stopped = False
            self.phase_A(es0)
            if self.stopped:
                S.barrier()
                if self.stop_after in ("A3", "A"):
                    self.dump("yaT", self.yaT, self.yaT.t[:], [128, 4, OWN], BF16)
                self.fake_out()
                S.emit()
                return nc
            self.ybT = self.sb(es0, "ybT", [128, 4, OWN], BF16)
            S.barrier()
            if self.stop_after == "A":
                self.dump("yaT", self.yaT, self.yaT.t[:], [128, 4, OWN], BF16)
                self.fake_out()
                S.emit()
                return nc
            self.phase_B(es0)
            S.barrier()
            if self.stopped:
                self.fake_out()
                S.emit()
                return nc
            if self.stop_after == "B":
                self.dump("yaT", self.yaT, self.yaT.t[:], [128, 4, OWN], BF16)
                self.dump("ybT", self.ybT, self.ybT.t[:], [128, 4, OWN], BF16)
                self.fake_out()
                S.emit()
                return nc
            self.phase_C(es0)
            if self.debug:
                self.dump("yaT", self.yaT, self.yaT.t[:], [128, 4, OWN], BF16)
                self.dump("ybT", self.ybT, self.ybT.t[:], [128, 4, OWN], BF16)
            S.emit()
        return nc

    def fake_out(self):
        S = self.S
        xt = self.xring.next()
        for t in range(NT):
            S.dma(I("dma_start", out=xt.t[:], in_=self.x_own[t * 128:(t + 1) * 128, :]), writes=[xt])
            S.dma(I("dma_start", out=self.out[t * 128:(t + 1) * 128, :], in_=xt.t[:]), reads=[xt])

    def attn_unit(self, score_mms, n, mask_fn, pv_list):
        S = self.S
        if getattr(self, "_collect", None) is not None:
            self._collect.append((score_mms, n, mask_fn, pv_list, None))
            return
        pss = self.psS.next()
        for i, (l, r, rd) in enumerate(score_mms):
            S.op("pe", I("matmul", pss.t[:, 0:n], lhsT=l, rhs=r, start=(i == 0), stop=(i == len(score_mms) - 1)),
                 reads=rd, writes=[pss])
        pt = self.ptr.next()
        S.op("act", I("activation", out=pt.t[:, 0:n], in_=pss.t[:, 0:n], func=AF.Exp, scale=SCALE), reads=[pss], writes=[pt])
        if mask_fn is not None:
            mask_fn(pt)
        for (pso, out_ap, vaug, c0, ncol, st, sp, rd) in pv_list:
            S.op("pe", I("matmul", out_ap, lhsT=vaug, rhs=pt.t[:, c0:c0 + ncol], start=st, stop=sp),
                 reads=[pt] + rd, writes=[pso])

    def attn_seq(self, units):
        S = self.S
        prev = None
        for u in list(units) + [None]:
            cur = None
            if u is not None:
                score_mms, n = u[0], u[1]
                pss = self.psS.next()
                for i, (l, r, rd) in enumerate(score_mms):
                    S.op("pe", I("matmul", pss.t[:, 0:n], lhsT=l, rhs=r, start=(i == 0), stop=(i == len(score_mms) - 1)), reads=rd, writes=[pss])
                cur = (u, pss)
            if prev is not None:
                (pu, ppss) = prev
                n = pu[1]
                pt = self.ptr.next()
                S.op("act", I("activation", out=pt.t[:, 0:n], in_=ppss.t[:, 0:n], func=AF.Exp, scale=SCALE), reads=[ppss], writes=[pt])
                if pu[2] is not None:
                    pu[2](pt)
                for (pso, out_ap, vaug, c0, ncol, st, sp, rd) in pu[3]:
                    S.op("pe", I("matmul", out_ap, lhsT=vaug, rhs=pt.t[:, c0:c0 + ncol], start=st, stop=sp), reads=[pt] + rd, writes=[pso])
                if len(pu) > 4 and pu[4] is not None:
                    pu[4]()
            prev = cur

    def attn_steps(self, steps, ring):
        S = self.S
        prev = None
        for stp in list(steps) + [None]:
            cur = None
            if stp is not None:
                cur = []
                for u in stp:
                    score_mms, n = u[0], u[1]
                    pss = ring.next()
                    cur.append((u, pss))
                nmm = max(len(u[0]) for u in stp)
                for i in range(nmm):
                    for (u, pss) in cur:
                        if i < len(u[0]):
                            l, r, rd = u[0][i]
                            S.op("pe", I("matmul", pss.t[:, 0:u[1]], lhsT=l, rhs=r, start=(i == 0), stop=(i == len(u[0]) - 1)), reads=rd, writes=[pss])
            if prev is not None:
                for (pu, ppss) in prev:
                    n = pu[1]
                    pt = self.ptr.next()
                    S.op("act", I("activation", out=pt.t[:, 0:n], in_=ppss.t[:, 0:n], func=AF.Exp, scale=SCALE), reads=[ppss], writes=[pt])
                    if pu[2] is not None:
                        pu[2](pt)
                    for (pso, out_ap, vaug, c0, ncol, st, sp, rd) in pu[3]:
                        S.op("pe", I("matmul", out_ap, lhsT=vaug, rhs=pt.t[:, c0:c0 + ncol], start=st, stop=sp), reads=[pt] + rd, writes=[pso])
            prev = cur

    def mask_mul(self, pt, c0, n, mask_ap):
        self.S.op("dve", I("tensor_tensor", out=pt.t[:, c0:c0 + n], in0=pt.t[:, c0:c0 + n], in1=mask_ap, op=ALU.mult), reads=[pt, self.cbf], writes=[pt])

    def phase_A(self, es0):
        S = self.S
        with ExitStack() as esA:
            self.phase_A_body(esA)
        S.barrier()

    def phase_A_body(self, esA):
        S = self.S
        self.uTh = self.sb(esA, "uTh", [128, 8, OWN], BF16)
        self.uTo = self.sb(esA, "uTo", [128, 8, OWN], BF16)
        self.wst = self.ring(esA, "wstA", [128, 8, 384], F32, 2)
        self.wsl = self.ring(esA, "wslA", [128, 8, 384], BF16, 2)
        for t in range(NT):
            self.norm_tile(self.x_halo[t * 128:(t + 1) * 128, :], self.uTh, self.uTh.t[:, :, t * 128:(t + 1) * 128])
        for t in range(NT):
            self.norm_tile(self.x_own[t * 128:(t + 1) * 128, :], self.uTo, self.uTo.t[:, :, t * 128:(t + 1) * 128])
        if self.stop_after == "A0":
            self.stopped = True
            return
        with ExitStack() as es:
            self.phase_A_inner(es)

    def phase_A_inner(self, es):
        S = self.S
        if True:
            qT = self.sb(es, "a_qT", [128, OWN], BF16)
            kT = self.sb(es, "a_kT", [128, 32 * 128], BF16)
            vaug = self.sb(es, "a_v", [128, 32, 2, 128], BF16)
            qk = self.ring(es, "a_qk", [128, 256], BF16, 2)
            if os.environ.get("PADLOW"):
                pad = self.sb(es, "a_pad", [128, int(os.environ["PADLOW"]) * 256], F32)
            acc = [self.sb(es, "a_acc%d" % i, [128, OWN], F32) for i in range(2)]
            rd_ = self.ring(es, "a_rd", [128, 512], F32, 2)
            ones64 = self.cbf.t[:, BFC["ones"]:BFC["ones"] + 64]
            hv64 = self.pcb.t[:, PCB["hv64"]:PCB["hv64"] + 64]
            hvcol = self.pcf.t[:, PCF["hv"]:PCF["hv"] + 1]
            for p in range(4):
                for g, d in enumerate(DILS):
                    nt = NT // d
                    st = self.wst.next()
                    sl = self.wsl.next()
                    for i, cb in enumerate((C_QA, C_KA, C_VA)):
                        c0 = cb + g * 512 + p * 128
                        for kc in range(8):
                            S.dma(I("dma_start", out=st.t[:, kc, i * 128:(i + 1) * 128], in_=self.w_in[kc * 128:(kc + 1) * 128, c0:c0 + 128]), writes=[st])
                    gb = self.gmix.t[:, 0:8].unsqueeze(2).to_broadcast([128, 8, 384])
                    if int(os.environ.get("A1CUT", "99")) >= 0:
                        S.op("pool", I("tensor_tensor", out=sl.t[:, :, 0:384], in0=st.t[:, :, 0:384], in1=gb, op=ALU.mult), reads=[st, self.gmix], writes=[sl])
                    if int(os.environ.get("A1CUT", "99")) <= 0:
                        self.stopped = True
                        return
                    for r in range(d):
                        for j in range(-1, nt):
                            slot = r * (nt + 1) + j + 1
                            start = 2048 + 128 * d * j + r
                            if start < 2048:
                                ut, s0 = self.uTh, start
                            else:
                                ut, s0 = self.uTo, start - 2048
                            lhs = lambda c, ut=ut, s0=s0, d=d: ut.t[:, c, s0:s0 + 127 * d + 1:d]
                            ridx = (15 + slot) if g == 0 else ((32 + slot) if g == 1 else (52 + slot))
                            ps = self.psA.next()
                            halo = (j == -1)
                            if halo:
                                self.proj_tm(lhs, ut, sl, 128, 256, ps)
                                kc0, vc0 = 0, 128
                            else:
                                self.proj_tm(lhs, ut, sl, 0, 384, ps)
                                kc0, vc0 = 128, 256

                            CUT = int(os.environ.get("A1CUT", "99"))
                            if CUT <= 1:
                                continue
                            t = qk.next()
                            if not halo:
                                self.rope_evac(ps, 0, 4, ridx, t, t.t[:, 0:256])
                            else:
                                self.rope_evac(ps, kc0, 2, ridx, t, t.t[:, 128:256])
                            if CUT <= 2:
                                continue
                            vsrc = ps.t[:, vc0:vc0 + 128].rearrange("p (h d) -> p h d", h=2)
                            if halo:
                                S.op("dve", I("tensor_scalar", out=vaug.t[:, slot, :, 0:64], in0=vsrc, scalar1=hvcol, scalar2=None, op0=ALU.mult), reads=[ps, self.pcf], writes=[vaug])
                                for hh in range(2):
                                    S.op("pool", I("tensor_copy", out=vaug.t[:, slot, hh, 64:128], in_=hv64), reads=[self.pcb], writes=[vaug])
                            else:
                                S.op("act", I("copy", out=vaug.t[:, slot, :, 0:64], in_=vsrc), reads=[ps], writes=[vaug])
                                for hh in range(2):
                                    S.op("pool", I("tensor_copy", out=vaug.t[:, slot, hh, 64:128], in_=ones64), reads=[self.cbf], writes=[vaug])
                            if CUT <= 3:
                                continue
                            pt = self.psT
                            if not halo:
                                S.op("pe", I("transpose", out=pt.t[:, 0:128], in_=t.t[:, 0:128], identity=self.ident), reads=[t, self.cbf], writes=[pt])
                            S.op("pe", I("transpose", out=pt.t[:, 128:256], in_=t.t[:, 128:256], identity=self.ident), reads=[t, self.cbf], writes=[pt])
                            if not halo:
                                qi = r * nt + j
                                S.op("act", I("copy", out=qT.t[:, qi * 128:(qi + 1) * 128], in_=pt.t[:, 0:128]), reads=[pt], writes=[qT])
                            S.op("dve", I("tensor_copy", out=kT.t[:, slot * 128:(slot + 1) * 128], in_=pt.t[:, 128:256]), reads=[pt], writes=[kT])
                    if self.stop_after == "A1" and int(os.environ.get("A1CUT", "99")) < 99:
                        self.stopped = True
                        return
                    if self.stop_after == "A1":
                        self.dump("qT", qT, qT.t[:], [128, OWN], BF16)
                        self.dump("kT", kT, kT.t[:, 0:17 * 128], [128, 17 * 128], BF16)
                        self.dump("vaug", vaug, vaug.t[:, 0:17], [128, 17, 2, 128], BF16)
                        self.stopped = True
                        return
                    for hh in range(2):
                        pb = 64 * hh
                        banks = {}
                        units = []
                        for r in range(d):
                            for j in range(-1, nt):
                                slot = r * (nt + 1) + j + 1
                                qlo = max(j, 0)
                                qhi = min(j + 1, nt - 1)
                                nq = qhi - qlo + 1
                                qc0 = (r * nt + qlo) * 128
                                n = nq * 128
                                if j == -1:
                                    mk = [(0, 128, self.cbf.t[:, BFC["tri_prev"]:BFC["tri_prev"] + 128])]
                                elif nq == 1:
                                    mk = [(0, 128, self.cbf.t[:, BFC["tri_diag"]:BFC["tri_diag"] + 128])]
                                else:
                                    mk = [(0, 256, self.cbf.t[:, BFC["tri_diag"]:BFC["tri_diag"] + 256])]

                                def mask_fn(pt, mk=mk):
                                    for (c0, nn, ap) in mk:
                                        self.mask_mul(pt, c0, nn, ap)
                                pv = []
                                for qt in range(qlo, qhi + 1):
                                    qi = r * nt + qt
                                    if (qt == j + 1) and (qi % 4 == 0):
                                        banks[qi // 4] = self.psO.next()
                                    pso = banks[qi // 4]
                                    col = (qi % 4) * 128
                                    pv.append((pso, pso.t[:, col:col + 128], vaug.t[:, slot, hh, :], (qt - qlo) * 128, 128, qt == j + 1, qt == j, [vaug]))
                                after = None
                                if j >= 0 and (r * nt + j) % 4 == 3:
                                    bk = (r * nt + j) // 4
                                    pso = banks[bk]
                                    av = acc[hh].t[:]
                                    if d == 1:
                                        dst = av[:, bk * 512:(bk + 1) * 512]
                                        src = pso.t[:, 0:512]
                                    elif d == 4:
                                        dst = av.rearrange("p (i r) -> p r i", r=4)[:, r, :]
                                        src = pso.t[:, 0:512]
                                    else:
                                        dst = av.rearrange("p (i r) -> p r i", r=16)[:, 4 * bk:4 * bk + 4, :]
                                        src = pso.t[:, 0:512].rearrange("p (r i) -> p r i", r=4)

                                    def after(dst=dst, src=src, pso=pso, hh=hh, g=g):
                                        if g == 0:
                                            S.op("act", I("copy", out=dst, in_=src), reads=[pso], writes=[acc[hh]])
                                        else:
                                            S.op("dve", I("tensor_tensor", out=dst, in0=dst, in1=src, op=ALU.add), reads=[pso, acc[hh]], writes=[acc[hh]])
                                units.append(([(kT.t[pb:pb + 64, slot * 128:(slot + 1) * 128], qT.t[pb:pb + 64, qc0:qc0 + n], [kT, qT])], n, mask_fn, pv, after))
                        self.attn_seq(units)
                if self.stop_after == "A2":
                    self.stopped = True
                    return
                for hh in range(2):
                    for c in range(4):
                        self.finalize(acc[hh], acc[hh].t[:, c * 512:(c + 1) * 512], hh, None, rd_, self.yaT, self.yaT.t[:, p, c * 512:(c + 1) * 512], first=True, last=True, ybacc=None)
                if self.stop_after == "A3":
                    self.stopped = True
                    return
        S.barrier()

    def finalize(self, src_tn, src_ap, hh, gate_row, rdring, dst_tn, dst_ap, first, last, ybacc):
        S = self.S
        psx = self.psX
        S.op("pe", I("matmul", psx.t[:, 0:512], lhsT=self.cf32.t[:, 0:128], rhs=src_ap, start=True, stop=True), reads=[src_tn, self.cf32], writes=[psx])
        rd = rdring.next()
        lo, hi = 64 * hh, 64 * hh + 64
        if hh == 0:
            den = psx.t[0:64, 0:512]
            num = src_ap[0:64, :]
        else:
            den = src_ap[64:128, :]
            num = psx.t[64:128, 0:512]
        S.op("dve", I("tensor_scalar", out=rd.t[lo:hi, :], in0=den, scalar1=1e-30, scalar2=None, op0=ALU.max), reads=[psx, src_tn], writes=[rd])
        S.op("dve", I("reciprocal", out=rd.t[lo:hi, :], in_=rd.t[lo:hi, :]), reads=[rd], writes=[rd])
        if gate_row is None:
            S.op("dve", I("tensor_tensor", out=dst_ap[lo:hi, :], in0=num, in1=rd.t[lo:hi, :], op=ALU.mult), reads=[psx, src_tn, rd], writes=[dst_tn])
            return
        S.op("dve", I("tensor_tensor", out=rd.t[lo:hi, :], in0=num, in1=rd.t[lo:hi, :], op=ALU.mult), reads=[psx, src_tn, rd], writes=[rd])
        gsel, gbT, gcols = gate_row
        S.op("pe", I("matmul", psx.t[:, 0:512], lhsT=gsel.t[:], rhs=gcols, start=True, stop=True), reads=[gbT, gsel, rd], writes=[psx])
        if first:
            S.op("dve", I("tensor_tensor", out=ybacc.t[lo:hi, :], in0=rd.t[lo:hi, :], in1=psx.t[lo:hi, 0:512], op=ALU.mult), reads=[psx, rd], writes=[ybacc])
        else:
            S.op("dve", I("tensor_tensor", out=rd.t[lo:hi, :], in0=rd.t[lo:hi, :], in1=psx.t[lo:hi, 0:512], op=ALU.mult), reads=[psx, rd], writes=[rd])
            if last:
                S.op("dve", I("tensor_tensor", out=dst_ap[lo:hi, :], in0=rd.t[lo:hi, :], in1=ybacc.t[lo:hi, :], op=ALU.add), reads=[rd, ybacc], writes=[dst_tn])
            else:
                S.op("dve", I("tensor_tensor", out=ybacc.t[lo:hi, :], in0=rd.t[lo:hi, :], in1=ybacc.t[lo:hi, :], op=ALU.add), reads=[rd, ybacc], writes=[ybacc])

    def phase_B(self, es0):
        S = self.S
        cbf, pcf, pcb = self.cbf, self.pcf, self.pcb
        ones64 = cbf.t[:, BFC["ones"]:BFC["ones"] + 64]
        ones2 = cbf.t[:, BFC["ones"]:BFC["ones"] + 128].rearrange("p (g d) -> p g d", g=2)
        with ExitStack() as esB:
            kslcT = self.sb(esB, "b_kslcT", [128, 48 * 128], BF16)
            vslc = self.sb(esB, "b_vslc", [128, 48, 2, 128], BF16)
            kcT = self.sb(esB, "b_kcT", [128, 512], BF16)
            vc = self.sb(esB, "b_vc", [128, 4, 2, 128], BF16)
            S.op("pool", I("memset", kcT.t[:], 0.0), writes=[kcT])
            S.op("pool", I("memset", vc.t[:], 0.0), writes=[vc])
            with ExitStack() as es:
                self.wst = self.ring(es, "wstB", [128, 8, 256], F32, 2)
                kcmpT = self.sb(es, "b_kcmpT", [128, S_LEN], BF16)
                vcmpT = self.sb(es, "b_vcmpT", [128, S_LEN], BF16)
                slab = self.sb(es, "b_slab", [128, 8, 512], BF16)
                uTt = self.ring(es, "b_uTt", [128, 8, 128], BF16, 2)
                tm = self.ring(es, "b_tm", [128, 512], BF16, 2)
                for dcol, scol in ((0, 0), (128, 256), (256, 128), (384, 384)):
                    self.cast_into(slab, lambda c0, n, dcol=dcol: slab.t[:, :, dcol + c0:dcol + c0 + n], self.w_in[:, C_KVB + scol:C_KVB + scol + 128], 8, 128, self.gmix, piece=128)
                for kt in range(64):
                    u = uTt.next()
                    self.norm_tile(self.x_full[kt * 128:(kt + 1) * 128, :], u, u.t[:])
                    ps = self.psA.next()
                    self.proj_tm(lambda c, u=u: u.t[:, c, :], u, slab, 0, 512, ps)
                    t = tm.next()
                    self.rope_evac(ps, 0, 4, 84 + kt, t, t.t[:, 0:256])
                    S.op("act", I("copy", out=t.t[:, 256:384], in_=ps.t[:, 256:384]), reads=[ps], writes=[t])
                    if kt < 48:
                        pvc = pcf.t[:, PCF["pv"] + kt:PCF["pv"] + kt + 1]
                        S.op("dve", I("tensor_scalar", out=vslc.t[:, kt, :, 0:64], in0=ps.t[:, 384:512].rearrange("p (g d) -> p g d", g=2), scalar1=pvc, scalar2=None, op0=ALU.mult), reads=[ps, pcf], writes=[vslc])
                        S.op("pool", I("tensor_scalar", out=vslc.t[:, kt, :, 64:128], in0=ones2, scalar1=pvc, scalar2=None, op0=ALU.mult), reads=[cbf, pcf], writes=[vslc])
                    pt = self.psT
                    for k in range(3):
                        S.op("pe", I("transpose", out=pt.t[:, k * 128:(k + 1) * 128], in_=t.t[:, k * 128:(k + 1) * 128], identity=self.ident), reads=[t, cbf], writes=[pt])
                    S.op("act", I("copy", out=kcmpT.t[:, kt * 128:(kt + 1) * 128], in_=pt.t[:, 0:128]), reads=[pt], writes=[kcmpT])
                    S.op("act", I("copy", out=vcmpT.t[:, kt * 128:(kt + 1) * 128], in_=pt.t[:, 256:384]), reads=[pt], writes=[vcmpT])
                    if kt < 48:
                        S.op("act", I("copy", out=kslcT.t[:, kt * 128:(kt + 1) * 128], in_=pt.t[:, 128:256]), reads=[pt], writes=[kslcT])
                if self.stop_after == "B2":
                    self.dump("kslcT", kslcT, kslcT.t[:], [128, 48 * 128], BF16)
                    self.dump("kcmpT", kcmpT, kcmpT.t[:], [128, S_LEN], BF16)
                    self.dump("vslc", vslc, vslc.t[:], [128, 48, 2, 128], BF16)
                    self.stopped = True
                    return
                w1sb = self.sb(es, "b_w1", [128, 32, 256], BF16)
                w2sb = self.sb(es, "b_w2", [128, 2, 128], BF16)
                posb = self.sb(es, "b_posb", [32, 128], BF16)
                posf = self.sb(es, "b_posf", [32, 64], F32)
                posT = self.sb(es, "b_posT", [128, 32], BF16)
                b1sb = self.sb(es, "b_b1", [128, 2], F32)
                gel = [self.sb(es, "b_gel%d" % i, [128, 512], BF16) for i in range(2)]
                hA = self.sb(es, "b_hA", [128, 512], F32)
                hB = self.sb(es, "b_hB", [128, 512], F32)
                for kv in range(2):
                    src = kcmpT if kv == 0 else vcmpT
                    w1d = self.cmp_w1_k if kv == 0 else self.cmp_w1_v
                    w2d = self.cmp_w2_k if kv == 0 else self.cmp_w2_v
                    posd = self.cmp_pos_k if kv == 0 else self.cmp_pos_v
                    w1v = w1d.rearrange("(j d) h -> d j h", d=64)
                    for j0 in range(0, 32, 8):
                        st = self.wst.next()
                        for half in range(2):
                            S.dma(I("dma_start", out=st.t[64 * half:64 * half + 64, :, :], in_=w1v[:, j0:j0 + 8, :]), writes=[st])
                        S.op("pool", I("tensor_copy", out=w1sb.t[:, j0:j0 + 8, :], in_=st.t[:, :, :]), reads=[st], writes=[w1sb])
                    st = self.wst.next()
                    S.dma(I("dma_start", out=st.t[:, 0:2, 0:64], in_=w2d.rearrange("(c p) n -> p c n", p=128)), writes=[st])
                    S.op("pool", I("tensor_copy", out=w2sb.t[:, :, 0:64], in_=st.t[:, 0:2, 0:64]), reads=[st], writes=[w2sb])
                    S.op("pool", I("tensor_copy", out=w2sb.t[:, :, 64:128], in_=st.t[:, 0:2, 0:64]), reads=[st], writes=[w2sb])
                    S.dma(I("dma_start", out=posf.t[:], in_=posd), writes=[posf])
                    S.op("dve", I("tensor_copy", out=posb.t[:, 0:64], in_=posf.t[:]), reads=[posf], writes=[posb])
                    S.op("dve", I("tensor_copy", out=posb.t[:, 64:128], in_=posf.t[:]), reads=[posf], writes=[posb])
                    pt = self.psT
                    S.op("pe", I("transpose", out=pt.t[:, 0:32], in_=posb.t[:], identity=cbf.t[0:32, 0:32]), reads=[posb, cbf], writes=[pt])
                    S.op("act", I("copy", out=posT.t[:], in_=pt.t[:, 0:32]), reads=[pt], writes=[posT])
                    psx = self.psX
                    for mh in range(2):
                        for j in range(32):
                            S.op("pe", I("matmul", psx.t[:, mh:mh + 1], lhsT=w1sb.t[0:64, j, mh * 128:(mh + 1) * 128], rhs=posT.t[0:64, j:j + 1], start=(j == 0), stop=(j == 31)), reads=[w1sb, posT], writes=[psx])
                    S.op("dve", I("tensor_copy", out=b1sb.t[:], in_=psx.t[:, 0:2]), reads=[psx], writes=[b1sb])
                    for g in range(2):
                        pb = 64 * g
                        for mh in range(2):
                            ps = self.psA.next()
                            for j in range(32):
                                S.op("pe", I("matmul", ps.t[:, 0:511], lhsT=w1sb.t[pb:pb + 64, j, mh * 128:(mh + 1) * 128], rhs=src.t[pb:pb + 64, j:j + 16 * 510 + 1:16], start=(j == 0), stop=(j == 31)), reads=[w1sb, src], writes=[ps])
                            S.op("act", I("activation", out=hA.t[:, 0:511], in_=ps.t[:, 0:511], func=AF.Identity, bias=b1sb.t[:, mh:mh + 1]), reads=[ps, b1sb], writes=[hA])
                            S.op("dve", I("tensor_tensor", out=hB.t[:, 0:511], in0=hA.t[:, 0:511], in1=hA.t[:, 0:511], op=ALU.mult), reads=[hA], writes=[hB])
                            S.op("dve", I("tensor_scalar", out=hB.t[:, 0:511], in0=hB.t[:, 0:511], scalar1=0.044715, scalar2=1.0, op0=ALU.mult, op1=ALU.add), reads=[hB], writes=[hB])
                            S.op("dve", I("tensor_tensor", out=hB.t[:, 0:511], in0=hB.t[:, 0:511], in1=hA.t[:, 0:511], op=ALU.mult), reads=[hA, hB], writes=[hB])
                            S.op("act", I("activation", out=hB.t[:, 0:511], in_=hB.t[:, 0:511], func=AF.Sigmoid, scale=2.0 * 0.7978845608028654), reads=[hB], writes=[hB])
                            S.op("dve", I("tensor_tensor", out=gel[mh].t[:, 0:511], in0=hA.t[:, 0:511], in1=hB.t[:, 0:511], op=ALU.mult), reads=[hA, hB], writes=[gel[mh]])
                        if kv == 0:
                            ps = self.psA.next()
                            for mh in range(2):
                                S.op("pe", I("matmul", ps.t[:, 0:511], lhsT=w2sb.t[:, mh, :], rhs=gel[mh].t[:, 0:511], start=(mh == 0), stop=(mh == 1)), reads=[w2sb, gel[mh]], writes=[ps])
                            S.op("act", I("copy", out=kcT.t[pb:pb + 64, 0:511], in_=ps.t[pb:pb + 64, 0:511]), reads=[ps], writes=[kcT])
                        else:
                            for bt in range(4):
                                n = 128 if bt < 3 else 127
                                ps = self.psA.next()
                                for mh in range(2):
                                    S.op("pe", I("matmul", ps.t[0:n, 0:64], lhsT=gel[mh].t[:, bt * 128:bt * 128 + n], rhs=w2sb.t[:, mh, 0:64], start=(mh == 0), stop=(mh == 1)), reads=[w2sb, gel[mh]], writes=[ps])
                                S.op("act", I("copy", out=vc.t[0:n, bt, g, 0:64], in_=ps.t[0:n, 0:64]), reads=[ps], writes=[vc])
                                S.op("pool", I("tensor_copy", out=vc.t[0:n, bt, g, 64:128], in_=ones64[0:n, :]), reads=[cbf], writes=[vc])
            S.barrier()
            if self.stop_after == "B3":
                self.dump("kcT", kcT, kcT.t[:], [128, 512], BF16)
                self.dump("vc", vc, vc.t[:], [128, 4, 2, 128], BF16)
                self.stopped = True
                return
            qbT = self.sb(esB, "b_qbT", [128, 4, OWN], BF16)
            gbT = self.sb(esB, "b_gbT", [32, OWN], F32)
            kwinT = self.sb(esB, "b_kwinT", [128, 20 * 128], BF16)
            vwin = self.sb(esB, "b_vwin", [128, 20, 2, 128], BF16)
            kso = self.sb(esB, "b_kso", [128, OWN], BF16)
            vso = self.sb(esB, "b_vso", [128, 16, 2, 128], BF16)
            S.op("pool", I("memset", gbT.t[:], 0.0), writes=[gbT])
            hvcol = pcf.t[:, PCF["hv"]:PCF["hv"] + 1]
            with ExitStack() as es:
                self.wst = self.ring(es, "wstB1", [128, 8, 256], F32, 2)
                slq = self.sb(es, "b_slq", [128, 8, 512], BF16)
                slkv = self.sb(es, "b_slkv", [128, 8, 512], BF16)
                slg = self.sb(es, "b_slg", [128, 8, 32], BF16)
                uTt = self.ring(es, "b_uTt1", [128, 8, 128], BF16, 2)
                tq = self.ring(es, "b_tq", [128, 512], BF16, 2)
                tk = self.ring(es, "b_tk", [128, 256], BF16, 2)
                self.cast_into(slq, lambda c0, n: slq.t[:, :, c0:c0 + n], self.w_in[:, C_QB:C_QB + 512], 8, 512, self.gmix, piece=256)
                self.cast_into(slkv, lambda c0, n: slkv.t[:, :, c0:c0 + n], self.w_in[:, C_KVB + 256:C_KVB + 768], 8, 512, self.gmix, piece=256)
                self.cast_into(slg, lambda c0, n: slg.t[:, :, c0:c0 + n], self.w_in[:, C_GB:C_GB + 24], 8, 24, self.gmix, piece=256)
                for e_ in range(12, 32):
                    slot = e_ - 12
                    own = e_ >= 16
                    i = e_ - 16
                    u = uTt.next()
                    xs = self.x_own[i * 128:(i + 1) * 128, :] if own else self.x_halo[e_ * 128:(e_ + 1) * 128, :]
                    self.norm_tile(xs, u, u.t[:])
                    lhs = lambda c, u=u: u.t[:, c, :]
                    if own:
                        ps = self.psA.next()
                        self.proj_tm(lhs, u, slq, 0, 512, ps)
                        t = tq.next()
                        self.rope_evac(ps, 0, 8, e_, t, t.t[:, 0:512], perm=True)
                        pt = self.psT
                        for hp in range(4):
                            S.op("pe", I("transpose", out=pt.t[:, hp * 128:(hp + 1) * 128], in_=t.t[:, hp * 128:(hp + 1) * 128], identity=self.ident), reads=[t, cbf], writes=[pt])
                        S.op("act", I("copy", out=qbT.t[:, :, i * 128:(i + 1) * 128], in_=pt.t[:, 0:512].rearrange("p (h t) -> p h t", h=4)), reads=[pt], writes=[qbT])
                        psx = self.psX
                        for c in range(8):
                            S.op("pe", I("matmul", psx.t[0:24, 0:128], lhsT=slg.t[:, c, 0:24], rhs=u.t[:, c, :], start=(c == 0), stop=(c == 7)), reads=[slg, u], writes=[psx])
                        S.op("act", I("activation", out=gbT.t[0:24, i * 128:(i + 1) * 128], in_=psx.t[0:24, 0:128], func=AF.Sigmoid), reads=[psx], writes=[gbT])
                    ps = self.psA.next()
                    t = tk.next()
                    if own:
                        self.proj_tm(lhs, u, slkv, 0, 512, ps)
                        self.rope_evac(ps, 0, 2, e_, t, t.t[:, 0:128])
                        self.rope_evac(ps, 256, 2, e_, t, t.t[:, 128:256])
                        S.op("dve", I("tensor_copy", out=vso.t[:, i, :, 0:64], in_=ps.t[:, 128:256].rearrange("p (g d) -> p g d", g=2)), reads=[ps], writes=[vso])
                        S.op("pool", I("tensor_copy", out=vso.t[:, i, :, 64:128], in_=ones2), reads=[cbf], writes=[vso])
                        S.op("dve", I("tensor_copy", out=vwin.t[:, slot, :, 0:64], in_=ps.t[:, 384:512].rearrange("p (g d) -> p g d", g=2)), reads=[ps], writes=[vwin])
                        S.op("pool", I("tensor_copy", out=vwin.t[:, slot, :, 64:128], in_=ones2), reads=[cbf], writes=[vwin])
                    else:
                        self.proj_tm(lhs, u, slkv, 256, 256, ps)
                        self.rope_evac(ps, 0, 2, e_, t, t.t[:, 128:256])
                        S.op("dve", I("tensor_scalar", out=vwin.t[:, slot, :, 0:64], in0=ps.t[:, 128:256].rearrange("p (g d) -> p g d", g=2), scalar1=hvcol, scalar2=None, op0=ALU.mult), reads=[ps, pcf], writes=[vwin])
                        S.op("pool", I("tensor_scalar", out=vwin.t[:, slot, :, 64:128], in0=ones2, scalar1=hvcol, scalar2=None, op0=ALU.mult), reads=[cbf, pcf], writes=[vwin])
                    pt = self.psT
                    if own:
                        S.op("pe", I("transpose", out=pt.t[:, 0:128], in_=t.t[:, 0:128], identity=self.ident), reads=[t, cbf], writes=[pt])
                    S.op("pe", I("transpose", out=pt.t[:, 128:256], in_=t.t[:, 128:256], identity=self.ident), reads=[t, cbf], writes=[pt])
                    if own:
                        S.op("act", I("copy", out=kso.t[:, i * 128:(i + 1) * 128], in_=pt.t[:, 0:128]), reads=[pt], writes=[kso])
                    S.op("act", I("copy", out=kwinT.t[:, slot * 128:(slot + 1) * 128], in_=pt.t[:, 128:256]), reads=[pt], writes=[kwinT])
            S.barrier()
            biasT = self.sb(esB, "b_biasT", [128, 2, OWN], BF16)
            t16 = self.sb(esB, "b_t16", [128, 2048], F32)
            S.dma(I("dma_start", out=t16.t[:], in_=self.c_t16), writes=[t16])
            with ExitStack() as es:
                et = self.ring(es, "b_et", [128, 512], F32, 4)
                pp = self.ring(es, "b_pp", [128, 520], F32, 4)
                lohi = self.ring(es, "b_lohi", [128, 256], F32, 2)
                imp = self.ring(es, "b_imp", [128, 128], F32, 2)
                imp2 = self.ring(es, "b_imp2", [128, 128], F32, 2)
                sm = self.ring(es, "b_sm", [128, 24], F32, 8)
                btm = self.ring(es, "b_btm", [128, 128], BF16, 2)
                for pq in pp.items:
                    S.op("pool", I("memset", pq.t[:], 0.0), writes=[pq])
                for i in range(NT):
                    lh = lohi.next()
                    S.dma(I("dma_start", out=lh.t[:, 0:128], in_=self.pc_lohi[:, i * 128:(i + 1) * 128]), writes=[lh])
                    S.dma(I("dma_start", out=lh.t[:, 128:256], in_=self.pc_lohi[:, 2048 + i * 128:2048 + (i + 1) * 128]), writes=[lh])
                    thr_i = pcf.t[:, PCF["thrc"] + i:PCF["thrc"] + i + 1]
                    def gen(g, i=i, lh=lh, thr_i=thr_i):
                        pb = 64 * g
                        P = pp.next()
                        P2 = pp.next()
                        for r in range(4):
                            ps = self.psS.next()
                            S.op("pe", I("matmul", ps.t[:, 0:511], lhsT=qbT.t[pb:pb + 64, r, i * 128:(i + 1) * 128], rhs=kcT.t[pb:pb + 64, 0:511], start=True, stop=True), reads=[qbT, kcT], writes=[ps])
                            e_ = et.next()
                            s_ = sm.next()
                            eng = "dve"
                            Pr = P if r % 2 == 0 else P2
                            S.op("act", I("activation", out=e_.t[:, 0:511], in_=ps.t[:, 0:511], func=AF.Exp, scale=SCALE), reads=[ps], writes=[e_])
                            S.op(eng, I("scalar_tensor_tensor", out=e_.t[:, 0:511], in0=t16.t[:, 0:511], scalar=thr_i, in1=e_.t[:, 0:511], op0=ALU.is_le, op1=ALU.mult, accum_out=s_.t[:, 0:1]), reads=[t16, pcf, e_], writes=[e_, s_])
                            yield
                            S.op(eng, I("tensor_scalar", out=s_.t[:, 1:2], in0=s_.t[:, 0:1], scalar1=1e-30, scalar2=None, op0=ALU.max), reads=[s_], writes=[s_])
                            yield
                            S.op("dve", I("reciprocal", out=s_.t[:, 2:3], in_=s_.t[:, 1:2]), reads=[s_], writes=[s_])
                            yield
                            if r < 2:
                                S.op(eng, I("tensor_scalar", out=Pr.t[:, 1:512], in0=e_.t[:, 0:511], scalar1=s_.t[:, 2:3], scalar2=None, op0=ALU.mult), reads=[e_, s_], writes=[Pr])
                                yield
                            else:
                                S.op(eng, I("scalar_tensor_tensor", out=Pr.t[:, 1:512], in0=e_.t[:, 0:511], scalar=s_.t[:, 2:3], in1=Pr.t[:, 1:512], op0=ALU.mult, op1=ALU.add), reads=[e_, s_, Pr], writes=[Pr])
                                yield
                        S.op("dve", I("tensor_tensor", out=P.t[:, 1:512], in0=P.t[:, 1:512], in1=P2.t[:, 1:512], op=ALU.add), reads=[P, P2], writes=[P])
                        yield
                        im = imp.next()
                        S.op("dve", I("tensor_tensor", out=im.t[:], in0=P.t[:, 0:512:4], in1=P.t[:, 1:513:4], op=ALU.add), reads=[P], writes=[im])
                        yield
                        for k in range(2, 5):
                            S.op("dve", I("tensor_tensor", out=im.t[:], in0=im.t[:], in1=P.t[:, k:k + 512:4], op=ALU.add), reads=[P, im], writes=[im])
                            yield
                        S.op("dve", I("tensor_tensor", out=im.t[:], in0=im.t[:], in1=lh.t[:, 0:128], op=ALU.max), reads=[lh, im], writes=[im])
                        yield
                        S.op("dve", I("tensor_tensor", out=im.t[:], in0=im.t[:], in1=lh.t[:, 128:256], op=ALU.min), reads=[lh, im], writes=[im])
                        yield
                        s_ = sm.next()
                        i2 = imp2.next()
                        S.op("dve", I("max", out=s_.t[:, 0:8], in_=im.t[:]), reads=[im], writes=[s_])
                        yield
                        S.op("dve", I("match_replace", out=i2.t[:], in_to_replace=s_.t[:, 0:8], in_values=im.t[:], imm_value=-1e9), reads=[im, s_], writes=[i2])
                        yield
                        S.op("dve", I("max", out=s_.t[:, 8:16], in_=i2.t[:]), reads=[i2], writes=[s_])
                        yield
                        S.op("dve", I("tensor_scalar", out=s_.t[:, 16:17], in0=s_.t[:, 15:16], scalar1=-1.5e4, scalar2=None, op0=ALU.max), reads=[s_], writes=[s_])
                        yield
                        bt_ = btm.next()
                        S.op("dve", I("tensor_scalar", out=bt_.t[:], in0=im.t[:], scalar1=s_.t[:, 16:17], scalar2=-30000.0, op0=ALU.is_lt, op1=ALU.mult), reads=[im, s_], writes=[bt_])
                        yield
                        pt = self.psT
                        S.op("pe", I("transpose", out=pt.t[:, 0:128], in_=bt_.t[:], identity=self.ident), reads=[bt_, cbf], writes=[pt])
                        S.op("act", I("copy", out=biasT.t[:, g, i * 128:(i + 1) * 128], in_=pt.t[:, 0:128]), reads=[pt], writes=[biasT])

                    gens = [gen(0), gen(1)]
                    while gens:
                        for gg in list(gens):
                            try:
                                next(gg)
                            except StopIteration:
                                gens.remove(gg)
            S.barrier()
            if self.stop_after == "B6":
                self.dump("biasT", biasT, biasT.t[:], [128, 2, OWN], BF16)
                self.dump("qbT", qbT, qbT.t[:], [128, 4, OWN], BF16)
                self.stopped = True
                return
            with ExitStack() as es:
                osb = self.ring(es, "b_osb", [128, 512], F32, 2)
                rdr = self.ring(es, "b_rd", [128, 512], F32, 2)
                ybacc = self.sb(es, "b_ybacc", [128, 512], F32)
                ybacc2 = self.sb(es, "b_ybacc2", [128, 512], F32)
                self.ptr = self.ring(es, "b_ptr", [128, 512], BF16, 5)
                gsel = self.ring(es, "b_gsel", [32, 128], F32, 6)
                id32 = self.cf32.t[0:32, F32C["id32"]:F32C["id32"] + 32]
                eown = pcb.t[:, PCB["eown"]:PCB["eown"] + 2048]
                e32 = cbf.t[:, BFC["e32"]:BFC["e32"] + 2048]
                m4 = cbf.t[:, BFC["m4"]:BFC["m4"] + 2048]
                tri_diag = cbf.t[:, BFC["tri_diag"]:BFC["tri_diag"] + 128]
                win_far = cbf.t[:, BFC["win_far"]:BFC["win_far"] + 128]
                BRS = os.environ.get("BRS", "012")
                ps4 = Ring([self.psS.items[0], self.psS.items[1], self.psA.items[0], self.psA.items[1]])
                ybaccs = [ybacc, ybacc2]
                for hp in range(4):
                    sels = {}
                    for gi in range(2):
                        h = hp + 4 * gi
                        for br in range(3):
                            gs = gsel.next()
                            jrow = h * 3 + br
                            S.op("pool", I("tensor_copy", out=gs.t[:], in_=id32[:, jrow:jrow + 1].to_broadcast([32, 128])), reads=[self.cf32], writes=[gs])
                            sels[(gi, br)] = gs
                    for c in range(4):
                        gcols = gbT.t[0:32, c * 512:(c + 1) * 512]
                        dst = self.ybT.t[:, hp, c * 512:(c + 1) * 512]
                        Qs = [qbT.t[64 * gi:64 * gi + 64, hp, c * 512:(c + 1) * 512] for gi in range(2)]

                        def fin(psos, br, first, last):
                            for gi in range(2):
                                o = osb.next()
                                S.op("act", I("copy", out=o.t[:], in_=psos[gi].t[:]), reads=[psos[gi]], writes=[o])
                                self.finalize(o, o.t[:], gi, (sels[(gi, br)], gbT, gcols), rdr, self.ybT, dst, first=first, last=last, ybacc=ybaccs[gi])
                        psos = [self.psO.next(), self.psO.next()]
                        steps = []
                        for bt in range(4):
                            crel = pcf.t[:, PCF["crel"] + bt:PCF["crel"] + bt + 1]

                            def mask_c(pt, crel=crel, c=c):
                                S.op("dve", I("scalar_tensor_tensor", out=pt.t[:, 0:512], in0=t16.t[:, c * 512:(c + 1) * 512], scalar=crel, in1=pt.t[:, 0:512], op0=ALU.is_ge, op1=ALU.mult), reads=[t16, pcf, pt], writes=[pt])
                            stp = []
                            for gi in range(2):
                                pb = 64 * gi
                                stp.append(([(kcT.t[pb:pb + 64, bt * 128:(bt + 1) * 128], Qs[gi], [kcT, qbT])], 512, mask_c,
                                            [(psos[gi], psos[gi].t[:, 0:512], vc.t[:, bt, gi, :], 0, 512, bt == 0, bt == 3, [vc])]))
                            steps.append(stp)
                        self.attn_steps(steps, ps4)
                        fin(psos, 0, True, False)
                        psos = [self.psO.next(), self.psO.next()]
                        steps = []
                        for kt in range(48):
                            pb32 = 32 * (kt // 16)
                            kc_ = (kt % 16) * 128
                            stp = []
                            for gi in range(2):
                                pb = 64 * gi
                                stp.append(([(kslcT.t[pb:pb + 64, kt * 128:(kt + 1) * 128], Qs[gi], [kslcT, qbT]),
                                             (e32[pb32:pb32 + 32, kc_:kc_ + 128], biasT.t[pb32:pb32 + 32, gi, c * 512:(c + 1) * 512], [cbf, biasT])], 512, None,
                                            [(psos[gi], psos[gi].t[:, 0:512], vslc.t[:, kt, gi, :], 0, 512, kt == 0, False, [vslc])]))
                            steps.append(stp)
                        for j in range(4 * c + 4):
                            mf = None
                            if j >= 4 * c:
                                mk = m4[:, (j - 4 * c) * 512:(j - 4 * c + 1) * 512]

                                def mf(pt, mk=mk):
                                    self.mask_mul(pt, 0, 512, mk)
                            stp = []
                            for gi in range(2):
                                pb = 64 * gi
                                stp.append(([(kso.t[pb:pb + 64, j * 128:(j + 1) * 128], Qs[gi], [kso, qbT]),
                                             (eown[:, j * 128:(j + 1) * 128], biasT.t[:, gi, c * 512:(c + 1) * 512], [pcb, biasT])], 512, mf,
                                            [(psos[gi], psos[gi].t[:, 0:512], vso.t[:, j, gi, :], 0, 512, False, j == 4 * c + 3, [vso])]))
                            steps.append(stp)
                        self.attn_steps(steps, ps4)
                        fin(psos, 1, False, False)
                        psos = [self.psO.next(), self.psO.next()]
                        steps = []
                        for tq_ in range(4):
                            i = 4 * c + tq_
                            sq = 4 + i
                            for s_ in range(sq - 4, sq + 1):
                                mf = None
                                if s_ == sq - 4:
                                    def mf(pt):
                                        self.mask_mul(pt, 0, 128, win_far)
                                elif s_ == sq:
                                    def mf(pt):
                                        self.mask_mul(pt, 0, 128, tri_diag)
                                stp = []
                                for gi in range(2):
                                    pb = 64 * gi
                                    stp.append(([(kwinT.t[pb:pb + 64, s_ * 128:(s_ + 1) * 128], qbT.t[pb:pb + 64, hp, i * 128:(i + 1) * 128], [kwinT, qbT])], 128, mf,
                                                [(psos[gi], psos[gi].t[:, tq_ * 128:(tq_ + 1) * 128], vwin.t[:, s_, gi, :], 0, 128, s_ == sq - 4, s_ == sq, [vwin])]))
                                steps.append(stp)
                        self.attn_steps(steps, ps4)
                        fin(psos, 2, False, True)
        S.barrier()

    def phase_C(self, es0):
        S = self.S
        with ExitStack() as es:
            slabs = self.ring(es, "c_slab", [128, 8, 512], BF16, 5)
            self.wbslab = self.sb(es, "c_wb", [128, 4, 512], BF16)
            gfin = self.sb(es, "c_gfin", [128, D], F32)
            xc = self.sb(es, "c_xc", [128, 4, D], F32)
            uTc = self.sb(es, "c_uTc", [128, 8, 512], BF16)
            mTc = self.sb(es, "c_mTc", [128, 8, 512], BF16)
            u2Tc = self.sb(es, "c_u2Tc", [128, 8, 512], BF16)
            hT = self.sb(es, "c_hT", [128, 32, 512], BF16)
            sg = self.ring(es, "c_sg", [128, 512], BF16, 2)
            tf = self.ring(es, "c_tf", [128, 512], F32, 3)
            S.dma(I("dma_start", out=gfin.t[:], in_=self.g_fin), writes=[gfin])

            slab_ids = {id(t): i for i, t in enumerate(slabs.items)}

            def slab_from(src_ap, kchunks, gain):
                sl = slabs.next()
                fns = [I("dma_start", out=sl.t[:, 0:kchunks, c0:c0 + 256], in_=src_ap[:, c0:c0 + 256].rearrange("(c p) n -> p c n", p=128)) for c0 in (0, 256)]
                S.dma_sw(fns, [sl], slab_ids[id(sl)])
                return sl

            for c in range(4):
                cs = slice(c * 512, (c + 1) * 512)
                for tt in range(4):
                    r0 = c * 512 + tt * 128
                    S.dma(I("dma_start", out=xc.t[:, tt, :], in_=self.x_own[r0:r0 + 128, :]), writes=[xc])
                    self.norm_sb(xc, xc.t[:, tt, :], uTc, uTc.t[:, :, tt * 128:(tt + 1) * 128], keep_rstd=self.gmix)
                for ctg in range(2):
                    gA = slab_from(self.w_in[:, C_GM + ctg * 512:C_GM + ctg * 512 + 512], 8, self.gmix)
                    gB = slab_from(self.w_in[:, C_GM + 1024 + ctg * 512:C_GM + 1024 + ctg * 512 + 512], 8, self.gmix)
                    wa = slab_from(self.w_a[:, ctg * 512:(ctg + 1) * 512], 4, None)
                    wb = self.wbslab
                    fns = [I("dma_start", out=wb.t[64 * two:64 * two + 64, 0:4, 0:512],
                             in_=self.w_b[two * 256:(two + 1) * 256, ctg * 512:ctg * 512 + 512].rearrange("(hp d) n -> d hp n", d=64)) for two in range(2)]
                    S.dma_sw(fns, [wb], 99)
                    for j in range(4):
                        ct = ctg * 4 + j
                        js = slice(j * 128, (j + 1) * 128)
                        sgs = []
                        for gw in (gA, gB):
                            ps = self.psA.next()
                            for k in range(8):
                                S.op("pe", I("matmul", ps.t[:, 0:512], lhsT=gw.t[:, k, js], rhs=uTc.t[:, k, :], start=(k == 0), stop=(k == 7)), reads=[gw, uTc], writes=[ps])
                            sgt = sg.next()
                            S.op("act", I("activation", out=sgt.t[:], in_=ps.t[:, 0:512], func=AF.Sigmoid), reads=[ps], writes=[sgt])
                            sgs.append(sgt)
                        psa = self.psS.next()
                        for k in range(4):
                            S.op("pe", I("matmul", psa.t[:, 0:512], lhsT=wa.t[:, k, js], rhs=self.yaT.t[:, k, cs], start=(k == 0), stop=(k == 3)), reads=[wa, self.yaT], writes=[psa])
                        psb = self.psO.next()
                        for k in range(4):
                            S.op("pe", I("matmul", psb.t[:, 0:512], lhsT=wb.t[:, k, js], rhs=self.ybT.t[:, k, cs], start=(k == 0), stop=(k == 3)), reads=[wb, self.ybT], writes=[psb])
                        t0 = tf.next()
                        t1 = tf.next()
                        S.op("dve", I("tensor_tensor", out=t0.t[:], in0=psa.t[:, 0:512], in1=sgs[0].t[:], op=ALU.mult), reads=[psa, sgs[0]], writes=[t0])
                        S.op("dve", I("tensor_tensor", out=t1.t[:], in0=psb.t[:, 0:512], in1=sgs[1].t[:], op=ALU.mult), reads=[psb, sgs[1]], writes=[t1])
                        S.op("dve", I("tensor_tensor", out=mTc.t[:, ct, :], in0=t0.t[:], in1=t1.t[:], op=ALU.add), reads=[t0, t1], writes=[mTc])
                for nh in range(2):
                    wo = slab_from(self.w_out[:, nh * 512:(nh + 1) * 512], 8, None)
                    for tt in range(4):
                        ps = self.psA.next()
                        for k in range(8):
                            S.op("pe", I("matmul", ps.t[:, 0:512], lhsT=mTc.t[:, k, tt * 128:(tt + 1) * 128], rhs=wo.t[:, k, :], start=(k == 0), stop=(k == 7)), reads=[wo, mTc], writes=[ps])
                        S.op("dve", I("tensor_tensor", out=xc.t[:, tt, nh * 512:(nh + 1) * 512], in0=ps.t[:, 0:512], in1=xc.t[:, tt, nh * 512:(nh + 1) * 512], op=ALU.add), reads=[ps, xc], writes=[xc])
                for tt in range(4):
                    self.norm_sb(xc, xc.t[:, tt, :], u2Tc, u2Tc.t[:, :, tt * 128:(tt + 1) * 128], keep_rstd=self.gmlp)
                for s_ in range(8):
                    wu = slab_from(self.w_up[:, s_ * 512:(s_ + 1) * 512], 8, self.gmlp)
                    for j in range(4):
                        ft = 4 * s_ + j
                        ps = self.psA.next()
                        for k in range(8):
                            S.op("pe", I("matmul", ps.t[:, 0:512], lhsT=wu.t[:, k, j * 128:(j + 1) * 128], rhs=u2Tc.t[:, k, :], start=(k == 0), stop=(k == 7)), reads=[wu, u2Tc], writes=[ps])
                        r = tf.next()
                        S.op("act", I("activation", out=r.t[:], in_=ps.t[:, 0:512], func=AF.Relu), reads=[ps], writes=[r])
                        S.op("dve", I("tensor_tensor", out=hT.t[:, ft, :], in0=r.t[:], in1=r.t[:], op=ALU.mult), reads=[r], writes=[hT])
                accs = [self.psA.items[0], self.psA.items[1], self.psS.items[0], self.psS.items[1]]
                for nh in range(2):
                    for kg in range(4):
                        wd = slab_from(self.w_down[kg * 1024:(kg + 1) * 1024, nh * 512:(nh + 1) * 512], 8, None)
                        for tt in range(4):
                            for k in range(8):
                                S.op("pe", I("matmul", accs[tt].t[:, 0:512], lhsT=hT.t[:, kg * 8 + k, tt * 128:(tt + 1) * 128], rhs=wd.t[:, k, :], start=(kg == 0 and k == 0), stop=(kg == 3 and k == 7)), reads=[wd, hT], writes=[accs[tt]])
                    for tt in range(4):
                        S.op("dve", I("tensor_tensor", out=xc.t[:, tt, nh * 512:(nh + 1) * 512], in0=accs[tt].t[:, 0:512], in1=xc.t[:, tt, nh * 512:(nh + 1) * 512], op=ALU.add), reads=[accs[tt], xc], writes=[xc])
                for tt in range(4):
                    jk = self.junk.next()
                    st = self.stat.next()
                    S.op("act", I("activation", out=jk.t[:], in_=xc.t[:, tt, :], func=AF.Square, accum_out=st.t[:, 0:1]), reads=[xc], writes=[jk, st])
                    S.op("act", I("activation", out=st.t[:, 1:2], in_=st.t[:, 0:1], func=AF.Sqrt, scale=1.0 / D, bias=self.epsc.t[:, 0:1]), reads=[st, self.epsc], writes=[st])
                    S.op("dve", I("reciprocal", out=st.t[:, 2:3], in_=st.t[:, 1:2]), reads=[st], writes=[st])
                    S.op("dve", I("scalar_tensor_tensor", out=xc.t[:, tt, :], in0=xc.t[:, tt, :], scalar=st.t[:, 2:3], in1=gfin.t[:], op0=ALU.mult, op1=ALU.mult), reads=[xc, st, gfin], writes=[xc])
                    r0 = c * 512 + tt * 128
                    S.dma(I("dma_start", out=self.out[r0:r0 + 128, :], in_=xc.t[:, tt, :]), reads=[xc])

def make_in_maps(inputs):
    x = np.ascontiguousarray(np.asarray(inputs["x"], np.float32))
    cbf, cf32, t16 = _static_tables()
    sq = lambda n: np.ascontiguousarray(np.asarray(inputs[n], np.float32)[0])
    gl = lambda v: np.ascontiguousarray(np.asarray(v, np.float32).reshape(8, 128).T)
    common = {
        "w_in": sq("w_in"), "g_mix": gl(inputs["norm_mix_g"][0]), "g_mlp": gl(inputs["norm_mlp_g"][0]),
        "g_fin": np.ascontiguousarray(np.broadcast_to(np.asarray(inputs["norm_final_g"], np.float32)[None, :], (128, D))),
        "cmp_w1_k": sq("cmp_w1_k"), "cmp_w1_v": sq("cmp_w1_v"), "cmp_w2_k": sq("cmp_w2_k"), "cmp_w2_v": sq("cmp_w2_v"),
        "cmp_pos_k": sq("cmp_pos_k"), "cmp_pos_v": sq("cmp_pos_v"),
        "w_a": sq("w_branch_a"), "w_b": sq("w_branch_b"), "w_out": sq("w_out"), "w_up": sq("w_up"), "w_down": sq("w_down"),
        "c_bf": cbf, "c_f32": cf32, "c_t16": t16,
    }
    tabs = [_percore_tables(q) for q in range(4)]
    maps = []
    for c in range(8):
        b, q = c // 4, c % 4
        T0 = OWN * q
        halo = x[b, T0 - OWN:T0] if q > 0 else np.zeros((OWN, D), np.float32)
        m = dict(common)
        m.update({"x_own": np.ascontiguousarray(x[b, T0:T0 + OWN]), "x_halo": np.ascontiguousarray(halo), "x_full": x[b],
                  "pc_f": tabs[q][0], "pc_lohi": tabs[q][1], "pc_bf": tabs[q][2]})
        maps.append(m)
    return maps


_CACHE = {}


def kernel(**inputs):
    if "nc" not in _CACHE:
        b = Builder()
        _CACHE["nc"] = b.build()
        _CACHE["decl"] = set(b._decl.keys())
    nc = _CACHE["nc"]
    maps = make_in_maps(inputs)
    decl = _CACHE["decl"]
    maps = [{k: v for k, v in m.items() if k in decl} for m in maps]
    res = run_bass_kernel_spmd(nc, maps, core_ids=list(range(8)))
    out = np.zeros((2, S_LEN, D), np.float32)
    for c in range(8):
        b, q = c // 4, c % 4
        out[b, OWN * q:OWN * (q + 1)] = res.results[c]["out"]
    return out
```

```python
import os
import numpy as np
import ml_dtypes
from contextlib import ExitStack
import concourse.bass as bass
import concourse.mybir as mybir
from concourse.bass_utils import run_bass_kernel_spmd

F32 = mybir.dt.float32
BF16 = mybir.dt.bfloat16
ALU = mybir.AluOpType
AF = mybir.ActivationFunctionType
NPBF = ml_dtypes.bfloat16

D = 1024
S_LEN = 8192
OWN = 2048
NT = 16
EPS = 1e-6
SCALE = 0.125
IN_COLS = 7960
C_QA, C_KA, C_VA = 0, 1536, 3072
C_QB = 4608
C_KVB = 5120
C_GB = 5888
C_GM = 5912
DILS = (1, 4, 16)

ENGS = ("pe", "act", "dve", "pool")
NDMA = 24


class Res:
    __slots__ = ("lw", "rd", "excl")

    def __init__(self):
        self.lw = None
        self.rd = {}
        self.excl = False


class Tn:
    __slots__ = ("t", "r")

    def __init__(self, t):
        self.t = t
        self.r = Res()


class Sched:
    def __init__(self, nc):
        self.nc = nc
        self.q = {e: [] for e in ENGS + ("sp",)}
        self.cnt = {e: 0 for e in ENGS}
        self.dcnt = [0] * NDMA
        self.seen = {e: {} for e in ENGS + ("sp",)}
        self.dnext = 0
        self.pgen = {}
        self.plast = {}

    def dma_sw(self, fns, writes, slot):
        deps = self._deps([], writes)
        self.pgen[slot] = self.pgen.get(slot, 0)
        key = ("p", slot)
        base = self.pgen[slot]
        waits = self._waits("pool", deps)
        for i, fn in enumerate(fns):
            self.q["pool"].append((waits if i == 0 else [], fn, key, base + 16 * (i + 1)))
        self.pgen[slot] = base + 16 * len(fns)
        self.plast[slot] = (key, self.pgen[slot])
        self._mark(key, self.pgen[slot], [], writes)

    def _deps(self, reads, writes, mykey=None):
        deps = {}
        for r in reads:
            r = r.r if isinstance(r, Tn) else r
            if r.lw is not None and r.lw[1] > deps.get(r.lw[0], 0):
                deps[r.lw[0]] = r.lw[1]
            if r.excl:
                for k, v in r.rd.items():
                    if k != mykey and v > deps.get(k, 0):
                        deps[k] = v
        for w in writes:
            w = w.r if isinstance(w, Tn) else w
            if w.lw is not None and w.lw[0] != mykey and w.lw[1] > deps.get(w.lw[0], 0):
                deps[w.lw[0]] = w.lw[1]
            for k, v in w.rd.items():
                if v > deps.get(k, 0):
                    deps[k] = v
        return deps

    def _waits(self, eng, deps):
        waits = []
        seen = self.seen[eng]
        for k, v in deps.items():
            if v > seen.get(k, 0):
                waits.append((k, v))
                seen[k] = v
        return waits

    def _mark(self, key, my, reads, writes):
        for r in reads:
            r = r.r if isinstance(r, Tn) else r
            if my > r.rd.get(key, 0):
                r.rd[key] = my
        for w in writes:
            w = w.r if isinstance(w, Tn) else w
            w.lw = (key, my)
            w.rd = {}

    def op(self, eng, fn, reads=(), writes=()):
        deps = self._deps(reads, writes, ("e", eng))
        if eng == "pe":
            deps.pop(("e", "pe"), None)
        self.cnt[eng] += 1
        my = self.cnt[eng]
        key = ("e", eng)
        self.q[eng].append((self._waits(eng, deps), fn, key, my))
        self._mark(key, my, reads, writes)

    def dma(self, fn, reads=(), writes=(), queue="sp"):
        deps = self._deps(reads, writes)
        k = self.dnext
        self.dnext = (self.dnext + 1) % NDMA
        key = ("d", k)
        if self.dcnt[k] > 0:
            deps[key] = max(deps.get(key, 0), self.dcnt[k])
        self.dcnt[k] += 16
        my = self.dcnt[k]
        self.q[queue].append((self._waits(queue, deps), fn, key, my))
        self._mark(key, my, reads, writes)

    def barrier(self):
        allc = {}
        for e in ENGS:
            if self.cnt[e]:
                allc[("e", e)] = self.cnt[e]
        for k in range(NDMA):
            if self.dcnt[k]:
                allc[("d", k)] = self.dcnt[k]
        for slot, (key, v) in self.plast.items():
            allc[key] = v
        for e in ENGS + ("sp",):
            w = self._waits(e, dict(allc))
            if w:
                self.q[e].append((w, None, None, 0))

    def emit(self):
        nc = self.nc
        with ExitStack() as es:
            esem = {e: es.enter_context(nc.semaphore("s_" + e)) for e in ENGS}
            dsem = [es.enter_context(nc.semaphore("s_d%d" % i)) for i in range(NDMA)]

            psem = {slot: es.enter_context(nc.semaphore("s_p%d" % i)) for i, slot in enumerate(sorted(self.pgen))}

            def semof(key):
                if key[0] == "p":
                    return psem[key[1]]
                return esem[key[1]] if key[0] == "e" else dsem[key[1]]
            fin = {}
            for e in ENGS:
                if self.cnt[e]:
                    fin[("e", e)] = self.cnt[e]
            for k in range(NDMA):
                if self.dcnt[k]:
                    fin[("d", k)] = self.dcnt[k]
            for slot, (key, v) in self.plast.items():
                fin[key] = v
            allsems = list(esem.values()) + dsem + list(psem.values())
            with nc.Block() as b0:
                @b0.sync
                def _(e):
                    for sm in allsems:
                        e.sem_clear(sm)
            block = es.enter_context(nc.Block())

            sig = {e: set() for e in ENGS}
            for name in self.q:
                for waits, fn, key, my in self.q[name]:
                    for (k, v) in waits:
                        if k[0] == "e":
                            sig[k[1]].add(v)
            for e in ENGS:
                if self.cnt[e]:
                    sig[e].add(self.cnt[e])
            rank = {}
            for e in ENGS:
                for i, v in enumerate(sorted(sig[e])):
                    rank[(e, v)] = i + 1

            def wval(k, v):
                return rank[(k[1], v)] if k[0] == "e" else v

            def run(name, engobj, final=False):
                for waits, fn, key, my in self.q[name]:
                    for (k, v) in waits:
                        engobj.wait_ge(semof(k), wval(k, v))
                    if isinstance(fn, tuple):
                        engobj.sem_clear(psem[fn[1]])
                    elif fn is not None:
                        ins = fn(engobj)
                        if key[0] in ("d", "p"):
                            ins.then_inc(semof(key), 16)
                        elif my in sig[key[1]]:
                            ins.then_inc(semof(key), 1)
                if final:
                    for k, v in fin.items():
                        engobj.wait_ge(semof(k), wval(k, v))

            @block.sync
            def _(e):
                run("sp", e, final=True)

            @block.tensor
            def _(e):
                run("pe", e)

            @block.scalar
            def _(e):
                run("act", e)

            @block.vector
            def _(e):
                run("dve", e)

            @block.gpsimd
            def _(e):
                run("pool", e)


def I(name, *a, **k):
    return lambda e: getattr(e, name)(*a, **k)


class Ring:
    def __init__(self, items):
        self.items = items
        self.i = 0

    def next(self):
        it = self.items[self.i % len(self.items)]
        self.i += 1
        return it


NROPE = 148
BFC = dict(ident=0, tri_diag=128, tri_prev=256, win_far=384, m4=512, e32=2560, ones=4608)
NBFC = 4736
F32C = dict(swap=0, id32=128)
NF32C = 160
PCF = dict(rope=0, thrc=NROPE * 16, pv=NROPE * 16 + 16, crel=NROPE * 16 + 80, hv=NROPE * 16 + 84)
NPCF = NROPE * 16 + 85
PCB = dict(eown=0, hv64=2048)
NPCB = 2112


def _static_tables():
    bf = np.zeros((128, NBFC), np.float32)
    k = np.arange(128)[:, None]
    q = np.arange(128)[None, :]
    bf[:, 0:128] = np.eye(128)
    bf[:, 128:256] = (q >= k)
    bf[:, 256:384] = (q <= k)
    bf[:, 384:512] = (q < k)
    for m in range(4):
        blk = np.zeros((128, 512), np.float32)
        for tq in range(4):
            if tq == m:
                blk[:, tq * 128:(tq + 1) * 128] = (q >= k)
            elif tq > m:
                blk[:, tq * 128:(tq + 1) * 128] = 1.0
        bf[:, 512 + m * 512: 512 + (m + 1) * 512] = blk
    b = np.arange(128)[:, None]
    for kt in range(16):
        i = np.arange(128)[None, :]
        bf[:, 2560 + kt * 128: 2560 + (kt + 1) * 128] = ((b % 32) == 2 * kt + (i >= 64))
    bf[:, 4608:4736] = 1.0
    f = np.zeros((128, NF32C), np.float32)
    f[:, 0:128] = (np.abs(k - q) == 64)
    f[0:32, 128:160] = np.eye(32)
    t16 = np.ascontiguousarray(np.broadcast_to(16.0 * np.arange(2048, dtype=np.float32)[None, :], (128, 2048)))
    return bf.astype(NPBF), f, t16


def _rope_rows(pos):
    inv = (500000.0 ** (-np.arange(0, 16, 2, dtype=np.float32) / np.float32(16))).astype(np.float32)
    ang = (pos.astype(np.float32)[:, None] * inv[None, :]).astype(np.float32)
    return np.concatenate([np.cos(ang), np.sin(ang)], axis=1).astype(np.float32)


def _percore_tables(qtr):
    T0 = OWN * qtr
    i = np.arange(128)
    f = np.zeros((128, NPCF), np.float32)
    rope = np.zeros((128, NROPE, 16), np.float32)
    for t in range(32):
        rope[:, t] = _rope_rows(T0 - OWN + 128 * t + i)
    for r in range(4):
        for j in range(-1, 4):
            rope[:, 32 + r * 5 + j + 1] = _rope_rows(T0 - OWN + 2048 + 512 * j + r + 4 * i)
    for r in range(16):
        for j in range(-1, 1):
            rope[:, 52 + r * 2 + j + 1] = _rope_rows(T0 - OWN + 2048 + 2048 * j + r + 16 * i)
    for kt in range(64):
        rope[:, 84 + kt] = _rope_rows(128 * kt + i)
    f[:, 0:NROPE * 16] = rope.reshape(128, -1)
    for ti in range(16):
        f[:, PCF["thrc"] + ti] = T0 + 128 * ti + i - 31
    for kt in range(64):
        f[:, PCF["pv"] + kt] = 1.0 if 128 * kt < T0 else 0.0
    for bt in range(4):
        f[:, PCF["crel"] + bt] = 16.0 * (16 * (128 * bt + i) + 31 - T0)
    f[:, PCF["hv"]] = 0.0 if qtr == 0 else 1.0
    lo = np.full((128, 16, 128), -3e4, np.float32)
    hi = np.full((128, 16, 128), 3e4, np.float32)
    m = np.arange(128)[None, :]
    for ti in range(16):
        cur = ((T0 + 128 * ti + i) // 64)[:, None]
        forced = (m == 0) | (m == cur) | (m == cur - 1)
        fut = m > cur
        lo[:, ti][forced] = 1e4
        hi[:, ti][forced] = 1e4
        lo[:, ti][fut] = -3e4
        hi[:, ti][fut] = -3e4
    lohi = np.concatenate([lo.reshape(128, -1), hi.reshape(128, -1)], axis=1)
    bfp = np.zeros((128, NPCB), np.float32)
    b = np.arange(128)[:, None]
    for j in range(16):
        ii = np.arange(128)[None, :]
        bfp[:, j * 128:(j + 1) * 128] = (b == 2 * (T0 // 128 + j) + (ii >= 64))
    bfp[:, 2048:2112] = 0.0 if qtr == 0 else 1.0
    return f, lohi.astype(np.float32), bfp.astype(NPBF)


class StopBuild(Exception):
    pass


class Builder:
    def __init__(self, debug=False, stop_after=None):
        self.debug = debug
        self.stop_after = stop_after
        self.nc = nc = bass.Bass("TRN2", target_bir_lowering=False)
        self.S = Sched(nc)
        self._decl = {}
        self._shapes = {
            "x_own": ([OWN, D], F32), "x_halo": ([OWN, D], F32), "x_full": ([S_LEN, D], F32), "w_in": ([D, IN_COLS], F32),
            "g_mix": ([128, 8], F32), "g_mlp": ([128, 8], F32), "g_fin": ([128, D], F32),
            "cmp_w1_k": ([2048, 256], F32), "cmp_w1_v": ([2048, 256], F32), "cmp_w2_k": ([256, 64], F32), "cmp_w2_v": ([256, 64], F32),
            "cmp_pos_k": ([32, 64], F32), "cmp_pos_v": ([32, 64], F32), "w_a": ([512, D], F32), "w_b": ([512, D], F32),
            "w_out": ([D, D], F32), "w_up": ([D, 4096], F32), "w_down": ([4096, D], F32),
            "c_bf": ([128, NBFC], BF16), "c_f32": ([128, NF32C], F32), "c_t16": ([128, 2048], F32),
            "pc_f": ([128, NPCF], F32), "pc_lohi": ([128, 4096], F32), "pc_bf": ([128, NPCB], BF16),
        }
        self.out = nc.dram_tensor("out", [OWN, D], F32, kind="ExternalOutput").ap()
        self.dbg = {}

    def __getattr__(self, name):
        sh = self.__dict__.get("_shapes", {})
        if name in sh:
            if name not in self._decl:
                self._decl[name] = self.nc.dram_tensor(name, list(sh[name][0]), sh[name][1], kind="ExternalInput").ap()
            return self._decl[name]
        raise AttributeError(name)

    def sb(self, es, name, shape, dt):
        return Tn(es.enter_context(self.nc.sbuf_tensor(name, list(shape), dt)))

    def ps(self, es, name, shape, dt):
        t = Tn(es.enter_context(self.nc.psum_tensor(name, list(shape), dt)))
        t.r.excl = True
        return t

    def ring(self, es, name, shape, dt, n):
        return Ring([self.sb(es, "%s%d" % (name, i), shape, dt) for i in range(n)])

    def dump(self, name, tn, ap, shape, dt):
        if not self.debug:
            return
        o = self.nc.dram_tensor("dbg_" + name, list(shape), dt, kind="ExternalOutput").ap()
        self.dbg[name] = True
        self.S.dma(I("dma_start", out=o, in_=ap), reads=[tn])

    def load_wslab(self, src_ap, ncols, gain, kchunks=8):
        S = self.S
        st = self.wst.next()
        sl = self.wsl.next()
        S.dma(I("dma_start", out=st.t[:, 0:kchunks, 0:ncols], in_=src_ap.rearrange("(c p) n -> p c n", p=128)), writes=[st])
        if gain is not None:
            gb = gain.t[:, 0:kchunks].unsqueeze(2).to_broadcast([128, kchunks, ncols])
            S.op("pool", I("tensor_tensor", out=sl.t[:, 0:kchunks, 0:ncols], in0=st.t[:, 0:kchunks, 0:ncols], in1=gb, op=ALU.mult),
                 reads=[st, gain], writes=[sl])
        else:
            S.op("pool", I("tensor_copy", out=sl.t[:, 0:kchunks, 0:ncols], in_=st.t[:, 0:kchunks, 0:ncols]), reads=[st], writes=[sl])
        return sl

    def cast_into(self, dst_tn, dst_ap_fn, src_ap, kchunks, ncols, gain, piece=512):
        S = self.S
        for c0 in range(0, ncols, piece):
            n = min(piece, ncols - c0)
            st = self.wst.next()
            S.dma(I("dma_start", out=st.t[:, 0:kchunks, 0:n], in_=src_ap[:, c0:c0 + n].rearrange("(c p) n -> p c n", p=128)), writes=[st])
            dst = dst_ap_fn(c0, n)
            engs = getattr(self, "cast_engs", ("pool",))
            self._ci = getattr(self, "_ci", 0) + 1
            ce = engs[self._ci % len(engs)]
            if gain is not None:
                gb = gain.t[:, 0:kchunks].unsqueeze(2).to_broadcast([128, kchunks, n])
                S.op(ce, I("tensor_tensor", out=dst, in0=st.t[:, 0:kchunks, 0:n], in1=gb, op=ALU.mult),
                     reads=[st, gain], writes=[dst_tn])
            else:
                S.op(ce, I("tensor_copy", out=dst, in_=st.t[:, 0:kchunks, 0:n]), reads=[st], writes=[dst_tn])

    def norm_tile(self, x_ap, ut_tn, ut_ap):
        S = self.S
        xt = self.xring.next()
        S.dma(I("dma_start", out=xt.t[:], in_=x_ap), writes=[xt])
        self.norm_sb(xt, xt.t[:], ut_tn, ut_ap)

    def norm_sb(self, xt, x_sb_ap, ut_tn, ut_ap, keep_rstd=None):
        S = self.S
        jk = self.junk.next()
        st = self.stat.next()
        S.op("act", I("activation", out=jk.t[:], in_=x_sb_ap, func=AF.Square, accum_out=st.t[:, 0:1]), reads=[xt], writes=[jk, st])
        S.op("act", I("activation", out=st.t[:, 1:2], in_=st.t[:, 0:1], func=AF.Sqrt, scale=1.0 / D, bias=self.epsc.t[:, 0:1]), reads=[st, self.epsc], writes=[st])
        S.op("dve", I("reciprocal", out=st.t[:, 2:3], in_=st.t[:, 1:2]), reads=[st], writes=[st])
        xn = self.xnring.next()
        S.op("dve", I("tensor_scalar", out=xn.t[:], in0=x_sb_ap, scalar1=st.t[:, 2:3], scalar2=None, op0=ALU.mult), reads=[xt, st], writes=[xn])
        pt = self.psT
        for c in range(8):
            S.op("pe", I("transpose", out=pt.t[:, c * 128:(c + 1) * 128], in_=xn.t[:, c * 128:(c + 1) * 128], identity=self.ident), reads=[xn, self.cbf], writes=[pt])
        if keep_rstd is None:
            S.op("act", I("copy", out=ut_ap, in_=pt.t[:, 0:1024].rearrange("p (c t) -> p c t", c=8)), reads=[pt], writes=[ut_tn])
        else:
            gb = keep_rstd.t[:, 0:8].unsqueeze(2).to_broadcast([128, 8, 128])
            S.op("dve", I("tensor_tensor", out=ut_ap, in0=pt.t[:, 0:1024].rearrange("p (c t) -> p c t", c=8), in1=gb, op=ALU.mult), reads=[pt, keep_rstd], writes=[ut_tn])
        return st

    def proj_tm(self, lhs_fn, lhs_tn, slab, c0, ncols, ps):
        for c in range(8):
            self.S.op("pe", I("matmul", ps.t[:, 0:ncols], lhsT=lhs_fn(c), rhs=slab.t[:, c, c0:c0 + ncols], start=(c == 0), stop=(c == 7)),
                      reads=[lhs_tn, slab], writes=[ps])

    def rope_evac(self, ps, pc0, nh, ropeidx, dst_tn, dst_ap, perm=False):
        S = self.S
        ro = PCF["rope"] + ropeidx * 16
        ta = self.rtmp.next()
        if not perm:
            psv = ps.t[:, pc0:pc0 + 64 * nh].rearrange("p (h d) -> p h d", h=nh)
            dv = dst_ap.rearrange("p (h d) -> p h d", h=nh)
            tav = ta.t[:, 0:nh * 32].rearrange("p (h d) -> p h d", h=nh)
            cos1 = self.pcf.t[:, ro:ro + 8].unsqueeze(1).to_broadcast([128, nh, 8])
            sin1 = self.pcf.t[:, ro + 8:ro + 16].unsqueeze(1).to_broadcast([128, nh, 8])
            sl = lambda v, a, b: v[:, :, a:b]
        else:
            psv = ps.t[:, pc0:pc0 + 512].rearrange("p (two hp d) -> p two hp d", two=2, hp=4)
            dv = dst_ap.rearrange("p (hp two d) -> p two hp d", two=2, hp=4)
            tav = ta.t[:, 0:256].rearrange("p (two hp d) -> p two hp d", two=2, hp=4)
            cos1 = self.pcf.t[:, ro:ro + 8].unsqueeze(1).unsqueeze(1).to_broadcast([128, 2, 4, 8])
            sin1 = self.pcf.t[:, ro + 8:ro + 16].unsqueeze(1).unsqueeze(1).to_broadcast([128, 2, 4, 8])
            sl = lambda v, a, b: v[:, :, :, a:b]
        S.op("dve", I("tensor_copy", out=sl(dv, 16, 64), in_=sl(psv, 16, 64)), reads=[ps], writes=[dst_tn])
        S.op("dve", I("tensor_tensor", out=sl(tav, 0, 8), in0=sl(psv, 0, 8), in1=cos1, op=ALU.mult), reads=[ps, self.pcf], writes=[ta])
        S.op("dve", I("tensor_tensor", out=sl(tav, 8, 16), in0=sl(psv, 8, 16), in1=cos1, op=ALU.mult), reads=[ps, self.pcf], writes=[ta])
        S.op("dve", I("tensor_tensor", out=sl(tav, 16, 24), in0=sl(psv, 8, 16), in1=sin1, op=ALU.mult), reads=[ps, self.pcf], writes=[ta])
        S.op("dve", I("tensor_tensor", out=sl(tav, 24, 32), in0=sl(psv, 0, 8), in1=sin1, op=ALU.mult), reads=[ps, self.pcf], writes=[ta])
        S.op("dve", I("tensor_tensor", out=sl(dv, 0, 8), in0=sl(tav, 0, 8), in1=sl(tav, 16, 24), op=ALU.subtract), reads=[ta], writes=[dst_tn])
        S.op("dve", I("tensor_tensor", out=sl(dv, 8, 16), in0=sl(tav, 8, 16), in1=sl(tav, 24, 32), op=ALU.add), reads=[ta], writes=[dst_tn])

    def build(self):
        nc, S = self.nc, self.S
        with ExitStack() as es0:
            self.cbf = self.sb(es0, "cbf", [128, NBFC], BF16)
            self.cf32 = self.sb(es0, "cf32", [128, NF32C], F32)
            self.pcf = self.sb(es0, "pcf", [128, NPCF], F32)
            self.pcb = self.sb(es0, "pcb", [128, NPCB], BF16)
            self.gmix = self.sb(es0, "gmix", [128, 8], F32)
            self.gmlp = self.sb(es0, "gmlp", [128, 8], F32)
            self.epsc = self.sb(es0, "epsc", [128, 1], F32)
            S.dma(I("dma_start", out=self.cbf.t[:], in_=self.c_bf), writes=[self.cbf])
            S.dma(I("dma_start", out=self.cf32.t[:], in_=self.c_f32), writes=[self.cf32])
            S.dma(I("dma_start", out=self.pcf.t[:], in_=self.pc_f), writes=[self.pcf])
            S.dma(I("dma_start", out=self.pcb.t[:], in_=self.pc_bf), writes=[self.pcb])
            S.dma(I("dma_start", out=self.gmix.t[:], in_=self.g_mix), writes=[self.gmix])
            S.dma(I("dma_start", out=self.gmlp.t[:], in_=self.g_mlp), writes=[self.gmlp])
            S.op("dve", I("memset", self.epsc.t[:], EPS), writes=[self.epsc])
            self.ident = self.cbf.t[:, 0:128]
            self.xring = self.ring(es0, "xr", [128, D], F32, 2)
            self.junk = self.ring(es0, "jk", [128, D], BF16, 1)
            self.stat = self.ring(es0, "st", [128, 4], F32, 4)
            self.xnring = self.ring(es0, "xn", [128, D], BF16, 2)
            self.rtmp = self.ring(es0, "rtmp", [128, 256], F32, 2)
            self.ptr = self.ring(es0, "ptr", [128, 512], BF16, 3)
            self.psA = Ring([self.ps(es0, "psA%d" % i, [128, 512], F32) for i in range(2)])
            self.psT = self.ps(es0, "psT", [128, 1024], BF16)
            self.psS = Ring([self.ps(es0, "psS%d" % i, [128, 512], F32) for i in range(2)])
            self.psO = Ring([self.ps(es0, "psO%d" % i, [128, 512], F32) for i in range(2)])
            self.psX = self.ps(es0, "psX", [128, 512], F32)
            self.ring3 = Ring([self.psS.items[0], self.psS.items[1], self.psX])
            self.yaT = self.sb(es0, "yaT", [128, 4, OWN], BF16)
            self.stopped = False
            self.phase_A(es0)
            if self.stopped:
                S.barrier()
                if self.stop_after in ("A3", "A"):
                    self.dump("yaT", self.yaT, self.yaT.t[:], [128, 4, OWN], BF16)
                self.fake_out()
                S.emit()
                return nc
            self.ybT = self.sb(es0, "ybT", [128, 4, OWN], BF16)
            S.barrier()
            if self.stop_after == "A":
                self.dump("yaT", self.yaT, self.yaT.t[:], [128, 4, OWN], BF16)
                self.fake_out()
                S.emit()
                return nc
            self.phase_B(es0)
            S.barrier()
            if self.stopped:
                self.fake_out()
                S.emit()
                return nc
            if self.stop_after == "B":
                self.dump("yaT", self.yaT, self.yaT.t[:], [128, 4, OWN], BF16)
                self.dump("ybT", self.ybT, self.ybT.t[:], [128, 4, OWN], BF16)
                self.fake_out()
                S.emit()
                return nc
            self.phase_C(es0)
            if self.debug:
                self.dump("yaT", self.yaT, self.yaT.t[:], [128, 4, OWN], BF16)
                self.dump("ybT", self.ybT, self.ybT.t[:], [128, 4, OWN], BF16)
            S.emit()
        return nc

    def fake_out(self):
        S = self.S
        xt = self.xring.next()
        for t in range(NT):
            S.dma(I("dma_start", out=xt.t[:], in_=self.x_own[t * 128:(t + 1) * 128, :]), writes=[xt])
            S.dma(I("dma_start", out=self.out[t * 128:(t + 1) * 128, :], in_=xt.t[:]), reads=[xt])

    def attn_unit(self, score_mms, n, mask_fn, pv_list):
        S = self.S
        if getattr(self, "_collect", None) is not None:
            self._collect.append((score_mms, n, mask_fn, pv_list, None))
            return
        pss = self.psS.next()
        for i, (l, r, rd) in enumerate(score_mms):
            S.op("pe", I("matmul", pss.t[:, 0:n], lhsT=l, rhs=r, start=(i == 0), stop=(i == len(score_mms) - 1)),
                 reads=rd, writes=[pss])
        pt = self.ptr.next()
        S.op("act", I("activation", out=pt.t[:, 0:n], in_=pss.t[:, 0:n], func=AF.Exp, scale=SCALE), reads=[pss], writes=[pt])
        if mask_fn is not None:
            mask_fn(pt)
        for (pso, out_ap, vaug, c0, ncol, st, sp, rd) in pv_list:
            S.op("pe", I("matmul", out_ap, lhsT=vaug, rhs=pt.t[:, c0:c0 + ncol], start=st, stop=sp),
                 reads=[pt] + rd, writes=[pso])

    def attn_seq(self, units, ring=None, depth=1):
        S = self.S
        ring = ring or self.psS
        units = list(units)
        pend = []
        for k in range(len(units) + depth):
            if k < len(units):
                u = units[k]
                score_mms, n = u[0], u[1]
                pss = ring.next()
                for i, (l, r, rd) in enumerate(score_mms):
                    S.op("pe", I("matmul", pss.t[:, 0:n], lhsT=l, rhs=r, start=(i == 0), stop=(i == len(score_mms) - 1)), reads=rd, writes=[pss])
                pend.append((u, pss))
            if k >= depth and pend:
                (pu, ppss) = pend.pop(0)
                n = pu[1]
                pt = self.ptr.next()
                S.op("act", I("activation", out=pt.t[:, 0:n], in_=ppss.t[:, 0:n], func=AF.Exp, scale=SCALE), reads=[ppss], writes=[pt])
                if pu[2] is not None:
                    pu[2](pt)
                for (pso, out_ap, vaug, c0, ncol, st, sp, rd) in pu[3]:
                    S.op("pe", I("matmul", out_ap, lhsT=vaug, rhs=pt.t[:, c0:c0 + ncol], start=st, stop=sp), reads=[pt] + rd, writes=[pso])
                if len(pu) > 4 and pu[4] is not None:
                    pu[4]()
        while pend:
            (pu, ppss) = pend.pop(0)
            n = pu[1]
            pt = self.ptr.next()
            S.op("act", I("activation", out=pt.t[:, 0:n], in_=ppss.t[:, 0:n], func=AF.Exp, scale=SCALE), reads=[ppss], writes=[pt])
            if pu[2] is not None:
                pu[2](pt)
            for (pso, out_ap, vaug, c0, ncol, st, sp, rd) in pu[3]:
                S.op("pe", I("matmul", out_ap, lhsT=vaug, rhs=pt.t[:, c0:c0 + ncol], start=st, stop=sp), reads=[pt] + rd, writes=[pso])
            if len(pu) > 4 and pu[4] is not None:
                pu[4]()

    def attn_steps(self, steps, ring):
        S = self.S
        prev = None
        for stp in list(steps) + [None]:
            cur = None
            if stp is not None:
                cur = []
                for u in stp:
                    score_mms, n = u[0], u[1]
                    pss = ring.next()
                    cur.append((u, pss))
                nmm = max(len(u[0]) for u in stp)
                for i in range(nmm):
                    for (u, pss) in cur:
                        if i < len(u[0]):
                            l, r, rd = u[0][i]
                            S.op("pe", I("matmul", pss.t[:, 0:u[1]], lhsT=l, rhs=r, start=(i == 0), stop=(i == len(u[0]) - 1)), reads=rd, writes=[pss])
            if prev is not None:
                for (pu, ppss) in prev:
                    n = pu[1]
                    pt = self.ptr.next()
                    S.op("act", I("activation", out=pt.t[:, 0:n], in_=ppss.t[:, 0:n], func=AF.Exp, scale=SCALE), reads=[ppss], writes=[pt])
                    if pu[2] is not None:
                        pu[2](pt)
                    for (pso, out_ap, vaug, c0, ncol, st, sp, rd) in pu[3]:
                        S.op("pe", I("matmul", out_ap, lhsT=vaug, rhs=pt.t[:, c0:c0 + ncol], start=st, stop=sp), reads=[pt] + rd, writes=[pso])
            prev = cur

    def mask_mul(self, pt, c0, n, mask_ap):
        self.S.op("dve", I("tensor_tensor", out=pt.t[:, c0:c0 + n], in0=pt.t[:, c0:c0 + n], in1=mask_ap, op=ALU.mult), reads=[pt, self.cbf], writes=[pt])

    def phase_A(self, es0):
        S = self.S
        with ExitStack() as esA:
            self.phase_A_body(esA)
        S.barrier()

    def phase_A_body(self, esA):
        S = self.S
        self.uTh = self.sb(esA, "uTh", [128, 8, OWN], BF16)
        self.uTo = self.sb(esA, "uTo", [128, 8, OWN], BF16)
        self.wst = self.ring(esA, "wstA", [128, 8, 384], F32, 2)
        self.wsl = self.ring(esA, "wslA", [128, 8, 384], BF16, 2)
        for t in range(NT):
            self.norm_tile(self.x_halo[t * 128:(t + 1) * 128, :], self.uTh, self.uTh.t[:, :, t * 128:(t + 1) * 128])
        for t in range(NT):
            self.norm_tile(self.x_own[t * 128:(t + 1) * 128, :], self.uTo, self.uTo.t[:, :, t * 128:(t + 1) * 128])
        if self.stop_after == "A0":
            self.stopped = True
            return
        with ExitStack() as es:
            self.phase_A_inner(es)

    def phase_A_inner(self, es):
        S = self.S
        if True:
            qT = self.sb(es, "a_qT", [128, OWN], BF16)
            kT = self.sb(es, "a_kT", [128, 32 * 128], BF16)
            vaug = self.sb(es, "a_v", [128, 32, 2, 128], BF16)
            qk = self.ring(es, "a_qk", [128, 256], BF16, 2)
            if os.environ.get("PADLOW"):
                pad = self.sb(es, "a_pad", [128, int(os.environ["PADLOW"]) * 256], F32)
            acc = [self.sb(es, "a_acc%d" % i, [128, OWN], F32) for i in range(2)]
            rd_ = self.ring(es, "a_rd", [128, 512], F32, 2)
            ones64 = self.cbf.t[:, BFC["ones"]:BFC["ones"] + 64]
            hv64 = self.pcb.t[:, PCB["hv64"]:PCB["hv64"] + 64]
            hvcol = self.pcf.t[:, PCF["hv"]:PCF["hv"] + 1]
            for p in range(4):
                for g, d in enumerate(DILS):
                    nt = NT // d
                    st = self.wst.next()
                    sl = self.wsl.next()
                    for i, cb in enumerate((C_QA, C_KA, C_VA)):
                        c0 = cb + g * 512 + p * 128
                        for kc in range(8):
                            S.dma(I("dma_start", out=st.t[:, kc, i * 128:(i + 1) * 128], in_=self.w_in[kc * 128:(kc + 1) * 128, c0:c0 + 128]), writes=[st])
                    gb = self.gmix.t[:, 0:8].unsqueeze(2).to_broadcast([128, 8, 384])
                    if int(os.environ.get("A1CUT", "99")) >= 0:
                        S.op("pool", I("tensor_tensor", out=sl.t[:, :, 0:384], in0=st.t[:, :, 0:384], in1=gb, op=ALU.mult), reads=[st, self.gmix], writes=[sl])
                    if int(os.environ.get("A1CUT", "99")) <= 0:
                        self.stopped = True
                        return
                    for r in range(d):
                        for j in range(-1, nt):
                            slot = r * (nt + 1) + j + 1
                            start = 2048 + 128 * d * j + r
                            if start < 2048:
                                ut, s0 = self.uTh, start
                            else:
                                ut, s0 = self.uTo, start - 2048
                            lhs = lambda c, ut=ut, s0=s0, d=d: ut.t[:, c, s0:s0 + 127 * d + 1:d]
                            ridx = (15 + slot) if g == 0 else ((32 + slot) if g == 1 else (52 + slot))
                            ps = self.psA.next()
                            halo = (j == -1)
                            if halo:
                                self.proj_tm(lhs, ut, sl, 128, 256, ps)
                                kc0, vc0 = 0, 128
                            else:
                                self.proj_tm(lhs, ut, sl, 0, 384, ps)
                                kc0, vc0 = 128, 256

                            CUT = int(os.environ.get("A1CUT", "99"))
                            if CUT <= 1:
                                continue
                            t = qk.next()
                            if not halo:
                                self.rope_evac(ps, 0, 4, ridx, t, t.t[:, 0:256])
                            else:
                                self.rope_evac(ps, kc0, 2, ridx, t, t.t[:, 128:256])
                            if CUT <= 2:
                                continue
                            vsrc = ps.t[:, vc0:vc0 + 128].rearrange("p (h d) -> p h d", h=2)
                            if halo:
                                S.op("dve", I("tensor_scalar", out=vaug.t[:, slot, :, 0:64], in0=vsrc, scalar1=hvcol, scalar2=None, op0=ALU.mult), reads=[ps, self.pcf], writes=[vaug])
                                for hh in range(2):
                                    S.op("pool", I("tensor_copy", out=vaug.t[:, slot, hh, 64:128], in_=hv64), reads=[self.pcb], writes=[vaug])
                            else:
                                S.op("act", I("copy", out=vaug.t[:, slot, :, 0:64], in_=vsrc), reads=[ps], writes=[vaug])
                                for hh in range(2):
                                    S.op("pool", I("tensor_copy", out=vaug.t[:, slot, hh, 64:128], in_=ones64), reads=[self.cbf], writes=[vaug])
                            if CUT <= 3:
                                continue
                            pt = self.psT
                            if not halo:
                                S.op("pe", I("transpose", out=pt.t[:, 0:128], in_=t.t[:, 0:128], identity=self.ident), reads=[t, self.cbf], writes=[pt])
                            S.op("pe", I("transpose", out=pt.t[:, 128:256], in_=t.t[:, 128:256], identity=self.ident), reads=[t, self.cbf], writes=[pt])
                            if not halo:
                                qi = r * nt + j
                                S.op("act", I("copy", out=qT.t[:, qi * 128:(qi + 1) * 128], in_=pt.t[:, 0:128]), reads=[pt], writes=[qT])
                            S.op("dve", I("tensor_copy", out=kT.t[:, slot * 128:(slot + 1) * 128], in_=pt.t[:, 128:256]), reads=[pt], writes=[kT])
                    if self.stop_after == "A1" and int(os.environ.get("A1CUT", "99")) < 99:
                        self.stopped = True
                        return
                    if self.stop_after == "A1":
                        self.dump("qT", qT, qT.t[:], [128, OWN], BF16)
                        self.dump("kT", kT, kT.t[:, 0:17 * 128], [128, 17 * 128], BF16)
                        self.dump("vaug", vaug, vaug.t[:, 0:17], [128, 17, 2, 128], BF16)
                        self.stopped = True
                        return
                    for hh in range(2):
                        pb = 64 * hh
                        banks = {}
                        ring3 = self.ring3
                        units = []
                        for r in range(d):
                            for j in range(-1, nt):
                                slot = r * (nt + 1) + j + 1
                                qlo = max(j, 0)
                                qhi = min(j + 1, nt - 1)
                                nq = qhi - qlo + 1
                                qc0 = (r * nt + qlo) * 128
                                n = nq * 128
                                if j == -1:
                                    mk = [(0, 128, self.cbf.t[:, BFC["tri_prev"]:BFC["tri_prev"] + 128])]
                                elif nq == 1:
                                    mk = [(0, 128, self.cbf.t[:, BFC["tri_diag"]:BFC["tri_diag"] + 128])]
                                else:
                                    mk = [(0, 256, self.cbf.t[:, BFC["tri_diag"]:BFC["tri_diag"] + 256])]

                                def mask_fn(pt, mk=mk):
                                    for (c0, nn, ap) in mk:
                                        self.mask_mul(pt, c0, nn, ap)
                                pv = []
                                for qt in range(qlo, qhi + 1):
                                    qi = r * nt + qt
                                    if (qt == j + 1) and (qi % 4 == 0):
                                        banks[qi // 4] = self.psO.next()
                                    pso = banks[qi // 4]
                                    col = (qi % 4) * 128
                                    pv.append((pso, pso.t[:, col:col + 128], vaug.t[:, slot, hh, :], (qt - qlo) * 128, 128, qt == j + 1, qt == j, [vaug]))
                                after = None
                                if j >= 0 and (r * nt + j) % 4 == 3:
                                    bk = (r * nt + j) // 4
                                    pso = banks[bk]
                                    av = acc[hh].t[:]
                                    if d == 1:
                                        dst = av[:, bk * 512:(bk + 1) * 512]
                                        src = pso.t[:, 0:512]
                                    elif d == 4:
                                        dst = av.rearrange("p (i r) -> p r i", r=4)[:, r, :]
                                        src = pso.t[:, 0:512]
                                    else:
                                        dst = av.rearrange("p (i r) -> p r i", r=16)[:, 4 * bk:4 * bk + 4, :]
                                        src = pso.t[:, 0:512].rearrange("p (r i) -> p r i", r=4)

                                    def after(dst=dst, src=src, pso=pso, hh=hh, g=g):
                                        if g == 0:
                                            S.op("act", I("copy", out=dst, in_=src), reads=[pso], writes=[acc[hh]])
                                        else:
                                            S.op("dve", I("tensor_tensor", out=dst, in0=dst, in1=src, op=ALU.add), reads=[pso, acc[hh]], writes=[acc[hh]])
                                units.append(([(kT.t[pb:pb + 64, slot * 128:(slot + 1) * 128], qT.t[pb:pb + 64, qc0:qc0 + n], [kT, qT])], n, mask_fn, pv, after))
                        self.attn_seq(units, ring=ring3, depth=2)
                if self.stop_after == "A2":
                    self.stopped = True
                    return
                for hh in range(2):
                    for c in range(4):
                        self.finalize(acc[hh], acc[hh].t[:, c * 512:(c + 1) * 512], hh, None, rd_, self.yaT, self.yaT.t[:, p, c * 512:(c + 1) * 512], first=True, last=True, ybacc=None)
                if self.stop_after == "A3":
                    self.stopped = True
                    return
        S.barrier()

    def finalize(self, src_tn, src_ap, hh, gate_row, rdring, dst_tn, dst_ap, first, last, ybacc):
        S = self.S
        psx = self.psX
        S.op("pe", I("matmul", psx.t[:, 0:512], lhsT=self.cf32.t[:, 0:128], rhs=src_ap, start=True, stop=True), reads=[src_tn, self.cf32], writes=[psx])
        rd = rdring.next()
        lo, hi = 64 * hh, 64 * hh + 64
        if hh == 0:
            den = psx.t[0:64, 0:512]
            num = src_ap[0:64, :]
        else:
            den = src_ap[64:128, :]
            num = psx.t[64:128, 0:512]
        S.op("dve", I("tensor_scalar", out=rd.t[lo:hi, :], in0=den, scalar1=1e-30, scalar2=None, op0=ALU.max), reads=[psx, src_tn], writes=[rd])
        S.op("dve", I("reciprocal", out=rd.t[lo:hi, :], in_=rd.t[lo:hi, :]), reads=[rd], writes=[rd])
        if gate_row is None:
            S.op("dve", I("tensor_tensor", out=dst_ap[lo:hi, :], in0=num, in1=rd.t[lo:hi, :], op=ALU.mult), reads=[psx, src_tn, rd], writes=[dst_tn])
            return
        S.op("dve", I("tensor_tensor", out=rd.t[lo:hi, :], in0=num, in1=rd.t[lo:hi, :], op=ALU.mult), reads=[psx, src_tn, rd], writes=[rd])
        gsel, gbT, gcols = gate_row
        S.op("pe", I("matmul", psx.t[:, 0:512], lhsT=gsel.t[:], rhs=gcols, start=True, stop=True), reads=[gbT, gsel, rd], writes=[psx])
        if first:
            S.op("dve", I("tensor_tensor", out=ybacc.t[lo:hi, :], in0=rd.t[lo:hi, :], in1=psx.t[lo:hi, 0:512], op=ALU.mult), reads=[psx, rd], writes=[ybacc])
        else:
            S.op("dve", I("tensor_tensor", out=rd.t[lo:hi, :], in0=rd.t[lo:hi, :], in1=psx.t[lo:hi, 0:512], op=ALU.mult), reads=[psx, rd], writes=[rd])
            if last:
                S.op("dve", I("tensor_tensor", out=dst_ap[lo:hi, :], in0=rd.t[lo:hi, :], in1=ybacc.t[lo:hi, :], op=ALU.add), reads=[rd, ybacc], writes=[dst_tn])
            else:
                S.op("dve", I("tensor_tensor", out=ybacc.t[lo:hi, :], in0=rd.t[lo:hi, :], in1=ybacc.t[lo:hi, :], op=ALU.add), reads=[rd, ybacc], writes=[ybacc])

    def phase_B(self, es0):
        S = self.S
        cbf, pcf, pcb = self.cbf, self.pcf, self.pcb
        ones64 = cbf.t[:, BFC["ones"]:BFC["ones"] + 64]
        ones2 = cbf.t[:, BFC["ones"]:BFC["ones"] + 128].rearrange("p (g d) -> p g d", g=2)
        with ExitStack() as esB:
            kslcT = self.sb(esB, "b_kslcT", [128, 48 * 128], BF16)
            vslc = self.sb(esB, "b_vslc", [128, 48, 2, 128], BF16)
            kcT = self.sb(esB, "b_kcT", [128, 512], BF16)
            vc = self.sb(esB, "b_vc", [128, 4, 2, 128], BF16)
            S.op("pool", I("memset", kcT.t[:], 0.0), writes=[kcT])
            S.op("pool", I("memset", vc.t[:], 0.0), writes=[vc])
            with ExitStack() as es:
                self.wst = self.ring(es, "wstB", [128, 8, 256], F32, 2)
                kcmpT = self.sb(es, "b_kcmpT", [128, S_LEN], BF16)
                vcmpT = self.sb(es, "b_vcmpT", [128, S_LEN], BF16)
                slab = self.sb(es, "b_slab", [128, 8, 512], BF16)
                uTt = self.ring(es, "b_uTt", [128, 8, 128], BF16, 2)
                tm = self.ring(es, "b_tm", [128, 512], BF16, 2)
                for dcol, scol in ((0, 0), (128, 256), (256, 128), (384, 384)):
                    self.cast_into(slab, lambda c0, n, dcol=dcol: slab.t[:, :, dcol + c0:dcol + c0 + n], self.w_in[:, C_KVB + scol:C_KVB + scol + 128], 8, 128, self.gmix, piece=128)
                for kt in range(64):
                    u = uTt.next()
                    self.norm_tile(self.x_full[kt * 128:(kt + 1) * 128, :], u, u.t[:])
                    ps = self.psA.next()
                    self.proj_tm(lambda c, u=u: u.t[:, c, :], u, slab, 0, 512, ps)
                    t = tm.next()
                    self.rope_evac(ps, 0, 4, 84 + kt, t, t.t[:, 0:256])
                    S.op("act", I("copy", out=t.t[:, 256:384], in_=ps.t[:, 256:384]), reads=[ps], writes=[t])
                    if kt < 48:
                        pvc = pcf.t[:, PCF["pv"] + kt:PCF["pv"] + kt + 1]
                        S.op("dve", I("tensor_scalar", out=vslc.t[:, kt, :, 0:64], in0=ps.t[:, 384:512].rearrange("p (g d) -> p g d", g=2), scalar1=pvc, scalar2=None, op0=ALU.mult), reads=[ps, pcf], writes=[vslc])
                        S.op("pool", I("tensor_scalar", out=vslc.t[:, kt, :, 64:128], in0=ones2, scalar1=pvc, scalar2=None, op0=ALU.mult), reads=[cbf, pcf], writes=[vslc])
                    pt = self.psT
                    for k in range(3):
                        S.op("pe", I("transpose", out=pt.t[:, k * 128:(k + 1) * 128], in_=t.t[:, k * 128:(k + 1) * 128], identity=self.ident), reads=[t, cbf], writes=[pt])
                    S.op("act", I("copy", out=kcmpT.t[:, kt * 128:(kt + 1) * 128], in_=pt.t[:, 0:128]), reads=[pt], writes=[kcmpT])
                    S.op("act", I("copy", out=vcmpT.t[:, kt * 128:(kt + 1) * 128], in_=pt.t[:, 256:384]), reads=[pt], writes=[vcmpT])
                    if kt < 48:
                        S.op("act", I("copy", out=kslcT.t[:, kt * 128:(kt + 1) * 128], in_=pt.t[:, 128:256]), reads=[pt], writes=[kslcT])
                if self.stop_after == "B2":
                    self.dump("kslcT", kslcT, kslcT.t[:], [128, 48 * 128], BF16)
                    self.dump("kcmpT", kcmpT, kcmpT.t[:], [128, S_LEN], BF16)
                    self.dump("vslc", vslc, vslc.t[:], [128, 48, 2, 128], BF16)
                    self.stopped = True
                    return
                w1sb = self.sb(es, "b_w1", [128, 32, 256], BF16)
                w2sb = self.sb(es, "b_w2", [128, 2, 128], BF16)
                posb = self.sb(es, "b_posb", [32, 128], BF16)
                posf = self.sb(es, "b_posf", [32, 64], F32)
                posT = self.sb(es, "b_posT", [128, 32], BF16)
                b1sb = self.sb(es, "b_b1", [128, 2], F32)
                gel = [self.sb(es, "b_gel%d" % i, [128, 512], BF16) for i in range(2)]
                hA = self.sb(es, "b_hA", [128, 512], F32)
                hB = self.sb(es, "b_hB", [128, 512], F32)
                for kv in range(2):
                    src = kcmpT if kv == 0 else vcmpT
                    w1d = self.cmp_w1_k if kv == 0 else self.cmp_w1_v
                    w2d = self.cmp_w2_k if kv == 0 else self.cmp_w2_v
                    posd = self.cmp_pos_k if kv == 0 else self.cmp_pos_v
                    w1v = w1d.rearrange("(j d) h -> d j h", d=64)
                    for j0 in range(0, 32, 8):
                        st = self.wst.next()
                        for half in range(2):
                            S.dma(I("dma_start", out=st.t[64 * half:64 * half + 64, :, :], in_=w1v[:, j0:j0 + 8, :]), writes=[st])
                        S.op("pool", I("tensor_copy", out=w1sb.t[:, j0:j0 + 8, :], in_=st.t[:, :, :]), reads=[st], writes=[w1sb])
                    st = self.wst.next()
                    S.dma(I("dma_start", out=st.t[:, 0:2, 0:64], in_=w2d.rearrange("(c p) n -> p c n", p=128)), writes=[st])
                    S.op("pool", I("tensor_copy", out=w2sb.t[:, :, 0:64], in_=st.t[:, 0:2, 0:64]), reads=[st], writes=[w2sb])
                    S.op("pool", I("tensor_copy", out=w2sb.t[:, :, 64:128], in_=st.t[:, 0:2, 0:64]), reads=[st], writes=[w2sb])
                    S.dma(I("dma_start", out=posf.t[:], in_=posd), writes=[posf])
                    S.op("dve", I("tensor_copy", out=posb.t[:, 0:64], in_=posf.t[:]), reads=[posf], writes=[posb])
                    S.op("dve", I("tensor_copy", out=posb.t[:, 64:128], in_=posf.t[:]), reads=[posf], writes=[posb])
                    pt = self.psT
                    S.op("pe", I("transpose", out=pt.t[:, 0:32], in_=posb.t[:], identity=cbf.t[0:32, 0:32]), reads=[posb, cbf], writes=[pt])
                    S.op("act", I("copy", out=posT.t[:], in_=pt.t[:, 0:32]), reads=[pt], writes=[posT])
                    psx = self.psX
                    for mh in range(2):
                        for j in range(32):
                            S.op("pe", I("matmul", psx.t[:, mh:mh + 1], lhsT=w1sb.t[0:64, j, mh * 128:(mh + 1) * 128], rhs=posT.t[0:64, j:j + 1], start=(j == 0), stop=(j == 31)), reads=[w1sb, posT], writes=[psx])
                    S.op("dve", I("tensor_copy", out=b1sb.t[:], in_=psx.t[:, 0:2]), reads=[psx], writes=[b1sb])
                    for g in range(2):
                        pb = 64 * g
                        for mh in range(2):
                            ps = self.psA.next()
                            for j in range(32):
                                S.op("pe", I("matmul", ps.t[:, 0:511], lhsT=w1sb.t[pb:pb + 64, j, mh * 128:(mh + 1) * 128], rhs=src.t[pb:pb + 64, j:j + 16 * 510 + 1:16], start=(j == 0), stop=(j == 31)), reads=[w1sb, src], writes=[ps])
                            S.op("act", I("activation", out=hA.t[:, 0:511], in_=ps.t[:, 0:511], func=AF.Identity, bias=b1sb.t[:, mh:mh + 1]), reads=[ps, b1sb], writes=[hA])
                            S.op("dve", I("tensor_tensor", out=hB.t[:, 0:511], in0=hA.t[:, 0:511], in1=hA.t[:, 0:511], op=ALU.mult), reads=[hA], writes=[hB])
                            S.op("dve", I("tensor_scalar", out=hB.t[:, 0:511], in0=hB.t[:, 0:511], scalar1=0.044715, scalar2=1.0, op0=ALU.mult, op1=ALU.add), reads=[hB], writes=[hB])
                            S.op("dve", I("tensor_tensor", out=hB.t[:, 0:511], in0=hB.t[:, 0:511], in1=hA.t[:, 0:511], op=ALU.mult), reads=[hA, hB], writes=[hB])
                            S.op("act", I("activation", out=hB.t[:, 0:511], in_=hB.t[:, 0:511], func=AF.Sigmoid, scale=2.0 * 0.7978845608028654), reads=[hB], writes=[hB])
                            S.op("dve", I("tensor_tensor", out=gel[mh].t[:, 0:511], in0=hA.t[:, 0:511], in1=hB.t[:, 0:511], op=ALU.mult), reads=[hA, hB], writes=[gel[mh]])
                        if kv == 0:
                            ps = self.psA.next()
                            for mh in range(2):
                                S.op("pe", I("matmul", ps.t[:, 0:511], lhsT=w2sb.t[:, mh, :], rhs=gel[mh].t[:, 0:511], start=(mh == 0), stop=(mh == 1)), reads=[w2sb, gel[mh]], writes=[ps])
                            S.op("act", I("copy", out=kcT.t[pb:pb + 64, 0:511], in_=ps.t[pb:pb + 64, 0:511]), reads=[ps], writes=[kcT])
                        else:
                            for bt in range(4):
                                n = 128 if bt < 3 else 127
                                ps = self.psA.next()
                                for mh in range(2):
                                    S.op("pe", I("matmul", ps.t[0:n, 0:64], lhsT=gel[mh].t[:, bt * 128:bt * 128 + n], rhs=w2sb.t[:, mh, 0:64], start=(mh == 0), stop=(mh == 1)), reads=[w2sb, gel[mh]], writes=[ps])
                                S.op("act", I("copy", out=vc.t[0:n, bt, g, 0:64], in_=ps.t[0:n, 0:64]), reads=[ps], writes=[vc])
                                S.op("pool", I("tensor_copy", out=vc.t[0:n, bt, g, 64:128], in_=ones64[0:n, :]), reads=[cbf], writes=[vc])
            S.barrier()
            if self.stop_after == "B3":
                self.dump("kcT", kcT, kcT.t[:], [128, 512], BF16)
                self.dump("vc", vc, vc.t[:], [128, 4, 2, 128], BF16)
                self.stopped = True
                return
            qbT = self.sb(esB, "b_qbT", [128, 4, OWN], BF16)
            gbT = self.sb(esB, "b_gbT", [32, OWN], F32)
            kwinT = self.sb(esB, "b_kwinT", [128, 20 * 128], BF16)
            vwin = self.sb(esB, "b_vwin", [128, 20, 2, 128], BF16)
            kso = self.sb(esB, "b_kso", [128, OWN], BF16)
            vso = self.sb(esB, "b_vso", [128, 16, 2, 128], BF16)
            S.op("pool", I("memset", gbT.t[:], 0.0), writes=[gbT])
            hvcol = pcf.t[:, PCF["hv"]:PCF["hv"] + 1]
            with ExitStack() as es:
                self.wst = self.ring(es, "wstB1", [128, 8, 256], F32, 2)
                slq = self.sb(es, "b_slq", [128, 8, 512], BF16)
                slkv = self.sb(es, "b_slkv", [128, 8, 512], BF16)
                slg = self.sb(es, "b_slg", [128, 8, 32], BF16)
                uTt = self.ring(es, "b_uTt1", [128, 8, 128], BF16, 2)
                tq = self.ring(es, "b_tq", [128, 512], BF16, 2)
                tk = self.ring(es, "b_tk", [128, 256], BF16, 2)
                self.cast_into(slq, lambda c0, n: slq.t[:, :, c0:c0 + n], self.w_in[:, C_QB:C_QB + 512], 8, 512, self.gmix, piece=256)
                self.cast_into(slkv, lambda c0, n: slkv.t[:, :, c0:c0 + n], self.w_in[:, C_KVB + 256:C_KVB + 768], 8, 512, self.gmix, piece=256)
                self.cast_into(slg, lambda c0, n: slg.t[:, :, c0:c0 + n], self.w_in[:, C_GB:C_GB + 24], 8, 24, self.gmix, piece=256)
                for e_ in range(12, 32):
                    slot = e_ - 12
                    own = e_ >= 16
                    i = e_ - 16
                    u = uTt.next()
                    xs = self.x_own[i * 128:(i + 1) * 128, :] if own else self.x_halo[e_ * 128:(e_ + 1) * 128, :]
                    self.norm_tile(xs, u, u.t[:])
                    lhs = lambda c, u=u: u.t[:, c, :]
                    if own:
                        ps = self.psA.next()
                        self.proj_tm(lhs, u, slq, 0, 512, ps)
                        t = tq.next()
                        self.rope_evac(ps, 0, 8, e_, t, t.t[:, 0:512], perm=True)
                        pt = self.psT
                        for hp in range(4):
                            S.op("pe", I("transpose", out=pt.t[:, hp * 128:(hp + 1) * 128], in_=t.t[:, hp * 128:(hp + 1) * 128], identity=self.ident), reads=[t, cbf], writes=[pt])
                        S.op("act", I("copy", out=qbT.t[:, :, i * 128:(i + 1) * 128], in_=pt.t[:, 0:512].rearrange("p (h t) -> p h t", h=4)), reads=[pt], writes=[qbT])
                        psx = self.psX
                        for c in range(8):
                            S.op("pe", I("matmul", psx.t[0:24, 0:128], lhsT=slg.t[:, c, 0:24], rhs=u.t[:, c, :], start=(c == 0), stop=(c == 7)), reads=[slg, u], writes=[psx])
                        S.op("act", I("activation", out=gbT.t[0:24, i * 128:(i + 1) * 128], in_=psx.t[0:24, 0:128], func=AF.Sigmoid), reads=[psx], writes=[gbT])
                    ps = self.psA.next()
                    t = tk.next()
                    if own:
                        self.proj_tm(lhs, u, slkv, 0, 512, ps)
                        self.rope_evac(ps, 0, 2, e_, t, t.t[:, 0:128])
                        self.rope_evac(ps, 256, 2, e_, t, t.t[:, 128:256])
                        S.op("dve", I("tensor_copy", out=vso.t[:, i, :, 0:64], in_=ps.t[:, 128:256].rearrange("p (g d) -> p g d", g=2)), reads=[ps], writes=[vso])
                        S.op("pool", I("tensor_copy", out=vso.t[:, i, :, 64:128], in_=ones2), reads=[cbf], writes=[vso])
                        S.op("dve", I("tensor_copy", out=vwin.t[:, slot, :, 0:64], in_=ps.t[:, 384:512].rearrange("p (g d) -> p g d", g=2)), reads=[ps], writes=[vwin])
                        S.op("pool", I("tensor_copy", out=vwin.t[:, slot, :, 64:128], in_=ones2), reads=[cbf], writes=[vwin])
                    else:
                        self.proj_tm(lhs, u, slkv, 256, 256, ps)
                        self.rope_evac(ps, 0, 2, e_, t, t.t[:, 128:256])
                        S.op("dve", I("tensor_scalar", out=vwin.t[:, slot, :, 0:64], in0=ps.t[:, 128:256].rearrange("p (g d) -> p g d", g=2), scalar1=hvcol, scalar2=None, op0=ALU.mult), reads=[ps, pcf], writes=[vwin])
                        S.op("pool", I("tensor_scalar", out=vwin.t[:, slot, :, 64:128], in0=ones2, scalar1=hvcol, scalar2=None, op0=ALU.mult), reads=[cbf, pcf], writes=[vwin])
                    pt = self.psT
                    if own:
                        S.op("pe", I("transpose", out=pt.t[:, 0:128], in_=t.t[:, 0:128], identity=self.ident), reads=[t, cbf], writes=[pt])
                    S.op("pe", I("transpose", out=pt.t[:, 128:256], in_=t.t[:, 128:256], identity=self.ident), reads=[t, cbf], writes=[pt])
                    if own:
                        S.op("act", I("copy", out=kso.t[:, i * 128:(i + 1) * 128], in_=pt.t[:, 0:128]), reads=[pt], writes=[kso])
                    S.op("act", I("copy", out=kwinT.t[:, slot * 128:(slot + 1) * 128], in_=pt.t[:, 128:256]), reads=[pt], writes=[kwinT])
            S.barrier()
            biasT = self.sb(esB, "b_biasT", [128, 2, OWN], BF16)
            t16 = self.sb(esB, "b_t16", [128, 2048], F32)
            S.dma(I("dma_start", out=t16.t[:], in_=self.c_t16), writes=[t16])
            with ExitStack() as es:
                et = self.ring(es, "b_et", [128, 512], F32, 4)
                pp = self.ring(es, "b_pp", [128, 520], F32, 4)
                lohi = self.ring(es, "b_lohi", [128, 256], F32, 2)
                imp = self.ring(es, "b_imp", [128, 128], F32, 2)
                imp2 = self.ring(es, "b_imp2", [128, 128], F32, 2)
                sm = self.ring(es, "b_sm", [128, 24], F32, 8)
                btm = self.ring(es, "b_btm", [128, 128], BF16, 2)
                for pq in pp.items:
                    S.op("pool", I("memset", pq.t[:], 0.0), writes=[pq])
                for i in range(NT):
                    lh = lohi.next()
                    S.dma(I("dma_start", out=lh.t[:, 0:128], in_=self.pc_lohi[:, i * 128:(i + 1) * 128]), writes=[lh])
                    S.dma(I("dma_start", out=lh.t[:, 128:256], in_=self.pc_lohi[:, 2048 + i * 128:2048 + (i + 1) * 128]), writes=[lh])
                    thr_i = pcf.t[:, PCF["thrc"] + i:PCF["thrc"] + i + 1]
                    def gen(g, i=i, lh=lh, thr_i=thr_i):
                        pb = 64 * g
                        P = pp.next()
                        P2 = pp.next()
                        for r in range(4):
                            ps = self.psS.next()
                            S.op("pe", I("matmul", ps.t[:, 0:511], lhsT=qbT.t[pb:pb + 64, r, i * 128:(i + 1) * 128], rhs=kcT.t[pb:pb + 64, 0:511], start=True, stop=True), reads=[qbT, kcT], writes=[ps])
                            e_ = et.next()
                            s_ = sm.next()
                            eng = "dve"
                            Pr = P if r % 2 == 0 else P2
                            S.op("act", I("activation", out=e_.t[:, 0:511], in_=ps.t[:, 0:511], func=AF.Exp, scale=SCALE), reads=[ps], writes=[e_])
                            S.op(eng, I("scalar_tensor_tensor", out=e_.t[:, 0:511], in0=t16.t[:, 0:511], scalar=thr_i, in1=e_.t[:, 0:511], op0=ALU.is_le, op1=ALU.mult, accum_out=s_.t[:, 0:1]), reads=[t16, pcf, e_], writes=[e_, s_])
                            yield
                            S.op(eng, I("tensor_scalar", out=s_.t[:, 1:2], in0=s_.t[:, 0:1], scalar1=1e-30, scalar2=None, op0=ALU.max), reads=[s_], writes=[s_])
                            yield
                            S.op("dve", I("reciprocal", out=s_.t[:, 2:3], in_=s_.t[:, 1:2]), reads=[s_], writes=[s_])
                            yield
                            if r < 2:
                                S.op(eng, I("tensor_scalar", out=Pr.t[:, 1:512], in0=e_.t[:, 0:511], scalar1=s_.t[:, 2:3], scalar2=None, op0=ALU.mult), reads=[e_, s_], writes=[Pr])
                                yield
                            else:
                                S.op(eng, I("scalar_tensor_tensor", out=Pr.t[:, 1:512], in0=e_.t[:, 0:511], scalar=s_.t[:, 2:3], in1=Pr.t[:, 1:512], op0=ALU.mult, op1=ALU.add), reads=[e_, s_, Pr], writes=[Pr])
                                yield
                        S.op("dve", I("tensor_tensor", out=P.t[:, 1:512], in0=P.t[:, 1:512], in1=P2.t[:, 1:512], op=ALU.add), reads=[P, P2], writes=[P])
                        yield
                        im = imp.next()
                        S.op("dve", I("tensor_tensor", out=im.t[:], in0=P.t[:, 0:512:4], in1=P.t[:, 1:513:4], op=ALU.add), reads=[P], writes=[im])
                        yield
                        for k in range(2, 5):
                            S.op("dve", I("tensor_tensor", out=im.t[:], in0=im.t[:], in1=P.t[:, k:k + 512:4], op=ALU.add), reads=[P, im], writes=[im])
                            yield
                        S.op("dve", I("tensor_tensor", out=im.t[:], in0=im.t[:], in1=lh.t[:, 0:128], op=ALU.max), reads=[lh, im], writes=[im])
                        yield
                        S.op("dve", I("tensor_tensor", out=im.t[:], in0=im.t[:], in1=lh.t[:, 128:256], op=ALU.min), reads=[lh, im], writes=[im])
                        yield
                        s_ = sm.next()
                        i2 = imp2.next()
                        S.op("dve", I("max", out=s_.t[:, 0:8], in_=im.t[:]), reads=[im], writes=[s_])
                        yield
                        S.op("dve", I("match_replace", out=i2.t[:], in_to_replace=s_.t[:, 0:8], in_values=im.t[:], imm_value=-1e9), reads=[im, s_], writes=[i2])
                        yield
                        S.op("dve", I("max", out=s_.t[:, 8:16], in_=i2.t[:]), reads=[i2], writes=[s_])
                        yield
                        S.op("dve", I("tensor_scalar", out=s_.t[:, 16:17], in0=s_.t[:, 15:16], scalar1=-1.5e4, scalar2=None, op0=ALU.max), reads=[s_], writes=[s_])
                        yield
                        bt_ = btm.next()
                        S.op("dve", I("tensor_scalar", out=bt_.t[:], in0=im.t[:], scalar1=s_.t[:, 16:17], scalar2=-30000.0, op0=ALU.is_lt, op1=ALU.mult), reads=[im, s_], writes=[bt_])
                        yield
                        pt = self.psT
                        S.op("pe", I("transpose", out=pt.t[:, 0:128], in_=bt_.t[:], identity=self.ident), reads=[bt_, cbf], writes=[pt])
                        S.op("act", I("copy", out=biasT.t[:, g, i * 128:(i + 1) * 128], in_=pt.t[:, 0:128]), reads=[pt], writes=[biasT])

                    gens = [gen(0), gen(1)]
                    while gens:
                        for gg in list(gens):
                            try:
                                next(gg)
                            except StopIteration:
                                gens.remove(gg)
            S.barrier()
            if self.stop_after == "B6":
                self.dump("biasT", biasT, biasT.t[:], [128, 2, OWN], BF16)
                self.dump("qbT", qbT, qbT.t[:], [128, 4, OWN], BF16)
                self.stopped = True
                return
            with ExitStack() as es:
                osb = self.ring(es, "b_osb", [128, 512], F32, 2)
                rdr = self.ring(es, "b_rd", [128, 512], F32, 2)
                ybacc = self.sb(es, "b_ybacc", [128, 512], F32)
                ybacc2 = self.sb(es, "b_ybacc2", [128, 512], F32)
                self.ptr = self.ring(es, "b_ptr", [128, 512], BF16, 5)
                gsel = self.ring(es, "b_gsel", [32, 128], F32, 6)
                id32 = self.cf32.t[0:32, F32C["id32"]:F32C["id32"] + 32]
                eown = pcb.t[:, PCB["eown"]:PCB["eown"] + 2048]
                e32 = cbf.t[:, BFC["e32"]:BFC["e32"] + 2048]
                m4 = cbf.t[:, BFC["m4"]:BFC["m4"] + 2048]
                tri_diag = cbf.t[:, BFC["tri_diag"]:BFC["tri_diag"] + 128]
                win_far = cbf.t[:, BFC["win_far"]:BFC["win_far"] + 128]
                BRS = os.environ.get("BRS", "012")
                ps4 = Ring([self.psS.items[0], self.psS.items[1], self.psA.items[0], self.psA.items[1]])
                ybaccs = [ybacc, ybacc2]
                for hp in range(4):
                    sels = {}
                    for gi in range(2):
                        h = hp + 4 * gi
                        for br in range(3):
                            gs = gsel.next()
                            jrow = h * 3 + br
                            S.op("pool", I("tensor_copy", out=gs.t[:], in_=id32[:, jrow:jrow + 1].to_broadcast([32, 128])), reads=[self.cf32], writes=[gs])
                            sels[(gi, br)] = gs
                    for c in range(4):
                        gcols = gbT.t[0:32, c * 512:(c + 1) * 512]
                        dst = self.ybT.t[:, hp, c * 512:(c + 1) * 512]
                        Qs = [qbT.t[64 * gi:64 * gi + 64, hp, c * 512:(c + 1) * 512] for gi in range(2)]

                        def fin(psos, br, first, last):
                            for gi in range(2):
                                o = osb.next()
                                S.op("act", I("copy", out=o.t[:], in_=psos[gi].t[:]), reads=[psos[gi]], writes=[o])
                                self.finalize(o, o.t[:], gi, (sels[(gi, br)], gbT, gcols), rdr, self.ybT, dst, first=first, last=last, ybacc=ybaccs[gi])
                        psos = [self.psO.next(), self.psO.next()]
                        steps = []
                        for bt in range(4):
                            crel = pcf.t[:, PCF["crel"] + bt:PCF["crel"] + bt + 1]

                            def mask_c(pt, crel=crel, c=c):
                                S.op("dve", I("scalar_tensor_tensor", out=pt.t[:, 0:512], in0=t16.t[:, c * 512:(c + 1) * 512], scalar=crel, in1=pt.t[:, 0:512], op0=ALU.is_ge, op1=ALU.mult), reads=[t16, pcf, pt], writes=[pt])
                            stp = []
                            for gi in range(2):
                                pb = 64 * gi
                                stp.append(([(kcT.t[pb:pb + 64, bt * 128:(bt + 1) * 128], Qs[gi], [kcT, qbT])], 512, mask_c,
                                            [(psos[gi], psos[gi].t[:, 0:512], vc.t[:, bt, gi, :], 0, 512, bt == 0, bt == 3, [vc])]))
                            steps.append(stp)
                        self.attn_steps(steps, ps4)
                        fin(psos, 0, True, False)
                        psos = [self.psO.next(), self.psO.next()]
                        steps = []
                        for kt in range(48):
                            pb32 = 32 * (kt // 16)
                            kc_ = (kt % 16) * 128
                            stp = []
                            for gi in range(2):
                                pb = 64 * gi
                                stp.append(([(kslcT.t[pb:pb + 64, kt * 128:(kt + 1) * 128], Qs[gi], [kslcT, qbT]),
                                             (e32[pb32:pb32 + 32, kc_:kc_ + 128], biasT.t[pb32:pb32 + 32, gi, c * 512:(c + 1) * 512], [cbf, biasT])], 512, None,
                                            [(psos[gi], psos[gi].t[:, 0:512], vslc.t[:, kt, gi, :], 0, 512, kt == 0, False, [vslc])]))
                            steps.append(stp)
                        for j in range(4 * c + 4):
                            mf = None
                            if j >= 4 * c:
                                mk = m4[:, (j - 4 * c) * 512:(j - 4 * c + 1) * 512]

                                def mf(pt, mk=mk):
                                    self.mask_mul(pt, 0, 512, mk)
                            stp = []
                            for gi in range(2):
                                pb = 64 * gi
                                stp.append(([(kso.t[pb:pb + 64, j * 128:(j + 1) * 128], Qs[gi], [kso, qbT]),
                                             (eown[:, j * 128:(j + 1) * 128], biasT.t[:, gi, c * 512:(c + 1) * 512], [pcb, biasT])], 512, mf,
                                            [(psos[gi], psos[gi].t[:, 0:512], vso.t[:, j, gi, :], 0, 512, False, j == 4 * c + 3, [vso])]))
                            steps.append(stp)
                        self.attn_steps(steps, ps4)
                        fin(psos, 1, False, False)
                        psos = [self.psO.next(), self.psO.next()]
                        steps = []
                        for tq_ in range(4):
                            i = 4 * c + tq_
                            sq = 4 + i
                            for s_ in range(sq - 4, sq + 1):
                                mf = None
                                if s_ == sq - 4:
                                    def mf(pt):
                                        self.mask_mul(pt, 0, 128, win_far)
                                elif s_ == sq:
                                    def mf(pt):
                                        self.mask_mul(pt, 0, 128, tri_diag)
                                stp = []
                                for gi in range(2):
                                    pb = 64 * gi
                                    stp.append(([(kwinT.t[pb:pb + 64, s_ * 128:(s_ + 1) * 128], qbT.t[pb:pb + 64, hp, i * 128:(i + 1) * 128], [kwinT, qbT])], 128, mf,
                                                [(psos[gi], psos[gi].t[:, tq_ * 128:(tq_ + 1) * 128], vwin.t[:, s_, gi, :], 0, 128, s_ == sq - 4, s_ == sq, [vwin])]))
                                steps.append(stp)
                        self.attn_steps(steps, ps4)
                        fin(psos, 2, False, True)
        S.barrier()

    def phase_C(self, es0):
        S = self.S
        with ExitStack() as es:
            slabs = self.ring(es, "c_slab", [128, 8, 512], BF16, 5)
            self.wbslab = self.sb(es, "c_wb", [128, 4, 512], BF16)
            gfin = self.sb(es, "c_gfin", [128, D], F32)
            xc = self.sb(es, "c_xc", [128, 4, D], F32)
            uTc = self.sb(es, "c_uTc", [128, 8, 512], BF16)
            mTc = self.sb(es, "c_mTc", [128, 8, 512], BF16)
            u2Tc = self.sb(es, "c_u2Tc", [128, 8, 512], BF16)
            hT = self.sb(es, "c_hT", [128, 32, 512], BF16)
            sg = self.ring(es, "c_sg", [128, 512], BF16, 2)
            tf = self.ring(es, "c_tf", [128, 512], F32, 3)
            S.dma(I("dma_start", out=gfin.t[:], in_=self.g_fin), writes=[gfin])

            slab_ids = {id(t): i for i, t in enumerate(slabs.items)}

            def slab_from(src_ap, kchunks, gain):
                sl = slabs.next()
                fns = [I("dma_start", out=sl.t[:, 0:kchunks, c0:c0 + 256], in_=src_ap[:, c0:c0 + 256].rearrange("(c p) n -> p c n", p=128)) for c0 in (0, 256)]
                S.dma_sw(fns, [sl], slab_ids[id(sl)])
                return sl

            for c in range(4):
                cs = slice(c * 512, (c + 1) * 512)
                for tt in range(4):
                    r0 = c * 512 + tt * 128
                    S.dma(I("dma_start", out=xc.t[:, tt, :], in_=self.x_own[r0:r0 + 128, :]), writes=[xc])
                    self.norm_sb(xc, xc.t[:, tt, :], uTc, uTc.t[:, :, tt * 128:(tt + 1) * 128], keep_rstd=self.gmix)
                for ctg in range(2):
                    gA = slab_from(self.w_in[:, C_GM + ctg * 512:C_GM + ctg * 512 + 512], 8, self.gmix)
                    gB = slab_from(self.w_in[:, C_GM + 1024 + ctg * 512:C_GM + 1024 + ctg * 512 + 512], 8, self.gmix)
                    wa = slab_from(self.w_a[:, ctg * 512:(ctg + 1) * 512], 4, None)
                    wb = self.wbslab
                    fns = [I("dma_start", out=wb.t[64 * two:64 * two + 64, 0:4, 0:512],
                             in_=self.w_b[two * 256:(two + 1) * 256, ctg * 512:ctg * 512 + 512].rearrange("(hp d) n -> d hp n", d=64)) for two in range(2)]
                    S.dma_sw(fns, [wb], 99)
                    for j in range(4):
                        ct = ctg * 4 + j
                        js = slice(j * 128, (j + 1) * 128)
                        sgs = []
                        for gw in (gA, gB):
                            ps = self.psA.next()
                            for k in range(8):
                                S.op("pe", I("matmul", ps.t[:, 0:512], lhsT=gw.t[:, k, js], rhs=uTc.t[:, k, :], start=(k == 0), stop=(k == 7)), reads=[gw, uTc], writes=[ps])
                            sgt = sg.next()
                            S.op("act", I("activation", out=sgt.t[:], in_=ps.t[:, 0:512], func=AF.Sigmoid), reads=[ps], writes=[sgt])
                            sgs.append(sgt)
                        psa = self.psS.next()
                        for k in range(4):
                            S.op("pe", I("matmul", psa.t[:, 0:512], lhsT=wa.t[:, k, js], rhs=self.yaT.t[:, k, cs], start=(k == 0), stop=(k == 3)), reads=[wa, self.yaT], writes=[psa])
                        psb = self.psO.next()
                        for k in range(4):
                            S.op("pe", I("matmul", psb.t[:, 0:512], lhsT=wb.t[:, k, js], rhs=self.ybT.t[:, k, cs], start=(k == 0), stop=(k == 3)), reads=[wb, self.ybT], writes=[psb])
                        t0 = tf.next()
                        t1 = tf.next()
                        S.op("dve", I("tensor_tensor", out=t0.t[:], in0=psa.t[:, 0:512], in1=sgs[0].t[:], op=ALU.mult), reads=[psa, sgs[0]], writes=[t0])
                        S.op("dve", I("tensor_tensor", out=t1.t[:], in0=psb.t[:, 0:512], in1=sgs[1].t[:], op=ALU.mult), reads=[psb, sgs[1]], writes=[t1])
                        S.op("dve", I("tensor_tensor", out=mTc.t[:, ct, :], in0=t0.t[:], in1=t1.t[:], op=ALU.add), reads=[t0, t1], writes=[mTc])
                for nh in range(2):
                    wo = slab_from(self.w_out[:, nh * 512:(nh + 1) * 512], 8, None)
                    for tt in range(4):
                        ps = self.psA.next()
                        for k in range(8):
                            S.op("pe", I("matmul", ps.t[:, 0:512], lhsT=mTc.t[:, k, tt * 128:(tt + 1) * 128], rhs=wo.t[:, k, :], start=(k == 0), stop=(k == 7)), reads=[wo, mTc], writes=[ps])
                        S.op("dve", I("tensor_tensor", out=xc.t[:, tt, nh * 512:(nh + 1) * 512], in0=ps.t[:, 0:512], in1=xc.t[:, tt, nh * 512:(nh + 1) * 512], op=ALU.add), reads=[ps, xc], writes=[xc])
                for tt in range(4):
                    self.norm_sb(xc, xc.t[:, tt, :], u2Tc, u2Tc.t[:, :, tt * 128:(tt + 1) * 128], keep_rstd=self.gmlp)
                for s_ in range(8):
                    wu = slab_from(self.w_up[:, s_ * 512:(s_ + 1) * 512], 8, self.gmlp)
                    for j in range(4):
                        ft = 4 * s_ + j
                        ps = self.psA.next()
                        for k in range(8):
                            S.op("pe", I("matmul", ps.t[:, 0:512], lhsT=wu.t[:, k, j * 128:(j + 1) * 128], rhs=u2Tc.t[:, k, :], start=(k == 0), stop=(k == 7)), reads=[wu, u2Tc], writes=[ps])
                        r = tf.next()
                        S.op("act", I("activation", out=r.t[:], in_=ps.t[:, 0:512], func=AF.Relu), reads=[ps], writes=[r])
                        S.op("dve", I("tensor_tensor", out=hT.t[:, ft, :], in0=r.t[:], in1=r.t[:], op=ALU.mult), reads=[r], writes=[hT])
                accs = [self.psA.items[0], self.psA.items[1], self.psS.items[0], self.psS.items[1]]
                for nh in range(2):
                    for kg in range(4):
                        wd = slab_from(self.w_down[kg * 1024:(kg + 1) * 1024, nh * 512:(nh + 1) * 512], 8, None)
                        for tt in range(4):
                            for k in range(8):
                                S.op("pe", I("matmul", accs[tt].t[:, 0:512], lhsT=hT.t[:, kg * 8 + k, tt * 128:(tt + 1) * 128], rhs=wd.t[:, k, :], start=(kg == 0 and k == 0), stop=(kg == 3 and k == 7)), reads=[wd, hT], writes=[accs[tt]])
                    for tt in range(4):
                        S.op("dve", I("tensor_tensor", out=xc.t[:, tt, nh * 512:(nh + 1) * 512], in0=accs[tt].t[:, 0:512], in1=xc.t[:, tt, nh * 512:(nh + 1) * 512], op=ALU.add), reads=[accs[tt], xc], writes=[xc])
                for tt in range(4):
                    jk = self.junk.next()
                    st = self.stat.next()
                    S.op("act", I("activation", out=jk.t[:], in_=xc.t[:, tt, :], func=AF.Square, accum_out=st.t[:, 0:1]), reads=[xc], writes=[jk, st])
                    S.op("act", I("activation", out=st.t[:, 1:2], in_=st.t[:, 0:1], func=AF.Sqrt, scale=1.0 / D, bias=self.epsc.t[:, 0:1]), reads=[st, self.epsc], writes=[st])
                    S.op("dve", I("reciprocal", out=st.t[:, 2:3], in_=st.t[:, 1:2]), reads=[st], writes=[st])
                    S.op("dve", I("scalar_tensor_tensor", out=xc.t[:, tt, :], in0=xc.t[:, tt, :], scalar=st.t[:, 2:3], in1=gfin.t[:], op0=ALU.mult, op1=ALU.mult), reads=[xc, st, gfin], writes=[xc])
                    r0 = c * 512 + tt * 128
                    S.dma(I("dma_start", out=self.out[r0:r0 + 128, :], in_=xc.t[:, tt, :]), reads=[xc])

def make_in_maps(inputs):
    x = np.ascontiguousarray(np.asarray(inputs["x"], np.float32))
    cbf, cf32, t16 = _static_tables()
    sq = lambda n: np.ascontiguousarray(np.asarray(inputs[n], np.float32)[0])
    gl = lambda v: np.ascontiguousarray(np.asarray(v, np.float32).reshape(8, 128).T)
    common = {
        "w_in": sq("w_in"), "g_mix": gl(inputs["norm_mix_g"][0]), "g_mlp": gl(inputs["norm_mlp_g"][0]),
        "g_fin": np.ascontiguousarray(np.broadcast_to(np.asarray(inputs["norm_final_g"], np.float32)[None, :], (128, D))),
        "cmp_w1_k": sq("cmp_w1_k"), "cmp_w1_v": sq("cmp_w1_v"), "cmp_w2_k": sq("cmp_w2_k"), "cmp_w2_v": sq("cmp_w2_v"),
        "cmp_pos_k": sq("cmp_pos_k"), "cmp_pos_v": sq("cmp_pos_v"),
        "w_a": sq("w_branch_a"), "w_b": sq("w_branch_b"), "w_out": sq("w_out"), "w_up": sq("w_up"), "w_down": sq("w_down"),
        "c_bf": cbf, "c_f32": cf32, "c_t16": t16,
    }
    tabs = [_percore_tables(q) for q in range(4)]
    maps = []
    for c in range(8):
        b, q = c // 4, c % 4
        T0 = OWN * q
        halo = x[b, T0 - OWN:T0] if q > 0 else np.zeros((OWN, D), np.float32)
        m = dict(common)
        m.update({"x_own": np.ascontiguousarray(x[b, T0:T0 + OWN]), "x_halo": np.ascontiguousarray(halo), "x_full": x[b],
                  "pc_f": tabs[q][0], "pc_lohi": tabs[q][1], "pc_bf": tabs[q][2]})
        maps.append(m)
    return maps


_CACHE = {}


def kernel(**inputs):
    if "nc" not in _CACHE:
        b = Builder()
        _CACHE["nc"] = b.build()
        _CACHE["decl"] = set(b._decl.keys())
    nc = _CACHE["nc"]
    maps = make_in_maps(inputs)
    decl = _CACHE["decl"]
    maps = [{k: v for k, v in m.items() if k in decl} for m in maps]
    res = run_bass_kernel_spmd(nc, maps, core_ids=list(range(8)))
    out = np.zeros((2, S_LEN, D), np.float32)
    for c in range(8):
        b, q = c // 4, c % 4
        out[b, OWN * q:OWN * (q + 1)] = res.results[c]["out"]
    return out
```

```python
import os
import numpy as np
import ml_dtypes
from contextlib import ExitStack
import concourse.bass as bass
import concourse.mybir as mybir
from concourse.bass_utils import run_bass_kernel_spmd

F32 = mybir.dt.float32
BF16 = mybir.dt.bfloat16
ALU = mybir.AluOpType
AF = mybir.ActivationFunctionType
NPBF = ml_dtypes.bfloat16

D = 1024
S_LEN = 8192
OWN = 2048
NT = 16
EPS = 1e-6
SCALE = 0.125
IN_COLS = 7960
C_QA, C_KA, C_VA = 0, 1536, 3072
C_QB = 4608
C_KVB = 5120
C_GB = 5888
C_GM = 5912
DILS = (1, 4, 16)

ENGS = ("pe", "act", "dve", "pool")
NDMA = 24


class Res:
    __slots__ = ("lw", "rd", "excl")

    def __init__(self):
        self.lw = None
        self.rd = {}
        self.excl = False


class Tn:
    __slots__ = ("t", "r")

    def __init__(self, t):
        self.t = t
        self.r = Res()


class Sched:
    def __init__(self, nc):
        self.nc = nc
        self.q = {e: [] for e in ENGS + ("sp",)}
        self.cnt = {e: 0 for e in ENGS}
        self.dcnt = [0] * NDMA
        self.seen = {e: {} for e in ENGS + ("sp",)}
        self.dnext = 0
        self.pgen = {}
        self.plast = {}

    def dma_sw(self, fns, writes, slot):
        deps = self._deps([], writes)
        self.pgen[slot] = self.pgen.get(slot, 0)
        key = ("p", slot)
        base = self.pgen[slot]
        waits = self._waits("pool", deps)
        for i, fn in enumerate(fns):
            self.q["pool"].append((waits if i == 0 else [], fn, key, base + 16 * (i + 1)))
        self.pgen[slot] = base + 16 * len(fns)
        self.plast[slot] = (key, self.pgen[slot])
        self._mark(key, self.pgen[slot], [], writes)

    def _deps(self, reads, writes, mykey=None):
        deps = {}
        for r in reads:
            r = r.r if isinstance(r, Tn) else r
            if r.lw is not None and r.lw[1] > deps.get(r.lw[0], 0):
                deps[r.lw[0]] = r.lw[1]
            if r.excl:
                for k, v in r.rd.items():
                    if k != mykey and v > deps.get(k, 0):
                        deps[k] = v
        for w in writes:
            w = w.r if isinstance(w, Tn) else w
            if w.lw is not None and w.lw[0] != mykey and w.lw[1] > deps.get(w.lw[0], 0):
                deps[w.lw[0]] = w.lw[1]
            for k, v in w.rd.items():
                if v > deps.get(k, 0):
                    deps[k] = v
        return deps

    def _waits(self, eng, deps):
        waits = []
        seen = self.seen[eng]
        for k, v in deps.items():
            if v > seen.get(k, 0):
                waits.append((k, v))
                seen[k] = v
        return waits

    def _mark(self, key, my, reads, writes):
        for r in reads:
            r = r.r if isinstance(r, Tn) else r
            if my > r.rd.get(key, 0):
                r.rd[key] = my
        for w in writes:
            w = w.r if isinstance(w, Tn) else w
            w.lw = (key, my)
            w.rd = {}

    def op(self, eng, fn, reads=(), writes=()):
        deps = self._deps(reads, writes, ("e", eng))
        if eng == "pe":
            deps.pop(("e", "pe"), None)
        self.cnt[eng] += 1
        my = self.cnt[eng]
        key = ("e", eng)
        self.q[eng].append((self._waits(eng, deps), fn, key, my))
        self._mark(key, my, reads, writes)

    def dma(self, fn, reads=(), writes=(), queue="sp"):
        deps = self._deps(reads, writes)
        k = self.dnext
        self.dnext = (self.dnext + 1) % NDMA
        key = ("d", k)
        if self.dcnt[k] > 0:
            deps[key] = max(deps.get(key, 0), self.dcnt[k])
        self.dcnt[k] += 16
        my = self.dcnt[k]
        self.q[queue].append((self._waits(queue, deps), fn, key, my))
        self._mark(key, my, reads, writes)

    def barrier(self):
        allc = {}
        for e in ENGS:
            if self.cnt[e]:
                allc[("e", e)] = self.cnt[e]
        for k in range(NDMA):
            if self.dcnt[k]:
                allc[("d", k)] = self.dcnt[k]
        for slot, (key, v) in self.plast.items():
            allc[key] = v
        for e in ENGS + ("sp",):
            w = self._waits(e, dict(allc))
            if w:
                self.q[e].append((w, None, None, 0))

    def emit(self):
        nc = self.nc
        with ExitStack() as es:
            esem = {e: es.enter_context(nc.semaphore("s_" + e)) for e in ENGS}
            dsem = [es.enter_context(nc.semaphore("s_d%d" % i)) for i in range(NDMA)]

            psem = {slot: es.enter_context(nc.semaphore("s_p%d" % i)) for i, slot in enumerate(sorted(self.pgen))}

            def semof(key):
                if key[0] == "p":
                    return psem[key[1]]
                return esem[key[1]] if key[0] == "e" else dsem[key[1]]
            fin = {}
            for e in ENGS:
                if self.cnt[e]:
                    fin[("e", e)] = self.cnt[e]
            for k in range(NDMA):
                if self.dcnt[k]:
                    fin[("d", k)] = self.dcnt[k]
            for slot, (key, v) in self.plast.items():
                fin[key] = v
            allsems = list(esem.values()) + dsem + list(psem.values())
            with nc.Block() as b0:
                @b0.sync
                def _(e):
                    for sm in allsems:
                        e.sem_clear(sm)
            block = es.enter_context(nc.Block())

            sig = {e: set() for e in ENGS}
            for name in self.q:
                for waits, fn, key, my in self.q[name]:
                    for (k, v) in waits:
                        if k[0] == "e":
                            sig[k[1]].add(v)
            for e in ENGS:
                if self.cnt[e]:
                    sig[e].add(self.cnt[e])
            rank = {}
            for e in ENGS:
                for i, v in enumerate(sorted(sig[e])):
                    rank[(e, v)] = i + 1

            def wval(k, v):
                return rank[(k[1], v)] if k[0] == "e" else v

            def run(name, engobj, final=False):
                for waits, fn, key, my in self.q[name]:
                    for (k, v) in waits:
                        engobj.wait_ge(semof(k), wval(k, v))
                    if isinstance(fn, tuple):
                        engobj.sem_clear(psem[fn[1]])
                    elif fn is not None:
                        ins = fn(engobj)
                        if key[0] in ("d", "p"):
                            ins.then_inc(semof(key), 16)
                        elif my in sig[key[1]]:
                            ins.then_inc(semof(key), 1)
                if final:
                    for k, v in fin.items():
                        engobj.wait_ge(semof(k), wval(k, v))

            @block.sync
            def _(e):
                run("sp", e, final=True)

            @block.tensor
            def _(e):
                run("pe", e)

            @block.scalar
            def _(e):
                run("act", e)

            @block.vector
            def _(e):
                run("dve", e)

            @block.gpsimd
            def _(e):
                run("pool", e)


def I(name, *a, **k):
    return lambda e: getattr(e, name)(*a, **k)


class Ring:
    def __init__(self, items):
        self.items = items
        self.i = 0

    def next(self):
        it = self.items[self.i % len(self.items)]
        self.i += 1
        return it


NROPE = 148
BFC = dict(ident=0, tri_diag=128, tri_prev=256, win_far=384, m4=512, e32=2560, ones=4608)
NBFC = 4736
F32C = dict(swap=0, id32=128)
NF32C = 160
PCF = dict(rope=0, thrc=NROPE * 16, pv=NROPE * 16 + 16, crel=NROPE * 16 + 80, hv=NROPE * 16 + 84)
NPCF = NROPE * 16 + 85
PCB = dict(eown=0, hv64=2048)
NPCB = 2112


def _static_tables():
    bf = np.zeros((128, NBFC), np.float32)
    k = np.arange(128)[:, None]
    q = np.arange(128)[None, :]
    bf[:, 0:128] = np.eye(128)
    bf[:, 128:256] = (q >= k)
    bf[:, 256:384] = (q <= k)
    bf[:, 384:512] = (q < k)
    for m in range(4):
        blk = np.zeros((128, 512), np.float32)
        for tq in range(4):
            if tq == m:
                blk[:, tq * 128:(tq + 1) * 128] = (q >= k)
            elif tq > m:
                blk[:, tq * 128:(tq + 1) * 128] = 1.0
        bf[:, 512 + m * 512: 512 + (m + 1) * 512] = blk
    b = np.arange(128)[:, None]
    for kt in range(16):
        i = np.arange(128)[None, :]
        bf[:, 2560 + kt * 128: 2560 + (kt + 1) * 128] = ((b % 32) == 2 * kt + (i >= 64))
    bf[:, 4608:4736] = 1.0
    f = np.zeros((128, NF32C), np.float32)
    f[:, 0:128] = (np.abs(k - q) == 64)
    f[0:32, 128:160] = np.eye(32)
    t16 = np.ascontiguousarray(np.broadcast_to(16.0 * np.arange(2048, dtype=np.float32)[None, :], (128, 2048)))
    return bf.astype(NPBF), f, t16


def _rope_rows(pos):
    inv = (500000.0 ** (-np.arange(0, 16, 2, dtype=np.float32) / np.float32(16))).astype(np.float32)
    ang = (pos.astype(np.float32)[:, None] * inv[None, :]).astype(np.float32)
    return np.concatenate([np.cos(ang), np.sin(ang)], axis=1).astype(np.float32)


def _percore_tables(qtr):
    T0 = OWN * qtr
    i = np.arange(128)
    f = np.zeros((128, NPCF), np.float32)
    rope = np.zeros((128, NROPE, 16), np.float32)
    for t in range(32):
        rope[:, t] = _rope_rows(T0 - OWN + 128 * t + i)
    for r in range(4):
        for j in range(-1, 4):
            rope[:, 32 + r * 5 + j + 1] = _rope_rows(T0 - OWN + 2048 + 512 * j + r + 4 * i)
    for r in range(16):
        for j in range(-1, 1):
            rope[:, 52 + r * 2 + j + 1] = _rope_rows(T0 - OWN + 2048 + 2048 * j + r + 16 * i)
    for kt in range(64):
        rope[:, 84 + kt] = _rope_rows(128 * kt + i)
    f[:, 0:NROPE * 16] = rope.reshape(128, -1)
    for ti in range(16):
        f[:, PCF["thrc"] + ti] = T0 + 128 * ti + i - 31
    for kt in range(64):
        f[:, PCF["pv"] + kt] = 1.0 if 128 * kt < T0 else 0.0
    for bt in range(4):
        f[:, PCF["crel"] + bt] = 16.0 * (16 * (128 * bt + i) + 31 - T0)
    f[:, PCF["hv"]] = 0.0 if qtr == 0 else 1.0
    lo = np.full((128, 16, 128), -3e4, np.float32)
    hi = np.full((128, 16, 128), 3e4, np.float32)
    m = np.arange(128)[None, :]
    for ti in range(16):
        cur = ((T0 + 128 * ti + i) // 64)[:, None]
        forced = (m == 0) | (m == cur) | (m == cur - 1)
        fut = m > cur
        lo[:, ti][forced] = 1e4
        hi[:, ti][forced] = 1e4
        lo[:, ti][fut] = -3e4
        hi[:, ti][fut] = -3e4
    lohi = np.concatenate([lo.reshape(128, -1), hi.reshape(128, -1)], axis=1)
    bfp = np.zeros((128, NPCB), np.float32)
    b = np.arange(128)[:, None]
    for j in range(16):
        ii = np.arange(128)[None, :]
        bfp[:, j * 128:(j + 1) * 128] = (b == 2 * (T0 // 128 + j) + (ii >= 64))
    bfp[:, 2048:2112] = 0.0 if qtr == 0 else 1.0
    return f, lohi.astype(np.float32), bfp.astype(NPBF)


class StopBuild(Exception):
    pass


class Builder:
    def __init__(self, debug=False, stop_after=None):
        self.debug = debug
        self.stop_after = stop_after
        self.nc = nc = bass.Bass("TRN2", target_bir_lowering=False)
        self.S = Sched(nc)
        self._decl = {}
        self._shapes = {
            "x_own": ([OWN, D], F32), "x_halo": ([OWN, D], F32), "x_full": ([S_LEN, D], F32), "w_in": ([D, IN_COLS], F32),
            "g_mix": ([128, 8], F32), "g_mlp": ([128, 8], F32), "g_fin": ([128, D], F32),
            "cmp_w1_k": ([2048, 256], F32), "cmp_w1_v": ([2048, 256], F32), "cmp_w2_k": ([256, 64], F32), "cmp_w2_v": ([256, 64], F32),
            "cmp_pos_k": ([32, 64], F32), "cmp_pos_v": ([32, 64], F32), "w_a": ([512, D], F32), "w_b": ([512, D], F32),
            "w_out": ([D, D], F32), "w_up": ([D, 4096], F32), "w_down": ([4096, D], F32),
            "c_bf": ([128, NBFC], BF16), "c_f32": ([128, NF32C], F32), "c_t16": ([128, 2048], F32),
            "pc_f": ([128, NPCF], F32), "pc_lohi": ([128, 4096], F32), "pc_bf": ([128, NPCB], BF16),
        }
        self.out = nc.dram_tensor("out", [OWN, D], F32, kind="ExternalOutput").ap()
        self.dbg = {}

    def __getattr__(self, name):
        sh = self.__dict__.get("_shapes", {})
        if name in sh:
            if name not in self._decl:
                self._decl[name] = self.nc.dram_tensor(name, list(sh[name][0]), sh[name][1], kind="ExternalInput").ap()
            return self._decl[name]
        raise AttributeError(name)

    def sb(self, es, name, shape, dt):
        return Tn(es.enter_context(self.nc.sbuf_tensor(name, list(shape), dt)))

    def ps(self, es, name, shape, dt):
        t = Tn(es.enter_context(self.nc.psum_tensor(name, list(shape), dt)))
        t.r.excl = True
        return t

    def ring(self, es, name, shape, dt, n):
        return Ring([self.sb(es, "%s%d" % (name, i), shape, dt) for i in range(n)])

    def dump(self, name, tn, ap, shape, dt):
        if not self.debug:
            return
        o = self.nc.dram_tensor("dbg_" + name, list(shape), dt, kind="ExternalOutput").ap()
        self.dbg[name] = True
        self.S.dma(I("dma_start", out=o, in_=ap), reads=[tn])

    def load_wslab(self, src_ap, ncols, gain, kchunks=8):
        S = self.S
        st = self.wst.next()
        sl = self.wsl.next()
        S.dma(I("dma_start", out=st.t[:, 0:kchunks, 0:ncols], in_=src_ap.rearrange("(c p) n -> p c n", p=128)), writes=[st])
        if gain is not None:
            gb = gain.t[:, 0:kchunks].unsqueeze(2).to_broadcast([128, kchunks, ncols])
            S.op("pool", I("tensor_tensor", out=sl.t[:, 0:kchunks, 0:ncols], in0=st.t[:, 0:kchunks, 0:ncols], in1=gb, op=ALU.mult),
                 reads=[st, gain], writes=[sl])
        else:
            S.op("pool", I("tensor_copy", out=sl.t[:, 0:kchunks, 0:ncols], in_=st.t[:, 0:kchunks, 0:ncols]), reads=[st], writes=[sl])
        return sl

    def cast_into(self, dst_tn, dst_ap_fn, src_ap, kchunks, ncols, gain, piece=512):
        S = self.S
        for c0 in range(0, ncols, piece):
            n = min(piece, ncols - c0)
            st = self.wst.next()
            S.dma(I("dma_start", out=st.t[:, 0:kchunks, 0:n], in_=src_ap[:, c0:c0 + n].rearrange("(c p) n -> p c n", p=128)), writes=[st])
            dst = dst_ap_fn(c0, n)
            engs = getattr(self, "cast_engs", ("pool",))
            self._ci = getattr(self, "_ci", 0) + 1
            ce = engs[self._ci % len(engs)]
            if gain is not None:
                gb = gain.t[:, 0:kchunks].unsqueeze(2).to_broadcast([128, kchunks, n])
                S.op(ce, I("tensor_tensor", out=dst, in0=st.t[:, 0:kchunks, 0:n], in1=gb, op=ALU.mult),
                     reads=[st, gain], writes=[dst_tn])
            else:
                S.op(ce, I("tensor_copy", out=dst, in_=st.t[:, 0:kchunks, 0:n]), reads=[st], writes=[dst_tn])

    def norm_tile(self, x_ap, ut_tn, ut_ap):
        S = self.S
        xt = self.xring.next()
        S.dma(I("dma_start", out=xt.t[:], in_=x_ap), writes=[xt])
        self.norm_sb(xt, xt.t[:], ut_tn, ut_ap)

    def norm_sb(self, xt, x_sb_ap, ut_tn, ut_ap, keep_rstd=None):
        S = self.S
        jk = self.junk.next()
        st = self.stat.next()
        S.op("act", I("activation", out=jk.t[:], in_=x_sb_ap, func=AF.Square, accum_out=st.t[:, 0:1]), reads=[xt], writes=[jk, st])
        S.op("act", I("activation", out=st.t[:, 1:2], in_=st.t[:, 0:1], func=AF.Sqrt, scale=1.0 / D, bias=self.epsc.t[:, 0:1]), reads=[st, self.epsc], writes=[st])
        S.op("dve", I("reciprocal", out=st.t[:, 2:3], in_=st.t[:, 1:2]), reads=[st], writes=[st])
        xn = self.xnring.next()
        S.op("dve", I("tensor_scalar", out=xn.t[:], in0=x_sb_ap, scalar1=st.t[:, 2:3], scalar2=None, op0=ALU.mult), reads=[xt, st], writes=[xn])
        pt = self.psT
        for c in range(8):
            S.op("pe", I("transpose", out=pt.t[:, c * 128:(c + 1) * 128], in_=xn.t[:, c * 128:(c + 1) * 128], identity=self.ident), reads=[xn, self.cbf], writes=[pt])
        if keep_rstd is None:
            S.op("act", I("copy", out=ut_ap, in_=pt.t[:, 0:1024].rearrange("p (c t) -> p c t", c=8)), reads=[pt], writes=[ut_tn])
        else:
            gb = keep_rstd.t[:, 0:8].unsqueeze(2).to_broadcast([128, 8, 128])
            S.op("dve", I("tensor_tensor", out=ut_ap, in0=pt.t[:, 0:1024].rearrange("p (c t) -> p c t", c=8), in1=gb, op=ALU.mult), reads=[pt, keep_rstd], writes=[ut_tn])
        return st

    def proj_tm(self, lhs_fn, lhs_tn, slab, c0, ncols, ps):
        for c in range(8):
            self.S.op("pe", I("matmul", ps.t[:, 0:ncols], lhsT=lhs_fn(c), rhs=slab.t[:, c, c0:c0 + ncols], start=(c == 0), stop=(c == 7)),
                      reads=[lhs_tn, slab], writes=[ps])

    def rope_evac(self, ps, pc0, nh, ropeidx, dst_tn, dst_ap, perm=False):
        S = self.S
        ro = PCF["rope"] + ropeidx * 16
        ta = self.rtmp.next()
        if not perm:
            psv = ps.t[:, pc0:pc0 + 64 * nh].rearrange("p (h d) -> p h d", h=nh)
            dv = dst_ap.rearrange("p (h d) -> p h d", h=nh)
            tav = ta.t[:, 0:nh * 32].rearrange("p (h d) -> p h d", h=nh)
            cos1 = self.pcf.t[:, ro:ro + 8].unsqueeze(1).to_broadcast([128, nh, 8])
            sin1 = self.pcf.t[:, ro + 8:ro + 16].unsqueeze(1).to_broadcast([128, nh, 8])
            sl = lambda v, a, b: v[:, :, a:b]
        else:
            psv = ps.t[:, pc0:pc0 + 512].rearrange("p (two hp d) -> p two hp d", two=2, hp=4)
            dv = dst_ap.rearrange("p (hp two d) -> p two hp d", two=2, hp=4)
            tav = ta.t[:, 0:256].rearrange("p (two hp d) -> p two hp d", two=2, hp=4)
            cos1 = self.pcf.t[:, ro:ro + 8].unsqueeze(1).unsqueeze(1).to_broadcast([128, 2, 4, 8])
            sin1 = self.pcf.t[:, ro + 8:ro + 16].unsqueeze(1).unsqueeze(1).to_broadcast([128, 2, 4, 8])
            sl = lambda v, a, b: v[:, :, :, a:b]
        S.op("dve", I("tensor_copy", out=sl(dv, 16, 64), in_=sl(psv, 16, 64)), reads=[ps], writes=[dst_tn])
        S.op("dve", I("tensor_tensor", out=sl(tav, 0, 8), in0=sl(psv, 0, 8), in1=cos1, op=ALU.mult), reads=[ps, self.pcf], writes=[ta])
        S.op("dve", I("tensor_tensor", out=sl(tav, 8, 16), in0=sl(psv, 8, 16), in1=cos1, op=ALU.mult), reads=[ps, self.pcf], writes=[ta])
        S.op("dve", I("tensor_tensor", out=sl(tav, 16, 24), in0=sl(psv, 8, 16), in1=sin1, op=ALU.mult), reads=[ps, self.pcf], writes=[ta])
        S.op("dve", I("tensor_tensor", out=sl(tav, 24, 32), in0=sl(psv, 0, 8), in1=sin1, op=ALU.mult), reads=[ps, self.pcf], writes=[ta])
        S.op("dve", I("tensor_tensor", out=sl(dv, 0, 8), in0=sl(tav, 0, 8), in1=sl(tav, 16, 24), op=ALU.subtract), reads=[ta], writes=[dst_tn])
        S.op("dve", I("tensor_tensor", out=sl(dv, 8, 16), in0=sl(tav, 8, 16), in1=sl(tav, 24, 32), op=ALU.add), reads=[ta], writes=[dst_tn])

    def build(self):
        nc, S = self.nc, self.S
        with ExitStack() as es0:
            self.cbf = self.sb(es0, "cbf", [128, NBFC], BF16)
            self.cf32 = self.sb(es0, "cf32", [128, NF32C], F32)
            self.pcf = self.sb(es0, "pcf", [128, NPCF], F32)
            self.pcb = self.sb(es0, "pcb", [128, NPCB], BF16)
            self.gmix = self.sb(es0, "gmix", [128, 8], F32)
            self.gmlp = self.sb(es0, "gmlp", [128, 8], F32)
            self.epsc = self.sb(es0, "epsc", [128, 1], F32)
            S.dma(I("dma_start", out=self.cbf.t[:], in_=self.c_bf), writes=[self.cbf])
            S.dma(I("dma_start", out=self.cf32.t[:], in_=self.c_f32), writes=[self.cf32])
            S.dma(I("dma_start", out=self.pcf.t[:], in_=self.pc_f), writes=[self.pcf])
            S.dma(I("dma_start", out=self.pcb.t[:], in_=self.pc_bf), writes=[self.pcb])
            S.dma(I("dma_start", out=self.gmix.t[:], in_=self.g_mix), writes=[self.gmix])
            S.dma(I("dma_start", out=self.gmlp.t[:], in_=self.g_mlp), writes=[self.gmlp])
            S.op("dve", I("memset", self.epsc.t[:], EPS), writes=[self.epsc])
            self.ident = self.cbf.t[:, 0:128]
            self.xring = self.ring(es0, "xr", [128, D], F32, 2)
            self.junk = self.ring(es0, "jk", [128, D], BF16, 1)
            self.stat = self.ring(es0, "st", [128, 4], F32, 4)
            self.xnring = self.ring(es0, "xn", [128, D], BF16, 2)
            self.rtmp = self.ring(es0, "rtmp", [128, 256], F32, 2)
            self.ptr = self.ring(es0, "ptr", [128, 512], BF16, 3)
            self.psA = Ring([self.ps(es0, "psA%d" % i, [128, 512], F32) for i in range(2)])
            self.psT = self.ps(es0, "psT", [128, 1024], BF16)
            self.psS = Ring([self.ps(es0, "psS%d" % i, [128, 512], F32) for i in range(2)])
            self.psO = Ring([self.ps(es0, "psO%d" % i, [128, 512], F32) for i in range(2)])
            self.psX = self.ps(es0, "psX", [128, 512], F32)
            self.ring3 = Ring([self.psS.items[0], self.psS.items[1], self.psX])
            self.yaT = self.sb(es0, "yaT", [128, 4, OWN], BF16)
            self.stopped = False
            self.phase_A(es0)
            if self.stopped:
                S.barrier()
                if self.stop_after in ("A3", "A"):
                    self.dump("yaT", self.yaT, self.yaT.t[:], [128, 4, OWN], BF16)
                self.fake_out()
                S.emit()
                return nc
            self.ybT = self.sb(es0, "ybT", [128, 4, OWN], BF16)
            S.barrier()
            if self.stop_after == "A":
                self.dump("yaT", self.yaT, self.yaT.t[:], [128, 4, OWN], BF16)
                self.fake_out()
                S.emit()
                return nc
            self.phase_B(es0)
            S.barrier()
            if self.stopped:
                self.fake_out()
                S.emit()
                return nc
            if self.stop_after == "B":
                self.dump("yaT", self.yaT, self.yaT.t[:], [128, 4, OWN], BF16)
                self.dump("ybT", self.ybT, self.ybT.t[:], [128, 4, OWN], BF16)
                self.fake_out()
                S.emit()
                return nc
            self.phase_C(es0)
            if self.debug:
                self.dump("yaT", self.yaT, self.yaT.t[:], [128, 4, OWN], BF16)
                self.dump("ybT", self.ybT, self.ybT.t[:], [128, 4, OWN], BF16)
            S.emit()
        return nc

    def fake_out(self):
        S = self.S
        xt = self.xring.next()
        for t in range(NT):
            S.dma(I("dma_start", out=xt.t[:], in_=self.x_own[t * 128:(t + 1) * 128, :]), writes=[xt])
            S.dma(I("dma_start", out=self.out[t * 128:(t + 1) * 128, :], in_=xt.t[:]), reads=[xt])

    def attn_unit(self, score_mms, n, mask_fn, pv_list):
        S = self.S
        if getattr(self, "_collect", None) is not None:
            self._collect.append((score_mms, n, mask_fn, pv_list, None))
            return
        pss = self.psS.next()
        for i, (l, r, rd) in enumerate(score_mms):
            S.op("pe", I("matmul", pss.t[:, 0:n], lhsT=l, rhs=r, start=(i == 0), stop=(i == len(score_mms) - 1)),
                 reads=rd, writes=[pss])
        pt = self.ptr.next()
        S.op("act", I("activation", out=pt.t[:, 0:n], in_=pss.t[:, 0:n], func=AF.Exp, scale=SCALE), reads=[pss], writes=[pt])
        if mask_fn is not None:
            mask_fn(pt)
        for (pso, out_ap, vaug, c0, ncol, st, sp, rd) in pv_list:
            S.op("pe", I("matmul", out_ap, lhsT=vaug, rhs=pt.t[:, c0:c0 + ncol], start=st, stop=sp),
                 reads=[pt] + rd, writes=[pso])

    def attn_seq(self, units, ring=None, depth=1):
        S = self.S
        ring = ring or self.psS
        units = list(units)
        pend = []
        for k in range(len(units) + depth):
            if k < len(units):
                u = units[k]
                score_mms, n = u[0], u[1]
                pss = ring.next()
                for i, (l, r, rd) in enumerate(score_mms):
                    S.op("pe", I("matmul", pss.t[:, 0:n], lhsT=l, rhs=r, start=(i == 0), stop=(i == len(score_mms) - 1)), reads=rd, writes=[pss])
                pend.append((u, pss))
            if k >= depth and pend:
                (pu, ppss) = pend.pop(0)
                n = pu[1]
                pt = self.ptr.next()
                S.op("act", I("activation", out=pt.t[:, 0:n], in_=ppss.t[:, 0:n], func=AF.Exp, scale=SCALE), reads=[ppss], writes=[pt])
                if pu[2] is not None:
                    pu[2](pt)
                for (pso, out_ap, vaug, c0, ncol, st, sp, rd) in pu[3]:
                    S.op("pe", I("matmul", out_ap, lhsT=vaug, rhs=pt.t[:, c0:c0 + ncol], start=st, stop=sp), reads=[pt] + rd, writes=[pso])
                if len(pu) > 4 and pu[4] is not None:
                    pu[4]()
        while pend:
            (pu, ppss) = pend.pop(0)
            n = pu[1]
            pt = self.ptr.next()
            S.op("act", I("activation", out=pt.t[:, 0:n], in_=ppss.t[:, 0:n], func=AF.Exp, scale=SCALE), reads=[ppss], writes=[pt])
            if pu[2] is not None:
                pu[2](pt)
            for (pso, out_ap, vaug, c0, ncol, st, sp, rd) in pu[3]:
                S.op("pe", I("matmul", out_ap, lhsT=vaug, rhs=pt.t[:, c0:c0 + ncol], start=st, stop=sp), reads=[pt] + rd, writes=[pso])
            if len(pu) > 4 and pu[4] is not None:
                pu[4]()

    def attn_steps(self, steps, ring):
        S = self.S
        prev = None
        for stp in list(steps) + [None]:
            cur = None
            if stp is not None:
                cur = []
                for u in stp:
                    score_mms, n = u[0], u[1]
                    pss = ring.next()
                    cur.append((u, pss))
                nmm = max(len(u[0]) for u in stp)
                for i in range(nmm):
                    for (u, pss) in cur:
                        if i < len(u[0]):
                            l, r, rd = u[0][i]
                            S.op("pe", I("matmul", pss.t[:, 0:u[1]], lhsT=l, rhs=r, start=(i == 0), stop=(i == len(u[0]) - 1)), reads=rd, writes=[pss])
            if prev is not None:
                for (pu, ppss) in prev:
                    n = pu[1]
                    pt = self.ptr.next()
                    S.op("act", I("activation", out=pt.t[:, 0:n], in_=ppss.t[:, 0:n], func=AF.Exp, scale=SCALE), reads=[ppss], writes=[pt])
                    if pu[2] is not None:
                        pu[2](pt)
                    for (pso, out_ap, vaug, c0, ncol, st, sp, rd) in pu[3]:
                        S.op("pe", I("matmul", out_ap, lhsT=vaug, rhs=pt.t[:, c0:c0 + ncol], start=st, stop=sp), reads=[pt] + rd, writes=[pso])
            prev = cur

    def mask_mul(self, pt, c0, n, mask_ap):
        self.S.op("dve", I("tensor_tensor", out=pt.t[:, c0:c0 + n], in0=pt.t[:, c0:c0 + n], in1=mask_ap, op=ALU.mult), reads=[pt, self.cbf], writes=[pt])

    def phase_A(self, es0):
        S = self.S
        with ExitStack() as esA:
            self.phase_A_body(esA)
        S.barrier()

    def phase_A_body(self, esA):
        S = self.S
        self.uTh = self.sb(esA, "uTh", [128, 8, OWN], BF16)
        self.uTo = self.sb(esA, "uTo", [128, 8, OWN], BF16)
        self.wst = self.ring(esA, "wstA", [128, 8, 384], F32, 2)
        self.wsl = self.ring(esA, "wslA", [128, 8, 384], BF16, 2)
        for t in range(NT):
            self.norm_tile(self.x_halo[t * 128:(t + 1) * 128, :], self.uTh, self.uTh.t[:, :, t * 128:(t + 1) * 128])
        for t in range(NT):
            self.norm_tile(self.x_own[t * 128:(t + 1) * 128, :], self.uTo, self.uTo.t[:, :, t * 128:(t + 1) * 128])
        if self.stop_after == "A0":
            self.stopped = True
            return
        with ExitStack() as es:
            self.phase_A_inner(es)

    def phase_A_inner(self, es):
        S = self.S
        if True:
            qT = self.sb(es, "a_qT", [128, OWN], BF16)
            kT = self.sb(es, "a_kT", [128, 32 * 128], BF16)
            vaug = self.sb(es, "a_v", [128, 32, 2, 128], BF16)
            qk = self.ring(es, "a_qk", [128, 256], BF16, 2)
            if os.environ.get("PADLOW"):
                pad = self.sb(es, "a_pad", [128, int(os.environ["PADLOW"]) * 256], F32)
            acc = [self.sb(es, "a_acc%d" % i, [128, OWN], F32) for i in range(2)]
            rd_ = self.ring(es, "a_rd", [128, 512], F32, 2)
            ones64 = self.cbf.t[:, BFC["ones"]:BFC["ones"] + 64]
            hv64 = self.pcb.t[:, PCB["hv64"]:PCB["hv64"] + 64]
            hvcol = self.pcf.t[:, PCF["hv"]:PCF["hv"] + 1]
            for p in range(4):
                for g, d in enumerate(DILS):
                    nt = NT // d
                    st = self.wst.next()
                    sl = self.wsl.next()
                    for i, cb in enumerate((C_QA, C_KA, C_VA)):
                        c0 = cb + g * 512 + p * 128
                        for kc in range(8):
                            S.dma(I("dma_start", out=st.t[:, kc, i * 128:(i + 1) * 128], in_=self.w_in[kc * 128:(kc + 1) * 128, c0:c0 + 128]), writes=[st])
                    gb = self.gmix.t[:, 0:8].unsqueeze(2).to_broadcast([128, 8, 384])
                    if int(os.environ.get("A1CUT", "99")) >= 0:
                        S.op("pool", I("tensor_tensor", out=sl.t[:, :, 0:384], in0=st.t[:, :, 0:384], in1=gb, op=ALU.mult), reads=[st, self.gmix], writes=[sl])
                    if int(os.environ.get("A1CUT", "99")) <= 0:
                        self.stopped = True
                        return
                    for r in range(d):
                        for j in range(-1, nt):
                            slot = r * (nt + 1) + j + 1
                            start = 2048 + 128 * d * j + r
                            if start < 2048:
                                ut, s0 = self.uTh, start
                            else:
                                ut, s0 = self.uTo, start - 2048
                            lhs = lambda c, ut=ut, s0=s0, d=d: ut.t[:, c, s0:s0 + 127 * d + 1:d]
                            ridx = (15 + slot) if g == 0 else ((32 + slot) if g == 1 else (52 + slot))
                            ps = self.psA.next()
                            halo = (j == -1)
                            if halo:
                                self.proj_tm(lhs, ut, sl, 128, 256, ps)
                                kc0, vc0 = 0, 128
                            else:
                                self.proj_tm(lhs, ut, sl, 0, 384, ps)
                                kc0, vc0 = 128, 256

                            CUT = int(os.environ.get("A1CUT", "99"))
                            if CUT <= 1:
                                continue
                            t = qk.next()
                            if not halo:
                                self.rope_evac(ps, 0, 4, ridx, t, t.t[:, 0:256])
                            else:
                                self.rope_evac(ps, kc0, 2, ridx, t, t.t[:, 128:256])
                            if CUT <= 2:
                                continue
                            vsrc = ps.t[:, vc0:vc0 + 128].rearrange("p (h d) -> p h d", h=2)
                            if halo:
                                S.op("dve", I("tensor_scalar", out=vaug.t[:, slot, :, 0:64], in0=vsrc, scalar1=hvcol, scalar2=None, op0=ALU.mult), reads=[ps, self.pcf], writes=[vaug])
                                for hh in range(2):
                                    S.op("pool", I("tensor_copy", out=vaug.t[:, slot, hh, 64:128], in_=hv64), reads=[self.pcb], writes=[vaug])
                            else:
                                S.op("act", I("copy", out=vaug.t[:, slot, :, 0:64], in_=vsrc), reads=[ps], writes=[vaug])
                                for hh in range(2):
                                    S.op("pool", I("tensor_copy", out=vaug.t[:, slot, hh, 64:128], in_=ones64), reads=[self.cbf], writes=[vaug])
                            if CUT <= 3:
                                continue
                            pt = self.psT
                            if not halo:
                                S.op("pe", I("transpose", out=pt.t[:, 0:128], in_=t.t[:, 0:128], identity=self.ident), reads=[t, self.cbf], writes=[pt])
                            S.op("pe", I("transpose", out=pt.t[:, 128:256], in_=t.t[:, 128:256], identity=self.ident), reads=[t, self.cbf], writes=[pt])
                            if not halo:
                                qi = r * nt + j
                                S.op("act", I("copy", out=qT.t[:, qi * 128:(qi + 1) * 128], in_=pt.t[:, 0:128]), reads=[pt], writes=[qT])
                            S.op("dve", I("tensor_copy", out=kT.t[:, slot * 128:(slot + 1) * 128], in_=pt.t[:, 128:256]), reads=[pt], writes=[kT])
                    if self.stop_after == "A1" and int(os.environ.get("A1CUT", "99")) < 99:
                        self.stopped = True
                        return
                    if self.stop_after == "A1":
                        self.dump("qT", qT, qT.t[:], [128, OWN], BF16)
                        self.dump("kT", kT, kT.t[:, 0:17 * 128], [128, 17 * 128], BF16)
                        self.dump("vaug", vaug, vaug.t[:, 0:17], [128, 17, 2, 128], BF16)
                        self.stopped = True
                        return
                    for hh in range(2):
                        pb = 64 * hh
                        banks = {}
                        ring3 = self.ring3
                        units = []
                        for r in range(d):
                            for j in range(-1, nt):
                                slot = r * (nt + 1) + j + 1
                                qlo = max(j, 0)
                                qhi = min(j + 1, nt - 1)
                                nq = qhi - qlo + 1
                                qc0 = (r * nt + qlo) * 128
                                n = nq * 128
                                if j == -1:
                                    mk = [(0, 128, self.cbf.t[:, BFC["tri_prev"]:BFC["tri_prev"] + 128])]
                                elif nq == 1:
                                    mk = [(0, 128, self.cbf.t[:, BFC["tri_diag"]:BFC["tri_diag"] + 128])]
                                else:
                                    mk = [(0, 256, self.cbf.t[:, BFC["tri_diag"]:BFC["tri_diag"] + 256])]

                                def mask_fn(pt, mk=mk):
                                    for (c0, nn, ap) in mk:
                                        self.mask_mul(pt, c0, nn, ap)
                                pv = []
                                for qt in range(qlo, qhi + 1):
                                    qi = r * nt + qt
                                    if (qt == j + 1) and (qi % 4 == 0):
                                        banks[qi // 4] = self.psO.next()
                                    pso = banks[qi // 4]
                                    col = (qi % 4) * 128
                                    pv.append((pso, pso.t[:, col:col + 128], vaug.t[:, slot, hh, :], (qt - qlo) * 128, 128, qt == j + 1, qt == j, [vaug]))
                                after = None
                                if j >= 0 and (r * nt + j) % 4 == 3:
                                    bk = (r * nt + j) // 4
                                    pso = banks[bk]
                                    av = acc[hh].t[:]
                                    if d == 1:
                                        dst = av[:, bk * 512:(bk + 1) * 512]
                                        src = pso.t[:, 0:512]
                                    elif d == 4:
                                        dst = av.rearrange("p (i r) -> p r i", r=4)[:, r, :]
                                        src = pso.t[:, 0:512]
                                    else:
                                        dst = av.rearrange("p (i r) -> p r i", r=16)[:, 4 * bk:4 * bk + 4, :]
                                        src = pso.t[:, 0:512].rearrange("p (r i) -> p r i", r=4)

                                    def after(dst=dst, src=src, pso=pso, hh=hh, g=g):
                                        if g == 0:
                                            S.op("act", I("copy", out=dst, in_=src), reads=[pso], writes=[acc[hh]])
                                        else:
                                            S.op("dve", I("tensor_tensor", out=dst, in0=dst, in1=src, op=ALU.add), reads=[pso, acc[hh]], writes=[acc[hh]])
                                units.append(([(kT.t[pb:pb + 64, slot * 128:(slot + 1) * 128], qT.t[pb:pb + 64, qc0:qc0 + n], [kT, qT])], n, mask_fn, pv, after))
                        self.attn_seq(units, ring=ring3, depth=2)
                if self.stop_after == "A2":
                    self.stopped = True
                    return
                for hh in range(2):
                    for c in range(4):
                        self.finalize(acc[hh], acc[hh].t[:, c * 512:(c + 1) * 512], hh, None, rd_, self.yaT, self.yaT.t[:, p, c * 512:(c + 1) * 512], first=True, last=True, ybacc=None)
                if self.stop_after == "A3":
                    self.stopped = True
                    return
        S.barrier()

    def finalize(self, src_tn, src_ap, hh, gate_row, rdring, dst_tn, dst_ap, first, last, ybacc):
        S = self.S
        psx = self.psX
        S.op("pe", I("matmul", psx.t[:, 0:512], lhsT=self.cf32.t[:, 0:128], rhs=src_ap, start=True, stop=True), reads=[src_tn, self.cf32], writes=[psx])
        rd = rdring.next()
        lo, hi = 64 * hh, 64 * hh + 64
        if hh == 0:
            den = psx.t[0:64, 0:512]
            num = src_ap[0:64, :]
        else:
            den = src_ap[64:128, :]
            num = psx.t[64:128, 0:512]
        S.op("dve", I("tensor_scalar", out=rd.t[lo:hi, :], in0=den, scalar1=1e-30, scalar2=None, op0=ALU.max), reads=[psx, src_tn], writes=[rd])
        S.op("dve", I("reciprocal", out=rd.t[lo:hi, :], in_=rd.t[lo:hi, :]), reads=[rd], writes=[rd])
        if gate_row is None:
            S.op("dve", I("tensor_tensor", out=dst_ap[lo:hi, :], in0=num, in1=rd.t[lo:hi, :], op=ALU.mult), reads=[psx, src_tn, rd], writes=[dst_tn])
            return
        S.op("dve", I("tensor_tensor", out=rd.t[lo:hi, :], in0=num, in1=rd.t[lo:hi, :], op=ALU.mult), reads=[psx, src_tn, rd], writes=[rd])
        gsel, gbT, gcols = gate_row
        S.op("pe", I("matmul", psx.t[:, 0:512], lhsT=gsel.t[:], rhs=gcols, start=True, stop=True), reads=[gbT, gsel, rd], writes=[psx])
        if first:
            S.op("dve", I("tensor_tensor", out=ybacc.t[lo:hi, :], in0=rd.t[lo:hi, :], in1=psx.t[lo:hi, 0:512], op=ALU.mult), reads=[psx, rd], writes=[ybacc])
        else:
            S.op("dve", I("tensor_tensor", out=rd.t[lo:hi, :], in0=rd.t[lo:hi, :], in1=psx.t[lo:hi, 0:512], op=ALU.mult), reads=[psx, rd], writes=[rd])
            if last:
                S.op("dve", I("tensor_tensor", out=dst_ap[lo:hi, :], in0=rd.t[lo:hi, :], in1=ybacc.t[lo:hi, :], op=ALU.add), reads=[rd, ybacc], writes=[dst_tn])
            else:
                S.op("dve", I("tensor_tensor", out=ybacc.t[lo:hi, :], in0=rd.t[lo:hi, :], in1=ybacc.t[lo:hi, :], op=ALU.add), reads=[rd, ybacc], writes=[ybacc])

    def phase_B(self, es0):
        S = self.S
        cbf, pcf, pcb = self.cbf, self.pcf, self.pcb
        ones64 = cbf.t[:, BFC["ones"]:BFC["ones"] + 64]
        ones2 = cbf.t[:, BFC["ones"]:BFC["ones"] + 128].rearrange("p (g d) -> p g d", g=2)
        with ExitStack() as esB:
            kslcT = self.sb(esB, "b_kslcT", [128, 48 * 128], BF16)
            vslc = self.sb(esB, "b_vslc", [128, 48, 2, 128], BF16)
            kcT = self.sb(esB, "b_kcT", [128, 512], BF16)
            vc = self.sb(esB, "b_vc", [128, 4, 2, 128], BF16)
            S.op("pool", I("memset", kcT.t[:], 0.0), writes=[kcT])
            S.op("pool", I("memset", vc.t[:], 0.0), writes=[vc])
            with ExitStack() as es:
                self.wst = self.ring(es, "wstB", [128, 8, 256], F32, 2)
                kcmpT = self.sb(es, "b_kcmpT", [128, S_LEN], BF16)
                vcmpT = self.sb(es, "b_vcmpT", [128, S_LEN], BF16)
                slab = self.sb(es, "b_slab", [128, 8, 512], BF16)
                uTt = self.ring(es, "b_uTt", [128, 8, 128], BF16, 3)
                tm = self.ring(es, "b_tm", [128, 512], BF16, 2)
                for dcol, scol in ((0, 0), (128, 256), (256, 128), (384, 384)):
                    self.cast_into(slab, lambda c0, n, dcol=dcol: slab.t[:, :, dcol + c0:dcol + c0 + n], self.w_in[:, C_KVB + scol:C_KVB + scol + 128], 8, 128, self.gmix, piece=128)
                b2u = {}

                def b2_stage1(kt):
                    u = uTt.next()
                    self.norm_tile(self.x_full[kt * 128:(kt + 1) * 128, :], u, u.t[:])
                    b2u[kt] = u

                def b2_stage2(kt):
                    u = b2u.pop(kt)
                    ps = self.psA.next()
                    self.proj_tm(lambda c, u=u: u.t[:, c, :], u, slab, 0, 512, ps)
                    t = tm.next()
                    self.rope_evac(ps, 0, 4, 84 + kt, t, t.t[:, 0:256])
                    S.op("act", I("copy", out=t.t[:, 256:384], in_=ps.t[:, 256:384]), reads=[ps], writes=[t])
                    if kt < 48:
                        pvc = pcf.t[:, PCF["pv"] + kt:PCF["pv"] + kt + 1]
                        S.op("dve", I("tensor_scalar", out=vslc.t[:, kt, :, 0:64], in0=ps.t[:, 384:512].rearrange("p (g d) -> p g d", g=2), scalar1=pvc, scalar2=None, op0=ALU.mult), reads=[ps, pcf], writes=[vslc])
                        S.op("pool", I("tensor_scalar", out=vslc.t[:, kt, :, 64:128], in0=ones2, scalar1=pvc, scalar2=None, op0=ALU.mult), reads=[cbf, pcf], writes=[vslc])
                    pt = self.psT
                    for k in range(3):
                        S.op("pe", I("transpose", out=pt.t[:, k * 128:(k + 1) * 128], in_=t.t[:, k * 128:(k + 1) * 128], identity=self.ident), reads=[t, cbf], writes=[pt])
                    S.op("act", I("copy", out=kcmpT.t[:, kt * 128:(kt + 1) * 128], in_=pt.t[:, 0:128]), reads=[pt], writes=[kcmpT])
                    S.op("act", I("copy", out=vcmpT.t[:, kt * 128:(kt + 1) * 128], in_=pt.t[:, 256:384]), reads=[pt], writes=[vcmpT])
                    if kt < 48:
                        S.op("act", I("copy", out=kslcT.t[:, kt * 128:(kt + 1) * 128], in_=pt.t[:, 128:256]), reads=[pt], writes=[kslcT])

                b2_stage1(0)
                for kt in range(64):
                    if kt + 1 < 64:
                        b2_stage1(kt + 1)
                    b2_stage2(kt)

                if self.stop_after == "B2":
                    self.dump("kslcT", kslcT, kslcT.t[:], [128, 48 * 128], BF16)
                    self.dump("kcmpT", kcmpT, kcmpT.t[:], [128, S_LEN], BF16)
                    self.dump("vslc", vslc, vslc.t[:], [128, 48, 2, 128], BF16)
                    self.stopped = True
                    return
                w1sb = self.sb(es, "b_w1", [128, 32, 256], BF16)
                w2sb = self.sb(es, "b_w2", [128, 2, 128], BF16)
                posb = self.sb(es, "b_posb", [32, 128], BF16)
                posf = self.sb(es, "b_posf", [32, 64], F32)
                posT = self.sb(es, "b_posT", [128, 32], BF16)
                b1sb = self.sb(es, "b_b1", [128, 2], F32)
                gel = [self.sb(es, "b_gel%d" % i, [128, 512], BF16) for i in range(2)]
                hA = self.sb(es, "b_hA", [128, 512], F32)
                hB = self.sb(es, "b_hB", [128, 512], F32)
                for kv in range(2):
                    src = kcmpT if kv == 0 else vcmpT
                    w1d = self.cmp_w1_k if kv == 0 else self.cmp_w1_v
                    w2d = self.cmp_w2_k if kv == 0 else self.cmp_w2_v
                    posd = self.cmp_pos_k if kv == 0 else self.cmp_pos_v
                    w1v = w1d.rearrange("(j d) h -> d j h", d=64)
                    for j0 in range(0, 32, 8):
                        st = self.wst.next()
                        for half in range(2):
                            S.dma(I("dma_start", out=st.t[64 * half:64 * half + 64, :, :], in_=w1v[:, j0:j0 + 8, :]), writes=[st])
                        S.op("pool", I("tensor_copy", out=w1sb.t[:, j0:j0 + 8, :], in_=st.t[:, :, :]), reads=[st], writes=[w1sb])
                    st = self.wst.next()
                    S.dma(I("dma_start", out=st.t[:, 0:2, 0:64], in_=w2d.rearrange("(c p) n -> p c n", p=128)), writes=[st])
                    S.op("pool", I("tensor_copy", out=w2sb.t[:, :, 0:64], in_=st.t[:, 0:2, 0:64]), reads=[st], writes=[w2sb])
                    S.op("pool", I("tensor_copy", out=w2sb.t[:, :, 64:128], in_=st.t[:, 0:2, 0:64]), reads=[st], writes=[w2sb])
                    S.dma(I("dma_start", out=posf.t[:], in_=posd), writes=[posf])
                    S.op("dve", I("tensor_copy", out=posb.t[:, 0:64], in_=posf.t[:]), reads=[posf], writes=[posb])
                    S.op("dve", I("tensor_copy", out=posb.t[:, 64:128], in_=posf.t[:]), reads=[posf], writes=[posb])
                    pt = self.psT
                    S.op("pe", I("transpose", out=pt.t[:, 0:32], in_=posb.t[:], identity=cbf.t[0:32, 0:32]), reads=[posb, cbf], writes=[pt])
                    S.op("act", I("copy", out=posT.t[:], in_=pt.t[:, 0:32]), reads=[pt], writes=[posT])
                    psx = self.psX
                    for mh in range(2):
                        for j in range(32):
                            S.op("pe", I("matmul", psx.t[:, mh:mh + 1], lhsT=w1sb.t[0:64, j, mh * 128:(mh + 1) * 128], rhs=posT.t[0:64, j:j + 1], start=(j == 0), stop=(j == 31)), reads=[w1sb, posT], writes=[psx])
                    S.op("dve", I("tensor_copy", out=b1sb.t[:], in_=psx.t[:, 0:2]), reads=[psx], writes=[b1sb])
                    for g in range(2):
                        pb = 64 * g
                        for mh in range(2):
                            ps = self.psA.next()
                            for j in range(32):
                                S.op("pe", I("matmul", ps.t[:, 0:511], lhsT=w1sb.t[pb:pb + 64, j, mh * 128:(mh + 1) * 128], rhs=src.t[pb:pb + 64, j:j + 16 * 510 + 1:16], start=(j == 0), stop=(j == 31)), reads=[w1sb, src], writes=[ps])
                            S.op("act", I("activation", out=hA.t[:, 0:511], in_=ps.t[:, 0:511], func=AF.Identity, bias=b1sb.t[:, mh:mh + 1]), reads=[ps, b1sb], writes=[hA])
                            S.op("dve", I("tensor_tensor", out=hB.t[:, 0:511], in0=hA.t[:, 0:511], in1=hA.t[:, 0:511], op=ALU.mult), reads=[hA], writes=[hB])
                            S.op("dve", I("tensor_scalar", out=hB.t[:, 0:511], in0=hB.t[:, 0:511], scalar1=0.044715, scalar2=1.0, op0=ALU.mult, op1=ALU.add), reads=[hB], writes=[hB])
                            S.op("dve", I("tensor_tensor", out=hB.t[:, 0:511], in0=hB.t[:, 0:511], in1=hA.t[:, 0:511], op=ALU.mult), reads=[hA, hB], writes=[hB])
                            S.op("act", I("activation", out=hB.t[:, 0:511], in_=hB.t[:, 0:511], func=AF.Sigmoid, scale=2.0 * 0.7978845608028654), reads=[hB], writes=[hB])
                            S.op("dve", I("tensor_tensor", out=gel[mh].t[:, 0:511], in0=hA.t[:, 0:511], in1=hB.t[:, 0:511], op=ALU.mult), reads=[hA, hB], writes=[gel[mh]])
                        if kv == 0:
                            ps = self.psA.next()
                            for mh in range(2):
                                S.op("pe", I("matmul", ps.t[:, 0:511], lhsT=w2sb.t[:, mh, :], rhs=gel[mh].t[:, 0:511], start=(mh == 0), stop=(mh == 1)), reads=[w2sb, gel[mh]], writes=[ps])
                            S.op("act", I("copy", out=kcT.t[pb:pb + 64, 0:511], in_=ps.t[pb:pb + 64, 0:511]), reads=[ps], writes=[kcT])
                        else:
                            for bt in range(4):
                                n = 128 if bt < 3 else 127
                                ps = self.psA.next()
                                for mh in range(2):
                                    S.op("pe", I("matmul", ps.t[0:n, 0:64], lhsT=gel[mh].t[:, bt * 128:bt * 128 + n], rhs=w2sb.t[:, mh, 0:64], start=(mh == 0), stop=(mh == 1)), reads=[w2sb, gel[mh]], writes=[ps])
                                S.op("act", I("copy", out=vc.t[0:n, bt, g, 0:64], in_=ps.t[0:n, 0:64]), reads=[ps], writes=[vc])
                                S.op("pool", I("tensor_copy", out=vc.t[0:n, bt, g, 64:128], in_=ones64[0:n, :]), reads=[cbf], writes=[vc])
            S.barrier()
            if self.stop_after == "B3":
                self.dump("kcT", kcT, kcT.t[:], [128, 512], BF16)
                self.dump("vc", vc, vc.t[:], [128, 4, 2, 128], BF16)
                self.stopped = True
                return
            qbT = self.sb(esB, "b_qbT", [128, 4, OWN], BF16)
            gbT = self.sb(esB, "b_gbT", [32, OWN], F32)
            kwinT = self.sb(esB, "b_kwinT", [128, 20 * 128], BF16)
            vwin = self.sb(esB, "b_vwin", [128, 20, 2, 128], BF16)
            kso = self.sb(esB, "b_kso", [128, OWN], BF16)
            vso = self.sb(esB, "b_vso", [128, 16, 2, 128], BF16)
            S.op("pool", I("memset", gbT.t[:], 0.0), writes=[gbT])
            hvcol = pcf.t[:, PCF["hv"]:PCF["hv"] + 1]
            with ExitStack() as es:
                self.wst = self.ring(es, "wstB1", [128, 8, 256], F32, 2)
                slq = self.sb(es, "b_slq", [128, 8, 512], BF16)
                slkv = self.sb(es, "b_slkv", [128, 8, 512], BF16)
                slg = self.sb(es, "b_slg", [128, 8, 32], BF16)
                uTt = self.ring(es, "b_uTt1", [128, 8, 128], BF16, 2)
                tq = self.ring(es, "b_tq", [128, 512], BF16, 2)
                tk = self.ring(es, "b_tk", [128, 256], BF16, 2)
                self.cast_into(slq, lambda c0, n: slq.t[:, :, c0:c0 + n], self.w_in[:, C_QB:C_QB + 512], 8, 512, self.gmix, piece=256)
                self.cast_into(slkv, lambda c0, n: slkv.t[:, :, c0:c0 + n], self.w_in[:, C_KVB + 256:C_KVB + 768], 8, 512, self.gmix, piece=256)
                self.cast_into(slg, lambda c0, n: slg.t[:, :, c0:c0 + n], self.w_in[:, C_GB:C_GB + 24], 8, 24, self.gmix, piece=256)
                for e_ in range(12, 32):
                    slot = e_ - 12
                    own = e_ >= 16
                    i = e_ - 16
                    u = uTt.next()
                    xs = self.x_own[i * 128:(i + 1) * 128, :] if own else self.x_halo[e_ * 128:(e_ + 1) * 128, :]
                    self.norm_tile(xs, u, u.t[:])
                    lhs = lambda c, u=u: u.t[:, c, :]
                    if own:
                        ps = self.psA.next()
                        self.proj_tm(lhs, u, slq, 0, 512, ps)
                        t = tq.next()
                        self.rope_evac(ps, 0, 8, e_, t, t.t[:, 0:512], perm=True)
                        pt = self.psT
                        for hp in range(4):
                            S.op("pe", I("transpose", out=pt.t[:, hp * 128:(hp + 1) * 128], in_=t.t[:, hp * 128:(hp + 1) * 128], identity=self.ident), reads=[t, cbf], writes=[pt])
                        S.op("act", I("copy", out=qbT.t[:, :, i * 128:(i + 1) * 128], in_=pt.t[:, 0:512].rearrange("p (h t) -> p h t", h=4)), reads=[pt], writes=[qbT])
                        psx = self.psX
                        for c in range(8):
                            S.op("pe", I("matmul", psx.t[0:24, 0:128], lhsT=slg.t[:, c, 0:24], rhs=u.t[:, c, :], start=(c == 0), stop=(c == 7)), reads=[slg, u], writes=[psx])
                        S.op("act", I("activation", out=gbT.t[0:24, i * 128:(i + 1) * 128], in_=psx.t[0:24, 0:128], func=AF.Sigmoid), reads=[psx], writes=[gbT])
                    ps = self.psA.next()
                    t = tk.next()
                    if own:
                        self.proj_tm(lhs, u, slkv, 0, 512, ps)
                        self.rope_evac(ps, 0, 2, e_, t, t.t[:, 0:128])
                        self.rope_evac(ps, 256, 2, e_, t, t.t[:, 128:256])
                        S.op("dve", I("tensor_copy", out=vso.t[:, i, :, 0:64], in_=ps.t[:, 128:256].rearrange("p (g d) -> p g d", g=2)), reads=[ps], writes=[vso])
                        S.op("pool", I("tensor_copy", out=vso.t[:, i, :, 64:128], in_=ones2), reads=[cbf], writes=[vso])
                        S.op("dve", I("tensor_copy", out=vwin.t[:, slot, :, 0:64], in_=ps.t[:, 384:512].rearrange("p (g d) -> p g d", g=2)), reads=[ps], writes=[vwin])
                        S.op("pool", I("tensor_copy", out=vwin.t[:, slot, :, 64:128], in_=ones2), reads=[cbf], writes=[vwin])
                    else:
                        self.proj_tm(lhs, u, slkv, 256, 256, ps)
                        self.rope_evac(ps, 0, 2, e_, t, t.t[:, 128:256])
                        S.op("dve", I("tensor_scalar", out=vwin.t[:, slot, :, 0:64], in0=ps.t[:, 128:256].rearrange("p (g d) -> p g d", g=2), scalar1=hvcol, scalar2=None, op0=ALU.mult), reads=[ps, pcf], writes=[vwin])
                        S.op("pool", I("tensor_scalar", out=vwin.t[:, slot, :, 64:128], in0=ones2, scalar1=hvcol, scalar2=None, op0=ALU.mult), reads=[cbf, pcf], writes=[vwin])
                    pt = self.psT
                    if own:
                        S.op("pe", I("transpose", out=pt.t[:, 0:128], in_=t.t[:, 0:128], identity=self.ident), reads=[t, cbf], writes=[pt])
                    S.op("pe", I("transpose", out=pt.t[:, 128:256], in_=t.t[:, 128:256], identity=self.ident), reads=[t, cbf], writes=[pt])
                    if own:
                        S.op("act", I("copy", out=kso.t[:, i * 128:(i + 1) * 128], in_=pt.t[:, 0:128]), reads=[pt], writes=[kso])
                    S.op("act", I("copy", out=kwinT.t[:, slot * 128:(slot + 1) * 128], in_=pt.t[:, 128:256]), reads=[pt], writes=[kwinT])
            S.barrier()
            biasT = self.sb(esB, "b_biasT", [128, 2, OWN], BF16)
            t16 = self.sb(esB, "b_t16", [128, 2048], F32)
            S.dma(I("dma_start", out=t16.t[:], in_=self.c_t16), writes=[t16])
            with ExitStack() as es:
                et = self.ring(es, "b_et", [128, 512], F32, 4)
                pp = self.ring(es, "b_pp", [128, 520], F32, 4)
                lohi = self.ring(es, "b_lohi", [128, 256], F32, 2)
                imp = self.ring(es, "b_imp", [128, 128], F32, 2)
                imp2 = self.ring(es, "b_imp2", [128, 128], F32, 2)
                sm = self.ring(es, "b_sm", [128, 24], F32, 8)
                btm = self.ring(es, "b_btm", [128, 128], BF16, 2)
                for pq in pp.items:
                    S.op("pool", I("memset", pq.t[:], 0.0), writes=[pq])
                for i in range(NT):
                    lh = lohi.next()
                    S.dma(I("dma_start", out=lh.t[:, 0:128], in_=self.pc_lohi[:, i * 128:(i + 1) * 128]), writes=[lh])
                    S.dma(I("dma_start", out=lh.t[:, 128:256], in_=self.pc_lohi[:, 2048 + i * 128:2048 + (i + 1) * 128]), writes=[lh])
                    thr_i = pcf.t[:, PCF["thrc"] + i:PCF["thrc"] + i + 1]
                    def gen(g, i=i, lh=lh, thr_i=thr_i):
                        pb = 64 * g
                        P = pp.next()
                        P2 = pp.next()
                        for r in range(4):
                            ps = self.psS.next()
                            S.op("pe", I("matmul", ps.t[:, 0:511], lhsT=qbT.t[pb:pb + 64, r, i * 128:(i + 1) * 128], rhs=kcT.t[pb:pb + 64, 0:511], start=True, stop=True), reads=[qbT, kcT], writes=[ps])
                            e_ = et.next()
                            s_ = sm.next()
                            eng = "dve"
                            Pr = P if r % 2 == 0 else P2
                            S.op("act", I("activation", out=e_.t[:, 0:511], in_=ps.t[:, 0:511], func=AF.Exp, scale=SCALE), reads=[ps], writes=[e_])
                            S.op(eng, I("scalar_tensor_tensor", out=e_.t[:, 0:511], in0=t16.t[:, 0:511], scalar=thr_i, in1=e_.t[:, 0:511], op0=ALU.is_le, op1=ALU.mult, accum_out=s_.t[:, 0:1]), reads=[t16, pcf, e_], writes=[e_, s_])
                            yield
                            S.op(eng, I("tensor_scalar", out=s_.t[:, 1:2], in0=s_.t[:, 0:1], scalar1=1e-30, scalar2=None, op0=ALU.max), reads=[s_], writes=[s_])
                            yield
                            S.op("dve", I("reciprocal", out=s_.t[:, 2:3], in_=s_.t[:, 1:2]), reads=[s_], writes=[s_])
                            yield
                            if r < 2:
                                S.op(eng, I("tensor_scalar", out=Pr.t[:, 1:512], in0=e_.t[:, 0:511], scalar1=s_.t[:, 2:3], scalar2=None, op0=ALU.mult), reads=[e_, s_], writes=[Pr])
                                yield
                            else:
                                S.op(eng, I("scalar_tensor_tensor", out=Pr.t[:, 1:512], in0=e_.t[:, 0:511], scalar=s_.t[:, 2:3], in1=Pr.t[:, 1:512], op0=ALU.mult, op1=ALU.add), reads=[e_, s_, Pr], writes=[Pr])
                                yield
                        S.op("dve", I("tensor_tensor", out=P.t[:, 1:512], in0=P.t[:, 1:512], in1=P2.t[:, 1:512], op=ALU.add), reads=[P, P2], writes=[P])
                        yield
                        im = imp.next()
                        S.op("dve", I("tensor_tensor", out=im.t[:], in0=P.t[:, 0:512:4], in1=P.t[:, 1:513:4], op=ALU.add), reads=[P], writes=[im])
                        yield
                        for k in range(2, 5):
                            S.op("dve", I("tensor_tensor", out=im.t[:], in0=im.t[:], in1=P.t[:, k:k + 512:4], op=ALU.add), reads=[P, im], writes=[im])
                            yield
                        S.op("dve", I("tensor_tensor", out=im.t[:], in0=im.t[:], in1=lh.t[:, 0:128], op=ALU.max), reads=[lh, im], writes=[im])
                        yield
                        S.op("dve", I("tensor_tensor", out=im.t[:], in0=im.t[:], in1=lh.t[:, 128:256], op=ALU.min), reads=[lh, im], writes=[im])
                        yield
                        s_ = sm.next()
                        i2 = imp2.next()
                        S.op("dve", I("max", out=s_.t[:, 0:8], in_=im.t[:]), reads=[im], writes=[s_])
                        yield
                        S.op("dve", I("match_replace", out=i2.t[:], in_to_replace=s_.t[:, 0:8], in_values=im.t[:], imm_value=-1e9), reads=[im, s_], writes=[i2])
                        yield
                        S.op("dve", I("max", out=s_.t[:, 8:16], in_=i2.t[:]), reads=[i2], writes=[s_])
                        yield
                        S.op("dve", I("tensor_scalar", out=s_.t[:, 16:17], in0=s_.t[:, 15:16], scalar1=-1.5e4, scalar2=None, op0=ALU.max), reads=[s_], writes=[s_])
                        yield
                        bt_ = btm.next()
                        S.op("dve", I("tensor_scalar", out=bt_.t[:], in0=im.t[:], scalar1=s_.t[:, 16:17], scalar2=-30000.0, op0=ALU.is_lt, op1=ALU.mult), reads=[im, s_], writes=[bt_])
                        yield
                        pt = self.psT
                        S.op("pe", I("transpose", out=pt.t[:, 0:128], in_=bt_.t[:], identity=self.ident), reads=[bt_, cbf], writes=[pt])
                        S.op("act", I("copy", out=biasT.t[:, g, i * 128:(i + 1) * 128], in_=pt.t[:, 0:128]), reads=[pt], writes=[biasT])

                    gens = [gen(0), gen(1)]
                    while gens:
                        for gg in list(gens):
                            try:
                                next(gg)
                            except StopIteration:
                                gens.remove(gg)
            S.barrier()
            if self.stop_after == "B6":
                self.dump("biasT", biasT, biasT.t[:], [128, 2, OWN], BF16)
                self.dump("qbT", qbT, qbT.t[:], [128, 4, OWN], BF16)
                self.stopped = True
                return
            with ExitStack() as es:
                osb = self.ring(es, "b_osb", [128, 512], F32, 2)
                rdr = self.ring(es, "b_rd", [128, 512], F32, 2)
                ybacc = self.sb(es, "b_ybacc", [128, 512], F32)
                ybacc2 = self.sb(es, "b_ybacc2", [128, 512], F32)
                self.ptr = self.ring(es, "b_ptr", [128, 512], BF16, 5)
                gsel = self.ring(es, "b_gsel", [32, 128], F32, 6)
                id32 = self.cf32.t[0:32, F32C["id32"]:F32C["id32"] + 32]
                eown = pcb.t[:, PCB["eown"]:PCB["eown"] + 2048]
                e32 = cbf.t[:, BFC["e32"]:BFC["e32"] + 2048]
                m4 = cbf.t[:, BFC["m4"]:BFC["m4"] + 2048]
                tri_diag = cbf.t[:, BFC["tri_diag"]:BFC["tri_diag"] + 128]
                win_far = cbf.t[:, BFC["win_far"]:BFC["win_far"] + 128]
                BRS = os.environ.get("BRS", "012")
                ps4 = Ring([self.psS.items[0], self.psS.items[1], self.psA.items[0], self.psA.items[1]])
                ybaccs = [ybacc, ybacc2]
                for hp in range(4):
                    sels = {}
                    for gi in range(2):
                        h = hp + 4 * gi
                        for br in range(3):
                            gs = gsel.next()
                            jrow = h * 3 + br
                            S.op("pool", I("tensor_copy", out=gs.t[:], in_=id32[:, jrow:jrow + 1].to_broadcast([32, 128])), reads=[self.cf32], writes=[gs])
                            sels[(gi, br)] = gs
                    for c in range(4):
                        gcols = gbT.t[0:32, c * 512:(c + 1) * 512]
                        dst = self.ybT.t[:, hp, c * 512:(c + 1) * 512]
                        Qs = [qbT.t[64 * gi:64 * gi + 64, hp, c * 512:(c + 1) * 512] for gi in range(2)]

                        def fin(psos, br, first, last):
                            for gi in range(2):
                                o = osb.next()
                                S.op("act", I("copy", out=o.t[:], in_=psos[gi].t[:]), reads=[psos[gi]], writes=[o])
                                self.finalize(o, o.t[:], gi, (sels[(gi, br)], gbT, gcols), rdr, self.ybT, dst, first=first, last=last, ybacc=ybaccs[gi])
                        psos = [self.psO.next(), self.psO.next()]
                        steps = []
                        for bt in range(4):
                            crel = pcf.t[:, PCF["crel"] + bt:PCF["crel"] + bt + 1]

                            def mask_c(pt, crel=crel, c=c):
                                S.op("dve", I("scalar_tensor_tensor", out=pt.t[:, 0:512], in0=t16.t[:, c * 512:(c + 1) * 512], scalar=crel, in1=pt.t[:, 0:512], op0=ALU.is_ge, op1=ALU.mult), reads=[t16, pcf, pt], writes=[pt])
                            stp = []
                            for gi in range(2):
                                pb = 64 * gi
                                stp.append(([(kcT.t[pb:pb + 64, bt * 128:(bt + 1) * 128], Qs[gi], [kcT, qbT])], 512, mask_c,
                                            [(psos[gi], psos[gi].t[:, 0:512], vc.t[:, bt, gi, :], 0, 512, bt == 0, bt == 3, [vc])]))
                            steps.append(stp)
                        self.attn_steps(steps, ps4)
                        fin(psos, 0, True, False)
                        psos = [self.psO.next(), self.psO.next()]
                        steps = []
                        for kt in range(48):
                            pb32 = 32 * (kt // 16)
                            kc_ = (kt % 16) * 128
                            stp = []
                            for gi in range(2):
                                pb = 64 * gi
                                stp.append(([(kslcT.t[pb:pb + 64, kt * 128:(kt + 1) * 128], Qs[gi], [kslcT, qbT]),
                                             (e32[pb32:pb32 + 32, kc_:kc_ + 128], biasT.t[pb32:pb32 + 32, gi, c * 512:(c + 1) * 512], [cbf, biasT])], 512, None,
                                            [(psos[gi], psos[gi].t[:, 0:512], vslc.t[:, kt, gi, :], 0, 512, kt == 0, False, [vslc])]))
                            steps.append(stp)
                        for j in range(4 * c + 4):
                            mf = None
                            if j >= 4 * c:
                                mk = m4[:, (j - 4 * c) * 512:(j - 4 * c + 1) * 512]

                                def mf(pt, mk=mk):
                                    self.mask_mul(pt, 0, 512, mk)
                            stp = []
                            for gi in range(2):
                                pb = 64 * gi
                                stp.append(([(kso.t[pb:pb + 64, j * 128:(j + 1) * 128], Qs[gi], [kso, qbT]),
                                             (eown[:, j * 128:(j + 1) * 128], biasT.t[:, gi, c * 512:(c + 1) * 512], [pcb, biasT])], 512, mf,
                                            [(psos[gi], psos[gi].t[:, 0:512], vso.t[:, j, gi, :], 0, 512, False, j == 4 * c + 3, [vso])]))
                            steps.append(stp)
                        self.attn_steps(steps, ps4)
                        fin(psos, 1, False, False)
                        psos = [self.psO.next(), self.psO.next()]
                        steps = []
                        for tq_ in range(4):
                            i = 4 * c + tq_
                            sq = 4 + i
                            for s_ in range(sq - 4, sq + 1):
                                mf = None
                                if s_ == sq - 4:
                                    def mf(pt):
                                        self.mask_mul(pt, 0, 128, win_far)
                                elif s_ == sq:
                                    def mf(pt):
                                        self.mask_mul(pt, 0, 128, tri_diag)
                                stp = []
                                for gi in range(2):
                                    pb = 64 * gi
                                    stp.append(([(kwinT.t[pb:pb + 64, s_ * 128:(s_ + 1) * 128], qbT.t[pb:pb + 64, hp, i * 128:(i + 1) * 128], [kwinT, qbT])], 128, mf,
                                                [(psos[gi], psos[gi].t[:, tq_ * 128:(tq_ + 1) * 128], vwin.t[:, s_, gi, :], 0, 128, s_ == sq - 4, s_ == sq, [vwin])]))
                                steps.append(stp)
                        self.attn_steps(steps, ps4)
                        fin(psos, 2, False, True)
        S.barrier()

    def phase_C(self, es0):
        S = self.S
        with ExitStack() as es:
            slabs = self.ring(es, "c_slab", [128, 8, 512], BF16, 5)
            self.wbslab = self.sb(es, "c_wb", [128, 4, 512], BF16)
            gfin = self.sb(es, "c_gfin", [128, D], F32)
            xc = self.sb(es, "c_xc", [128, 4, D], F32)
            uTc = self.sb(es, "c_uTc", [128, 8, 512], BF16)
            mTc = self.sb(es, "c_mTc", [128, 8, 512], BF16)
            u2Tc = self.sb(es, "c_u2Tc", [128, 8, 512], BF16)
            hT = self.sb(es, "c_hT", [128, 32, 512], BF16)
            sg = self.ring(es, "c_sg", [128, 512], BF16, 2)
            tf = self.ring(es, "c_tf", [128, 512], F32, 3)
            S.dma(I("dma_start", out=gfin.t[:], in_=self.g_fin), writes=[gfin])

            slab_ids = {id(t): i for i, t in enumerate(slabs.items)}

            def slab_from(src_ap, kchunks, gain):
                sl = slabs.next()
                fns = [I("dma_start", out=sl.t[:, 0:kchunks, c0:c0 + 256], in_=src_ap[:, c0:c0 + 256].rearrange("(c p) n -> p c n", p=128)) for c0 in (0, 256)]
                S.dma_sw(fns, [sl], slab_ids[id(sl)])
                return sl

            for c in range(4):
                cs = slice(c * 512, (c + 1) * 512)
                for tt in range(4):
                    r0 = c * 512 + tt * 128
                    S.dma(I("dma_start", out=xc.t[:, tt, :], in_=self.x_own[r0:r0 + 128, :]), writes=[xc])
                    self.norm_sb(xc, xc.t[:, tt, :], uTc, uTc.t[:, :, tt * 128:(tt + 1) * 128], keep_rstd=self.gmix)
                for ctg in range(2):
                    gA = slab_from(self.w_in[:, C_GM + ctg * 512:C_GM + ctg * 512 + 512], 8, self.gmix)
                    gB = slab_from(self.w_in[:, C_GM + 1024 + ctg * 512:C_GM + 1024 + ctg * 512 + 512], 8, self.gmix)
                    wa = slab_from(self.w_a[:, ctg * 512:(ctg + 1) * 512], 4, None)
                    wb = self.wbslab
                    fns = [I("dma_start", out=wb.t[64 * two:64 * two + 64, 0:4, 0:512],
                             in_=self.w_b[two * 256:(two + 1) * 256, ctg * 512:ctg * 512 + 512].rearrange("(hp d) n -> d hp n", d=64)) for two in range(2)]
                    S.dma_sw(fns, [wb], 99)
                    for j in range(4):
                        ct = ctg * 4 + j
                        js = slice(j * 128, (j + 1) * 128)
                        sgs = []
                        for gw in (gA, gB):
                            ps = self.psA.next()
                            for k in range(8):
                                S.op("pe", I("matmul", ps.t[:, 0:512], lhsT=gw.t[:, k, js], rhs=uTc.t[:, k, :], start=(k == 0), stop=(k == 7)), reads=[gw, uTc], writes=[ps])
                            sgt = sg.next()
                            S.op("act", I("activation", out=sgt.t[:], in_=ps.t[:, 0:512], func=AF.Sigmoid), reads=[ps], writes=[sgt])
                            sgs.append(sgt)
                        psa = self.psS.next()
                        for k in range(4):
                            S.op("pe", I("matmul", psa.t[:, 0:512], lhsT=wa.t[:, k, js], rhs=self.yaT.t[:, k, cs], start=(k == 0), stop=(k == 3)), reads=[wa, self.yaT], writes=[psa])
                        psb = self.psO.next()
                        for k in range(4):
                            S.op("pe", I("matmul", psb.t[:, 0:512], lhsT=wb.t[:, k, js], rhs=self.ybT.t[:, k, cs], start=(k == 0), stop=(k == 3)), reads=[wb, self.ybT], writes=[psb])
                        t0 = tf.next()
                        t1 = tf.next()
                        S.op("dve", I("tensor_tensor", out=t0.t[:], in0=psa.t[:, 0:512], in1=sgs[0].t[:], op=ALU.mult), reads=[psa, sgs[0]], writes=[t0])
                        S.op("dve", I("tensor_tensor", out=t1.t[:], in0=psb.t[:, 0:512], in1=sgs[1].t[:], op=ALU.mult), reads=[psb, sgs[1]], writes=[t1])
                        S.op("dve", I("tensor_tensor", out=mTc.t[:, ct, :], in0=t0.t[:], in1=t1.t[:], op=ALU.add), reads=[t0, t1], writes=[mTc])
                for nh in range(2):
                    wo = slab_from(self.w_out[:, nh * 512:(nh + 1) * 512], 8, None)
                    for tt in range(4):
                        ps = self.psA.next()
                        for k in range(8):
                            S.op("pe", I("matmul", ps.t[:, 0:512], lhsT=mTc.t[:, k, tt * 128:(tt + 1) * 128], rhs=wo.t[:, k, :], start=(k == 0), stop=(k == 7)), reads=[wo, mTc], writes=[ps])
                        S.op("dve", I("tensor_tensor", out=xc.t[:, tt, nh * 512:(nh + 1) * 512], in0=ps.t[:, 0:512], in1=xc.t[:, tt, nh * 512:(nh + 1) * 512], op=ALU.add), reads=[ps, xc], writes=[xc])
                for tt in range(4):
                    self.norm_sb(xc, xc.t[:, tt, :], u2Tc, u2Tc.t[:, :, tt * 128:(tt + 1) * 128], keep_rstd=self.gmlp)
                for s_ in range(8):
                    wu = slab_from(self.w_up[:, s_ * 512:(s_ + 1) * 512], 8, self.gmlp)
                    for j in range(4):
                        ft = 4 * s_ + j
                        ps = self.psA.next()
                        for k in range(8):
                            S.op("pe", I("matmul", ps.t[:, 0:512], lhsT=wu.t[:, k, j * 128:(j + 1) * 128], rhs=u2Tc.t[:, k, :], start=(k == 0), stop=(k == 7)), reads=[wu, u2Tc], writes=[ps])
                        r = tf.next()
                        S.op("act", I("activation", out=r.t[:], in_=ps.t[:, 0:512], func=AF.Relu), reads=[ps], writes=[r])
                        S.op("dve", I("tensor_tensor", out=hT.t[:, ft, :], in0=r.t[:], in1=r.t[:], op=ALU.mult), reads=[r], writes=[hT])
                accs = [self.psA.items[0], self.psA.items[1], self.psS.items[0], self.psS.items[1]]
                for nh in range(2):
                    for kg in range(4):
                        wd = slab_from(self.w_down[kg * 1024:(kg + 1) * 1024, nh * 512:(nh + 1) * 512], 8, None)
                        for tt in range(4):
                            for k in range(8):
                                S.op("pe", I("matmul", accs[tt].t[:, 0:512], lhsT=hT.t[:, kg * 8 + k, tt * 128:(tt + 1) * 128], rhs=wd.t[:, k, :], start=(kg == 0 and k == 0), stop=(kg == 3 and k == 7)), reads=[wd, hT], writes=[accs[tt]])
                    for tt in range(4):
                        S.op("dve", I("tensor_tensor", out=xc.t[:, tt, nh * 512:(nh + 1) * 512], in0=accs[tt].t[:, 0:512], in1=xc.t[:, tt, nh * 512:(nh + 1) * 512], op=ALU.add), reads=[accs[tt], xc], writes=[xc])
                for tt in range(4):
                    jk = self.junk.next()
                    st = self.stat.next()
                    S.op("act", I("activation", out=jk.t[:], in_=xc.t[:, tt, :], func=AF.Square, accum_out=st.t[:, 0:1]), reads=[xc], writes=[jk, st])
                    S.op("act", I("activation", out=st.t[:, 1:2], in_=st.t[:, 0:1], func=AF.Sqrt, scale=1.0 / D, bias=self.epsc.t[:, 0:1]), reads=[st, self.epsc], writes=[st])
                    S.op("dve", I("reciprocal", out=st.t[:, 2:3], in_=st.t[:, 1:2]), reads=[st], writes=[st])
                    S.op("dve", I("scalar_tensor_tensor", out=xc.t[:, tt, :], in0=xc.t[:, tt, :], scalar=st.t[:, 2:3], in1=gfin.t[:], op0=ALU.mult, op1=ALU.mult), reads=[xc, st, gfin], writes=[xc])
                    r0 = c * 512 + tt * 128
                    S.dma(I("dma_start", out=self.out[r0:r0 + 128, :], in_=xc.t[:, tt, :]), reads=[xc])

def make_in_maps(inputs):
    x = np.ascontiguousarray(np.asarray(inputs["x"], np.float32))
    cbf, cf32, t16 = _static_tables()
    sq = lambda n: np.ascontiguousarray(np.asarray(inputs[n], np.float32)[0])
    gl = lambda v: np.ascontiguousarray(np.asarray(v, np.float32).reshape(8, 128).T)
    common = {
        "w_in": sq("w_in"), "g_mix": gl(inputs["norm_mix_g"][0]), "g_mlp": gl(inputs["norm_mlp_g"][0]),
        "g_fin": np.ascontiguousarray(np.broadcast_to(np.asarray(inputs["norm_final_g"], np.float32)[None, :], (128, D))),
        "cmp_w1_k": sq("cmp_w1_k"), "cmp_w1_v": sq("cmp_w1_v"), "cmp_w2_k": sq("cmp_w2_k"), "cmp_w2_v": sq("cmp_w2_v"),
        "cmp_pos_k": sq("cmp_pos_k"), "cmp_pos_v": sq("cmp_pos_v"),
        "w_a": sq("w_branch_a"), "w_b": sq("w_branch_b"), "w_out": sq("w_out"), "w_up": sq("w_up"), "w_down": sq("w_down"),
        "c_bf": cbf, "c_f32": cf32, "c_t16": t16,
    }
    tabs = [_percore_tables(q) for q in range(4)]
    maps = []
    for c in range(8):
        b, q = c // 4, c % 4
        T0 = OWN * q
        halo = x[b, T0 - OWN:T0] if q > 0 else np.zeros((OWN, D), np.float32)
        m = dict(common)
        m.update({"x_own": np.ascontiguousarray(x[b, T0:T0 + OWN]), "x_halo": np.ascontiguousarray(halo), "x_full": x[b],
                  "pc_f": tabs[q][0], "pc_lohi": tabs[q][1], "pc_bf": tabs[q][2]})
        maps.append(m)
    return maps


_CACHE = {}


def kernel(**inputs):
    if "nc" not in _CACHE:
        b = Builder()
        _CACHE["nc"] = b.build()
        _CACHE["decl"] = set(b._decl.keys())
    nc = _CACHE["nc"]
    maps = make_in_maps(inputs)
    decl = _CACHE["decl"]
    maps = [{k: v for k, v in m.items() if k in decl} for m in maps]
    res = run_bass_kernel_spmd(nc, maps, core_ids=list(range(8)))
    out = np.zeros((2, S_LEN, D), np.float32)
    for c in range(8):
        b, q = c // 4, c % 4
        out[b, OWN * q:OWN * (q + 1)] = res.results[c]["out"]
    return out
```

```python
import os
import numpy as np
import ml_dtypes
from contextlib import ExitStack
import concourse.bass as bass
import concourse.mybir as mybir
from concourse.bass_utils import run_bass_kernel_spmd

F32 = mybir.dt.float32
BF16 = mybir.dt.bfloat16
ALU = mybir.AluOpType
AF = mybir.ActivationFunctionType
NPBF = ml_dtypes.bfloat16

D = 1024
S_LEN = 8192
OWN = 2048
NT = 16
EPS = 1e-6
SCALE = 0.125
IN_COLS = 7960
C_QA, C_KA, C_VA = 0, 1536, 3072
C_QB = 4608
C_KVB = 5120
C_GB = 5888
C_GM = 5912
DILS = (1, 4, 16)

ENGS = ("pe", "act", "dve", "pool")
NDMA = 24


class Res:
    __slots__ = ("lw", "rd", "excl")

    def __init__(self):
        self.lw = None
        self.rd = {}
        self.excl = False


class Tn:
    __slots__ = ("t", "r")

    def __init__(self, t):
        self.t = t
        self.r = Res()


class Sched:
    def __init__(self, nc):
        self.nc = nc
        self.q = {e: [] for e in ENGS + ("sp",)}
        self.cnt = {e: 0 for e in ENGS}
        self.dcnt = [0] * NDMA
        self.seen = {e: {} for e in ENGS + ("sp",)}
        self.dnext = 0
        self.pgen = {}
        self.plast = {}

    def dma_sw(self, fns, writes, slot):
        deps = self._deps([], writes)
        self.pgen[slot] = self.pgen.get(slot, 0)
        key = ("p", slot)
        base = self.pgen[slot]
        waits = self._waits("pool", deps)
        for i, fn in enumerate(fns):
            self.q["pool"].append((waits if i == 0 else [], fn, key, base + 16 * (i + 1)))
        self.pgen[slot] = base + 16 * len(fns)
        self.plast[slot] = (key, self.pgen[slot])
        self._mark(key, self.pgen[slot], [], writes)

    def _deps(self, reads, writes, mykey=None):
        deps = {}
        for r in reads:
            r = r.r if isinstance(r, Tn) else r
            if r.lw is not None and r.lw[1] > deps.get(r.lw[0], 0):
                deps[r.lw[0]] = r.lw[1]
            if r.excl:
                for k, v in r.rd.items():
                    if k != mykey and v > deps.get(k, 0):
                        deps[k] = v
        for w in writes:
            w = w.r if isinstance(w, Tn) else w
            if w.lw is not None and w.lw[0] != mykey and w.lw[1] > deps.get(w.lw[0], 0):
                deps[w.lw[0]] = w.lw[1]
            for k, v in w.rd.items():
                if v > deps.get(k, 0):
                    deps[k] = v
        return deps

    def _waits(self, eng, deps):
        waits = []
        seen = self.seen[eng]
        for k, v in deps.items():
            if v > seen.get(k, 0):
                waits.append((k, v))
                seen[k] = v
        return waits

    def _mark(self, key, my, reads, writes):
        for r in reads:
            r = r.r if isinstance(r, Tn) else r
            if my > r.rd.get(key, 0):
                r.rd[key] = my
        for w in writes:
            w = w.r if isinstance(w, Tn) else w
            w.lw = (key, my)
            w.rd = {}

    def op(self, eng, fn, reads=(), writes=()):
        deps = self._deps(reads, writes, ("e", eng))
        if eng == "pe":
            deps.pop(("e", "pe"), None)
        self.cnt[eng] += 1
        my = self.cnt[eng]
        key = ("e", eng)
        self.q[eng].append((self._waits(eng, deps), fn, key, my))
        self._mark(key, my, reads, writes)

    def dma(self, fn, reads=(), writes=(), queue="sp"):
        deps = self._deps(reads, writes)
        k = self.dnext
        self.dnext = (self.dnext + 1) % NDMA
        key = ("d", k)
        if self.dcnt[k] > 0:
            deps[key] = max(deps.get(key, 0), self.dcnt[k])
        self.dcnt[k] += 16
        my = self.dcnt[k]
        self.q[queue].append((self._waits(queue, deps), fn, key, my))
        self._mark(key, my, reads, writes)

    def barrier(self):
        allc = {}
        for e in ENGS:
            if self.cnt[e]:
                allc[("e", e)] = self.cnt[e]
        for k in range(NDMA):
            if self.dcnt[k]:
                allc[("d", k)] = self.dcnt[k]
        for slot, (key, v) in self.plast.items():
            allc[key] = v
        for e in ENGS + ("sp",):
            w = self._waits(e, dict(allc))
            if w:
                self.q[e].append((w, None, None, 0))

    def emit(self):
        nc = self.nc
        with ExitStack() as es:
            esem = {e: es.enter_context(nc.semaphore("s_" + e)) for e in ENGS}
            dsem = [es.enter_context(nc.semaphore("s_d%d" % i)) for i in range(NDMA)]

            psem = {slot: es.enter_context(nc.semaphore("s_p%d" % i)) for i, slot in enumerate(sorted(self.pgen))}

            def semof(key):
                if key[0] == "p":
                    return psem[key[1]]
                return esem[key[1]] if key[0] == "e" else dsem[key[1]]
            fin = {}
            for e in ENGS:
                if self.cnt[e]:
                    fin[("e", e)] = self.cnt[e]
            for k in range(NDMA):
                if self.dcnt[k]:
                    fin[("d", k)] = self.dcnt[k]
            for slot, (key, v) in self.plast.items():
                fin[key] = v
            allsems = list(esem.values()) + dsem + list(psem.values())
            with nc.Block() as b0:
                @b0.sync
                def _(e):
                    for sm in allsems:
                        e.sem_clear(sm)
            block = es.enter_context(nc.Block())

            sig = {e: set() for e in ENGS}
            for name in self.q:
                for waits, fn, key, my in self.q[name]:
                    for (k, v) in waits:
                        if k[0] == "e":
                            sig[k[1]].add(v)
            for e in ENGS:
                if self.cnt[e]:
                    sig[e].add(self.cnt[e])
            rank = {}
            for e in ENGS:
                for i, v in enumerate(sorted(sig[e])):
                    rank[(e, v)] = i + 1

            def wval(k, v):
                return rank[(k[1], v)] if k[0] == "e" else v

            def run(name, engobj, final=False):
                for waits, fn, key, my in self.q[name]:
                    for (k, v) in waits:
                        engobj.wait_ge(semof(k), wval(k, v))
                    if isinstance(fn, tuple):
                        engobj.sem_clear(psem[fn[1]])
                    elif fn is not None:
                        ins = fn(engobj)
                        if key[0] in ("d", "p"):
                            ins.then_inc(semof(key), 16)
                        elif my in sig[key[1]]:
                            ins.then_inc(semof(key), 1)
                if final:
                    for k, v in fin.items():
                        engobj.wait_ge(semof(k), wval(k, v))

            @block.sync
            def _(e):
                run("sp", e, final=True)

            @block.tensor
            def _(e):
                run("pe", e)

            @block.scalar
            def _(e):
                run("act", e)

            @block.vector
            def _(e):
                run("dve", e)

            @block.gpsimd
            def _(e):
                run("pool", e)


def I(name, *a, **k):
    return lambda e: getattr(e, name)(*a, **k)


class Ring:
    def __init__(self, items):
        self.items = items
        self.i = 0

    def next(self):
        it = self.items[self.i % len(self.items)]
        self.i += 1
        return it


NROPE = 148
BFC = dict(ident=0, tri_diag=128, tri_prev=256, win_far=384, m4=512, e32=2560, ones=4608)
NBFC = 4736
F32C = dict(swap=0, id32=128)
NF32C = 160
PCF = dict(rope=0, thrc=NROPE * 16, pv=NROPE * 16 + 16, crel=NROPE * 16 + 80, hv=NROPE * 16 + 84)
NPCF = NROPE * 16 + 85
PCB = dict(eown=0, hv64=2048)
NPCB = 2112


def _static_tables():
    bf = np.zeros((128, NBFC), np.float32)
    k = np.arange(128)[:, None]
    q = np.arange(128)[None, :]
    bf[:, 0:128] = np.eye(128)
    bf[:, 128:256] = (q >= k)
    bf[:, 256:384] = (q <= k)
    bf[:, 384:512] = (q < k)
    for m in range(4):
        blk = np.zeros((128, 512), np.float32)
        for tq in range(4):
            if tq == m:
                blk[:, tq * 128:(tq + 1) * 128] = (q >= k)
            elif tq > m:
                blk[:, tq * 128:(tq + 1) * 128] = 1.0
        bf[:, 512 + m * 512: 512 + (m + 1) * 512] = blk
    b = np.arange(128)[:, None]
    for kt in range(16):
        i = np.arange(128)[None, :]
        bf[:, 2560 + kt * 128: 2560 + (kt + 1) * 128] = ((b % 32) == 2 * kt + (i >= 64))
    bf[:, 4608:4736] = 1.0
    f = np.zeros((128, NF32C), np.float32)
    f[:, 0:128] = (np.abs(k - q) == 64)
    f[0:32, 128:160] = np.eye(32)
    t16 = np.ascontiguousarray(np.broadcast_to(16.0 * np.arange(2048, dtype=np.float32)[None, :], (128, 2048)))
    return bf.astype(NPBF), f, t16


def _rope_rows(pos):
    inv = (500000.0 ** (-np.arange(0, 16, 2, dtype=np.float32) / np.float32(16))).astype(np.float32)
    ang = (pos.astype(np.float32)[:, None] * inv[None, :]).astype(np.float32)
    return np.concatenate([np.cos(ang), np.sin(ang)], axis=1).astype(np.float32)


def _percore_tables(qtr):
    T0 = OWN * qtr
    i = np.arange(128)
    f = np.zeros((128, NPCF), np.float32)
    rope = np.zeros((128, NROPE, 16), np.float32)
    for t in range(32):
        rope[:, t] = _rope_rows(T0 - OWN + 128 * t + i)
    for r in range(4):
        for j in range(-1, 4):
            rope[:, 32 + r * 5 + j + 1] = _rope_rows(T0 - OWN + 2048 + 512 * j + r + 4 * i)
    for r in range(16):
        for j in range(-1, 1):
            rope[:, 52 + r * 2 + j + 1] = _rope_rows(T0 - OWN + 2048 + 2048 * j + r + 16 * i)
    for kt in range(64):
        rope[:, 84 + kt] = _rope_rows(128 * kt + i)
    f[:, 0:NROPE * 16] = rope.reshape(128, -1)
    for ti in range(16):
        f[:, PCF["thrc"] + ti] = T0 + 128 * ti + i - 31
    for kt in range(64):
        f[:, PCF["pv"] + kt] = 1.0 if 128 * kt < T0 else 0.0
    for bt in range(4):
        f[:, PCF["crel"] + bt] = 16.0 * (16 * (128 * bt + i) + 31 - T0)
    f[:, PCF["hv"]] = 0.0 if qtr == 0 else 1.0
    lo = np.full((128, 16, 128), -3e4, np.float32)
    hi = np.full((128, 16, 128), 3e4, np.float32)
    m = np.arange(128)[None, :]
    for ti in range(16):
        cur = ((T0 + 128 * ti + i) // 64)[:, None]
        forced = (m == 0) | (m == cur) | (m == cur - 1)
        fut = m > cur
        lo[:, ti][forced] = 1e4
        hi[:, ti][forced] = 1e4
        lo[:, ti][fut] = -3e4
        hi[:, ti][fut] = -3e4
    lohi = np.concatenate([lo.reshape(128, -1), hi.reshape(128, -1)], axis=1)
    bfp = np.zeros((128, NPCB), np.float32)
    b = np.arange(128)[:, None]
    for j in range(16):
        ii = np.arange(128)[None, :]
        bfp[:, j * 128:(j + 1) * 128] = (b == 2 * (T0 // 128 + j) + (ii >= 64))
    bfp[:, 2048:2112] = 0.0 if qtr == 0 else 1.0
    return f, lohi.astype(np.float32), bfp.astype(NPBF)


class StopBuild(Exception):
    pass


class Builder:
    def __init__(self, debug=False, stop_after=None):
        self.debug = debug
        self.stop_after = stop_after
        self.nc = nc = bass.Bass("TRN2", target_bir_lowering=False)
        self.S = Sched(nc)
        self._decl = {}
        self._shapes = {
            "x_own": ([OWN, D], F32), "x_halo": ([OWN, D], F32), "x_full": ([S_LEN, D], F32), "w_in": ([D, IN_COLS], F32),
            "g_mix": ([128, 8], F32), "g_mlp": ([128, 8], F32), "g_fin": ([128, D], F32),
            "cmp_w1_k": ([2048, 256], F32), "cmp_w1_v": ([2048, 256], F32), "cmp_w2_k": ([256, 64], F32), "cmp_w2_v": ([256, 64], F32),
            "cmp_pos_k": ([32, 64], F32), "cmp_pos_v": ([32, 64], F32), "w_a": ([512, D], F32), "w_b": ([512, D], F32),
            "w_out": ([D, D], F32), "w_up": ([D, 4096], F32), "w_down": ([4096, D], F32),
            "c_bf": ([128, NBFC], BF16), "c_f32": ([128, NF32C], F32), "c_t16": ([128, 2048], F32),
            "pc_f": ([128, NPCF], F32), "pc_lohi": ([128, 4096], F32), "pc_bf": ([128, NPCB], BF16),
        }
        self.out = nc.dram_tensor("out", [OWN, D], F32, kind="ExternalOutput").ap()
        self.dbg = {}

    def __getattr__(self, name):
        sh = self.__dict__.get("_shapes", {})
        if name in sh:
            if name not in self._decl:
                self._decl[name] = self.nc.dram_tensor(name, list(sh[name][0]), sh[name][1], kind="ExternalInput").ap()
            return self._decl[name]
        raise AttributeError(name)

    def sb(self, es, name, shape, dt):
        return Tn(es.enter_context(self.nc.sbuf_tensor(name, list(shape), dt)))

    def ps(self, es, name, shape, dt):
        t = Tn(es.enter_context(self.nc.psum_tensor(name, list(shape), dt)))
        t.r.excl = True
        return t

    def ring(self, es, name, shape, dt, n):
        return Ring([self.sb(es, "%s%d" % (name, i), shape, dt) for i in range(n)])

    def dump(self, name, tn, ap, shape, dt):
        if not self.debug:
            return
        o = self.nc.dram_tensor("dbg_" + name, list(shape), dt, kind="ExternalOutput").ap()
        self.dbg[name] = True
        self.S.dma(I("dma_start", out=o, in_=ap), reads=[tn])

    def load_wslab(self, src_ap, ncols, gain, kchunks=8):
        S = self.S
        st = self.wst.next()
        sl = self.wsl.next()
        S.dma(I("dma_start", out=st.t[:, 0:kchunks, 0:ncols], in_=src_ap.rearrange("(c p) n -> p c n", p=128)), writes=[st])
        if gain is not None:
            gb = gain.t[:, 0:kchunks].unsqueeze(2).to_broadcast([128, kchunks, ncols])
            S.op("pool", I("tensor_tensor", out=sl.t[:, 0:kchunks, 0:ncols], in0=st.t[:, 0:kchunks, 0:ncols], in1=gb, op=ALU.mult),
                 reads=[st, gain], writes=[sl])
        else:
            S.op("pool", I("tensor_copy", out=sl.t[:, 0:kchunks, 0:ncols], in_=st.t[:, 0:kchunks, 0:ncols]), reads=[st], writes=[sl])
        return sl

    def cast_into(self, dst_tn, dst_ap_fn, src_ap, kchunks, ncols, gain, piece=512):
        S = self.S
        for c0 in range(0, ncols, piece):
            n = min(piece, ncols - c0)
            st = self.wst.next()
            S.dma(I("dma_start", out=st.t[:, 0:kchunks, 0:n], in_=src_ap[:, c0:c0 + n].rearrange("(c p) n -> p c n", p=128)), writes=[st])
            dst = dst_ap_fn(c0, n)
            engs = getattr(self, "cast_engs", ("pool",))
            self._ci = getattr(self, "_ci", 0) + 1
            ce = engs[self._ci % len(engs)]
            if gain is not None:
                gb = gain.t[:, 0:kchunks].unsqueeze(2).to_broadcast([128, kchunks, n])
                S.op(ce, I("tensor_tensor", out=dst, in0=st.t[:, 0:kchunks, 0:n], in1=gb, op=ALU.mult),
                     reads=[st, gain], writes=[dst_tn])
            else:
                S.op(ce, I("tensor_copy", out=dst, in_=st.t[:, 0:kchunks, 0:n]), reads=[st], writes=[dst_tn])

    def norm_tile(self, x_ap, ut_tn, ut_ap):
        S = self.S
        xt = self.xring.next()
        S.dma(I("dma_start", out=xt.t[:], in_=x_ap), writes=[xt])
        self.norm_sb(xt, xt.t[:], ut_tn, ut_ap)

    def norm_sb(self, xt, x_sb_ap, ut_tn, ut_ap, keep_rstd=None):
        S = self.S
        jk = self.junk.next()
        st = self.stat.next()
        S.op("act", I("activation", out=jk.t[:], in_=x_sb_ap, func=AF.Square, accum_out=st.t[:, 0:1]), reads=[xt], writes=[jk, st])
        S.op("act", I("activation", out=st.t[:, 1:2], in_=st.t[:, 0:1], func=AF.Sqrt, scale=1.0 / D, bias=self.epsc.t[:, 0:1]), reads=[st, self.epsc], writes=[st])
        S.op("dve", I("reciprocal", out=st.t[:, 2:3], in_=st.t[:, 1:2]), reads=[st], writes=[st])
        xn = self.xnring.next()
        S.op("dve", I("tensor_scalar", out=xn.t[:], in0=x_sb_ap, scalar1=st.t[:, 2:3], scalar2=None, op0=ALU.mult), reads=[xt, st], writes=[xn])
        pt = self.psT
        for c in range(8):
            S.op("pe", I("transpose", out=pt.t[:, c * 128:(c + 1) * 128], in_=xn.t[:, c * 128:(c + 1) * 128], identity=self.ident), reads=[xn, self.cbf], writes=[pt])
        if keep_rstd is None:
            S.op("act", I("copy", out=ut_ap, in_=pt.t[:, 0:1024].rearrange("p (c t) -> p c t", c=8)), reads=[pt], writes=[ut_tn])
        else:
            gb = keep_rstd.t[:, 0:8].unsqueeze(2).to_broadcast([128, 8, 128])
            S.op("dve", I("tensor_tensor", out=ut_ap, in0=pt.t[:, 0:1024].rearrange("p (c t) -> p c t", c=8), in1=gb, op=ALU.mult), reads=[pt, keep_rstd], writes=[ut_tn])
        return st

    def proj_tm(self, lhs_fn, lhs_tn, slab, c0, ncols, ps):
        for c in range(8):
            self.S.op("pe", I("matmul", ps.t[:, 0:ncols], lhsT=lhs_fn(c), rhs=slab.t[:, c, c0:c0 + ncols], start=(c == 0), stop=(c == 7)),
                      reads=[lhs_tn, slab], writes=[ps])

    def rope_evac(self, ps, pc0, nh, ropeidx, dst_tn, dst_ap, perm=False):
        S = self.S
        ro = PCF["rope"] + ropeidx * 16
        ta = self.rtmp.next()
        if not perm:
            psv = ps.t[:, pc0:pc0 + 64 * nh].rearrange("p (h d) -> p h d", h=nh)
            dv = dst_ap.rearrange("p (h d) -> p h d", h=nh)
            tav = ta.t[:, 0:nh * 32].rearrange("p (h d) -> p h d", h=nh)
            cos1 = self.pcf.t[:, ro:ro + 8].unsqueeze(1).to_broadcast([128, nh, 8])
            sin1 = self.pcf.t[:, ro + 8:ro + 16].unsqueeze(1).to_broadcast([128, nh, 8])
            sl = lambda v, a, b: v[:, :, a:b]
        else:
            psv = ps.t[:, pc0:pc0 + 512].rearrange("p (two hp d) -> p two hp d", two=2, hp=4)
            dv = dst_ap.rearrange("p (hp two d) -> p two hp d", two=2, hp=4)
            tav = ta.t[:, 0:256].rearrange("p (two hp d) -> p two hp d", two=2, hp=4)
            cos1 = self.pcf.t[:, ro:ro + 8].unsqueeze(1).unsqueeze(1).to_broadcast([128, 2, 4, 8])
            sin1 = self.pcf.t[:, ro + 8:ro + 16].unsqueeze(1).unsqueeze(1).to_broadcast([128, 2, 4, 8])
            sl = lambda v, a, b: v[:, :, :, a:b]
        S.op("dve", I("tensor_copy", out=sl(dv, 16, 64), in_=sl(psv, 16, 64)), reads=[ps], writes=[dst_tn])
        S.op("dve", I("tensor_tensor", out=sl(tav, 0, 8), in0=sl(psv, 0, 8), in1=cos1, op=ALU.mult), reads=[ps, self.pcf], writes=[ta])
        S.op("dve", I("tensor_tensor", out=sl(tav, 8, 16), in0=sl(psv, 8, 16), in1=cos1, op=ALU.mult), reads=[ps, self.pcf], writes=[ta])
        S.op("dve", I("tensor_tensor", out=sl(tav, 16, 24), in0=sl(psv, 8, 16), in1=sin1, op=ALU.mult), reads=[ps, self.pcf], writes=[ta])
        S.op("dve", I("tensor_tensor", out=sl(tav, 24, 32), in0=sl(psv, 0, 8), in1=sin1, op=ALU.mult), reads=[ps, self.pcf], writes=[ta])
        S.op("dve", I("tensor_tensor", out=sl(dv, 0, 8), in0=sl(tav, 0, 8), in1=sl(tav, 16, 24), op=ALU.subtract), reads=[ta], writes=[dst_tn])
        S.op("dve", I("tensor_tensor", out=sl(dv, 8, 16), in0=sl(tav, 8, 16), in1=sl(tav, 24, 32), op=ALU.add), reads=[ta], writes=[dst_tn])

    def build(self):
        nc, S = self.nc, self.S
        with ExitStack() as es0:
            self.cbf = self.sb(es0, "cbf", [128, NBFC], BF16)
            self.cf32 = self.sb(es0, "cf32", [128, NF32C], F32)
            self.pcf = self.sb(es0, "pcf", [128, NPCF], F32)
            self.pcb = self.sb(es0, "pcb", [128, NPCB], BF16)
            self.gmix = self.sb(es0, "gmix", [128, 8], F32)
            self.gmlp = self.sb(es0, "gmlp", [128, 8], F32)
            self.epsc = self.sb(es0, "epsc", [128, 1], F32)
            S.dma(I("dma_start", out=self.cbf.t[:], in_=self.c_bf), writes=[self.cbf])
            S.dma(I("dma_start", out=self.cf32.t[:], in_=self.c_f32), writes=[self.cf32])
            S.dma(I("dma_start", out=self.pcf.t[:], in_=self.pc_f), writes=[self.pcf])
            S.dma(I("dma_start", out=self.pcb.t[:], in_=self.pc_bf), writes=[self.pcb])
            S.dma(I("dma_start", out=self.gmix.t[:], in_=self.g_mix), writes=[self.gmix])
            S.dma(I("dma_start", out=self.gmlp.t[:], in_=self.g_mlp), writes=[self.gmlp])
            S.op("dve", I("memset", self.epsc.t[:], EPS), writes=[self.epsc])
            self.ident = self.cbf.t[:, 0:128]
            self.xring = self.ring(es0, "xr", [128, D], F32, 2)
            self.junk = self.ring(es0, "jk", [128, D], BF16, 1)
            self.stat = self.ring(es0, "st", [128, 4], F32, 4)
            self.xnring = self.ring(es0, "xn", [128, D], BF16, 2)
            self.rtmp = self.ring(es0, "rtmp", [128, 256], F32, 2)
            self.ptr = self.ring(es0, "ptr", [128, 512], BF16, 3)
            self.psA = Ring([self.ps(es0, "psA%d" % i, [128, 512], F32) for i in range(2)])
            self.psT = self.ps(es0, "psT", [128, 1024], BF16)
            self.psS = Ring([self.ps(es0, "psS%d" % i, [128, 512], F32) for i in range(2)])
            self.psO = Ring([self.ps(es0, "psO%d" % i, [128, 512], F32) for i in range(2)])
            self.psX = self.ps(es0, "psX", [128, 512], F32)
            self.ring3 = Ring([self.psS.items[0], self.psS.items[1], self.psX])
            self.yaT = self.sb(es0, "yaT", [128, 4, OWN], BF16)
            self.stopped = False
            self.phase_A(es0)
            if self.stopped:
                S.barrier()
                if self.stop_after in ("A3", "A"):
                    self.dump("yaT", self.yaT, self.yaT.t[:], [128, 4, OWN], BF16)
                self.fake_out()
                S.emit()
                return nc
            self.ybT = self.sb(es0, "ybT", [128, 4, OWN], BF16)
            S.barrier()
            if self.stop_after == "A":
                self.dump("yaT", self.yaT, self.yaT.t[:], [128, 4, OWN], BF16)
                self.fake_out()
                S.emit()
                return nc
            self.phase_B(es0)
            S.barrier()
            if self.stopped:
                self.fake_out()
                S.emit()
                return nc
            if self.stop_after == "B":
                self.dump("yaT", self.yaT, self.yaT.t[:], [128, 4, OWN], BF16)
                self.dump("ybT", self.ybT, self.ybT.t[:], [128, 4, OWN], BF16)
                self.fake_out()
                S.emit()
                return nc
            self.phase_C(es0)
            if self.debug:
                self.dump("yaT", self.yaT, self.yaT.t[:], [128, 4, OWN], BF16)
                self.dump("ybT", self.ybT, self.ybT.t[:], [128, 4, OWN], BF16)
            S.emit()
        return nc

    def fake_out(self):
        S = self.S
        xt = self.xring.next()
        for t in range(NT):
            S.dma(I("dma_start", out=xt.t[:], in_=self.x_own[t * 128:(t + 1) * 128, :]), writes=[xt])
            S.dma(I("dma_start", out=self.out[t * 128:(t + 1) * 128, :], in_=xt.t[:]), reads=[xt])

    def attn_unit(self, score_mms, n, mask_fn, pv_list):
        S = self.S
        if getattr(self, "_collect", None) is not None:
            self._collect.append((score_mms, n, mask_fn, pv_list, None))
            return
        pss = self.psS.next()
        for i, (l, r, rd) in enumerate(score_mms):
            S.op("pe", I("matmul", pss.t[:, 0:n], lhsT=l, rhs=r, start=(i == 0), stop=(i == len(score_mms) - 1)),
                 reads=rd, writes=[pss])
        pt = self.ptr.next()
        S.op("act", I("activation", out=pt.t[:, 0:n], in_=pss.t[:, 0:n], func=AF.Exp, scale=SCALE), reads=[pss], writes=[pt])
        if mask_fn is not None:
            mask_fn(pt)
        for (pso, out_ap, vaug, c0, ncol, st, sp, rd) in pv_list:
            S.op("pe", I("matmul", out_ap, lhsT=vaug, rhs=pt.t[:, c0:c0 + ncol], start=st, stop=sp),
                 reads=[pt] + rd, writes=[pso])

    def attn_seq(self, units, ring=None, depth=1):
        S = self.S
        ring = ring or self.psS
        units = list(units)
        pend = []
        for k in range(len(units) + depth):
            if k < len(units):
                u = units[k]
                score_mms, n = u[0], u[1]
                pss = ring.next()
                for i, (l, r, rd) in enumerate(score_mms):
                    S.op("pe", I("matmul", pss.t[:, 0:n], lhsT=l, rhs=r, start=(i == 0), stop=(i == len(score_mms) - 1)), reads=rd, writes=[pss])
                pend.append((u, pss))
            if k >= depth and pend:
                (pu, ppss) = pend.pop(0)
                n = pu[1]
                pt = self.ptr.next()
                S.op("act", I("activation", out=pt.t[:, 0:n], in_=ppss.t[:, 0:n], func=AF.Exp, scale=SCALE), reads=[ppss], writes=[pt])
                if pu[2] is not None:
                    pu[2](pt)
                for (pso, out_ap, vaug, c0, ncol, st, sp, rd) in pu[3]:
                    S.op("pe", I("matmul", out_ap, lhsT=vaug, rhs=pt.t[:, c0:c0 + ncol], start=st, stop=sp), reads=[pt] + rd, writes=[pso])
                if len(pu) > 4 and pu[4] is not None:
                    pu[4]()
        while pend:
            (pu, ppss) = pend.pop(0)
            n = pu[1]
            pt = self.ptr.next()
            S.op("act", I("activation", out=pt.t[:, 0:n], in_=ppss.t[:, 0:n], func=AF.Exp, scale=SCALE), reads=[ppss], writes=[pt])
            if pu[2] is not None:
                pu[2](pt)
            for (pso, out_ap, vaug, c0, ncol, st, sp, rd) in pu[3]:
                S.op("pe", I("matmul", out_ap, lhsT=vaug, rhs=pt.t[:, c0:c0 + ncol], start=st, stop=sp), reads=[pt] + rd, writes=[pso])
            if len(pu) > 4 and pu[4] is not None:
                pu[4]()

    def attn_steps(self, steps, ring):
        S = self.S
        prev = None
        for stp in list(steps) + [None]:
            cur = None
            if stp is not None:
                cur = []
                for u in stp:
                    score_mms, n = u[0], u[1]
                    pss = ring.next()
                    cur.append((u, pss))
                nmm = max(len(u[0]) for u in stp)
                for i in range(nmm):
                    for (u, pss) in cur:
                        if i < len(u[0]):
                            l, r, rd = u[0][i]
                            S.op("pe", I("matmul", pss.t[:, 0:u[1]], lhsT=l, rhs=r, start=(i == 0), stop=(i == len(u[0]) - 1)), reads=rd, writes=[pss])
            if prev is not None:
                for (pu, ppss) in prev:
                    n = pu[1]
                    pt = self.ptr.next()
                    S.op("act", I("activation", out=pt.t[:, 0:n], in_=ppss.t[:, 0:n], func=AF.Exp, scale=SCALE), reads=[ppss], writes=[pt])
                    if pu[2] is not None:
                        pu[2](pt)
                    for (pso, out_ap, vaug, c0, ncol, st, sp, rd) in pu[3]:
                        S.op("pe", I("matmul", out_ap, lhsT=vaug, rhs=pt.t[:, c0:c0 + ncol], start=st, stop=sp), reads=[pt] + rd, writes=[pso])
            prev = cur

    def mask_mul(self, pt, c0, n, mask_ap):
        self.S.op("dve", I("tensor_tensor", out=pt.t[:, c0:c0 + n], in0=pt.t[:, c0:c0 + n], in1=mask_ap, op=ALU.mult), reads=[pt, self.cbf], writes=[pt])

    def phase_A(self, es0):
        S = self.S
        with ExitStack() as esA:
            self.phase_A_body(esA)
        S.barrier()

    def phase_A_body(self, esA):
        S = self.S
        self.uTh = self.sb(esA, "uTh", [128, 8, OWN], BF16)
        self.uTo = self.sb(esA, "uTo", [128, 8, OWN], BF16)
        self.wst = self.ring(esA, "wstA", [128, 8, 384], F32, 2)
        self.wsl = self.ring(esA, "wslA", [128, 8, 384], BF16, 2)
        for t in range(NT):
            self.norm_tile(self.x_halo[t * 128:(t + 1) * 128, :], self.uTh, self.uTh.t[:, :, t * 128:(t + 1) * 128])
        for t in range(NT):
            self.norm_tile(self.x_own[t * 128:(t + 1) * 128, :], self.uTo, self.uTo.t[:, :, t * 128:(t + 1) * 128])
        if self.stop_after == "A0":
            self.stopped = True
            return
        with ExitStack() as es:
            self.phase_A_inner(es)

    def phase_A_inner(self, es):
        S = self.S
        if True:
            qT = self.sb(es, "a_qT", [128, OWN], BF16)
            kT = self.sb(es, "a_kT", [128, 32 * 128], BF16)
            vaug = self.sb(es, "a_v", [128, 32, 2, 128], BF16)
            qk = self.ring(es, "a_qk", [128, 256], BF16, 2)
            if os.environ.get("PADLOW"):
                pad = self.sb(es, "a_pad", [128, int(os.environ["PADLOW"]) * 256], F32)
            acc = [self.sb(es, "a_acc%d" % i, [128, OWN], F32) for i in range(2)]
            rd_ = self.ring(es, "a_rd", [128, 512], F32, 2)
            ones64 = self.cbf.t[:, BFC["ones"]:BFC["ones"] + 64]
            hv64 = self.pcb.t[:, PCB["hv64"]:PCB["hv64"] + 64]
            hvcol = self.pcf.t[:, PCF["hv"]:PCF["hv"] + 1]
            for p in range(4):
                for g, d in enumerate(DILS):
                    nt = NT // d
                    st = self.wst.next()
                    sl = self.wsl.next()
                    for i, cb in enumerate((C_QA, C_KA, C_VA)):
                        c0 = cb + g * 512 + p * 128
                        for kc in range(8):
                            S.dma(I("dma_start", out=st.t[:, kc, i * 128:(i + 1) * 128], in_=self.w_in[kc * 128:(kc + 1) * 128, c0:c0 + 128]), writes=[st])
                    gb = self.gmix.t[:, 0:8].unsqueeze(2).to_broadcast([128, 8, 384])
                    if int(os.environ.get("A1CUT", "99")) >= 0:
                        S.op("pool", I("tensor_tensor", out=sl.t[:, :, 0:384], in0=st.t[:, :, 0:384], in1=gb, op=ALU.mult), reads=[st, self.gmix], writes=[sl])
                    if int(os.environ.get("A1CUT", "99")) <= 0:
                        self.stopped = True
                        return
                    for r in range(d):
                        for j in range(-1, nt):
                            slot = r * (nt + 1) + j + 1
                            start = 2048 + 128 * d * j + r
                            if start < 2048:
                                ut, s0 = self.uTh, start
                            else:
                                ut, s0 = self.uTo, start - 2048
                            lhs = lambda c, ut=ut, s0=s0, d=d: ut.t[:, c, s0:s0 + 127 * d + 1:d]
                            ridx = (15 + slot) if g == 0 else ((32 + slot) if g == 1 else (52 + slot))
                            ps = self.psA.next()
                            halo = (j == -1)
                            if halo:
                                self.proj_tm(lhs, ut, sl, 128, 256, ps)
                                kc0, vc0 = 0, 128
                            else:
                                self.proj_tm(lhs, ut, sl, 0, 384, ps)
                                kc0, vc0 = 128, 256

                            CUT = int(os.environ.get("A1CUT", "99"))
                            if CUT <= 1:
                                continue
                            t = qk.next()
                            if not halo:
                                self.rope_evac(ps, 0, 4, ridx, t, t.t[:, 0:256])
                            else:
                                self.rope_evac(ps, kc0, 2, ridx, t, t.t[:, 128:256])
                            if CUT <= 2:
                                continue
                            vsrc = ps.t[:, vc0:vc0 + 128].rearrange("p (h d) -> p h d", h=2)
                            if halo:
                                S.op("dve", I("tensor_scalar", out=vaug.t[:, slot, :, 0:64], in0=vsrc, scalar1=hvcol, scalar2=None, op0=ALU.mult), reads=[ps, self.pcf], writes=[vaug])
                                for hh in range(2):
                                    S.op("pool", I("tensor_copy", out=vaug.t[:, slot, hh, 64:128], in_=hv64), reads=[self.pcb], writes=[vaug])
                            else:
                                S.op("act", I("copy", out=vaug.t[:, slot, :, 0:64], in_=vsrc), reads=[ps], writes=[vaug])
                                for hh in range(2):
                                    S.op("pool", I("tensor_copy", out=vaug.t[:, slot, hh, 64:128], in_=ones64), reads=[self.cbf], writes=[vaug])
                            if CUT <= 3:
                                continue
                            pt = self.psT
                            if not halo:
                                S.op("pe", I("transpose", out=pt.t[:, 0:128], in_=t.t[:, 0:128], identity=self.ident), reads=[t, self.cbf], writes=[pt])
                            S.op("pe", I("transpose", out=pt.t[:, 128:256], in_=t.t[:, 128:256], identity=self.ident), reads=[t, self.cbf], writes=[pt])
                            if not halo:
                                qi = r * nt + j
                                S.op("act", I("copy", out=qT.t[:, qi * 128:(qi + 1) * 128], in_=pt.t[:, 0:128]), reads=[pt], writes=[qT])
                            S.op("dve", I("tensor_copy", out=kT.t[:, slot * 128:(slot + 1) * 128], in_=pt.t[:, 128:256]), reads=[pt], writes=[kT])
                    if self.stop_after == "A1" and int(os.environ.get("A1CUT", "99")) < 99:
                        self.stopped = True
                        return
                    if self.stop_after == "A1":
                        self.dump("qT", qT, qT.t[:], [128, OWN], BF16)
                        self.dump("kT", kT, kT.t[:, 0:17 * 128], [128, 17 * 128], BF16)
                        self.dump("vaug", vaug, vaug.t[:, 0:17], [128, 17, 2, 128], BF16)
                        self.stopped = True
                        return
                    for hh in range(2):
                        pb = 64 * hh
                        banks = {}
                        ring3 = self.ring3
                        units = []
                        for r in range(d):
                            for j in range(-1, nt):
                                slot = r * (nt + 1) + j + 1
                                qlo = max(j, 0)
                                qhi = min(j + 1, nt - 1)
                                nq = qhi - qlo + 1
                                qc0 = (r * nt + qlo) * 128
                                n = nq * 128
                                if j == -1:
                                    mk = [(0, 128, self.cbf.t[:, BFC["tri_prev"]:BFC["tri_prev"] + 128])]
                                elif nq == 1:
                                    mk = [(0, 128, self.cbf.t[:, BFC["tri_diag"]:BFC["tri_diag"] + 128])]
                                else:
                                    mk = [(0, 256, self.cbf.t[:, BFC["tri_diag"]:BFC["tri_diag"] + 256])]

                                def mask_fn(pt, mk=mk):
                                    for (c0, nn, ap) in mk:
                                        self.mask_mul(pt, c0, nn, ap)
                                pv = []
                                for qt in range(qlo, qhi + 1):
                                    qi = r * nt + qt
                                    if (qt == j + 1) and (qi % 4 == 0):
                                        banks[qi // 4] = self.psO.next()
                                    pso = banks[qi // 4]
                                    col = (qi % 4) * 128
                                    pv.append((pso, pso.t[:, col:col + 128], vaug.t[:, slot, hh, :], (qt - qlo) * 128, 128, qt == j + 1, qt == j, [vaug]))
                                after = None
                                if j >= 0 and (r * nt + j) % 4 == 3:
                                    bk = (r * nt + j) // 4
                                    pso = banks[bk]
                                    av = acc[hh].t[:]
                                    if d == 1:
                                        dst = av[:, bk * 512:(bk + 1) * 512]
                                        src = pso.t[:, 0:512]
                                    elif d == 4:
                                        dst = av.rearrange("p (i r) -> p r i", r=4)[:, r, :]
                                        src = pso.t[:, 0:512]
                                    else:
                                        dst = av.rearrange("p (i r) -> p r i", r=16)[:, 4 * bk:4 * bk + 4, :]
                                        src = pso.t[:, 0:512].rearrange("p (r i) -> p r i", r=4)

                                    def after(dst=dst, src=src, pso=pso, hh=hh, g=g):
                                        if g == 0:
                                            S.op("act", I("copy", out=dst, in_=src), reads=[pso], writes=[acc[hh]])
                                        else:
                                            S.op("dve", I("tensor_tensor", out=dst, in0=dst, in1=src, op=ALU.add), reads=[pso, acc[hh]], writes=[acc[hh]])
                                units.append(([(kT.t[pb:pb + 64, slot * 128:(slot + 1) * 128], qT.t[pb:pb + 64, qc0:qc0 + n], [kT, qT])], n, mask_fn, pv, after))
                        self.attn_seq(units, ring=ring3, depth=2)
                if self.stop_after == "A2":
                    self.stopped = True
                    return
                for hh in range(2):
                    for c in range(4):
                        self.finalize(acc[hh], acc[hh].t[:, c * 512:(c + 1) * 512], hh, None, rd_, self.yaT, self.yaT.t[:, p, c * 512:(c + 1) * 512], first=True, last=True, ybacc=None)
                if self.stop_after == "A3":
                    self.stopped = True
                    return
        S.barrier()

    def finalize(self, src_tn, src_ap, hh, gate_row, rdring, dst_tn, dst_ap, first, last, ybacc):
        S = self.S
        psx = self.psX
        S.op("pe", I("matmul", psx.t[:, 0:512], lhsT=self.cf32.t[:, 0:128], rhs=src_ap, start=True, stop=True), reads=[src_tn, self.cf32], writes=[psx])
        rd = rdring.next()
        lo, hi = 64 * hh, 64 * hh + 64
        if hh == 0:
            den = psx.t[0:64, 0:512]
            num = src_ap[0:64, :]
        else:
            den = src_ap[64:128, :]
            num = psx.t[64:128, 0:512]
        S.op("dve", I("tensor_scalar", out=rd.t[lo:hi, :], in0=den, scalar1=1e-30, scalar2=None, op0=ALU.max), reads=[psx, src_tn], writes=[rd])
        S.op("dve", I("reciprocal", out=rd.t[lo:hi, :], in_=rd.t[lo:hi, :]), reads=[rd], writes=[rd])
        if gate_row is None:
            S.op("dve", I("tensor_tensor", out=dst_ap[lo:hi, :], in0=num, in1=rd.t[lo:hi, :], op=ALU.mult), reads=[psx, src_tn, rd], writes=[dst_tn])
            return
        S.op("dve", I("tensor_tensor", out=rd.t[lo:hi, :], in0=num, in1=rd.t[lo:hi, :], op=ALU.mult), reads=[psx, src_tn, rd], writes=[rd])
        gsel, gbT, gcols = gate_row
        S.op("pe", I("matmul", psx.t[:, 0:512], lhsT=gsel.t[:], rhs=gcols, start=True, stop=True), reads=[gbT, gsel, rd], writes=[psx])
        if first:
            S.op("dve", I("tensor_tensor", out=ybacc.t[lo:hi, :], in0=rd.t[lo:hi, :], in1=psx.t[lo:hi, 0:512], op=ALU.mult), reads=[psx, rd], writes=[ybacc])
        else:
            S.op("dve", I("tensor_tensor", out=rd.t[lo:hi, :], in0=rd.t[lo:hi, :], in1=psx.t[lo:hi, 0:512], op=ALU.mult), reads=[psx, rd], writes=[rd])
            if last:
                S.op("dve", I("tensor_tensor", out=dst_ap[lo:hi, :], in0=rd.t[lo:hi, :], in1=ybacc.t[lo:hi, :], op=ALU.add), reads=[rd, ybacc], writes=[dst_tn])
            else:
                S.op("dve", I("tensor_tensor", out=ybacc.t[lo:hi, :], in0=rd.t[lo:hi, :], in1=ybacc.t[lo:hi, :], op=ALU.add), reads=[rd, ybacc], writes=[ybacc])

    def phase_B(self, es0):
        S = self.S
        cbf, pcf, pcb = self.cbf, self.pcf, self.pcb
        ones64 = cbf.t[:, BFC["ones"]:BFC["ones"] + 64]
        ones2 = cbf.t[:, BFC["ones"]:BFC["ones"] + 128].rearrange("p (g d) -> p g d", g=2)
        with ExitStack() as esB:
            kslcT = self.sb(esB, "b_kslcT", [128, 48 * 128], BF16)
            vslc = self.sb(esB, "b_vslc", [128, 48, 2, 128], BF16)
            kcT = self.sb(esB, "b_kcT", [128, 512], BF16)
            vc = self.sb(esB, "b_vc", [128, 4, 2, 128], BF16)
            S.op("pool", I("memset", kcT.t[:], 0.0), writes=[kcT])
            S.op("pool", I("memset", vc.t[:], 0.0), writes=[vc])
            with ExitStack() as es:
                self.wst = self.ring(es, "wstB", [128, 8, 256], F32, 2)
                kcmpT = self.sb(es, "b_kcmpT", [128, S_LEN], BF16)
                vcmpT = self.sb(es, "b_vcmpT", [128, S_LEN], BF16)
                slab = self.sb(es, "b_slab", [128, 8, 512], BF16)
                uTt = self.ring(es, "b_uTt", [128, 8, 128], BF16, 3)
                tm = self.ring(es, "b_tm", [128, 512], BF16, 2)
                for dcol, scol in ((0, 0), (128, 256), (256, 128), (384, 384)):
                    self.cast_into(slab, lambda c0, n, dcol=dcol: slab.t[:, :, dcol + c0:dcol + c0 + n], self.w_in[:, C_KVB + scol:C_KVB + scol + 128], 8, 128, self.gmix, piece=128)
                b2u = {}

                def b2_stage1(kt):
                    u = uTt.next()
                    self.norm_tile(self.x_full[kt * 128:(kt + 1) * 128, :], u, u.t[:])
                    b2u[kt] = u

                def b2_stage2(kt):
                    u = b2u.pop(kt)
                    ps = self.psA.next()
                    self.proj_tm(lambda c, u=u: u.t[:, c, :], u, slab, 0, 512, ps)
                    t = tm.next()
                    self.rope_evac(ps, 0, 4, 84 + kt, t, t.t[:, 0:256])
                    S.op("act", I("copy", out=t.t[:, 256:384], in_=ps.t[:, 256:384]), reads=[ps], writes=[t])
                    if kt < 48:
                        pvc = pcf.t[:, PCF["pv"] + kt:PCF["pv"] + kt + 1]
                        S.op("dve", I("tensor_scalar", out=vslc.t[:, kt, :, 0:64], in0=ps.t[:, 384:512].rearrange("p (g d) -> p g d", g=2), scalar1=pvc, scalar2=None, op0=ALU.mult), reads=[ps, pcf], writes=[vslc])
                        S.op("pool", I("tensor_scalar", out=vslc.t[:, kt, :, 64:128], in0=ones2, scalar1=pvc, scalar2=None, op0=ALU.mult), reads=[cbf, pcf], writes=[vslc])
                    pt = self.psT
                    for k in range(3):
                        S.op("pe", I("transpose", out=pt.t[:, k * 128:(k + 1) * 128], in_=t.t[:, k * 128:(k + 1) * 128], identity=self.ident), reads=[t, cbf], writes=[pt])
                    S.op("act", I("copy", out=kcmpT.t[:, kt * 128:(kt + 1) * 128], in_=pt.t[:, 0:128]), reads=[pt], writes=[kcmpT])
                    S.op("act", I("copy", out=vcmpT.t[:, kt * 128:(kt + 1) * 128], in_=pt.t[:, 256:384]), reads=[pt], writes=[vcmpT])
                    if kt < 48:
                        S.op("act", I("copy", out=kslcT.t[:, kt * 128:(kt + 1) * 128], in_=pt.t[:, 128:256]), reads=[pt], writes=[kslcT])

                b2_stage1(0)
                for kt in range(64):
                    if kt + 1 < 64:
                        b2_stage1(kt + 1)
                    b2_stage2(kt)

                if self.stop_after == "B2":
                    self.dump("kslcT", kslcT, kslcT.t[:], [128, 48 * 128], BF16)
                    self.dump("kcmpT", kcmpT, kcmpT.t[:], [128, S_LEN], BF16)
                    self.dump("vslc", vslc, vslc.t[:], [128, 48, 2, 128], BF16)
                    self.stopped = True
                    return
                w1sb = self.sb(es, "b_w1", [128, 32, 256], BF16)
                w2sb = self.sb(es, "b_w2", [128, 2, 128], BF16)
                posb = self.sb(es, "b_posb", [32, 128], BF16)
                posf = self.sb(es, "b_posf", [32, 64], F32)
                posT = self.sb(es, "b_posT", [128, 32], BF16)
                b1sb = self.sb(es, "b_b1", [128, 2], F32)
                gel = [self.sb(es, "b_gel%d" % i, [128, 512], BF16) for i in range(4)]
                hA = self.sb(es, "b_hA", [128, 512], F32)
                hB = self.sb(es, "b_hB", [128, 512], F32)
                for kv in range(2):
                    src = kcmpT if kv == 0 else vcmpT
                    w1d = self.cmp_w1_k if kv == 0 else self.cmp_w1_v
                    w2d = self.cmp_w2_k if kv == 0 else self.cmp_w2_v
                    posd = self.cmp_pos_k if kv == 0 else self.cmp_pos_v
                    w1v = w1d.rearrange("(j d) h -> d j h", d=64)
                    for j0 in range(0, 32, 8):
                        st = self.wst.next()
                        for half in range(2):
                            S.dma(I("dma_start", out=st.t[64 * half:64 * half + 64, :, :], in_=w1v[:, j0:j0 + 8, :]), writes=[st])
                        S.op("pool", I("tensor_copy", out=w1sb.t[:, j0:j0 + 8, :], in_=st.t[:, :, :]), reads=[st], writes=[w1sb])
                    st = self.wst.next()
                    S.dma(I("dma_start", out=st.t[:, 0:2, 0:64], in_=w2d.rearrange("(c p) n -> p c n", p=128)), writes=[st])
                    S.op("pool", I("tensor_copy", out=w2sb.t[:, :, 0:64], in_=st.t[:, 0:2, 0:64]), reads=[st], writes=[w2sb])
                    S.op("pool", I("tensor_copy", out=w2sb.t[:, :, 64:128], in_=st.t[:, 0:2, 0:64]), reads=[st], writes=[w2sb])
                    S.dma(I("dma_start", out=posf.t[:], in_=posd), writes=[posf])
                    S.op("dve", I("tensor_copy", out=posb.t[:, 0:64], in_=posf.t[:]), reads=[posf], writes=[posb])
                    S.op("dve", I("tensor_copy", out=posb.t[:, 64:128], in_=posf.t[:]), reads=[posf], writes=[posb])
                    pt = self.psT
                    S.op("pe", I("transpose", out=pt.t[:, 0:32], in_=posb.t[:], identity=cbf.t[0:32, 0:32]), reads=[posb, cbf], writes=[pt])
                    S.op("act", I("copy", out=posT.t[:], in_=pt.t[:, 0:32]), reads=[pt], writes=[posT])
                    psx = self.psX
                    for mh in range(2):
                        for j in range(32):
                            S.op("pe", I("matmul", psx.t[:, mh:mh + 1], lhsT=w1sb.t[0:64, j, mh * 128:(mh + 1) * 128], rhs=posT.t[0:64, j:j + 1], start=(j == 0), stop=(j == 31)), reads=[w1sb, posT], writes=[psx])
                    S.op("dve", I("tensor_copy", out=b1sb.t[:], in_=psx.t[:, 0:2]), reads=[psx], writes=[b1sb])
                    for mh in range(2):
                        pss = [self.psA.next(), self.psA.next()]
                        for j in range(32):
                            for g in range(2):
                                pb = 64 * g
                                S.op("pe", I("matmul", pss[g].t[:, 0:511], lhsT=w1sb.t[pb:pb + 64, j, mh * 128:(mh + 1) * 128], rhs=src.t[pb:pb + 64, j:j + 16 * 510 + 1:16], start=(j == 0), stop=(j == 31)), reads=[w1sb, src], writes=[pss[g]])
                        for g in range(2):
                            ps = pss[g]
                            ge = gel[2 * g + mh]
                            S.op("act", I("activation", out=hA.t[:, 0:511], in_=ps.t[:, 0:511], func=AF.Identity, bias=b1sb.t[:, mh:mh + 1]), reads=[ps, b1sb], writes=[hA])
                            S.op("dve", I("tensor_tensor", out=hB.t[:, 0:511], in0=hA.t[:, 0:511], in1=hA.t[:, 0:511], op=ALU.mult), reads=[hA], writes=[hB])
                            S.op("dve", I("tensor_scalar", out=hB.t[:, 0:511], in0=hB.t[:, 0:511], scalar1=0.044715, scalar2=1.0, op0=ALU.mult, op1=ALU.add), reads=[hB], writes=[hB])
                            S.op("dve", I("tensor_tensor", out=hB.t[:, 0:511], in0=hB.t[:, 0:511], in1=hA.t[:, 0:511], op=ALU.mult), reads=[hA, hB], writes=[hB])
                            S.op("act", I("activation", out=hB.t[:, 0:511], in_=hB.t[:, 0:511], func=AF.Sigmoid, scale=2.0 * 0.7978845608028654), reads=[hB], writes=[hB])
                            S.op("dve", I("tensor_tensor", out=ge.t[:, 0:511], in0=hA.t[:, 0:511], in1=hB.t[:, 0:511], op=ALU.mult), reads=[hA, hB], writes=[ge])
                    for g in range(2):
                        pb = 64 * g
                        if kv == 0:
                            ps = self.psA.next()
                            for mh in range(2):
                                S.op("pe", I("matmul", ps.t[:, 0:511], lhsT=w2sb.t[:, mh, :], rhs=gel[2 * g + mh].t[:, 0:511], start=(mh == 0), stop=(mh == 1)), reads=[w2sb, gel[2 * g + mh]], writes=[ps])
                            S.op("act", I("copy", out=kcT.t[pb:pb + 64, 0:511], in_=ps.t[pb:pb + 64, 0:511]), reads=[ps], writes=[kcT])
                        else:
                            for bt in range(4):
                                n = 128 if bt < 3 else 127
                                ps = self.psA.next()
                                for mh in range(2):
                                    S.op("pe", I("matmul", ps.t[0:n, 0:64], lhsT=gel[2 * g + mh].t[:, bt * 128:bt * 128 + n], rhs=w2sb.t[:, mh, 0:64], start=(mh == 0), stop=(mh == 1)), reads=[w2sb, gel[2 * g + mh]], writes=[ps])
                                S.op("act", I("copy", out=vc.t[0:n, bt, g, 0:64], in_=ps.t[0:n, 0:64]), reads=[ps], writes=[vc])
                                S.op("pool", I("tensor_copy", out=vc.t[0:n, bt, g, 64:128], in_=ones64[0:n, :]), reads=[cbf], writes=[vc])
            S.barrier()
            if self.stop_after == "B3":
                self.dump("kcT", kcT, kcT.t[:], [128, 512], BF16)
                self.dump("vc", vc, vc.t[:], [128, 4, 2, 128], BF16)
                self.stopped = True
                return
            qbT = self.sb(esB, "b_qbT", [128, 4, OWN], BF16)
            gbT = self.sb(esB, "b_gbT", [32, OWN], F32)
            kwinT = self.sb(esB, "b_kwinT", [128, 20 * 128], BF16)
            vwin = self.sb(esB, "b_vwin", [128, 20, 2, 128], BF16)
            kso = self.sb(esB, "b_kso", [128, OWN], BF16)
            vso = self.sb(esB, "b_vso", [128, 16, 2, 128], BF16)
            S.op("pool", I("memset", gbT.t[:], 0.0), writes=[gbT])
            hvcol = pcf.t[:, PCF["hv"]:PCF["hv"] + 1]
            with ExitStack() as es:
                self.wst = self.ring(es, "wstB1", [128, 8, 256], F32, 2)
                slq = self.sb(es, "b_slq", [128, 8, 512], BF16)
                slkv = self.sb(es, "b_slkv", [128, 8, 512], BF16)
                slg = self.sb(es, "b_slg", [128, 8, 32], BF16)
                uTt = self.ring(es, "b_uTt1", [128, 8, 128], BF16, 2)
                tq = self.ring(es, "b_tq", [128, 512], BF16, 2)
                tk = self.ring(es, "b_tk", [128, 256], BF16, 2)
                self.cast_into(slq, lambda c0, n: slq.t[:, :, c0:c0 + n], self.w_in[:, C_QB:C_QB + 512], 8, 512, self.gmix, piece=256)
                self.cast_into(slkv, lambda c0, n: slkv.t[:, :, c0:c0 + n], self.w_in[:, C_KVB + 256:C_KVB + 768], 8, 512, self.gmix, piece=256)
                self.cast_into(slg, lambda c0, n: slg.t[:, :, c0:c0 + n], self.w_in[:, C_GB:C_GB + 24], 8, 24, self.gmix, piece=256)
                for e_ in range(12, 32):
                    slot = e_ - 12
                    own = e_ >= 16
                    i = e_ - 16
                    u = uTt.next()
                    xs = self.x_own[i * 128:(i + 1) * 128, :] if own else self.x_halo[e_ * 128:(e_ + 1) * 128, :]
                    self.norm_tile(xs, u, u.t[:])
                    lhs = lambda c, u=u: u.t[:, c, :]
                    if own:
                        ps = self.psA.next()
                        self.proj_tm(lhs, u, slq, 0, 512, ps)
                        t = tq.next()
                        self.rope_evac(ps, 0, 8, e_, t, t.t[:, 0:512], perm=True)
                        pt = self.psT
                        for hp in range(4):
                            S.op("pe", I("transpose", out=pt.t[:, hp * 128:(hp + 1) * 128], in_=t.t[:, hp * 128:(hp + 1) * 128], identity=self.ident), reads=[t, cbf], writes=[pt])
                        S.op("act", I("copy", out=qbT.t[:, :, i * 128:(i + 1) * 128], in_=pt.t[:, 0:512].rearrange("p (h t) -> p h t", h=4)), reads=[pt], writes=[qbT])
                        psx = self.psX
                        for c in range(8):
                            S.op("pe", I("matmul", psx.t[0:24, 0:128], lhsT=slg.t[:, c, 0:24], rhs=u.t[:, c, :], start=(c == 0), stop=(c == 7)), reads=[slg, u], writes=[psx])
                        S.op("act", I("activation", out=gbT.t[0:24, i * 128:(i + 1) * 128], in_=psx.t[0:24, 0:128], func=AF.Sigmoid), reads=[psx], writes=[gbT])
                    ps = self.psA.next()
                    t = tk.next()
                    if own:
                        self.proj_tm(lhs, u, slkv, 0, 512, ps)
                        self.rope_evac(ps, 0, 2, e_, t, t.t[:, 0:128])
                        self.rope_evac(ps, 256, 2, e_, t, t.t[:, 128:256])
                        S.op("dve", I("tensor_copy", out=vso.t[:, i, :, 0:64], in_=ps.t[:, 128:256].rearrange("p (g d) -> p g d", g=2)), reads=[ps], writes=[vso])
                        S.op("pool", I("tensor_copy", out=vso.t[:, i, :, 64:128], in_=ones2), reads=[cbf], writes=[vso])
                        S.op("dve", I("tensor_copy", out=vwin.t[:, slot, :, 0:64], in_=ps.t[:, 384:512].rearrange("p (g d) -> p g d", g=2)), reads=[ps], writes=[vwin])
                        S.op("pool", I("tensor_copy", out=vwin.t[:, slot, :, 64:128], in_=ones2), reads=[cbf], writes=[vwin])
                    else:
                        self.proj_tm(lhs, u, slkv, 256, 256, ps)
                        self.rope_evac(ps, 0, 2, e_, t, t.t[:, 128:256])
                        S.op("dve", I("tensor_scalar", out=vwin.t[:, slot, :, 0:64], in0=ps.t[:, 128:256].rearrange("p (g d) -> p g d", g=2), scalar1=hvcol, scalar2=None, op0=ALU.mult), reads=[ps, pcf], writes=[vwin])
                        S.op("pool", I("tensor_scalar", out=vwin.t[:, slot, :, 64:128], in0=ones2, scalar1=hvcol, scalar2=None, op0=ALU.mult), reads=[cbf, pcf], writes=[vwin])
                    pt = self.psT
                    if own:
                        S.op("pe", I("transpose", out=pt.t[:, 0:128], in_=t.t[:, 0:128], identity=self.ident), reads=[t, cbf], writes=[pt])
                    S.op("pe", I("transpose", out=pt.t[:, 128:256], in_=t.t[:, 128:256], identity=self.ident), reads=[t, cbf], writes=[pt])
                    if own:
                        S.op("act", I("copy", out=kso.t[:, i * 128:(i + 1) * 128], in_=pt.t[:, 0:128]), reads=[pt], writes=[kso])
                    S.op("act", I("copy", out=kwinT.t[:, slot * 128:(slot + 1) * 128], in_=pt.t[:, 128:256]), reads=[pt], writes=[kwinT])
            S.barrier()
            biasT = self.sb(esB, "b_biasT", [128, 2, OWN], BF16)
            t16 = self.sb(esB, "b_t16", [128, 2048], F32)
            S.dma(I("dma_start", out=t16.t[:], in_=self.c_t16), writes=[t16])
            with ExitStack() as es:
                et = self.ring(es, "b_et", [128, 512], F32, 4)
                pp = self.ring(es, "b_pp", [128, 520], F32, 4)
                lohi = self.ring(es, "b_lohi", [128, 256], F32, 2)
                imp = self.ring(es, "b_imp", [128, 128], F32, 2)
                imp2 = self.ring(es, "b_imp2", [128, 128], F32, 2)
                sm = self.ring(es, "b_sm", [128, 24], F32, 8)
                btm = self.ring(es, "b_btm", [128, 128], BF16, 2)
                for pq in pp.items:
                    S.op("pool", I("memset", pq.t[:], 0.0), writes=[pq])
                for i in range(NT):
                    lh = lohi.next()
                    S.dma(I("dma_start", out=lh.t[:, 0:128], in_=self.pc_lohi[:, i * 128:(i + 1) * 128]), writes=[lh])
                    S.dma(I("dma_start", out=lh.t[:, 128:256], in_=self.pc_lohi[:, 2048 + i * 128:2048 + (i + 1) * 128]), writes=[lh])
                    thr_i = pcf.t[:, PCF["thrc"] + i:PCF["thrc"] + i + 1]
                    def gen(g, i=i, lh=lh, thr_i=thr_i):
                        pb = 64 * g
                        P = pp.next()
                        P2 = pp.next()
                        for r in range(4):
                            ps = self.psS.next()
                            S.op("pe", I("matmul", ps.t[:, 0:511], lhsT=qbT.t[pb:pb + 64, r, i * 128:(i + 1) * 128], rhs=kcT.t[pb:pb + 64, 0:511], start=True, stop=True), reads=[qbT, kcT], writes=[ps])
                            e_ = et.next()
                            s_ = sm.next()
                            eng = "dve"
                            Pr = P if r % 2 == 0 else P2
                            S.op("act", I("activation", out=e_.t[:, 0:511], in_=ps.t[:, 0:511], func=AF.Exp, scale=SCALE), reads=[ps], writes=[e_])
                            S.op(eng, I("scalar_tensor_tensor", out=e_.t[:, 0:511], in0=t16.t[:, 0:511], scalar=thr_i, in1=e_.t[:, 0:511], op0=ALU.is_le, op1=ALU.mult, accum_out=s_.t[:, 0:1]), reads=[t16, pcf, e_], writes=[e_, s_])
                            yield
                            S.op(eng, I("tensor_scalar", out=s_.t[:, 1:2], in0=s_.t[:, 0:1], scalar1=1e-30, scalar2=None, op0=ALU.max), reads=[s_], writes=[s_])
                            yield
                            S.op("dve", I("reciprocal", out=s_.t[:, 2:3], in_=s_.t[:, 1:2]), reads=[s_], writes=[s_])
                            yield
                            if r < 2:
                                S.op(eng, I("tensor_scalar", out=Pr.t[:, 1:512], in0=e_.t[:, 0:511], scalar1=s_.t[:, 2:3], scalar2=None, op0=ALU.mult), reads=[e_, s_], writes=[Pr])
                                yield
                            else:
                                S.op(eng, I("scalar_tensor_tensor", out=Pr.t[:, 1:512], in0=e_.t[:, 0:511], scalar=s_.t[:, 2:3], in1=Pr.t[:, 1:512], op0=ALU.mult, op1=ALU.add), reads=[e_, s_, Pr], writes=[Pr])
                                yield
                        S.op("dve", I("tensor_tensor", out=P.t[:, 1:512], in0=P.t[:, 1:512], in1=P2.t[:, 1:512], op=ALU.add), reads=[P, P2], writes=[P])
                        yield
                        im = imp.next()
                        S.op("dve", I("tensor_tensor", out=im.t[:], in0=P.t[:, 0:512:4], in1=P.t[:, 1:513:4], op=ALU.add), reads=[P], writes=[im])
                        yield
                        for k in range(2, 5):
                            S.op("dve", I("tensor_tensor", out=im.t[:], in0=im.t[:], in1=P.t[:, k:k + 512:4], op=ALU.add), reads=[P, im], writes=[im])
                            yield
                        S.op("dve", I("tensor_tensor", out=im.t[:], in0=im.t[:], in1=lh.t[:, 0:128], op=ALU.max), reads=[lh, im], writes=[im])
                        yield
                        S.op("dve", I("tensor_tensor", out=im.t[:], in0=im.t[:], in1=lh.t[:, 128:256], op=ALU.min), reads=[lh, im], writes=[im])
                        yield
                        s_ = sm.next()
                        i2 = imp2.next()
                        S.op("dve", I("max", out=s_.t[:, 0:8], in_=im.t[:]), reads=[im], writes=[s_])
                        yield
                        S.op("dve", I("match_replace", out=i2.t[:], in_to_replace=s_.t[:, 0:8], in_values=im.t[:], imm_value=-1e9), reads=[im, s_], writes=[i2])
                        yield
                        S.op("dve", I("max", out=s_.t[:, 8:16], in_=i2.t[:]), reads=[i2], writes=[s_])
                        yield
                        S.op("dve", I("tensor_scalar", out=s_.t[:, 16:17], in0=s_.t[:, 15:16], scalar1=-1.5e4, scalar2=None, op0=ALU.max), reads=[s_], writes=[s_])
                        yield
                        bt_ = btm.next()
                        S.op("dve", I("tensor_scalar", out=bt_.t[:], in0=im.t[:], scalar1=s_.t[:, 16:17], scalar2=-30000.0, op0=ALU.is_lt, op1=ALU.mult), reads=[im, s_], writes=[bt_])
                        yield
                        pt = self.psT
                        S.op("pe", I("transpose", out=pt.t[:, 0:128], in_=bt_.t[:], identity=self.ident), reads=[bt_, cbf], writes=[pt])
                        S.op("act", I("copy", out=biasT.t[:, g, i * 128:(i + 1) * 128], in_=pt.t[:, 0:128]), reads=[pt], writes=[biasT])

                    gens = [gen(0), gen(1)]
                    while gens:
                        for gg in list(gens):
                            try:
                                next(gg)
                            except StopIteration:
                                gens.remove(gg)
            S.barrier()
            if self.stop_after == "B6":
                self.dump("biasT", biasT, biasT.t[:], [128, 2, OWN], BF16)
                self.dump("qbT", qbT, qbT.t[:], [128, 4, OWN], BF16)
                self.stopped = True
                return
            with ExitStack() as es:
                osb = self.ring(es, "b_osb", [128, 512], F32, 2)
                rdr = self.ring(es, "b_rd", [128, 512], F32, 2)
                ybacc = self.sb(es, "b_ybacc", [128, 512], F32)
                ybacc2 = self.sb(es, "b_ybacc2", [128, 512], F32)
                self.ptr = self.ring(es, "b_ptr", [128, 512], BF16, 5)
                gsel = self.ring(es, "b_gsel", [32, 128], F32, 6)
                id32 = self.cf32.t[0:32, F32C["id32"]:F32C["id32"] + 32]
                eown = pcb.t[:, PCB["eown"]:PCB["eown"] + 2048]
                e32 = cbf.t[:, BFC["e32"]:BFC["e32"] + 2048]
                m4 = cbf.t[:, BFC["m4"]:BFC["m4"] + 2048]
                tri_diag = cbf.t[:, BFC["tri_diag"]:BFC["tri_diag"] + 128]
                win_far = cbf.t[:, BFC["win_far"]:BFC["win_far"] + 128]
                BRS = os.environ.get("BRS", "012")
                ps4 = Ring([self.psS.items[0], self.psS.items[1], self.psA.items[0], self.psA.items[1]])
                ybaccs = [ybacc, ybacc2]
                for hp in range(4):
                    sels = {}
                    for gi in range(2):
                        h = hp + 4 * gi
                        for br in range(3):
                            gs = gsel.next()
                            jrow = h * 3 + br
                            S.op("pool", I("tensor_copy", out=gs.t[:], in_=id32[:, jrow:jrow + 1].to_broadcast([32, 128])), reads=[self.cf32], writes=[gs])
                            sels[(gi, br)] = gs
                    for c in range(4):
                        gcols = gbT.t[0:32, c * 512:(c + 1) * 512]
                        dst = self.ybT.t[:, hp, c * 512:(c + 1) * 512]
                        Qs = [qbT.t[64 * gi:64 * gi + 64, hp, c * 512:(c + 1) * 512] for gi in range(2)]

                        def fin(psos, br, first, last):
                            for gi in range(2):
                                o = osb.next()
                                S.op("act", I("copy", out=o.t[:], in_=psos[gi].t[:]), reads=[psos[gi]], writes=[o])
                                self.finalize(o, o.t[:], gi, (sels[(gi, br)], gbT, gcols), rdr, self.ybT, dst, first=first, last=last, ybacc=ybaccs[gi])
                        psos = [self.psO.next(), self.psO.next()]
                        steps = []
                        for bt in range(4):
                            crel = pcf.t[:, PCF["crel"] + bt:PCF["crel"] + bt + 1]

                            def mask_c(pt, crel=crel, c=c):
                                S.op("dve", I("scalar_tensor_tensor", out=pt.t[:, 0:512], in0=t16.t[:, c * 512:(c + 1) * 512], scalar=crel, in1=pt.t[:, 0:512], op0=ALU.is_ge, op1=ALU.mult), reads=[t16, pcf, pt], writes=[pt])
                            stp = []
                            for gi in range(2):
                                pb = 64 * gi
                                stp.append(([(kcT.t[pb:pb + 64, bt * 128:(bt + 1) * 128], Qs[gi], [kcT, qbT])], 512, mask_c,
                                            [(psos[gi], psos[gi].t[:, 0:512], vc.t[:, bt, gi, :], 0, 512, bt == 0, bt == 3, [vc])]))
                            steps.append(stp)
                        self.attn_steps(steps, ps4)
                        fin(psos, 0, True, False)
                        psos = [self.psO.next(), self.psO.next()]
                        steps = []
                        for kt in range(48):
                            pb32 = 32 * (kt // 16)
                            kc_ = (kt % 16) * 128
                            stp = []
                            for gi in range(2):
                                pb = 64 * gi
                                stp.append(([(kslcT.t[pb:pb + 64, kt * 128:(kt + 1) * 128], Qs[gi], [kslcT, qbT]),
                                             (e32[pb32:pb32 + 32, kc_:kc_ + 128], biasT.t[pb32:pb32 + 32, gi, c * 512:(c + 1) * 512], [cbf, biasT])], 512, None,
                                            [(psos[gi], psos[gi].t[:, 0:512], vslc.t[:, kt, gi, :], 0, 512, kt == 0, False, [vslc])]))
                            steps.append(stp)
                        for j in range(4 * c + 4):
                            mf = None
                            if j >= 4 * c:
                                mk = m4[:, (j - 4 * c) * 512:(j - 4 * c + 1) * 512]

                                def mf(pt, mk=mk):
                                    self.mask_mul(pt, 0, 512, mk)
                            stp = []
                            for gi in range(2):
                                pb = 64 * gi
                                stp.append(([(kso.t[pb:pb + 64, j * 128:(j + 1) * 128], Qs[gi], [kso, qbT]),
                                             (eown[:, j * 128:(j + 1) * 128], biasT.t[:, gi, c * 512:(c + 1) * 512], [pcb, biasT])], 512, mf,
                                            [(psos[gi], psos[gi].t[:, 0:512], vso.t[:, j, gi, :], 0, 512, False, j == 4 * c + 3, [vso])]))
                            steps.append(stp)
                        self.attn_steps(steps, ps4)
                        fin(psos, 1, False, False)
                        psos = [self.psO.next(), self.psO.next()]
                        steps = []
                        for tq_ in range(4):
                            i = 4 * c + tq_
                            sq = 4 + i
                            for s_ in range(sq - 4, sq + 1):
                                mf = None
                                if s_ == sq - 4:
                                    def mf(pt):
                                        self.mask_mul(pt, 0, 128, win_far)
                                elif s_ == sq:
                                    def mf(pt):
                                        self.mask_mul(pt, 0, 128, tri_diag)
                                stp = []
                                for gi in range(2):
                                    pb = 64 * gi
                                    stp.append(([(kwinT.t[pb:pb + 64, s_ * 128:(s_ + 1) * 128], qbT.t[pb:pb + 64, hp, i * 128:(i + 1) * 128], [kwinT, qbT])], 128, mf,
                                                [(psos[gi], psos[gi].t[:, tq_ * 128:(tq_ + 1) * 128], vwin.t[:, s_, gi, :], 0, 128, s_ == sq - 4, s_ == sq, [vwin])]))
                                steps.append(stp)
                        self.attn_steps(steps, ps4)
                        fin(psos, 2, False, True)
        S.barrier()

    def phase_C(self, es0):
        S = self.S
        with ExitStack() as es:
            slabs = self.ring(es, "c_slab", [128, 8, 512], BF16, 5)
            self.wbslab = self.sb(es, "c_wb", [128, 4, 512], BF16)
            gfin = self.sb(es, "c_gfin", [128, D], F32)
            xc = self.sb(es, "c_xc", [128, 4, D], F32)
            uTc = self.sb(es, "c_uTc", [128, 8, 512], BF16)
            mTc = self.sb(es, "c_mTc", [128, 8, 512], BF16)
            u2Tc = self.sb(es, "c_u2Tc", [128, 8, 512], BF16)
            hT = self.sb(es, "c_hT", [128, 32, 512], BF16)
            sg = self.ring(es, "c_sg", [128, 512], BF16, 2)
            tf = self.ring(es, "c_tf", [128, 512], F32, 3)
            S.dma(I("dma_start", out=gfin.t[:], in_=self.g_fin), writes=[gfin])

            slab_ids = {id(t): i for i, t in enumerate(slabs.items)}

            def slab_from(src_ap, kchunks, gain):
                sl = slabs.next()
                fns = [I("dma_start", out=sl.t[:, 0:kchunks, c0:c0 + 256], in_=src_ap[:, c0:c0 + 256].rearrange("(c p) n -> p c n", p=128)) for c0 in (0, 256)]
                S.dma_sw(fns, [sl], slab_ids[id(sl)])
                return sl

            for c in range(4):
                cs = slice(c * 512, (c + 1) * 512)
                for tt in range(4):
                    r0 = c * 512 + tt * 128
                    S.dma(I("dma_start", out=xc.t[:, tt, :], in_=self.x_own[r0:r0 + 128, :]), writes=[xc])
                    self.norm_sb(xc, xc.t[:, tt, :], uTc, uTc.t[:, :, tt * 128:(tt + 1) * 128], keep_rstd=self.gmix)
                for ctg in range(2):
                    gA = slab_from(self.w_in[:, C_GM + ctg * 512:C_GM + ctg * 512 + 512], 8, self.gmix)
                    gB = slab_from(self.w_in[:, C_GM + 1024 + ctg * 512:C_GM + 1024 + ctg * 512 + 512], 8, self.gmix)
                    wa = slab_from(self.w_a[:, ctg * 512:(ctg + 1) * 512], 4, None)
                    wb = self.wbslab
                    fns = [I("dma_start", out=wb.t[64 * two:64 * two + 64, 0:4, 0:512],
                             in_=self.w_b[two * 256:(two + 1) * 256, ctg * 512:ctg * 512 + 512].rearrange("(hp d) n -> d hp n", d=64)) for two in range(2)]
                    S.dma_sw(fns, [wb], 99)
                    for j in range(4):
                        ct = ctg * 4 + j
                        js = slice(j * 128, (j + 1) * 128)
                        sgs = []
                        for gw in (gA, gB):
                            ps = self.psA.next()
                            for k in range(8):
                                S.op("pe", I("matmul", ps.t[:, 0:512], lhsT=gw.t[:, k, js], rhs=uTc.t[:, k, :], start=(k == 0), stop=(k == 7)), reads=[gw, uTc], writes=[ps])
                            sgt = sg.next()
                            S.op("act", I("activation", out=sgt.t[:], in_=ps.t[:, 0:512], func=AF.Sigmoid), reads=[ps], writes=[sgt])
                            sgs.append(sgt)
                        psa = self.psS.next()
                        for k in range(4):
                            S.op("pe", I("matmul", psa.t[:, 0:512], lhsT=wa.t[:, k, js], rhs=self.yaT.t[:, k, cs], start=(k == 0), stop=(k == 3)), reads=[wa, self.yaT], writes=[psa])
                        psb = self.psO.next()
                        for k in range(4):
                            S.op("pe", I("matmul", psb.t[:, 0:512], lhsT=wb.t[:, k, js], rhs=self.ybT.t[:, k, cs], start=(k == 0), stop=(k == 3)), reads=[wb, self.ybT], writes=[psb])
                        t0 = tf.next()
                        t1 = tf.next()
                        S.op("dve", I("tensor_tensor", out=t0.t[:], in0=psa.t[:, 0:512], in1=sgs[0].t[:], op=ALU.mult), reads=[psa, sgs[0]], writes=[t0])
                        S.op("dve", I("tensor_tensor", out=t1.t[:], in0=psb.t[:, 0:512], in1=sgs[1].t[:], op=ALU.mult), reads=[psb, sgs[1]], writes=[t1])
                        S.op("dve", I("tensor_tensor", out=mTc.t[:, ct, :], in0=t0.t[:], in1=t1.t[:], op=ALU.add), reads=[t0, t1], writes=[mTc])
                for nh in range(2):
                    wo = slab_from(self.w_out[:, nh * 512:(nh + 1) * 512], 8, None)
                    for tt in range(4):
                        ps = self.psA.next()
                        for k in range(8):
                            S.op("pe", I("matmul", ps.t[:, 0:512], lhsT=mTc.t[:, k, tt * 128:(tt + 1) * 128], rhs=wo.t[:, k, :], start=(k == 0), stop=(k == 7)), reads=[wo, mTc], writes=[ps])
                        S.op("dve", I("tensor_tensor", out=xc.t[:, tt, nh * 512:(nh + 1) * 512], in0=ps.t[:, 0:512], in1=xc.t[:, tt, nh * 512:(nh + 1) * 512], op=ALU.add), reads=[ps, xc], writes=[xc])
                for tt in range(4):
                    self.norm_sb(xc, xc.t[:, tt, :], u2Tc, u2Tc.t[:, :, tt * 128:(tt + 1) * 128], keep_rstd=self.gmlp)
                for s_ in range(8):
                    wu = slab_from(self.w_up[:, s_ * 512:(s_ + 1) * 512], 8, self.gmlp)
                    for j in range(4):
                        ft = 4 * s_ + j
                        ps = self.psA.next()
                        for k in range(8):
                            S.op("pe", I("matmul", ps.t[:, 0:512], lhsT=wu.t[:, k, j * 128:(j + 1) * 128], rhs=u2Tc.t[:, k, :], start=(k == 0), stop=(k == 7)), reads=[wu, u2Tc], writes=[ps])
                        r = tf.next()
                        S.op("act", I("activation", out=r.t[:], in_=ps.t[:, 0:512], func=AF.Relu), reads=[ps], writes=[r])
                        S.op("dve", I("tensor_tensor", out=hT.t[:, ft, :], in0=r.t[:], in1=r.t[:], op=ALU.mult), reads=[r], writes=[hT])
                accs = [self.psA.items[0], self.psA.items[1], self.psS.items[0], self.psS.items[1]]
                for nh in range(2):
                    for kg in range(4):
                        wd = slab_from(self.w_down[kg * 1024:(kg + 1) * 1024, nh * 512:(nh + 1) * 512], 8, None)
                        for tt in range(4):
                            for k in range(8):
                                S.op("pe", I("matmul", accs[tt].t[:, 0:512], lhsT=hT.t[:, kg * 8 + k, tt * 128:(tt + 1) * 128], rhs=wd.t[:, k, :], start=(kg == 0 and k == 0), stop=(kg == 3 and k == 7)), reads=[wd, hT], writes=[accs[tt]])
                    for tt in range(4):
                        S.op("dve", I("tensor_tensor", out=xc.t[:, tt, nh * 512:(nh + 1) * 512], in0=accs[tt].t[:, 0:512], in1=xc.t[:, tt, nh * 512:(nh + 1) * 512], op=ALU.add), reads=[accs[tt], xc], writes=[xc])
                for tt in range(4):
                    jk = self.junk.next()
                    st = self.stat.next()
                    S.op("act", I("activation", out=jk.t[:], in_=xc.t[:, tt, :], func=AF.Square, accum_out=st.t[:, 0:1]), reads=[xc], writes=[jk, st])
                    S.op("act", I("activation", out=st.t[:, 1:2], in_=st.t[:, 0:1], func=AF.Sqrt, scale=1.0 / D, bias=self.epsc.t[:, 0:1]), reads=[st, self.epsc], writes=[st])
                    S.op("dve", I("reciprocal", out=st.t[:, 2:3], in_=st.t[:, 1:2]), reads=[st], writes=[st])
                    S.op("dve", I("scalar_tensor_tensor", out=xc.t[:, tt, :], in0=xc.t[:, tt, :], scalar=st.t[:, 2:3], in1=gfin.t[:], op0=ALU.mult, op1=ALU.mult), reads=[xc, st, gfin], writes=[xc])
                    r0 = c * 512 + tt * 128
                    S.dma(I("dma_start", out=self.out[r0:r0 + 128, :], in_=xc.t[:, tt, :]), reads=[xc])

def make_in_maps(inputs):
    x = np.ascontiguousarray(np.asarray(inputs["x"], np.float32))
    cbf, cf32, t16 = _static_tables()
    sq = lambda n: np.ascontiguousarray(np.asarray(inputs[n], np.float32)[0])
    gl = lambda v: np.ascontiguousarray(np.asarray(v, np.float32).reshape(8, 128).T)
    common = {
        "w_in": sq("w_in"), "g_mix": gl(inputs["norm_mix_g"][0]), "g_mlp": gl(inputs["norm_mlp_g"][0]),
        "g_fin": np.ascontiguousarray(np.broadcast_to(np.asarray(inputs["norm_final_g"], np.float32)[None, :], (128, D))),
        "cmp_w1_k": sq("cmp_w1_k"), "cmp_w1_v": sq("cmp_w1_v"), "cmp_w2_k": sq("cmp_w2_k"), "cmp_w2_v": sq("cmp_w2_v"),
        "cmp_pos_k": sq("cmp_pos_k"), "cmp_pos_v": sq("cmp_pos_v"),
        "w_a": sq("w_branch_a"), "w_b": sq("w_branch_b"), "w_out": sq("w_out"), "w_up": sq("w_up"), "w_down": sq("w_down"),
        "c_bf": cbf, "c_f32": cf32, "c_t16": t16,
    }
    tabs = [_percore_tables(q) for q in range(4)]
    maps = []
    for c in range(8):
        b, q = c // 4, c % 4
        T0 = OWN * q
        halo = x[b, T0 - OWN:T0] if q > 0 else np.zeros((OWN, D), np.float32)
        m = dict(common)
        m.update({"x_own": np.ascontiguousarray(x[b, T0:T0 + OWN]), "x_halo": np.ascontiguousarray(halo), "x_full": x[b],
                  "pc_f": tabs[q][0], "pc_lohi": tabs[q][1], "pc_bf": tabs[q][2]})
        maps.append(m)
    return maps


_CACHE = {}


def kernel(**inputs):
    if "nc" not in _CACHE:
        b = Builder()
        _CACHE["nc"] = b.build()
        _CACHE["decl"] = set(b._decl.keys())
    nc = _CACHE["nc"]
    maps = make_in_maps(inputs)
    decl = _CACHE["decl"]
    maps = [{k: v for k, v in m.items() if k in decl} for m in maps]
    res = run_bass_kernel_spmd(nc, maps, core_ids=list(range(8)))
    out = np.zeros((2, S_LEN, D), np.float32)
    for c in range(8):
        b, q = c // 4, c % 4
        out[b, OWN * q:OWN * (q + 1)] = res.results[c]["out"]
    return out
```

```python
import os
import numpy as np
import ml_dtypes
from contextlib import ExitStack
import concourse.bass as bass
import concourse.mybir as mybir
from concourse.bass_utils import run_bass_kernel_spmd

F32 = mybir.dt.float32
BF16 = mybir.dt.bfloat16
ALU = mybir.AluOpType
AF = mybir.ActivationFunctionType
NPBF = ml_dtypes.bfloat16

D = 1024
S_LEN = 8192
OWN = 2048
NT = 16
EPS = 1e-6
SCALE = 0.125
IN_COLS = 7960
C_QA, C_KA, C_VA = 0, 1536, 3072
C_QB = 4608
C_KVB = 5120
C_GB = 5888
C_GM = 5912
DILS = (1, 4, 16)

ENGS = ("pe", "act", "dve", "pool")
NDMA = 24


class Res:
    __slots__ = ("lw", "rd", "excl")

    def __init__(self):
        self.lw = None
        self.rd = {}
        self.excl = False


class Tn:
    __slots__ = ("t", "r")

    def __init__(self, t):
        self.t = t
        self.r = Res()


class Sched:
    def __init__(self, nc):
        self.nc = nc
        self.q = {e: [] for e in ENGS + ("sp",)}
        self.cnt = {e: 0 for e in ENGS}
        self.dcnt = [0] * NDMA
        self.seen = {e: {} for e in ENGS + ("sp",)}
        self.dnext = 0
        self.pgen = {}
        self.plast = {}

    def dma_sw(self, fns, writes, slot):
        deps = self._deps([], writes)
        self.pgen[slot] = self.pgen.get(slot, 0)
        key = ("p", slot)
        base = self.pgen[slot]
        waits = self._waits("pool", deps)
        for i, fn in enumerate(fns):
            self.q["pool"].append((waits if i == 0 else [], fn, key, base + 16 * (i + 1)))
        self.pgen[slot] = base + 16 * len(fns)
        self.plast[slot] = (key, self.pgen[slot])
        self._mark(key, self.pgen[slot], [], writes)

    def _deps(self, reads, writes, mykey=None):
        deps = {}
        for r in reads:
            r = r.r if isinstance(r, Tn) else r
            if r.lw is not None and r.lw[1] > deps.get(r.lw[0], 0):
                deps[r.lw[0]] = r.lw[1]
            if r.excl:
                for k, v in r.rd.items():
                    if k != mykey and v > deps.get(k, 0):
                        deps[k] = v
        for w in writes:
            w = w.r if isinstance(w, Tn) else w
            if w.lw is not None and w.lw[0] != mykey and w.lw[1] > deps.get(w.lw[0], 0):
                deps[w.lw[0]] = w.lw[1]
            for k, v in w.rd.items():
                if v > deps.get(k, 0):
                    deps[k] = v
        return deps

    def _waits(self, eng, deps):
        waits = []
        seen = self.seen[eng]
        for k, v in deps.items():
            if v > seen.get(k, 0):
                waits.append((k, v))
                seen[k] = v
        return waits

    def _mark(self, key, my, reads, writes):
        for r in reads:
            r = r.r if isinstance(r, Tn) else r
            if my > r.rd.get(key, 0):
                r.rd[key] = my
        for w in writes:
            w = w.r if isinstance(w, Tn) else w
            w.lw = (key, my)
            w.rd = {}

    def op(self, eng, fn, reads=(), writes=()):
        deps = self._deps(reads, writes, ("e", eng))
        if eng == "pe":
            deps.pop(("e", "pe"), None)
        self.cnt[eng] += 1
        my = self.cnt[eng]
        key = ("e", eng)
        self.q[eng].append((self._waits(eng, deps), fn, key, my))
        self._mark(key, my, reads, writes)

    def dma(self, fn, reads=(), writes=(), queue="sp"):
        deps = self._deps(reads, writes)
        k = self.dnext
        self.dnext = (self.dnext + 1) % NDMA
        key = ("d", k)
        if self.dcnt[k] > 0:
            deps[key] = max(deps.get(key, 0), self.dcnt[k])
        self.dcnt[k] += 16
        my = self.dcnt[k]
        self.q[queue].append((self._waits(queue, deps), fn, key, my))
        self._mark(key, my, reads, writes)

    def barrier(self):
        allc = {}
        for e in ENGS:
            if self.cnt[e]:
                allc[("e", e)] = self.cnt[e]
        for k in range(NDMA):
            if self.dcnt[k]:
                allc[("d", k)] = self.dcnt[k]
        for slot, (key, v) in self.plast.items():
            allc[key] = v
        for e in ENGS + ("sp",):
            w = self._waits(e, dict(allc))
            if w:
                self.q[e].append((w, None, None, 0))

    def emit(self):
        nc = self.nc
        with ExitStack() as es:
            esem = {e: es.enter_context(nc.semaphore("s_" + e)) for e in ENGS}
            dsem = [es.enter_context(nc.semaphore("s_d%d" % i)) for i in range(NDMA)]

            psem = {slot: es.enter_context(nc.semaphore("s_p%d" % i)) for i, slot in enumerate(sorted(self.pgen))}

            def semof(key):
                if key[0] == "p":
                    return psem[key[1]]
                return esem[key[1]] if key[0] == "e" else dsem[key[1]]
            fin = {}
            for e in ENGS:
                if self.cnt[e]:
                    fin[("e", e)] = self.cnt[e]
            for k in range(NDMA):
                if self.dcnt[k]:
                    fin[("d", k)] = self.dcnt[k]
            for slot, (key, v) in self.plast.items():
                fin[key] = v
            allsems = list(esem.values()) + dsem + list(psem.values())
            with nc.Block() as b0:
                @b0.sync
                def _(e):
                    for sm in allsems:
                        e.sem_clear(sm)
            block = es.enter_context(nc.Block())

            sig = {e: set() for e in ENGS}
            for name in self.q:
                for waits, fn, key, my in self.q[name]:
                    for (k, v) in waits:
                        if k[0] == "e":
                            sig[k[1]].add(v)
            for e in ENGS:
                if self.cnt[e]:
                    sig[e].add(self.cnt[e])
            rank = {}
            for e in ENGS:
                for i, v in enumerate(sorted(sig[e])):
                    rank[(e, v)] = i + 1

            def wval(k, v):
                return rank[(k[1], v)] if k[0] == "e" else v

            def run(name, engobj, final=False):
                for waits, fn, key, my in self.q[name]:
                    for (k, v) in waits:
                        engobj.wait_ge(semof(k), wval(k, v))
                    if isinstance(fn, tuple):
                        engobj.sem_clear(psem[fn[1]])
                    elif fn is not None:
                        ins = fn(engobj)
                        if key[0] in ("d", "p"):
                            ins.then_inc(semof(key), 16)
                        elif my in sig[key[1]]:
                            ins.then_inc(semof(key), 1)
                if final:
                    for k, v in fin.items():
                        engobj.wait_ge(semof(k), wval(k, v))

            @block.sync
            def _(e):
                run("sp", e, final=True)

            @block.tensor
            def _(e):
                run("pe", e)

            @block.scalar
            def _(e):
                run("act", e)

            @block.vector
            def _(e):
                run("dve", e)

            @block.gpsimd
            def _(e):
                run("pool", e)


def I(name, *a, **k):
    return lambda e: getattr(e, name)(*a, **k)


class Ring:
    def __init__(self, items):
        self.items = items
        self.i = 0

    def next(self):
        it = self.items[self.i % len(self.items)]
        self.i += 1
        return it


NROPE = 148
BFC = dict(ident=0, tri_diag=128, tri_prev=256, win_far=384, m4=512, e32=2560, ones=4608)
NBFC = 4736
F32C = dict(swap=0, id32=128)
NF32C = 160
PCF = dict(rope=0, thrc=NROPE * 16, pv=NROPE * 16 + 16, crel=NROPE * 16 + 80, hv=NROPE * 16 + 84)
NPCF = NROPE * 16 + 85
PCB = dict(eown=0, hv64=2048)
NPCB = 2112


def _static_tables():
    bf = np.zeros((128, NBFC), np.float32)
    k = np.arange(128)[:, None]
    q = np.arange(128)[None, :]
    bf[:, 0:128] = np.eye(128)
    bf[:, 128:256] = (q >= k)
    bf[:, 256:384] = (q <= k)
    bf[:, 384:512] = (q < k)
    for m in range(4):
        blk = np.zeros((128, 512), np.float32)
        for tq in range(4):
            if tq == m:
                blk[:, tq * 128:(tq + 1) * 128] = (q >= k)
            elif tq > m:
                blk[:, tq * 128:(tq + 1) * 128] = 1.0
        bf[:, 512 + m * 512: 512 + (m + 1) * 512] = blk
    b = np.arange(128)[:, None]
    for kt in range(16):
        i = np.arange(128)[None, :]
        bf[:, 2560 + kt * 128: 2560 + (kt + 1) * 128] = ((b % 32) == 2 * kt + (i >= 64))
    bf[:, 4608:4736] = 1.0
    f = np.zeros((128, NF32C), np.float32)
    f[:, 0:128] = (np.abs(k - q) == 64)
    f[0:32, 128:160] = np.eye(32)
    t16 = np.ascontiguousarray(np.broadcast_to(16.0 * np.arange(2048, dtype=np.float32)[None, :], (128, 2048)))
    return bf.astype(NPBF), f, t16


def _rope_rows(pos):
    inv = (500000.0 ** (-np.arange(0, 16, 2, dtype=np.float32) / np.float32(16))).astype(np.float32)
    ang = (pos.astype(np.float32)[:, None] * inv[None, :]).astype(np.float32)
    return np.concatenate([np.cos(ang), np.sin(ang)], axis=1).astype(np.float32)


def _percore_tables(qtr):
    T0 = OWN * qtr
    i = np.arange(128)
    f = np.zeros((128, NPCF), np.float32)
    rope = np.zeros((128, NROPE, 16), np.float32)
    for t in range(32):
        rope[:, t] = _rope_rows(T0 - OWN + 128 * t + i)
    for r in range(4):
        for j in range(-1, 4):
            rope[:, 32 + r * 5 + j + 1] = _rope_rows(T0 - OWN + 2048 + 512 * j + r + 4 * i)
    for r in range(16):
        for j in range(-1, 1):
            rope[:, 52 + r * 2 + j + 1] = _rope_rows(T0 - OWN + 2048 + 2048 * j + r + 16 * i)
    for kt in range(64):
        rope[:, 84 + kt] = _rope_rows(128 * kt + i)
    f[:, 0:NROPE * 16] = rope.reshape(128, -1)
    for ti in range(16):
        f[:, PCF["thrc"] + ti] = T0 + 128 * ti + i - 31
    for kt in range(64):
        f[:, PCF["pv"] + kt] = 1.0 if 128 * kt < T0 else 0.0
    for bt in range(4):
        f[:, PCF["crel"] + bt] = 16.0 * (16 * (128 * bt + i) + 31 - T0)
    f[:, PCF["hv"]] = 0.0 if qtr == 0 else 1.0
    lo = np.full((128, 16, 128), -3e4, np.float32)
    hi = np.full((128, 16, 128), 3e4, np.float32)
    m = np.arange(128)[None, :]
    for ti in range(16):
        cur = ((T0 + 128 * ti + i) // 64)[:, None]
        forced = (m == 0) | (m == cur) | (m == cur - 1)
        fut = m > cur
        lo[:, ti][forced] = 1e4
        hi[:, ti][forced] = 1e4
        lo[:, ti][fut] = -3e4
        hi[:, ti][fut] = -3e4
    lohi = np.concatenate([lo.reshape(128, -1), hi.reshape(128, -1)], axis=1)
    bfp = np.zeros((128, NPCB), np.float32)
    b = np.arange(128)[:, None]
    for j in range(16):
        ii = np.arange(128)[None, :]
        bfp[:, j * 128:(j + 1) * 128] = (b == 2 * (T0 // 128 + j) + (ii >= 64))
    bfp[:, 2048:2112] = 0.0 if qtr == 0 else 1.0
    return f, lohi.astype(np.float32), bfp.astype(NPBF)


class StopBuild(Exception):
    pass


class Builder:
    def __init__(self, debug=False, stop_after=None):
        self.debug = debug
        self.stop_after = stop_after
        self.nc = nc = bass.Bass("TRN2", target_bir_lowering=False)
        self.S = Sched(nc)
        self._decl = {}
        self._shapes = {
            "x_own": ([OWN, D], F32), "x_halo": ([OWN, D], F32), "x_full": ([S_LEN, D], F32), "w_in": ([D, IN_COLS], F32),
            "g_mix": ([128, 8], F32), "g_mlp": ([128, 8], F32), "g_fin": ([128, D], F32),
            "cmp_w1_k": ([2048, 256], F32), "cmp_w1_v": ([2048, 256], F32), "cmp_w2_k": ([256, 64], F32), "cmp_w2_v": ([256, 64], F32),
            "cmp_pos_k": ([32, 64], F32), "cmp_pos_v": ([32, 64], F32), "w_a": ([512, D], F32), "w_b": ([512, D], F32),
            "w_out": ([D, D], F32), "w_up": ([D, 4096], F32), "w_down": ([4096, D], F32),
            "c_bf": ([128, NBFC], BF16), "c_f32": ([128, NF32C], F32), "c_t16": ([128, 2048], F32),
            "pc_f": ([128, NPCF], F32), "pc_lohi": ([128, 4096], F32), "pc_bf": ([128, NPCB], BF16),
        }
        self.out = nc.dram_tensor("out", [OWN, D], F32, kind="ExternalOutput").ap()
        self.dbg = {}

    def __getattr__(self, name):
        sh = self.__dict__.get("_shapes", {})
        if name in sh:
            if name not in self._decl:
                self._decl[name] = self.nc.dram_tensor(name, list(sh[name][0]), sh[name][1], kind="ExternalInput").ap()
            return self._decl[name]
        raise AttributeError(name)

    def sb(self, es, name, shape, dt):
        return Tn(es.enter_context(self.nc.sbuf_tensor(name, list(shape), dt)))

    def ps(self, es, name, shape, dt):
        t = Tn(es.enter_context(self.nc.psum_tensor(name, list(shape), dt)))
        t.r.excl = True
        return t

    def ring(self, es, name, shape, dt, n):
        return Ring([self.sb(es, "%s%d" % (name, i), shape, dt) for i in range(n)])

    def dump(self, name, tn, ap, shape, dt):
        if not self.debug:
            return
        o = self.nc.dram_tensor("dbg_" + name, list(shape), dt, kind="ExternalOutput").ap()
        self.dbg[name] = True
        self.S.dma(I("dma_start", out=o, in_=ap), reads=[tn])

    def load_wslab(self, src_ap, ncols, gain, kchunks=8):
        S = self.S
        st = self.wst.next()
        sl = self.wsl.next()
        S.dma(I("dma_start", out=st.t[:, 0:kchunks, 0:ncols], in_=src_ap.rearrange("(c p) n -> p c n", p=128)), writes=[st])
        if gain is not None:
            gb = gain.t[:, 0:kchunks].unsqueeze(2).to_broadcast([128, kchunks, ncols])
            S.op("pool", I("tensor_tensor", out=sl.t[:, 0:kchunks, 0:ncols], in0=st.t[:, 0:kchunks, 0:ncols], in1=gb, op=ALU.mult),
                 reads=[st, gain], writes=[sl])
        else:
            S.op("pool", I("tensor_copy", out=sl.t[:, 0:kchunks, 0:ncols], in_=st.t[:, 0:kchunks, 0:ncols]), reads=[st], writes=[sl])
        return sl

    def cast_into(self, dst_tn, dst_ap_fn, src_ap, kchunks, ncols, gain, piece=512):
        S = self.S
        for c0 in range(0, ncols, piece):
            n = min(piece, ncols - c0)
            st = self.wst.next()
            S.dma(I("dma_start", out=st.t[:, 0:kchunks, 0:n], in_=src_ap[:, c0:c0 + n].rearrange("(c p) n -> p c n", p=128)), writes=[st])
            dst = dst_ap_fn(c0, n)
            engs = getattr(self, "cast_engs", ("pool",))
            self._ci = getattr(self, "_ci", 0) + 1
            ce = engs[self._ci % len(engs)]
            if gain is not None:
                gb = gain.t[:, 0:kchunks].unsqueeze(2).to_broadcast([128, kchunks, n])
                S.op(ce, I("tensor_tensor", out=dst, in0=st.t[:, 0:kchunks, 0:n], in1=gb, op=ALU.mult),
                     reads=[st, gain], writes=[dst_tn])
            else:
                S.op(ce, I("tensor_copy", out=dst, in_=st.t[:, 0:kchunks, 0:n]), reads=[st], writes=[dst_tn])

    def norm_tile(self, x_ap, ut_tn, ut_ap):
        S = self.S
        xt = self.xring.next()
        S.dma(I("dma_start", out=xt.t[:], in_=x_ap), writes=[xt])
        self.norm_sb(xt, xt.t[:], ut_tn, ut_ap)

    def norm_sb(self, xt, x_sb_ap, ut_tn, ut_ap, keep_rstd=None):
        S = self.S
        jk = self.junk.next()
        st = self.stat.next()
        S.op("act", I("activation", out=jk.t[:], in_=x_sb_ap, func=AF.Square, accum_out=st.t[:, 0:1]), reads=[xt], writes=[jk, st])
        S.op("act", I("activation", out=st.t[:, 1:2], in_=st.t[:, 0:1], func=AF.Sqrt, scale=1.0 / D, bias=self.epsc.t[:, 0:1]), reads=[st, self.epsc], writes=[st])
        S.op("dve", I("reciprocal", out=st.t[:, 2:3], in_=st.t[:, 1:2]), reads=[st], writes=[st])
        xn = self.xnring.next()
        S.op("dve", I("tensor_scalar", out=xn.t[:], in0=x_sb_ap, scalar1=st.t[:, 2:3], scalar2=None, op0=ALU.mult), reads=[xt, st], writes=[xn])
        pt = self.psT
        for c in range(8):
            S.op("pe", I("transpose", out=pt.t[:, c * 128:(c + 1) * 128], in_=xn.t[:, c * 128:(c + 1) * 128], identity=self.ident), reads=[xn, self.cbf], writes=[pt])
        if keep_rstd is None:
            S.op("act", I("copy", out=ut_ap, in_=pt.t[:, 0:1024].rearrange("p (c t) -> p c t", c=8)), reads=[pt], writes=[ut_tn])
        else:
            gb = keep_rstd.t[:, 0:8].unsqueeze(2).to_broadcast([128, 8, 128])
            S.op("dve", I("tensor_tensor", out=ut_ap, in0=pt.t[:, 0:1024].rearrange("p (c t) -> p c t", c=8), in1=gb, op=ALU.mult), reads=[pt, keep_rstd], writes=[ut_tn])
        return st

    def proj_tm(self, lhs_fn, lhs_tn, slab, c0, ncols, ps):
        for c in range(8):
            self.S.op("pe", I("matmul", ps.t[:, 0:ncols], lhsT=lhs_fn(c), rhs=slab.t[:, c, c0:c0 + ncols], start=(c == 0), stop=(c == 7)),
                      reads=[lhs_tn, slab], writes=[ps])

    def rope_evac(self, ps, pc0, nh, ropeidx, dst_tn, dst_ap, perm=False):
        S = self.S
        ro = PCF["rope"] + ropeidx * 16
        ta = self.rtmp.next()
        if not perm:
            psv = ps.t[:, pc0:pc0 + 64 * nh].rearrange("p (h d) -> p h d", h=nh)
            dv = dst_ap.rearrange("p (h d) -> p h d", h=nh)
            tav = ta.t[:, 0:nh * 32].rearrange("p (h d) -> p h d", h=nh)
            cos1 = self.pcf.t[:, ro:ro + 8].unsqueeze(1).to_broadcast([128, nh, 8])
            sin1 = self.pcf.t[:, ro + 8:ro + 16].unsqueeze(1).to_broadcast([128, nh, 8])
            sl = lambda v, a, b: v[:, :, a:b]
        else:
            psv = ps.t[:, pc0:pc0 + 512].rearrange("p (two hp d) -> p two hp d", two=2, hp=4)
            dv = dst_ap.rearrange("p (hp two d) -> p two hp d", two=2, hp=4)
            tav = ta.t[:, 0:256].rearrange("p (two hp d) -> p two hp d", two=2, hp=4)
            cos1 = self.pcf.t[:, ro:ro + 8].unsqueeze(1).unsqueeze(1).to_broadcast([128, 2, 4, 8])
            sin1 = self.pcf.t[:, ro + 8:ro + 16].unsqueeze(1).unsqueeze(1).to_broadcast([128, 2, 4, 8])
            sl = lambda v, a, b: v[:, :, :, a:b]
        S.op("dve", I("tensor_copy", out=sl(dv, 16, 64), in_=sl(psv, 16, 64)), reads=[ps], writes=[dst_tn])
        S.op("dve", I("tensor_tensor", out=sl(tav, 0, 8), in0=sl(psv, 0, 8), in1=cos1, op=ALU.mult), reads=[ps, self.pcf], writes=[ta])
        S.op("dve", I("tensor_tensor", out=sl(tav, 8, 16), in0=sl(psv, 8, 16), in1=cos1, op=ALU.mult), reads=[ps, self.pcf], writes=[ta])
        S.op("dve", I("tensor_tensor", out=sl(tav, 16, 24), in0=sl(psv, 8, 16), in1=sin1, op=ALU.mult), reads=[ps, self.pcf], writes=[ta])
        S.op("dve", I("tensor_tensor", out=sl(tav, 24, 32), in0=sl(psv, 0, 8), in1=sin1, op=ALU.mult), reads=[ps, self.pcf], writes=[ta])
        S.op("dve", I("tensor_tensor", out=sl(dv, 0, 8), in0=sl(tav, 0, 8), in1=sl(tav, 16, 24), op=ALU.subtract), reads=[ta], writes=[dst_tn])
        S.op("dve", I("tensor_tensor", out=sl(dv, 8, 16), in0=sl(tav, 8, 16), in1=sl(tav, 24, 32), op=ALU.add), reads=[ta], writes=[dst_tn])

    def build(self):
        nc, S = self.nc, self.S
        with ExitStack() as es0:
            self.cbf = self.sb(es0, "cbf", [128, NBFC], BF16)
            self.cf32 = self.sb(es0, "cf32", [128, NF32C], F32)
            self.pcf = self.sb(es0, "pcf", [128, NPCF], F32)
            self.pcb = self.sb(es0, "pcb", [128, NPCB], BF16)
            self.gmix = self.sb(es0, "gmix", [128, 8], F32)
            self.gmlp = self.sb(es0, "gmlp", [128, 8], F32)
            self.epsc = self.sb(es0, "epsc", [128, 1], F32)
            S.dma(I("dma_start", out=self.cbf.t[:], in_=self.c_bf), writes=[self.cbf])
            S.dma(I("dma_start", out=self.cf32.t[:], in_=self.c_f32), writes=[self.cf32])
            S.dma(I("dma_start", out=self.pcf.t[:], in_=self.pc_f), writes=[self.pcf])
            S.dma(I("dma_start", out=self.pcb.t[:], in_=self.pc_bf), writes=[self.pcb])
            S.dma(I("dma_start", out=self.gmix.t[:], in_=self.g_mix), writes=[self.gmix])
            S.dma(I("dma_start", out=self.gmlp.t[:], in_=self.g_mlp), writes=[self.gmlp])
            S.op("dve", I("memset", self.epsc.t[:], EPS), writes=[self.epsc])
            self.ident = self.cbf.t[:, 0:128]
            self.xring = self.ring(es0, "xr", [128, D], F32, 2)
            self.junk = self.ring(es0, "jk", [128, D], BF16, 1)
            self.stat = self.ring(es0, "st", [128, 4], F32, 4)
            self.xnring = self.ring(es0, "xn", [128, D], BF16, 2)
            self.rtmp = self.ring(es0, "rtmp", [128, 256], F32, 2)
            self.ptr = self.ring(es0, "ptr", [128, 512], BF16, 3)
            self.psA = Ring([self.ps(es0, "psA%d" % i, [128, 512], F32) for i in range(2)])
            self.psT = self.ps(es0, "psT", [128, 1024], BF16)
            self.psS = Ring([self.ps(es0, "psS%d" % i, [128, 512], F32) for i in range(2)])
            self.psO = Ring([self.ps(es0, "psO%d" % i, [128, 512], F32) for i in range(2)])
            self.psX = self.ps(es0, "psX", [128, 512], F32)
            self.ring3 = Ring([self.psS.items[0], self.psS.items[1], self.psX])
            self.yaT = self.sb(es0, "yaT", [128, 4, OWN], BF16)
            self.stopped = False
            self.phase_A(es0)
            if self.stopped:
                S.barrier()
                if self.stop_after in ("A3", "A"):
                    self.dump("yaT", self.yaT, self.yaT.t[:], [128, 4, OWN], BF16)
                self.fake_out()
                S.emit()
                return nc
            self.ybT = self.sb(es0, "ybT", [128, 4, OWN], BF16)
            S.barrier()
            if self.stop_after == "A":
                self.dump("yaT", self.yaT, self.yaT.t[:], [128, 4, OWN], BF16)
                self.fake_out()
                S.emit()
                return nc
            self.phase_B(es0)
            S.barrier()
            if self.stopped:
                self.fake_out()
                S.emit()
                return nc
            if self.stop_after == "B":
                self.dump("yaT", self.yaT, self.yaT.t[:], [128, 4, OWN], BF16)
                self.dump("ybT", self.ybT, self.ybT.t[:], [128, 4, OWN], BF16)
                self.fake_out()
                S.emit()
                return nc
            self.phase_C(es0)
            if self.debug:
                self.dump("yaT", self.yaT, self.yaT.t[:], [128, 4, OWN], BF16)
                self.dump("ybT", self.ybT, self.ybT.t[:], [128, 4, OWN], BF16)
            S.emit()
        return nc

    def fake_out(self):
        S = self.S
        xt = self.xring.next()
        for t in range(NT):
            S.dma(I("dma_start", out=xt.t[:], in_=self.x_own[t * 128:(t + 1) * 128, :]), writes=[xt])
            S.dma(I("dma_start", out=self.out[t * 128:(t + 1) * 128, :], in_=xt.t[:]), reads=[xt])

    def attn_unit(self, score_mms, n, mask_fn, pv_list):
        S = self.S
        if getattr(self, "_collect", None) is not None:
            self._collect.append((score_mms, n, mask_fn, pv_list, None))
            return
        pss = self.psS.next()
        for i, (l, r, rd) in enumerate(score_mms):
            S.op("pe", I("matmul", pss.t[:, 0:n], lhsT=l, rhs=r, start=(i == 0), stop=(i == len(score_mms) - 1)),
                 reads=rd, writes=[pss])
        pt = self.ptr.next()
        S.op("act", I("activation", out=pt.t[:, 0:n], in_=pss.t[:, 0:n], func=AF.Exp, scale=SCALE), reads=[pss], writes=[pt])
        if mask_fn is not None:
            mask_fn(pt)
        for (pso, out_ap, vaug, c0, ncol, st, sp, rd) in pv_list:
            S.op("pe", I("matmul", out_ap, lhsT=vaug, rhs=pt.t[:, c0:c0 + ncol], start=st, stop=sp),
                 reads=[pt] + rd, writes=[pso])

    def attn_seq(self, units, ring=None, depth=1):
        S = self.S
        ring = ring or self.psS
        units = list(units)
        pend = []
        for k in range(len(units) + depth):
            if k < len(units):
                u = units[k]
                score_mms, n = u[0], u[1]
                pss = ring.next()
                for i, (l, r, rd) in enumerate(score_mms):
                    S.op("pe", I("matmul", pss.t[:, 0:n], lhsT=l, rhs=r, start=(i == 0), stop=(i == len(score_mms) - 1)), reads=rd, writes=[pss])
                pend.append((u, pss))
            if k >= depth and pend:
                (pu, ppss) = pend.pop(0)
                n = pu[1]
                pt = self.ptr.next()
                S.op("act", I("activation", out=pt.t[:, 0:n], in_=ppss.t[:, 0:n], func=AF.Exp, scale=SCALE), reads=[ppss], writes=[pt])
                if pu[2] is not None:
                    pu[2](pt)
                for (pso, out_ap, vaug, c0, ncol, st, sp, rd) in pu[3]:
                    S.op("pe", I("matmul", out_ap, lhsT=vaug, rhs=pt.t[:, c0:c0 + ncol], start=st, stop=sp), reads=[pt] + rd, writes=[pso])
                if len(pu) > 4 and pu[4] is not None:
                    pu[4]()
        while pend:
            (pu, ppss) = pend.pop(0)
            n = pu[1]
            pt = self.ptr.next()
            S.op("act", I("activation", out=pt.t[:, 0:n], in_=ppss.t[:, 0:n], func=AF.Exp, scale=SCALE), reads=[ppss], writes=[pt])
            if pu[2] is not None:
                pu[2](pt)
            for (pso, out_ap, vaug, c0, ncol, st, sp, rd) in pu[3]:
                S.op("pe", I("matmul", out_ap, lhsT=vaug, rhs=pt.t[:, c0:c0 + ncol], start=st, stop=sp), reads=[pt] + rd, writes=[pso])
            if len(pu) > 4 and pu[4] is not None:
                pu[4]()

    def attn_steps(self, steps, ring):
        S = self.S
        prev = None
        for stp in list(steps) + [None]:
            cur = None
            if stp is not None:
                cur = []
                for u in stp:
                    score_mms, n = u[0], u[1]
                    pss = ring.next()
                    cur.append((u, pss))
                nmm = max(len(u[0]) for u in stp)
                for i in range(nmm):
                    for (u, pss) in cur:
                        if i < len(u[0]):
                            l, r, rd = u[0][i]
                            S.op("pe", I("matmul", pss.t[:, 0:u[1]], lhsT=l, rhs=r, start=(i == 0), stop=(i == len(u[0]) - 1)), reads=rd, writes=[pss])
            if prev is not None:
                for (pu, ppss) in prev:
                    n = pu[1]
                    pt = self.ptr.next()
                    S.op("act", I("activation", out=pt.t[:, 0:n], in_=ppss.t[:, 0:n], func=AF.Exp, scale=SCALE), reads=[ppss], writes=[pt])
                    if pu[2] is not None:
                        pu[2](pt)
                    for (pso, out_ap, vaug, c0, ncol, st, sp, rd) in pu[3]:
                        S.op("pe", I("matmul", out_ap, lhsT=vaug, rhs=pt.t[:, c0:c0 + ncol], start=st, stop=sp), reads=[pt] + rd, writes=[pso])
            prev = cur

    def mask_mul(self, pt, c0, n, mask_ap):
        self.S.op("dve", I("tensor_tensor", out=pt.t[:, c0:c0 + n], in0=pt.t[:, c0:c0 + n], in1=mask_ap, op=ALU.mult), reads=[pt, self.cbf], writes=[pt])

    def phase_A(self, es0):
        S = self.S
        with ExitStack() as esA:
            self.phase_A_body(esA)
        S.barrier()

    def phase_A_body(self, esA):
        S = self.S
        self.uTh = self.sb(esA, "uTh", [128, 8, OWN], BF16)
        self.uTo = self.sb(esA, "uTo", [128, 8, OWN], BF16)
        self.wst = self.ring(esA, "wstA", [128, 8, 384], F32, 2)
        self.wsl = self.ring(esA, "wslA", [128, 8, 384], BF16, 2)
        for t in range(NT):
            self.norm_tile(self.x_halo[t * 128:(t + 1) * 128, :], self.uTh, self.uTh.t[:, :, t * 128:(t + 1) * 128])
        for t in range(NT):
            self.norm_tile(self.x_own[t * 128:(t + 1) * 128, :], self.uTo, self.uTo.t[:, :, t * 128:(t + 1) * 128])
        if self.stop_after == "A0":
            self.stopped = True
            return
        with ExitStack() as es:
            self.phase_A_inner(es)

    def phase_A_inner(self, es):
        S = self.S
        if True:
            qT = self.sb(es, "a_qT", [128, OWN], BF16)
            kT = self.sb(es, "a_kT", [128, 32 * 128], BF16)
            vaug = self.sb(es, "a_v", [128, 32, 2, 128], BF16)
            qk = self.ring(es, "a_qk", [128, 256], BF16, 2)
            if os.environ.get("PADLOW"):
                pad = self.sb(es, "a_pad", [128, int(os.environ["PADLOW"]) * 256], F32)
            acc = [self.sb(es, "a_acc%d" % i, [128, OWN], F32) for i in range(2)]
            rd_ = self.ring(es, "a_rd", [128, 512], F32, 2)
            ones64 = self.cbf.t[:, BFC["ones"]:BFC["ones"] + 64]
            hv64 = self.pcb.t[:, PCB["hv64"]:PCB["hv64"] + 64]
            hvcol = self.pcf.t[:, PCF["hv"]:PCF["hv"] + 1]
            for p in range(4):
                for g, d in enumerate(DILS):
                    nt = NT // d
                    st = self.wst.next()
                    sl = self.wsl.next()
                    for i, cb in enumerate((C_QA, C_KA, C_VA)):
                        c0 = cb + g * 512 + p * 128
                        for kc in range(8):
                            S.dma(I("dma_start", out=st.t[:, kc, i * 128:(i + 1) * 128], in_=self.w_in[kc * 128:(kc + 1) * 128, c0:c0 + 128]), writes=[st])
                    gb = self.gmix.t[:, 0:8].unsqueeze(2).to_broadcast([128, 8, 384])
                    if int(os.environ.get("A1CUT", "99")) >= 0:
                        S.op("pool", I("tensor_tensor", out=sl.t[:, :, 0:384], in0=st.t[:, :, 0:384], in1=gb, op=ALU.mult), reads=[st, self.gmix], writes=[sl])
                    if int(os.environ.get("A1CUT", "99")) <= 0:
                        self.stopped = True
                        return
                    for r in range(d):
                        for j in range(-1, nt):
                            slot = r * (nt + 1) + j + 1
                            start = 2048 + 128 * d * j + r
                            if start < 2048:
                                ut, s0 = self.uTh, start
                            else:
                                ut, s0 = self.uTo, start - 2048
                            lhs = lambda c, ut=ut, s0=s0, d=d: ut.t[:, c, s0:s0 + 127 * d + 1:d]
                            ridx = (15 + slot) if g == 0 else ((32 + slot) if g == 1 else (52 + slot))
                            ps = self.psA.next()
                            halo = (j == -1)
                            if halo:
                                self.proj_tm(lhs, ut, sl, 128, 256, ps)
                                kc0, vc0 = 0, 128
                            else:
                                self.proj_tm(lhs, ut, sl, 0, 384, ps)
                                kc0, vc0 = 128, 256

                            CUT = int(os.environ.get("A1CUT", "99"))
                            if CUT <= 1:
                                continue
                            t = qk.next()
                            if not halo:
                                self.rope_evac(ps, 0, 4, ridx, t, t.t[:, 0:256])
                            else:
                                self.rope_evac(ps, kc0, 2, ridx, t, t.t[:, 128:256])
                            if CUT <= 2:
                                continue
                            vsrc = ps.t[:, vc0:vc0 + 128].rearrange("p (h d) -> p h d", h=2)
                            if halo:
                                S.op("dve", I("tensor_scalar", out=vaug.t[:, slot, :, 0:64], in0=vsrc, scalar1=hvcol, scalar2=None, op0=ALU.mult), reads=[ps, self.pcf], writes=[vaug])
                                for hh in range(2):
                                    S.op("pool", I("tensor_copy", out=vaug.t[:, slot, hh, 64:128], in_=hv64), reads=[self.pcb], writes=[vaug])
                            else:
                                S.op("act", I("copy", out=vaug.t[:, slot, :, 0:64], in_=vsrc), reads=[ps], writes=[vaug])
                                for hh in range(2):
                                    S.op("pool", I("tensor_copy", out=vaug.t[:, slot, hh, 64:128], in_=ones64), reads=[self.cbf], writes=[vaug])
                            if CUT <= 3:
                                continue
                            pt = self.psT
                            if not halo:
                                S.op("pe", I("transpose", out=pt.t[:, 0:128], in_=t.t[:, 0:128], identity=self.ident), reads=[t, self.cbf], writes=[pt])
                            S.op("pe", I("transpose", out=pt.t[:, 128:256], in_=t.t[:, 128:256], identity=self.ident), reads=[t, self.cbf], writes=[pt])
                            if not halo:
                                qi = r * nt + j
                                S.op("act", I("copy", out=qT.t[:, qi * 128:(qi + 1) * 128], in_=pt.t[:, 0:128]), reads=[pt], writes=[qT])
                            S.op("dve", I("tensor_copy", out=kT.t[:, slot * 128:(slot + 1) * 128], in_=pt.t[:, 128:256]), reads=[pt], writes=[kT])
                    if self.stop_after == "A1" and int(os.environ.get("A1CUT", "99")) < 99:
                        self.stopped = True
                        return
                    if self.stop_after == "A1":
                        self.dump("qT", qT, qT.t[:], [128, OWN], BF16)
                        self.dump("kT", kT, kT.t[:, 0:17 * 128], [128, 17 * 128], BF16)
                        self.dump("vaug", vaug, vaug.t[:, 0:17], [128, 17, 2, 128], BF16)
                        self.stopped = True
                        return
                    for hh in range(2):
                        pb = 64 * hh
                        banks = {}
                        ring3 = self.ring3
                        units = []
                        for r in range(d):
                            for j in range(-1, nt):
                                slot = r * (nt + 1) + j + 1
                                qlo = max(j, 0)
                                qhi = min(j + 1, nt - 1)
                                nq = qhi - qlo + 1
                                qc0 = (r * nt + qlo) * 128
                                n = nq * 128
                                if j == -1:
                                    mk = [(0, 128, self.cbf.t[:, BFC["tri_prev"]:BFC["tri_prev"] + 128])]
                                elif nq == 1:
                                    mk = [(0, 128, self.cbf.t[:, BFC["tri_diag"]:BFC["tri_diag"] + 128])]
                                else:
                                    mk = [(0, 256, self.cbf.t[:, BFC["tri_diag"]:BFC["tri_diag"] + 256])]

                                def mask_fn(pt, mk=mk):
                                    for (c0, nn, ap) in mk:
                                        self.mask_mul(pt, c0, nn, ap)
                                pv = []
                                for qt in range(qlo, qhi + 1):
                                    qi = r * nt + qt
                                    if (qt == j + 1) and (qi % 4 == 0):
                                        banks[qi // 4] = self.psO.next()
                                    pso = banks[qi // 4]
                                    col = (qi % 4) * 128
                                    pv.append((pso, pso.t[:, col:col + 128], vaug.t[:, slot, hh, :], (qt - qlo) * 128, 128, qt == j + 1, qt == j, [vaug]))
                                after = None
                                if j >= 0 and (r * nt + j) % 4 == 3:
                                    bk = (r * nt + j) // 4
                                    pso = banks[bk]
                                    av = acc[hh].t[:]
                                    if d == 1:
                                        dst = av[:, bk * 512:(bk + 1) * 512]
                                        src = pso.t[:, 0:512]
                                    elif d == 4:
                                        dst = av.rearrange("p (i r) -> p r i", r=4)[:, r, :]
                                        src = pso.t[:, 0:512]
                                    else:
                                        dst = av.rearrange("p (i r) -> p r i", r=16)[:, 4 * bk:4 * bk + 4, :]
                                        src = pso.t[:, 0:512].rearrange("p (r i) -> p r i", r=4)

                                    def after(dst=dst, src=src, pso=pso, hh=hh, g=g):
                                        if g == 0:
                                            S.op("act", I("copy", out=dst, in_=src), reads=[pso], writes=[acc[hh]])
                                        else:
                                            S.op("dve", I("tensor_tensor", out=dst, in0=dst, in1=src, op=ALU.add), reads=[pso, acc[hh]], writes=[acc[hh]])
                                units.append(([(kT.t[pb:pb + 64, slot * 128:(slot + 1) * 128], qT.t[pb:pb + 64, qc0:qc0 + n], [kT, qT])], n, mask_fn, pv, after))
                        self.attn_seq(units, ring=ring3, depth=2)
                if self.stop_after == "A2":
                    self.stopped = True
                    return
                for hh in range(2):
                    for c in range(4):
                        self.finalize(acc[hh], acc[hh].t[:, c * 512:(c + 1) * 512], hh, None, rd_, self.yaT, self.yaT.t[:, p, c * 512:(c + 1) * 512], first=True, last=True, ybacc=None)
                if self.stop_after == "A3":
                    self.stopped = True
                    return
        S.barrier()

    def finalize(self, src_tn, src_ap, hh, gate_row, rdring, dst_tn, dst_ap, first, last, ybacc, gring=None):
        S = self.S
        psx = self.psX
        S.op("pe", I("matmul", psx.t[:, 0:512], lhsT=self.cf32.t[:, 0:128], rhs=src_ap, start=True, stop=True), reads=[src_tn, self.cf32], writes=[psx])
        psg = None
        if gate_row is not None:
            gsel, gbT, gcols = gate_row
            psg = gring.next() if gring is not None else psx
            if gring is not None:
                S.op("pe", I("matmul", psg.t[:, 0:512], lhsT=gsel.t[:], rhs=gcols, start=True, stop=True), reads=[gbT, gsel], writes=[psg])
        rd = rdring.next()
        lo, hi = 64 * hh, 64 * hh + 64
        if hh == 0:
            den = psx.t[0:64, 0:512]
            num = src_ap[0:64, :]
        else:
            den = src_ap[64:128, :]
            num = psx.t[64:128, 0:512]
        S.op("dve", I("tensor_scalar", out=rd.t[lo:hi, :], in0=den, scalar1=1e-30, scalar2=None, op0=ALU.max), reads=[psx, src_tn], writes=[rd])
        S.op("dve", I("reciprocal", out=rd.t[lo:hi, :], in_=rd.t[lo:hi, :]), reads=[rd], writes=[rd])
        if gate_row is None:
            S.op("dve", I("tensor_tensor", out=dst_ap[lo:hi, :], in0=num, in1=rd.t[lo:hi, :], op=ALU.mult), reads=[psx, src_tn, rd], writes=[dst_tn])
            return
        S.op("dve", I("tensor_tensor", out=rd.t[lo:hi, :], in0=num, in1=rd.t[lo:hi, :], op=ALU.mult), reads=[psx, src_tn, rd], writes=[rd])
        if gring is None:
            S.op("pe", I("matmul", psx.t[:, 0:512], lhsT=gsel.t[:], rhs=gcols, start=True, stop=True), reads=[gbT, gsel, rd], writes=[psx])
        if first:
            S.op("dve", I("tensor_tensor", out=ybacc.t[lo:hi, :], in0=rd.t[lo:hi, :], in1=psg.t[lo:hi, 0:512], op=ALU.mult), reads=[psg, rd], writes=[ybacc])
        else:
            S.op("dve", I("tensor_tensor", out=rd.t[lo:hi, :], in0=rd.t[lo:hi, :], in1=psg.t[lo:hi, 0:512], op=ALU.mult), reads=[psg, rd], writes=[rd])
            if last:
                S.op("dve", I("tensor_tensor", out=dst_ap[lo:hi, :], in0=rd.t[lo:hi, :], in1=ybacc.t[lo:hi, :], op=ALU.add), reads=[rd, ybacc], writes=[dst_tn])
            else:
                S.op("dve", I("tensor_tensor", out=ybacc.t[lo:hi, :], in0=rd.t[lo:hi, :], in1=ybacc.t[lo:hi, :], op=ALU.add), reads=[rd, ybacc], writes=[ybacc])

    def phase_B(self, es0):
        S = self.S
        cbf, pcf, pcb = self.cbf, self.pcf, self.pcb
        ones64 = cbf.t[:, BFC["ones"]:BFC["ones"] + 64]
        ones2 = cbf.t[:, BFC["ones"]:BFC["ones"] + 128].rearrange("p (g d) -> p g d", g=2)
        with ExitStack() as esB:
            kslcT = self.sb(esB, "b_kslcT", [128, 48 * 128], BF16)
            vslc = self.sb(esB, "b_vslc", [128, 48, 2, 128], BF16)
            kcT = self.sb(esB, "b_kcT", [128, 512], BF16)
            vc = self.sb(esB, "b_vc", [128, 4, 2, 128], BF16)
            S.op("pool", I("memset", kcT.t[:], 0.0), writes=[kcT])
            S.op("pool", I("memset", vc.t[:], 0.0), writes=[vc])
            with ExitStack() as es:
                self.wst = self.ring(es, "wstB", [128, 8, 256], F32, 2)
                kcmpT = self.sb(es, "b_kcmpT", [128, S_LEN], BF16)
                vcmpT = self.sb(es, "b_vcmpT", [128, S_LEN], BF16)
                slab = self.sb(es, "b_slab", [128, 8, 512], BF16)
                uTt = self.ring(es, "b_uTt", [128, 8, 128], BF16, 3)
                tm = self.ring(es, "b_tm", [128, 512], BF16, 2)
                for dcol, scol in ((0, 0), (128, 256), (256, 128), (384, 384)):
                    self.cast_into(slab, lambda c0, n, dcol=dcol: slab.t[:, :, dcol + c0:dcol + c0 + n], self.w_in[:, C_KVB + scol:C_KVB + scol + 128], 8, 128, self.gmix, piece=128)
                b2u = {}

                def b2_stage1(kt):
                    u = uTt.next()
                    self.norm_tile(self.x_full[kt * 128:(kt + 1) * 128, :], u, u.t[:])
                    b2u[kt] = u

                def b2_stage2(kt):
                    u = b2u.pop(kt)
                    ps = self.psA.next()
                    self.proj_tm(lambda c, u=u: u.t[:, c, :], u, slab, 0, 512, ps)
                    t = tm.next()
                    self.rope_evac(ps, 0, 4, 84 + kt, t, t.t[:, 0:256])
                    S.op("act", I("copy", out=t.t[:, 256:384], in_=ps.t[:, 256:384]), reads=[ps], writes=[t])
                    if kt < 48:
                        pvc = pcf.t[:, PCF["pv"] + kt:PCF["pv"] + kt + 1]
                        S.op("dve", I("tensor_scalar", out=vslc.t[:, kt, :, 0:64], in0=ps.t[:, 384:512].rearrange("p (g d) -> p g d", g=2), scalar1=pvc, scalar2=None, op0=ALU.mult), reads=[ps, pcf], writes=[vslc])
                        S.op("pool", I("tensor_scalar", out=vslc.t[:, kt, :, 64:128], in0=ones2, scalar1=pvc, scalar2=None, op0=ALU.mult), reads=[cbf, pcf], writes=[vslc])
                    pt = self.psT
                    for k in range(3):
                        S.op("pe", I("transpose", out=pt.t[:, k * 128:(k + 1) * 128], in_=t.t[:, k * 128:(k + 1) * 128], identity=self.ident), reads=[t, cbf], writes=[pt])
                    S.op("act", I("copy", out=kcmpT.t[:, kt * 128:(kt + 1) * 128], in_=pt.t[:, 0:128]), reads=[pt], writes=[kcmpT])
                    S.op("act", I("copy", out=vcmpT.t[:, kt * 128:(kt + 1) * 128], in_=pt.t[:, 256:384]), reads=[pt], writes=[vcmpT])
                    if kt < 48:
                        S.op("act", I("copy", out=kslcT.t[:, kt * 128:(kt + 1) * 128], in_=pt.t[:, 128:256]), reads=[pt], writes=[kslcT])

                b2_stage1(0)
                for kt in range(64):
                    if kt + 1 < 64:
                        b2_stage1(kt + 1)
                    b2_stage2(kt)

                if self.stop_after == "B2":
                    self.dump("kslcT", kslcT, kslcT.t[:], [128, 48 * 128], BF16)
                    self.dump("kcmpT", kcmpT, kcmpT.t[:], [128, S_LEN], BF16)
                    self.dump("vslc", vslc, vslc.t[:], [128, 48, 2, 128], BF16)
                    self.stopped = True
                    return
                w1sb = self.sb(es, "b_w1", [128, 32, 256], BF16)
                w2sb = self.sb(es, "b_w2", [128, 2, 128], BF16)
                posb = self.sb(es, "b_posb", [32, 128], BF16)
                posf = self.sb(es, "b_posf", [32, 64], F32)
                posT = self.sb(es, "b_posT", [128, 32], BF16)
                b1sb = self.sb(es, "b_b1", [128, 2], F32)
                gel = [self.sb(es, "b_gel%d" % i, [128, 512], BF16) for i in range(4)]
                hA = self.sb(es, "b_hA", [128, 512], F32)
                hB = self.sb(es, "b_hB", [128, 512], F32)
                for kv in range(2):
                    src = kcmpT if kv == 0 else vcmpT
                    w1d = self.cmp_w1_k if kv == 0 else self.cmp_w1_v
                    w2d = self.cmp_w2_k if kv == 0 else self.cmp_w2_v
                    posd = self.cmp_pos_k if kv == 0 else self.cmp_pos_v
                    w1v = w1d.rearrange("(j d) h -> d j h", d=64)
                    for j0 in range(0, 32, 8):
                        st = self.wst.next()
                        for half in range(2):
                            S.dma(I("dma_start", out=st.t[64 * half:64 * half + 64, :, :], in_=w1v[:, j0:j0 + 8, :]), writes=[st])
                        S.op("pool", I("tensor_copy", out=w1sb.t[:, j0:j0 + 8, :], in_=st.t[:, :, :]), reads=[st], writes=[w1sb])
                    st = self.wst.next()
                    S.dma(I("dma_start", out=st.t[:, 0:2, 0:64], in_=w2d.rearrange("(c p) n -> p c n", p=128)), writes=[st])
                    S.op("pool", I("tensor_copy", out=w2sb.t[:, :, 0:64], in_=st.t[:, 0:2, 0:64]), reads=[st], writes=[w2sb])
                    S.op("pool", I("tensor_copy", out=w2sb.t[:, :, 64:128], in_=st.t[:, 0:2, 0:64]), reads=[st], writes=[w2sb])
                    S.dma(I("dma_start", out=posf.t[:], in_=posd), writes=[posf])
                    S.op("dve", I("tensor_copy", out=posb.t[:, 0:64], in_=posf.t[:]), reads=[posf], writes=[posb])
                    S.op("dve", I("tensor_copy", out=posb.t[:, 64:128], in_=posf.t[:]), reads=[posf], writes=[posb])
                    pt = self.psT
                    S.op("pe", I("transpose", out=pt.t[:, 0:32], in_=posb.t[:], identity=cbf.t[0:32, 0:32]), reads=[posb, cbf], writes=[pt])
                    S.op("act", I("copy", out=posT.t[:], in_=pt.t[:, 0:32]), reads=[pt], writes=[posT])
                    psx = self.psX
                    for mh in range(2):
                        for j in range(32):
                            S.op("pe", I("matmul", psx.t[:, mh:mh + 1], lhsT=w1sb.t[0:64, j, mh * 128:(mh + 1) * 128], rhs=posT.t[0:64, j:j + 1], start=(j == 0), stop=(j == 31)), reads=[w1sb, posT], writes=[psx])
                    S.op("dve", I("tensor_copy", out=b1sb.t[:], in_=psx.t[:, 0:2]), reads=[psx], writes=[b1sb])
                    for mh in range(2):
                        pss = [self.psA.next(), self.psA.next()]
                        for j in range(32):
                            for g in range(2):
                                pb = 64 * g
                                S.op("pe", I("matmul", pss[g].t[:, 0:511], lhsT=w1sb.t[pb:pb + 64, j, mh * 128:(mh + 1) * 128], rhs=src.t[pb:pb + 64, j:j + 16 * 510 + 1:16], start=(j == 0), stop=(j == 31)), reads=[w1sb, src], writes=[pss[g]])
                        for g in range(2):
                            ps = pss[g]
                            ge = gel[2 * g + mh]
                            S.op("act", I("activation", out=hA.t[:, 0:511], in_=ps.t[:, 0:511], func=AF.Identity, bias=b1sb.t[:, mh:mh + 1]), reads=[ps, b1sb], writes=[hA])
                            S.op("dve", I("tensor_tensor", out=hB.t[:, 0:511], in0=hA.t[:, 0:511], in1=hA.t[:, 0:511], op=ALU.mult), reads=[hA], writes=[hB])
                            S.op("dve", I("tensor_scalar", out=hB.t[:, 0:511], in0=hB.t[:, 0:511], scalar1=0.044715, scalar2=1.0, op0=ALU.mult, op1=ALU.add), reads=[hB], writes=[hB])
                            S.op("dve", I("tensor_tensor", out=hB.t[:, 0:511], in0=hB.t[:, 0:511], in1=hA.t[:, 0:511], op=ALU.mult), reads=[hA, hB], writes=[hB])
                            S.op("act", I("activation", out=hB.t[:, 0:511], in_=hB.t[:, 0:511], func=AF.Sigmoid, scale=2.0 * 0.7978845608028654), reads=[hB], writes=[hB])
                            S.op("dve", I("tensor_tensor", out=ge.t[:, 0:511], in0=hA.t[:, 0:511], in1=hB.t[:, 0:511], op=ALU.mult), reads=[hA, hB], writes=[ge])
                    for g in range(2):
                        pb = 64 * g
                        if kv == 0:
                            ps = self.psA.next()
                            for mh in range(2):
                                S.op("pe", I("matmul", ps.t[:, 0:511], lhsT=w2sb.t[:, mh, :], rhs=gel[2 * g + mh].t[:, 0:511], start=(mh == 0), stop=(mh == 1)), reads=[w2sb, gel[2 * g + mh]], writes=[ps])
                            S.op("act", I("copy", out=kcT.t[pb:pb + 64, 0:511], in_=ps.t[pb:pb + 64, 0:511]), reads=[ps], writes=[kcT])
                        else:
                            for bt in range(4):
                                n = 128 if bt < 3 else 127
                                ps = self.psA.next()
                                for mh in range(2):
                                    S.op("pe", I("matmul", ps.t[0:n, 0:64], lhsT=gel[2 * g + mh].t[:, bt * 128:bt * 128 + n], rhs=w2sb.t[:, mh, 0:64], start=(mh == 0), stop=(mh == 1)), reads=[w2sb, gel[2 * g + mh]], writes=[ps])
                                S.op("act", I("copy", out=vc.t[0:n, bt, g, 0:64], in_=ps.t[0:n, 0:64]), reads=[ps], writes=[vc])
                                S.op("pool", I("tensor_copy", out=vc.t[0:n, bt, g, 64:128], in_=ones64[0:n, :]), reads=[cbf], writes=[vc])
            S.barrier()
            if self.stop_after == "B3":
                self.dump("kcT", kcT, kcT.t[:], [128, 512], BF16)
                self.dump("vc", vc, vc.t[:], [128, 4, 2, 128], BF16)
                self.stopped = True
                return
            qbT = self.sb(esB, "b_qbT", [128, 4, OWN], BF16)
            gbT = self.sb(esB, "b_gbT", [32, OWN], F32)
            kwinT = self.sb(esB, "b_kwinT", [128, 20 * 128], BF16)
            vwin = self.sb(esB, "b_vwin", [128, 20, 2, 128], BF16)
            kso = self.sb(esB, "b_kso", [128, OWN], BF16)
            vso = self.sb(esB, "b_vso", [128, 16, 2, 128], BF16)
            S.op("pool", I("memset", gbT.t[:], 0.0), writes=[gbT])
            hvcol = pcf.t[:, PCF["hv"]:PCF["hv"] + 1]
            with ExitStack() as es:
                self.wst = self.ring(es, "wstB1", [128, 8, 256], F32, 2)
                slq = self.sb(es, "b_slq", [128, 8, 512], BF16)
                slkv = self.sb(es, "b_slkv", [128, 8, 512], BF16)
                slg = self.sb(es, "b_slg", [128, 8, 32], BF16)
                uTt = self.ring(es, "b_uTt1", [128, 8, 128], BF16, 2)
                tq = self.ring(es, "b_tq", [128, 512], BF16, 2)
                tk = self.ring(es, "b_tk", [128, 256], BF16, 2)
                self.cast_into(slq, lambda c0, n: slq.t[:, :, c0:c0 + n], self.w_in[:, C_QB:C_QB + 512], 8, 512, self.gmix, piece=256)
                self.cast_into(slkv, lambda c0, n: slkv.t[:, :, c0:c0 + n], self.w_in[:, C_KVB + 256:C_KVB + 768], 8, 512, self.gmix, piece=256)
                self.cast_into(slg, lambda c0, n: slg.t[:, :, c0:c0 + n], self.w_in[:, C_GB:C_GB + 24], 8, 24, self.gmix, piece=256)
                for e_ in range(12, 32):
                    slot = e_ - 12
                    own = e_ >= 16
                    i = e_ - 16
                    u = uTt.next()
                    xs = self.x_own[i * 128:(i + 1) * 128, :] if own else self.x_halo[e_ * 128:(e_ + 1) * 128, :]
                    self.norm_tile(xs, u, u.t[:])
                    lhs = lambda c, u=u: u.t[:, c, :]
                    if own:
                        ps = self.psA.next()
                        self.proj_tm(lhs, u, slq, 0, 512, ps)
                        t = tq.next()
                        self.rope_evac(ps, 0, 8, e_, t, t.t[:, 0:512], perm=True)
                        pt = self.psT
                        for hp in range(4):
                            S.op("pe", I("transpose", out=pt.t[:, hp * 128:(hp + 1) * 128], in_=t.t[:, hp * 128:(hp + 1) * 128], identity=self.ident), reads=[t, cbf], writes=[pt])
                        S.op("act", I("copy", out=qbT.t[:, :, i * 128:(i + 1) * 128], in_=pt.t[:, 0:512].rearrange("p (h t) -> p h t", h=4)), reads=[pt], writes=[qbT])
                        psx = self.psX
                        for c in range(8):
                            S.op("pe", I("matmul", psx.t[0:24, 0:128], lhsT=slg.t[:, c, 0:24], rhs=u.t[:, c, :], start=(c == 0), stop=(c == 7)), reads=[slg, u], writes=[psx])
                        S.op("act", I("activation", out=gbT.t[0:24, i * 128:(i + 1) * 128], in_=psx.t[0:24, 0:128], func=AF.Sigmoid), reads=[psx], writes=[gbT])
                    ps = self.psA.next()
                    t = tk.next()
                    if own:
                        self.proj_tm(lhs, u, slkv, 0, 512, ps)
                        self.rope_evac(ps, 0, 2, e_, t, t.t[:, 0:128])
                        self.rope_evac(ps, 256, 2, e_, t, t.t[:, 128:256])
                        S.op("dve", I("tensor_copy", out=vso.t[:, i, :, 0:64], in_=ps.t[:, 128:256].rearrange("p (g d) -> p g d", g=2)), reads=[ps], writes=[vso])
                        S.op("pool", I("tensor_copy", out=vso.t[:, i, :, 64:128], in_=ones2), reads=[cbf], writes=[vso])
                        S.op("dve", I("tensor_copy", out=vwin.t[:, slot, :, 0:64], in_=ps.t[:, 384:512].rearrange("p (g d) -> p g d", g=2)), reads=[ps], writes=[vwin])
                        S.op("pool", I("tensor_copy", out=vwin.t[:, slot, :, 64:128], in_=ones2), reads=[cbf], writes=[vwin])
                    else:
                        self.proj_tm(lhs, u, slkv, 256, 256, ps)
                        self.rope_evac(ps, 0, 2, e_, t, t.t[:, 128:256])
                        S.op("dve", I("tensor_scalar", out=vwin.t[:, slot, :, 0:64], in0=ps.t[:, 128:256].rearrange("p (g d) -> p g d", g=2), scalar1=hvcol, scalar2=None, op0=ALU.mult), reads=[ps, pcf], writes=[vwin])
                        S.op("pool", I("tensor_scalar", out=vwin.t[:, slot, :, 64:128], in0=ones2, scalar1=hvcol, scalar2=None, op0=ALU.mult), reads=[cbf, pcf], writes=[vwin])
                    pt = self.psT
                    if own:
                        S.op("pe", I("transpose", out=pt.t[:, 0:128], in_=t.t[:, 0:128], identity=self.ident), reads=[t, cbf], writes=[pt])
                    S.op("pe", I("transpose", out=pt.t[:, 128:256], in_=t.t[:, 128:256], identity=self.ident), reads=[t, cbf], writes=[pt])
                    if own:
                        S.op("act", I("copy", out=kso.t[:, i * 128:(i + 1) * 128], in_=pt.t[:, 0:128]), reads=[pt], writes=[kso])
                    S.op("act", I("copy", out=kwinT.t[:, slot * 128:(slot + 1) * 128], in_=pt.t[:, 128:256]), reads=[pt], writes=[kwinT])
            S.barrier()
            biasT = self.sb(esB, "b_biasT", [128, 2, OWN], BF16)
            t16 = self.sb(esB, "b_t16", [128, 2048], F32)
            S.dma(I("dma_start", out=t16.t[:], in_=self.c_t16), writes=[t16])
            with ExitStack() as es:
                et = self.ring(es, "b_et", [128, 512], F32, 4)
                pp = self.ring(es, "b_pp", [128, 520], F32, 4)
                lohi = self.ring(es, "b_lohi", [128, 256], F32, 2)
                imp = self.ring(es, "b_imp", [128, 128], F32, 2)
                imp2 = self.ring(es, "b_imp2", [128, 128], F32, 2)
                sm = self.ring(es, "b_sm", [128, 24], F32, 8)
                btm = self.ring(es, "b_btm", [128, 128], BF16, 2)
                for pq in pp.items:
                    S.op("pool", I("memset", pq.t[:], 0.0), writes=[pq])
                for i in range(NT):
                    lh = lohi.next()
                    S.dma(I("dma_start", out=lh.t[:, 0:128], in_=self.pc_lohi[:, i * 128:(i + 1) * 128]), writes=[lh])
                    S.dma(I("dma_start", out=lh.t[:, 128:256], in_=self.pc_lohi[:, 2048 + i * 128:2048 + (i + 1) * 128]), writes=[lh])
                    thr_i = pcf.t[:, PCF["thrc"] + i:PCF["thrc"] + i + 1]
                    def gen(g, i=i, lh=lh, thr_i=thr_i):
                        pb = 64 * g
                        P = pp.next()
                        P2 = pp.next()
                        for r in range(4):
                            ps = self.psS.next()
                            S.op("pe", I("matmul", ps.t[:, 0:511], lhsT=qbT.t[pb:pb + 64, r, i * 128:(i + 1) * 128], rhs=kcT.t[pb:pb + 64, 0:511], start=True, stop=True), reads=[qbT, kcT], writes=[ps])
                            e_ = et.next()
                            s_ = sm.next()
                            eng = "dve"
                            Pr = P if r % 2 == 0 else P2
                            S.op("act", I("activation", out=e_.t[:, 0:511], in_=ps.t[:, 0:511], func=AF.Exp, scale=SCALE), reads=[ps], writes=[e_])
                            S.op(eng, I("scalar_tensor_tensor", out=e_.t[:, 0:511], in0=t16.t[:, 0:511], scalar=thr_i, in1=e_.t[:, 0:511], op0=ALU.is_le, op1=ALU.mult, accum_out=s_.t[:, 0:1]), reads=[t16, pcf, e_], writes=[e_, s_])
                            yield
                            S.op(eng, I("tensor_scalar", out=s_.t[:, 1:2], in0=s_.t[:, 0:1], scalar1=1e-30, scalar2=None, op0=ALU.max), reads=[s_], writes=[s_])
                            yield
                            S.op("dve", I("reciprocal", out=s_.t[:, 2:3], in_=s_.t[:, 1:2]), reads=[s_], writes=[s_])
                            yield
                            if r < 2:
                                S.op(eng, I("tensor_scalar", out=Pr.t[:, 1:512], in0=e_.t[:, 0:511], scalar1=s_.t[:, 2:3], scalar2=None, op0=ALU.mult), reads=[e_, s_], writes=[Pr])
                                yield
                            else:
                                S.op(eng, I("scalar_tensor_tensor", out=Pr.t[:, 1:512], in0=e_.t[:, 0:511], scalar=s_.t[:, 2:3], in1=Pr.t[:, 1:512], op0=ALU.mult, op1=ALU.add), reads=[e_, s_, Pr], writes=[Pr])
                                yield
                        S.op("dve", I("tensor_tensor", out=P.t[:, 1:512], in0=P.t[:, 1:512], in1=P2.t[:, 1:512], op=ALU.add), reads=[P, P2], writes=[P])
                        yield
                        im = imp.next()
                        S.op("dve", I("tensor_tensor", out=im.t[:], in0=P.t[:, 0:512:4], in1=P.t[:, 1:513:4], op=ALU.add), reads=[P], writes=[im])
                        yield
                        for k in range(2, 5):
                            S.op("dve", I("tensor_tensor", out=im.t[:], in0=im.t[:], in1=P.t[:, k:k + 512:4], op=ALU.add), reads=[P, im], writes=[im])
                            yield
                        S.op("dve", I("tensor_tensor", out=im.t[:], in0=im.t[:], in1=lh.t[:, 0:128], op=ALU.max), reads=[lh, im], writes=[im])
                        yield
                        S.op("dve", I("tensor_tensor", out=im.t[:], in0=im.t[:], in1=lh.t[:, 128:256], op=ALU.min), reads=[lh, im], writes=[im])
                        yield
                        s_ = sm.next()
                        i2 = imp2.next()
                        S.op("dve", I("max", out=s_.t[:, 0:8], in_=im.t[:]), reads=[im], writes=[s_])
                        yield
                        S.op("dve", I("match_replace", out=i2.t[:], in_to_replace=s_.t[:, 0:8], in_values=im.t[:], imm_value=-1e9), reads=[im, s_], writes=[i2])
                        yield
                        S.op("dve", I("max", out=s_.t[:, 8:16], in_=i2.t[:]), reads=[i2], writes=[s_])
                        yield
                        S.op("dve", I("tensor_scalar", out=s_.t[:, 16:17], in0=s_.t[:, 15:16], scalar1=-1.5e4, scalar2=None, op0=ALU.max), reads=[s_], writes=[s_])
                        yield
                        bt_ = btm.next()
                        S.op("dve", I("tensor_scalar", out=bt_.t[:], in0=im.t[:], scalar1=s_.t[:, 16:17], scalar2=-30000.0, op0=ALU.is_lt, op1=ALU.mult), reads=[im, s_], writes=[bt_])
                        yield
                        pt = self.psT
                        S.op("pe", I("transpose", out=pt.t[:, 0:128], in_=bt_.t[:], identity=self.ident), reads=[bt_, cbf], writes=[pt])
                        S.op("act", I("copy", out=biasT.t[:, g, i * 128:(i + 1) * 128], in_=pt.t[:, 0:128]), reads=[pt], writes=[biasT])

                    gens = [gen(0), gen(1)]
                    while gens:
                        for gg in list(gens):
                            try:
                                next(gg)
                            except StopIteration:
                                gens.remove(gg)
            S.barrier()
            if self.stop_after == "B6":
                self.dump("biasT", biasT, biasT.t[:], [128, 2, OWN], BF16)
                self.dump("qbT", qbT, qbT.t[:], [128, 4, OWN], BF16)
                self.stopped = True
                return
            with ExitStack() as es:
                osb = self.ring(es, "b_osb", [128, 512], F32, 2)
                rdr = self.ring(es, "b_rd", [128, 512], F32, 2)
                ybacc = self.sb(es, "b_ybacc", [128, 512], F32)
                ybacc2 = self.sb(es, "b_ybacc2", [128, 512], F32)
                self.ptr = self.ring(es, "b_ptr", [128, 512], BF16, 5)
                gsel = self.ring(es, "b_gsel", [32, 128], F32, 6)
                id32 = self.cf32.t[0:32, F32C["id32"]:F32C["id32"] + 32]
                eown = pcb.t[:, PCB["eown"]:PCB["eown"] + 2048]
                e32 = cbf.t[:, BFC["e32"]:BFC["e32"] + 2048]
                m4 = cbf.t[:, BFC["m4"]:BFC["m4"] + 2048]
                tri_diag = cbf.t[:, BFC["tri_diag"]:BFC["tri_diag"] + 128]
                win_far = cbf.t[:, BFC["win_far"]:BFC["win_far"] + 128]
                BRS = os.environ.get("BRS", "012")
                ps4 = Ring([self.psS.items[0], self.psS.items[1], self.psA.items[0], self.psA.items[1]])
                ybaccs = [ybacc, ybacc2]
                for hp in range(4):
                    sels = {}
                    for gi in range(2):
                        h = hp + 4 * gi
                        for br in range(3):
                            gs = gsel.next()
                            jrow = h * 3 + br
                            S.op("pool", I("tensor_copy", out=gs.t[:], in_=id32[:, jrow:jrow + 1].to_broadcast([32, 128])), reads=[self.cf32], writes=[gs])
                            sels[(gi, br)] = gs
                    for c in range(4):
                        gcols = gbT.t[0:32, c * 512:(c + 1) * 512]
                        dst = self.ybT.t[:, hp, c * 512:(c + 1) * 512]
                        Qs = [qbT.t[64 * gi:64 * gi + 64, hp, c * 512:(c + 1) * 512] for gi in range(2)]

                        def fin(psos, br, first, last):
                            for gi in range(2):
                                o = osb.next()
                                S.op("act", I("copy", out=o.t[:], in_=psos[gi].t[:]), reads=[psos[gi]], writes=[o])
                                self.finalize(o, o.t[:], gi, (sels[(gi, br)], gbT, gcols), rdr, self.ybT, dst, first=first, last=last, ybacc=ybaccs[gi], gring=ps4)
                        psos = [self.psO.next(), self.psO.next()]
                        steps = []
                        for bt in range(4):
                            crel = pcf.t[:, PCF["crel"] + bt:PCF["crel"] + bt + 1]

                            def mask_c(pt, crel=crel, c=c):
                                S.op("dve", I("scalar_tensor_tensor", out=pt.t[:, 0:512], in0=t16.t[:, c * 512:(c + 1) * 512], scalar=crel, in1=pt.t[:, 0:512], op0=ALU.is_ge, op1=ALU.mult), reads=[t16, pcf, pt], writes=[pt])
                            stp = []
                            for gi in range(2):
                                pb = 64 * gi
                                stp.append(([(kcT.t[pb:pb + 64, bt * 128:(bt + 1) * 128], Qs[gi], [kcT, qbT])], 512, mask_c,
                                            [(psos[gi], psos[gi].t[:, 0:512], vc.t[:, bt, gi, :], 0, 512, bt == 0, bt == 3, [vc])]))
                            steps.append(stp)
                        self.attn_steps(steps, ps4)
                        fin(psos, 0, True, False)
                        psos = [self.psO.next(), self.psO.next()]
                        steps = []
                        for kt in range(48):
                            pb32 = 32 * (kt // 16)
                            kc_ = (kt % 16) * 128
                            stp = []
                            for gi in range(2):
                                pb = 64 * gi
                                stp.append(([(kslcT.t[pb:pb + 64, kt * 128:(kt + 1) * 128], Qs[gi], [kslcT, qbT]),
                                             (e32[pb32:pb32 + 32, kc_:kc_ + 128], biasT.t[pb32:pb32 + 32, gi, c * 512:(c + 1) * 512], [cbf, biasT])], 512, None,
                                            [(psos[gi], psos[gi].t[:, 0:512], vslc.t[:, kt, gi, :], 0, 512, kt == 0, False, [vslc])]))
                            steps.append(stp)
                        for j in range(4 * c + 4):
                            mf = None
                            if j >= 4 * c:
                                mk = m4[:, (j - 4 * c) * 512:(j - 4 * c + 1) * 512]

                                def mf(pt, mk=mk):
                                    self.mask_mul(pt, 0, 512, mk)
                            stp = []
                            for gi in range(2):
                                pb = 64 * gi
                                stp.append(([(kso.t[pb:pb + 64, j * 128:(j + 1) * 128], Qs[gi], [kso, qbT]),
                                             (eown[:, j * 128:(j + 1) * 128], biasT.t[:, gi, c * 512:(c + 1) * 512], [pcb, biasT])], 512, mf,
                                            [(psos[gi], psos[gi].t[:, 0:512], vso.t[:, j, gi, :], 0, 512, False, j == 4 * c + 3, [vso])]))
                            steps.append(stp)
                        self.attn_steps(steps, ps4)
                        fin(psos, 1, False, False)
                        psos = [self.psO.next(), self.psO.next()]
                        steps = []
                        for tq_ in range(4):
                            i = 4 * c + tq_
                            sq = 4 + i
                            for s_ in range(sq - 4, sq + 1):
                                mf = None
                                if s_ == sq - 4:
                                    def mf(pt):
                                        self.mask_mul(pt, 0, 128, win_far)
                                elif s_ == sq:
                                    def mf(pt):
                                        self.mask_mul(pt, 0, 128, tri_diag)
                                stp = []
                                for gi in range(2):
                                    pb = 64 * gi
                                    stp.append(([(kwinT.t[pb:pb + 64, s_ * 128:(s_ + 1) * 128], qbT.t[pb:pb + 64, hp, i * 128:(i + 1) * 128], [kwinT, qbT])], 128, mf,
                                                [(psos[gi], psos[gi].t[:, tq_ * 128:(tq_ + 1) * 128], vwin.t[:, s_, gi, :], 0, 128, s_ == sq - 4, s_ == sq, [vwin])]))
                                steps.append(stp)
                        self.attn_steps(steps, ps4)
                        fin(psos, 2, False, True)
        S.barrier()

    def phase_C(self, es0):
        S = self.S
        with ExitStack() as es:
            slabs = self.ring(es, "c_slab", [128, 8, 512], BF16, 5)
            self.wbslab = self.sb(es, "c_wb", [128, 4, 512], BF16)
            gfin = self.sb(es, "c_gfin", [128, D], F32)
            xc = self.sb(es, "c_xc", [128, 4, D], F32)
            uTc = self.sb(es, "c_uTc", [128, 8, 512], BF16)
            mTc = self.sb(es, "c_mTc", [128, 8, 512], BF16)
            u2Tc = self.sb(es, "c_u2Tc", [128, 8, 512], BF16)
            hT = self.sb(es, "c_hT", [128, 32, 512], BF16)
            sg = self.ring(es, "c_sg", [128, 512], BF16, 2)
            tf = self.ring(es, "c_tf", [128, 512], F32, 3)
            S.dma(I("dma_start", out=gfin.t[:], in_=self.g_fin), writes=[gfin])

            slab_ids = {id(t): i for i, t in enumerate(slabs.items)}

            def slab_from(src_ap, kchunks, gain):
                sl = slabs.next()
                fns = [I("dma_start", out=sl.t[:, 0:kchunks, c0:c0 + 256], in_=src_ap[:, c0:c0 + 256].rearrange("(c p) n -> p c n", p=128)) for c0 in (0, 256)]
                S.dma_sw(fns, [sl], slab_ids[id(sl)])
                return sl

            for c in range(4):
                cs = slice(c * 512, (c + 1) * 512)
                for tt in range(4):
                    r0 = c * 512 + tt * 128
                    S.dma(I("dma_start", out=xc.t[:, tt, :], in_=self.x_own[r0:r0 + 128, :]), writes=[xc])
                    self.norm_sb(xc, xc.t[:, tt, :], uTc, uTc.t[:, :, tt * 128:(tt + 1) * 128], keep_rstd=self.gmix)
                for ctg in range(2):
                    gA = slab_from(self.w_in[:, C_GM + ctg * 512:C_GM + ctg * 512 + 512], 8, self.gmix)
                    gB = slab_from(self.w_in[:, C_GM + 1024 + ctg * 512:C_GM + 1024 + ctg * 512 + 512], 8, self.gmix)
                    wa = slab_from(self.w_a[:, ctg * 512:(ctg + 1) * 512], 4, None)
                    wb = self.wbslab
                    fns = [I("dma_start", out=wb.t[64 * two:64 * two + 64, 0:4, 0:512],
                             in_=self.w_b[two * 256:(two + 1) * 256, ctg * 512:ctg * 512 + 512].rearrange("(hp d) n -> d hp n", d=64)) for two in range(2)]
                    S.dma_sw(fns, [wb], 99)
                    for j in range(4):
                        ct = ctg * 4 + j
                        js = slice(j * 128, (j + 1) * 128)
                        sgs = []
                        for gw in (gA, gB):
                            ps = self.psA.next()
                            for k in range(8):
                                S.op("pe", I("matmul", ps.t[:, 0:512], lhsT=gw.t[:, k, js], rhs=uTc.t[:, k, :], start=(k == 0), stop=(k == 7)), reads=[gw, uTc], writes=[ps])
                            sgt = sg.next()
                            S.op("act", I("activation", out=sgt.t[:], in_=ps.t[:, 0:512], func=AF.Sigmoid), reads=[ps], writes=[sgt])
                            sgs.append(sgt)
                        psa = self.psS.next()
                        for k in range(4):
                            S.op("pe", I("matmul", psa.t[:, 0:512], lhsT=wa.t[:, k, js], rhs=self.yaT.t[:, k, cs], start=(k == 0), stop=(k == 3)), reads=[wa, self.yaT], writes=[psa])
                        psb = self.psO.next()
                        for k in range(4):
                            S.op("pe", I("matmul", psb.t[:, 0:512], lhsT=wb.t[:, k, js], rhs=self.ybT.t[:, k, cs], start=(k == 0), stop=(k == 3)), reads=[wb, self.ybT], writes=[psb])
                        t0 = tf.next()
                        t1 = tf.next()
                        S.op("dve", I("tensor_tensor", out=t0.t[:], in0=psa.t[:, 0:512], in1=sgs[0].t[:], op=ALU.mult), reads=[psa, sgs[0]], writes=[t0])
                        S.op("dve", I("tensor_tensor", out=t1.t[:], in0=psb.t[:, 0:512], in1=sgs[1].t[:], op=ALU.mult), reads=[psb, sgs[1]], writes=[t1])
                        S.op("dve", I("tensor_tensor", out=mTc.t[:, ct, :], in0=t0.t[:], in1=t1.t[:], op=ALU.add), reads=[t0, t1], writes=[mTc])
                for nh in range(2):
                    wo = slab_from(self.w_out[:, nh * 512:(nh + 1) * 512], 8, None)
                    for tt in range(4):
                        ps = self.psA.next()
                        for k in range(8):
                            S.op("pe", I("matmul", ps.t[:, 0:512], lhsT=mTc.t[:, k, tt * 128:(tt + 1) * 128], rhs=wo.t[:, k, :], start=(k == 0), stop=(k == 7)), reads=[wo, mTc], writes=[ps])
                        S.op("dve", I("tensor_tensor", out=xc.t[:, tt, nh * 512:(nh + 1) * 512], in0=ps.t[:, 0:512], in1=xc.t[:, tt, nh * 512:(nh + 1) * 512], op=ALU.add), reads=[ps, xc], writes=[xc])
                for tt in range(4):
                    self.norm_sb(xc, xc.t[:, tt, :], u2Tc, u2Tc.t[:, :, tt * 128:(tt + 1) * 128], keep_rstd=self.gmlp)
                for s_ in range(8):
                    wu = slab_from(self.w_up[:, s_ * 512:(s_ + 1) * 512], 8, self.gmlp)
                    for j in range(4):
                        ft = 4 * s_ + j
                        ps = self.psA.next()
                        for k in range(8):
                            S.op("pe", I("matmul", ps.t[:, 0:512], lhsT=wu.t[:, k, j * 128:(j + 1) * 128], rhs=u2Tc.t[:, k, :], start=(k == 0), stop=(k == 7)), reads=[wu, u2Tc], writes=[ps])
                        r = tf.next()
                        S.op("act", I("activation", out=r.t[:], in_=ps.t[:, 0:512], func=AF.Relu), reads=[ps], writes=[r])
                        S.op("dve", I("tensor_tensor", out=hT.t[:, ft, :], in0=r.t[:], in1=r.t[:], op=ALU.mult), reads=[r], writes=[hT])
                accs = [self.psA.items[0], self.psA.items[1], self.psS.items[0], self.psS.items[1]]
                for nh in range(2):
                    for kg in range(4):
                        wd = slab_from(self.w_down[kg * 1024:(kg + 1) * 1024, nh * 512:(nh + 1) * 512], 8, None)
                        for tt in range(4):
                            for k in range(8):
                                S.op("pe", I("matmul", accs[tt].t[:, 0:512], lhsT=hT.t[:, kg * 8 + k, tt * 128:(tt + 1) * 128], rhs=wd.t[:, k, :], start=(kg == 0 and k == 0), stop=(kg == 3 and k == 7)), reads=[wd, hT], writes=[accs[tt]])
                    for tt in range(4):
                        S.op("dve", I("tensor_tensor", out=xc.t[:, tt, nh * 512:(nh + 1) * 512], in0=accs[tt].t[:, 0:512], in1=xc.t[:, tt, nh * 512:(nh + 1) * 512], op=ALU.add), reads=[accs[tt], xc], writes=[xc])
                for tt in range(4):
                    jk = self.junk.next()
                    st = self.stat.next()
                    S.op("act", I("activation", out=jk.t[:], in_=xc.t[:, tt, :], func=AF.Square, accum_out=st.t[:, 0:1]), reads=[xc], writes=[jk, st])
                    S.op("act", I("activation", out=st.t[:, 1:2], in_=st.t[:, 0:1], func=AF.Sqrt, scale=1.0 / D, bias=self.epsc.t[:, 0:1]), reads=[st, self.epsc], writes=[st])
                    S.op("dve", I("reciprocal", out=st.t[:, 2:3], in_=st.t[:, 1:2]), reads=[st], writes=[st])
                    S.op("dve", I("scalar_tensor_tensor", out=xc.t[:, tt, :], in0=xc.t[:, tt, :], scalar=st.t[:, 2:3], in1=gfin.t[:], op0=ALU.mult, op1=ALU.mult), reads=[xc, st, gfin], writes=[xc])
                    r0 = c * 512 + tt * 128
                    S.dma(I("dma_start", out=self.out[r0:r0 + 128, :], in_=xc.t[:, tt, :]), reads=[xc])

def make_in_maps(inputs):
    x = np.ascontiguousarray(np.asarray(inputs["x"], np.float32))
    cbf, cf32, t16 = _static_tables()
    sq = lambda n: np.ascontiguousarray(np.asarray(inputs[n], np.float32)[0])
    gl = lambda v: np.ascontiguousarray(np.asarray(v, np.float32).reshape(8, 128).T)
    common = {
        "w_in": sq("w_in"), "g_mix": gl(inputs["norm_mix_g"][0]), "g_mlp": gl(inputs["norm_mlp_g"][0]),
        "g_fin": np.ascontiguousarray(np.broadcast_to(np.asarray(inputs["norm_final_g"], np.float32)[None, :], (128, D))),
        "cmp_w1_k": sq("cmp_w1_k"), "cmp_w1_v": sq("cmp_w1_v"), "cmp_w2_k": sq("cmp_w2_k"), "cmp_w2_v": sq("cmp_w2_v"),
        "cmp_pos_k": sq("cmp_pos_k"), "cmp_pos_v": sq("cmp_pos_v"),
        "w_a": sq("w_branch_a"), "w_b": sq("w_branch_b"), "w_out": sq("w_out"), "w_up": sq("w_up"), "w_down": sq("w_down"),
        "c_bf": cbf, "c_f32": cf32, "c_t16": t16,
    }
    tabs = [_percore_tables(q) for q in range(4)]
    maps = []
    for c in range(8):
        b, q = c // 4, c % 4
        T0 = OWN * q
        halo = x[b, T0 - OWN:T0] if q > 0 else np.zeros((OWN, D), np.float32)
        m = dict(common)
        m.update({"x_own": np.ascontiguousarray(x[b, T0:T0 + OWN]), "x_halo": np.ascontiguousarray(halo), "x_full": x[b],
                  "pc_f": tabs[q][0], "pc_lohi": tabs[q][1], "pc_bf": tabs[q][2]})
        maps.append(m)
    return maps


_CACHE = {}


def kernel(**inputs):
    if "nc" not in _CACHE:
        b = Builder()
        _CACHE["nc"] = b.build()
        _CACHE["decl"] = set(b._decl.keys())
    nc = _CACHE["nc"]
    maps = make_in_maps(inputs)
    decl = _CACHE["decl"]
    maps = [{k: v for k, v in m.items() if k in decl} for m in maps]
    res = run_bass_kernel_spmd(nc, maps, core_ids=list(range(8)))
    out = np.zeros((2, S_LEN, D), np.float32)
    for c in range(8):
        b, q = c // 4, c % 4
        out[b, OWN * q:OWN * (q + 1)] = res.results[c]["out"]
    return out
```
